# Optimizing a Trainium2 kernel written in Bass

```python
import jax
import jax.numpy as jnp
from jax import lax
import numpy as np

D_MODEL = 2048
BATCH = 8
SEQ = 2048
DEPTH = 1

CTX_LEN = 256
GRID_W = 64
NORM_EPS = 1e-6
N_DIR = 2
N_BRANCH = 2
HG_WIDTH = 1024
HG_HEAD_DIM = 128
HG_HEADS = HG_WIDTH // HG_HEAD_DIM
HG_CHUNK = 64
RW_WIDTH = 1024
RW_HEAD_DIM = 64
RW_HEADS = RW_WIDTH // RW_HEAD_DIM
RW_DECAY_LORA = 64
RW_A_LORA = 64
RW_GN_EPS = 64e-5
HG_COLS = 5 * HG_WIDTH
RW_SHIFT_COLS = 3 * RW_WIDTH + N_DIR * (RW_DECAY_LORA + RW_A_LORA)
RW_COLS = RW_SHIFT_COLS + RW_WIDTH
GATE_COLS = N_BRANCH * D_MODEL
N_COLS = HG_COLS + RW_COLS + GATE_COLS

kernel_name = "hybrid_hgrn2_rwkv7_prefix_block"


def rmsnorm(x, g):
    x32 = x.astype(jnp.float32)
    return x32 * lax.rsqrt(jnp.mean(x32 * x32, axis=-1, keepdims=True) + NORM_EPS) * g


def ada_modulation(cond, w, b):
    mod = jax.nn.silu(cond.astype(jnp.float32)) @ w + b
    shift, scale, gate = jnp.split(mod, 3, axis=-1)
    return shift[:, None], scale[:, None], gate[:, None]


def grid_shift(p, mu, rows, cols, vertical):
    b, t, ch = p.shape
    p4 = p.reshape(b, rows, cols, ch)
    zc = jnp.zeros_like(p4[:, :, :1])
    left = jnp.concatenate([zc, p4[:, :, :-1]], axis=2)
    right = jnp.concatenate([p4[:, :, 1:], zc], axis=2)
    out = p4 + mu[0] * (left - p4) + mu[1] * (right - p4)
    if vertical:
        zr = jnp.zeros_like(p4[:, :1])
        up = jnp.concatenate([zr, p4[:, :-1]], axis=1)
        down = jnp.concatenate([p4[:, 1:], zr], axis=1)
        out = out + mu[2] * (up - p4) + mu[3] * (down - p4)
    return out.reshape(b, t, ch)


def hgrn2_chunk_scan(q, k, v, g, s0):
    b, t, h, _ = q.shape
    dv = v.shape[-1]
    n = t // HG_CHUNK

    def to_chunks(a):
        return a.reshape(b, n, HG_CHUNK, h, a.shape[-1]).transpose(1, 0, 3, 2, 4)

    tri = jnp.tril(jnp.ones((HG_CHUNK, HG_CHUNK), dtype=bool))

    def step(s, inp):
        qi, ki, vi, gi = inp
        bc = jnp.cumsum(gi, axis=2)
        diff = bc[:, :, :, None, :] - bc[:, :, None, :, :]
        decay = jnp.where(tri[:, :, None], jnp.exp(jnp.minimum(diff, 0.0)), 0.0)
        attn = jnp.einsum('bhtk,bhsk,bhtsk->bhts', qi, ki, decay)
        o = jnp.einsum('bhts,bhsv->bhtv', attn, vi) + jnp.einsum('bhtk,bhkv->bhtv', qi * jnp.exp(bc), s)
        blast = bc[:, :, -1:, :]
        s_new = jnp.exp(blast[:, :, 0, :])[..., None] * s + jnp.einsum('bhsk,bhsv->bhkv', ki * jnp.exp(blast - bc), vi)
        return s_new, o

    s_fin, oc = lax.scan(step, s0, (to_chunks(q), to_chunks(k), to_chunks(v), to_chunks(g)))
    return oc.transpose(1, 0, 3, 2, 4).reshape(b, t, h, dv), s_fin


def hgrn2_mixer(p, lb, norm_g, s0):
    b, t, _ = p.shape
    q_raw, i_in, f_fwd, f_bwd, z = jnp.split(p, 5, axis=-1)
    heads = lambda a: a.reshape(b, t, HG_HEADS, HG_HEAD_DIM)
    flip = lambda a: jnp.flip(a, axis=1)
    q = heads(jax.nn.silu(q_raw))
    v = heads(i_in)
    fg_f = lb[0] + (1.0 - lb[0]) * jax.nn.sigmoid(f_fwd)
    fg_b = lb[1] + (1.0 - lb[1]) * jax.nn.sigmoid(f_bwd)
    o_f, s_f = hgrn2_chunk_scan(q, heads(1.0 - fg_f), v, heads(jnp.log(fg_f)), s0[0])
    o_b, s_b = hgrn2_chunk_scan(flip(q), flip(heads(1.0 - fg_b)), flip(v), flip(heads(jnp.log(fg_b))), s0[1])
    o = o_f + flip(o_b)
    o = o * lax.rsqrt(jnp.mean(o * o, axis=-1, keepdims=True) + NORM_EPS)
    out = o.reshape(b, t, HG_WIDTH) * norm_g * jax.nn.silu(z)
    return out, jnp.stack([s_f, s_b])


def rwkv7_scan(r, w, kk, kb, v, k, s0, reverse):
    tm = lambda a: jnp.moveaxis(a, 1, 0)

    def step(s, inp):
        r_t, w_t, kk_t, b_t, v_t, k_t = inp
        sa = jnp.einsum('bhvk,bhk->bhv', s, kk_t)
        s = s * w_t[:, :, None, :] - sa[..., None] * b_t[:, :, None, :] + v_t[..., None] * k_t[:, :, None, :]
        return s, jnp.einsum('bhvk,bhk->bhv', s, r_t)

    s_fin, y = lax.scan(step, s0, (tm(r), tm(w), tm(kk), tm(kb), tm(v), tm(k)), reverse=reverse)
    return jnp.moveaxis(y, 0, 1), s_fin


def rwkv7_mixer(p, rw, s0, rows, cols, vertical):
    mu, w0, w2, a0, a2, k_k, k_a, r_k, gn_g, gn_b = rw
    b, t, _ = p.shape
    heads = lambda a: a.reshape(b, t, RW_HEADS, RW_HEAD_DIM)
    sh = grid_shift(p[..., :RW_SHIFT_COLS], mu, rows, cols, vertical)
    z = p[..., RW_SHIFT_COLS:]
    r = sh[..., :RW_WIDTH]
    k = sh[..., RW_WIDTH:2 * RW_WIDTH]
    v = sh[..., 2 * RW_WIDTH:3 * RW_WIDTH]
    lora = sh[..., 3 * RW_WIDTH:]
    w_lo = lora[..., :N_DIR * RW_DECAY_LORA].reshape(b, t, N_DIR, RW_DECAY_LORA)
    a_lo = lora[..., N_DIR * RW_DECAY_LORA:].reshape(b, t, N_DIR, RW_A_LORA)
    kk = heads(k * k_k)
    kk = kk * lax.rsqrt(jnp.sum(kk * kk, axis=-1, keepdims=True) + 1e-12)
    y_sum = 0.0
    k_sum = 0.0
    states = []
    for d in range(N_DIR):
        w_log = -jax.nn.softplus(-(w0[d] + jnp.tanh(w_lo[:, :, d]) @ w2[d])) - 0.5
        decay = jnp.exp(-jnp.exp(w_log))
        a = jax.nn.sigmoid(a0[d] + a_lo[:, :, d] @ a2[d])
        k_d = k * (1.0 + (a - 1.0) * k_a)
        y_d, s_d = rwkv7_scan(heads(r), heads(decay), kk, kk * heads(a), heads(v), heads(k_d), s0[d], d == 1)
        y_sum = y_sum + y_d
        k_sum = k_sum + k_d
        states.append(s_d)
    mean = jnp.mean(y_sum, axis=-1, keepdims=True)
    var = jnp.mean(jnp.square(y_sum - mean), axis=-1, keepdims=True)
    y = ((y_sum - mean) * lax.rsqrt(var + RW_GN_EPS)).reshape(b, t, RW_WIDTH) * gn_g + gn_b
    bonus = jnp.sum(heads(r * k_sum * r_k), axis=-1, keepdims=True) * heads(v)
    out = (y + bonus.reshape(b, t, RW_WIDTH)) * jax.nn.silu(z)
    return out, jnp.stack(states)


def mixer_stream(s, cond, ada_w, ada_b, norm_g, w_in, lb, hg_norm_g, rw, hg_s0, rw_s0, rows, cols, vertical):
    shift, scale, gate = ada_modulation(cond, ada_w, ada_b)
    h = rmsnorm(s, norm_g) * (1.0 + scale) + shift
    proj = h @ w_in
    y_hg, hg_s = hgrn2_mixer(proj[..., :HG_COLS], lb, hg_norm_g, hg_s0)
    y_rw, rw_s = rwkv7_mixer(proj[..., HG_COLS:HG_COLS + RW_COLS], rw, rw_s0, rows, cols, vertical)
    return y_hg, y_rw, proj[..., HG_COLS + RW_COLS:], gate, hg_s, rw_s


def merge_branches(gate_p, y_hg, y_rw, w_hg_o, w_rw_o, w_o):
    g_hg, g_rw = jnp.split(gate_p, N_BRANCH, axis=-1)
    merged = jax.nn.sigmoid(g_hg) * (y_hg @ w_hg_o) + jax.nn.sigmoid(g_rw) * (y_rw @ w_rw_o)
    return merged @ w_o


def setup_inputs(seed: int = 0) -> dict:
    key = jax.random.key(seed)
    ks = jax.random.split(key, 26)
    L, D = DEPTH, D_MODEL
    nrm = lambda k, shape, s: jax.random.normal(k, shape, jnp.float32) * s
    return {
        'x': nrm(ks[0], (BATCH, SEQ, D), 1.0),
        'c': nrm(ks[1], (BATCH, D), 1.0),
        'ctx': nrm(ks[2], (BATCH, CTX_LEN, D), 1.0),
        'c_ctx': nrm(ks[3], (D,), 1.0),
        'ada_w': nrm(ks[4], (L, D, 3 * D), 0.5 * D ** -0.5),
        'ada_b': nrm(ks[5], (L, 3 * D), 0.02),
        'norm_g': 1.0 + nrm(ks[6], (L, D), 0.02),
        'w_in': nrm(ks[7], (L, D, N_COLS), D ** -0.5),
        'hg_lb': nrm(ks[8], (N_DIR, L + 1, HG_WIDTH), 1.0),
        'hg_norm_g': 1.0 + nrm(ks[9], (L, HG_WIDTH), 0.02),
        'rw_mu': jax.random.uniform(ks[10], (L, 4, RW_SHIFT_COLS), jnp.float32, 0.0, 0.5),
        'rw_w0': jax.random.uniform(ks[11], (L, N_DIR, RW_WIDTH), jnp.float32, -5.0, 0.0),
        'rw_w2': nrm(ks[12], (L, N_DIR, RW_DECAY_LORA, RW_WIDTH), 0.5 * RW_DECAY_LORA ** -0.5),
        'rw_a0': nrm(ks[13], (L, N_DIR, RW_WIDTH), 0.1),
        'rw_a2': nrm(ks[14], (L, N_DIR, RW_A_LORA, RW_WIDTH), 0.5 * RW_A_LORA ** -0.5),
        'rw_kk': 0.85 + nrm(ks[15], (L, RW_WIDTH), 0.1),
        'rw_ka': 1.0 + nrm(ks[16], (L, RW_WIDTH), 0.1),
        'rw_rk': nrm(ks[17], (L, RW_WIDTH), 0.1),
        'rw_gn_g': 1.0 + nrm(ks[18], (L, RW_WIDTH), 0.02),
        'rw_gn_b': nrm(ks[19], (L, RW_WIDTH), 0.02),
        'w_hg_out': nrm(ks[20], (L, HG_WIDTH, D), HG_WIDTH ** -0.5),
        'w_rw_out': nrm(ks[21], (L, RW_WIDTH, D), RW_WIDTH ** -0.5),
        'w_out': nrm(ks[22], (L, D, D), D ** -0.5),
        'final_g': 1.0 + nrm(ks[23], (D,), 0.02),
    }


def reference(x, c, ctx, c_ctx, ada_w, ada_b, norm_g, w_in, hg_lb, hg_norm_g, rw_mu, rw_w0, rw_w2, rw_a0,
              rw_a2, rw_kk, rw_ka, rw_rk, rw_gn_g, rw_gn_b, w_hg_out, w_rw_out, w_out, final_g):
    out_dtype = x.dtype
    b, seq, _ = x.shape
    ctx_len = ctx.shape[1]
    rows = seq // GRID_W
    x = x.astype(jnp.float32)
    ctx = ctx.astype(jnp.float32)
    lb_all = jnp.cumsum(jax.nn.softmax(hg_lb.astype(jnp.float32), axis=1), axis=1)
    hg_zero = jnp.zeros((N_DIR, b, HG_HEADS, HG_HEAD_DIM, HG_HEAD_DIM), jnp.float32)
    rw_zero = jnp.zeros((N_DIR, b, RW_HEADS, RW_HEAD_DIM, RW_HEAD_DIM), jnp.float32)
    for l in range(DEPTH):
        lb = lb_all[:, l]
        rw = (rw_mu[l], rw_w0[l], rw_w2[l], rw_a0[l], rw_a2[l], rw_kk[l], rw_ka[l], rw_rk[l], rw_gn_g[l], rw_gn_b[l])
        yc_hg, yc_rw, gc, gate_c, hg_s, rw_s = mixer_stream(
            ctx, c_ctx[None], ada_w[l], ada_b[l], norm_g[l], w_in[l], lb, hg_norm_g[l], rw,
            hg_zero, rw_zero, 1, ctx_len, False)
        y_hg, y_rw, gl, gate_x, _, _ = mixer_stream(
            x, c, ada_w[l], ada_b[l], norm_g[l], w_in[l], lb, hg_norm_g[l], rw,
            hg_s, rw_s, rows, GRID_W, True)
        x = x + gate_x * merge_branches(gl, y_hg, y_rw, w_hg_out[l], w_rw_out[l], w_out[l])
        if l < DEPTH - 1:
            ctx = ctx + gate_c * merge_branches(gc, yc_hg, yc_rw, w_hg_out[l], w_rw_out[l], w_out[l])
    return rmsnorm(x, final_g).astype(out_dtype)
```

```python
import os
import numpy as np
from contextlib import ExitStack
import concourse.bass as bass
import concourse.mybir as mybir
from concourse.bass_utils import run_bass_kernel_spmd

F32 = mybir.dt.float32
AF = mybir.ActivationFunctionType
ALU = mybir.AluOpType
AX = mybir.AxisListType

D = 2048
SEQ = 2048
CTX = 256
NT = SEQ + CTX
NCOLS = 13568
HGW = 1024
RWW = 1024
CH = 32
DEBUG = bool(os.environ.get("KDEBUG"))
PH = int(os.environ.get("KPHASE", "9"))
SQ = os.environ.get("KSQ", "sp")
ONLY = int(os.environ.get("KONLY", "-1"))
KLIM = int(os.environ.get("KLIM", "-1"))
KCH = int(os.environ.get("KCH", "99"))
KX = int(os.environ.get("KX", "0"))
KD = int(os.environ.get("KD", "2"))
KT = int(os.environ.get("KT", "9"))
KJ = int(os.environ.get("KJ", "8"))
KT0 = int(os.environ.get("KT0", "0"))


_FWREF = []


def _phase(n):
    if PH >= n and (ONLY < 0 or n == ONLY):
        fw = _FWREF[-1]
        fw.emitted = 0
        fw.limit = KLIM if (n == ONLY and KLIM >= 0) else None
        with ExitStack() as ph:
            yield ph
        print('phase', n, 'emitted', fw.emitted, flush=True)
        fw.limit = None


class Buf:
    __slots__ = ("name", "w", "r")

    def __init__(self, name=""):
        self.name = name
        self.w = None
        self.r = {}


class FW:
    NDMA = 14

    def __init__(self, nc, stack):
        self.nc = nc
        self.eng = {"pe": nc.tensor, "act": nc.scalar, "dve": nc.vector, "pool": nc.gpsimd, "sp": nc.sync}
        self.sem = {}
        self.cnt = {}
        for k in ["pe", "act", "dve", "pool"]:
            self.sem[k] = stack.enter_context(nc.semaphore("s_" + k))
            self.cnt[k] = 0
        for i in range(self.NDMA):
            k = "dma%d" % i
            self.sem[k] = stack.enter_context(nc.semaphore("s_" + k))
            self.cnt[k] = 0
        self.dma_i = 0
        self.waited = {e: {} for e in self.eng}
        self.ninstr = 0
        self.emitted = 0
        self.limit = None

    def _need(self, e, deps):
        for k, v in deps.items():
            if self.waited[e].get(k, 0) >= v:
                continue
            self.eng[e].wait_ge(self.sem[k], v)
            self.waited[e][k] = v

    def _collect(self, e, reads, writes):
        deps = {}

        def add(ev):
            if ev is None:
                return
            k, v = ev
            if k == e and e == "pe":
                return
            if deps.get(k, 0) < v:
                deps[k] = v
        for b in reads:
            add(b.w)
        for b in writes:
            add(b.w)
            for k, v in b.r.items():
                add((k, v))
        return deps

    def _mark(self, ev, reads, writes):
        k, v = ev
        for b in reads:
            if b.r.get(k, 0) < v:
                b.r[k] = v
        for b in writes:
            b.w = ev
            b.r = {}

    def _skip(self):
        self.emitted += 1
        return self.limit is not None and self.emitted > self.limit

    def op(self, e, fn, reads=(), writes=(), inc=True):
        if self._skip():
            return None
        deps = self._collect(e, reads, writes)
        self._need(e, deps)
        ins = fn(self.eng[e])
        self.ninstr += 1
        if inc:
            self.cnt[e] += 1
            ins.then_inc(self.sem[e], 1)
            ev = (e, self.cnt[e])
        else:
            ev = (e, self.cnt[e] + 1)
        self._mark(ev, reads, writes)
        return ins

    def dma(self, out, in_, reads=(), writes=(), q="sp", **kw):
        if self._skip():
            return
        i = self.dma_i
        self.dma_i += 1
        k = "dma%d" % (i % self.NDMA)
        deps = self._collect(q, reads, writes)
        if self.cnt[k] > 0 and deps.get(k, 0) < self.cnt[k]:
            deps[k] = self.cnt[k]
        self._need(q, deps)
        self.cnt[k] += 16
        self.eng[q].dma_start(out=out, in_=in_, **kw).then_inc(self.sem[k], 16)
        self.ninstr += 1
        self._mark((k, self.cnt[k]), reads, writes)

    def dma3(self, out, in_, n, **kw):
        for k in range(n):
            self.dma(out[:, k], in_[:, k], **kw)

    def barrier(self):
        for e in self.eng:
            deps = {k: v for k, v in self.cnt.items() if v > 0 and k != e}
            self._need(e, deps)


C_ID, C_TF, C_TB, C_BONES, C_LT, C_GT = 0, 1, 2, 3, 4, 5
NCM = 6


def make_cm():
    p = np.arange(128)[:, None]
    f = np.arange(128)[None, :]
    cm = np.zeros((128, NCM, 128), np.float32)
    cm[:, C_ID] = (p == f)
    cm[:, C_TF] = (p <= f)
    cm[:, C_TB] = (p >= f)
    cm[:, C_BONES] = ((p // 64) == (f // 64))
    cm[:, C_LT] = (p < f)
    cm[:, C_GT] = (p > f)
    return cm


def build_program():
    nc = bass.Bass("TRN2", target_bir_lowering=False)
    dt = lambda name, shape, kind="ExternalInput": nc.dram_tensor(name, shape, F32, kind=kind).ap()
    SCR = "ExternalOutput" if DEBUG else "Internal"
    xc = dt("xc", [NT, D])
    cc_d = dt("cc", [128, 16, 2])
    ada_w = dt("ada_w", [16, 128, 3 * D])
    ada_b_fm = dt("ada_b_fm", [128, 48])
    ada_b_g = dt("ada_b_g", [1, D])
    norm_g_fm = dt("norm_g_fm", [128, 16])
    w_in = dt("w_in", [D, NCOLS])
    hg_lb_b = dt("hg_lb_b", [128, 2, 2, HGW])
    hgng_b = dt("hgng_b", [128, HGW])
    mu_fm = dt("mu_fm", [128, 26, 4])
    w0_fm = dt("w0_fm", [128, 8, 2])
    a0_fm = dt("a0_fm", [128, 8, 2])
    w2_d = dt("w2", [128, RWW])
    a2_d = dt("a2", [128, RWW])
    kvec_fm = dt("kvec_fm", [128, 8, 3])
    gng_b = dt("gng_b", [128, RWW])
    gnb_b = dt("gnb_b", [128, RWW])
    w_hg_o = dt("w_hg_o", [HGW, D])
    w_rw_o = dt("w_rw_o", [RWW, D])
    w_o = dt("w_o", [D, D])
    fg_b = dt("fg_b", [128, D])
    cm_d = dt("cm", [128, NCM, 128])
    cst_d = dt("cst", [128, 8])
    out = dt("out", [SEQ, D], kind="ExternalOutput")
    HGtok = dt("HGtok", [NT, 5120], SCR)
    RWfm = dt("RWfm", [NCOLS - 5120, NT], SCR)
    OFs = dt("OFs", [SEQ, HGW], SCR)
    YH = dt("YH", [HGW, SEQ], SCR)
    AL = [dt("AL%d" % d, [RWW, NT], SCR) for d in range(2)]
    BE = [dt("BE%d" % d, [RWW, NT], SCR) for d in range(2)]
    KA = [dt("KA%d" % d, [RWW, NT], SCR) for d in range(2)]
    RH = [dt("RH%d" % d, [RWW, NT], SCR) for d in range(2)]
    BEt = [dt("BEt%d" % d, [NT, RWW], SCR) for d in range(2)]
    KAt = [dt("KAt%d" % d, [NT, RWW], SCR) for d in range(2)]
    Vt = dt("Vt", [NT, RWW], SCR)
    BON = dt("BON", [RWW, SEQ], SCR)
    YF = dt("YF", [SEQ, RWW], SCR)
    YR = dt("YR", [RWW, SEQ], SCR)
    scr_bufs = {}

    def sbuf_of(name):
        if name not in scr_bufs:
            scr_bufs[name] = Buf(name)
        return scr_bufs[name]

    with ExitStack() as st:
        fw = FW(nc, st)
        _FWREF.append(fw)
        _acct = {}

        def sbt(stack, name, shape):
            _acct[id(stack)] = _acct.get(id(stack), 0) + int(np.prod(shape[1:])) * 4
            if os.environ.get("KACCT"):
                print("sbuf", name, shape, "stack total KiB", _acct[id(stack)] / 1024.0, flush=True)
            return stack.enter_context(nc.sbuf_tensor("sb_" + name, shape, F32))
        pst = lambda stack, name, shape: stack.enter_context(nc.psum_tensor("ps_" + name, shape, F32))
        op = fw.op
        cm = sbt(st, "cm", [128, NCM, 128]); b_cm = Buf()
        cst = sbt(st, "cst", [128, 8]); b_cst = Buf()
        modA = sbt(st, "modA", [128, 16, 2]); modB = sbt(st, "modB", [128, 16, 2]); b_mod = Buf()
        gate_b = sbt(st, "gate_b", [128, D]); b_gate = Buf()
        wc = [sbt(st, "wc%d" % d, [128, 8, 18]) for d in range(2)]; b_wc = [Buf(), Buf()]
        fw.dma(cm[:], cm_d, writes=[b_cm])
        fw.dma(cst[:], cst_d, writes=[b_cst])
        ident = cm[:, C_ID, :]
        EPS6, EPS12, EPSGN, ZERO, ONE = (cst[:, i:i + 1] for i in range(5))

        for ph in _phase(0):
            adw = [sbt(ph, "adw%d" % i, [128, 3 * D]) for i in range(2)]; b_adw = [Buf(), Buf()]
            cct = sbt(ph, "cct", [128, 16, 2]); scc = sbt(ph, "scc", [128, 16, 2]); b_cc = Buf(); b_scc = Buf()
            mod = sbt(ph, "mod", [128, 48, 2]); b_modt = Buf()
            adb = sbt(ph, "adb", [128, 48]); ng = sbt(ph, "ng", [128, 16]); b_sm = Buf()
            adbg = sbt(ph, "adbg", [1, D]); grow = sbt(ph, "grow", [1, D]); b_grow = Buf()
            ps_mod = pst(ph, "ps_mod", [128, 96]); b_psm = Buf()
            ps_g = [pst(ph, "ps_g%d" % i, [128, 512]) for i in range(4)]; b_psg = [Buf() for _ in range(4)]
            fw.dma(cct[:], cc_d, writes=[b_cc])
            fw.dma(adb[:], ada_b_fm, writes=[b_sm])
            fw.dma(ng[:], norm_g_fm, writes=[b_sm])
            fw.dma(adbg[:], ada_b_g, writes=[b_sm])
            op("act", lambda e: e.activation(scc[:], cct[:], AF.Silu), reads=[b_cc], writes=[b_scc])
            op("dve", lambda e: e.memset(mod[:], 0.0), writes=[b_modt])
            for k in range(16):
                fw.dma(adw[k % 2][:], ada_w[k], writes=[b_adw[k % 2]])
                for m in range(48):
                    op("pe", lambda e: e.matmul(ps_mod[:, 2 * m:2 * m + 2], adw[k % 2][:, m * 128:(m + 1) * 128],
                                                scc[:, k, :], start=True, stop=True),
                       reads=[b_adw[k % 2], b_scc], writes=[b_psm], inc=(m == 47))
                op("dve", lambda e: e.tensor_tensor(mod[:], mod[:], ps_mod[:].rearrange("p (m v) -> p m v", v=2), ALU.add),
                   reads=[b_psm, b_modt], writes=[b_modt])
                for n in range(4):
                    op("pe", lambda e: e.matmul(ps_g[n][0:1, :], scc[:, k, 0:1], adw[k % 2][:, 2 * D + n * 512:2 * D + (n + 1) * 512],
                                                start=(k == 0), stop=(k == 15)),
                       reads=[b_adw[k % 2], b_scc], writes=[b_psg[n]])
            op("dve", lambda e: e.tensor_tensor(mod[:], mod[:], adb[:].unsqueeze(2).to_broadcast([128, 48, 2]), ALU.add),
               reads=[b_sm, b_modt], writes=[b_modt])
            op("dve", lambda e: e.tensor_scalar(modA[:], mod[:, 16:32, :], 1.0, None, ALU.add), reads=[b_modt], writes=[b_mod])
            op("dve", lambda e: e.tensor_tensor(modA[:], modA[:], ng[:].unsqueeze(2).to_broadcast([128, 16, 2]), ALU.mult),
               reads=[b_sm, b_mod], writes=[b_mod])
            op("dve", lambda e: e.tensor_copy(modB[:], mod[:, 0:16, :]), reads=[b_modt], writes=[b_mod])
            for n in range(4):
                op("dve", lambda e: e.tensor_tensor(grow[:, n * 512:(n + 1) * 512], ps_g[n][0:1, :], adbg[:, n * 512:(n + 1) * 512], ALU.add),
                   reads=[b_psg[n], b_sm], writes=[b_grow])
            for n in range(4):
                op("pe", lambda e: e.matmul(ps_g[n][:], cm[0:1, C_TF, :], grow[:, n * 512:(n + 1) * 512], start=True, stop=True),
                   reads=[b_cm, b_grow], writes=[b_psg[n]])
                op("act", lambda e: e.activation(gate_b[:, n * 512:(n + 1) * 512], ps_g[n][:], AF.Identity, scale=1.0),
                   reads=[b_psg[n]], writes=[b_gate])
        fw.barrier()

        for ph in _phase(1):
            lbt = sbt(ph, "lbt", [128, 2, 2, HGW]); lb_b = sbt(ph, "lb_b", [128, 2, HGW]); oml_b = sbt(ph, "oml_b", [128, 2, HGW])
            b_lbt = Buf(); b_lb = Buf()
            xt = [sbt(ph, "xt%d" % i, [128, D]) for i in range(2)]; b_xt = [Buf(), Buf()]
            junk = sbt(ph, "junk", [128, D]); b_junk = Buf()
            ss = sbt(ph, "ss", [128, 2]); b_ss = Buf()
            hT = sbt(ph, "hT", [128, 16, 512]); b_hT = [Buf() for _ in range(4)]
            wt = [sbt(ph, "wt%d" % i, [128, 16, 512]) for i in range(2)]; b_wt = [Buf(), Buf()]
            NOT = 4
            ot = [sbt(ph, "ot%d" % i, [128, 512]) for i in range(NOT)]; b_ot = [Buf() for _ in range(NOT)]
            tps = [pst(ph, "tps%d" % i, [128, 512]) for i in range(4)]; b_tps = [Buf() for _ in range(4)]
            acc = [pst(ph, "acc%d" % i, [128, 512]) for i in range(4)]; b_acc = [Buf() for _ in range(4)]
            fw.dma(lbt[:], hg_lb_b, writes=[b_lbt])
            op("dve", lambda e: e.tensor_tensor(lb_b[:], lbt[:, :, 0, :], lbt[:, :, 1, :], ALU.subtract), reads=[b_lbt], writes=[b_lb])
            op("act", lambda e: e.activation(lb_b[:], lb_b[:], AF.Sigmoid), reads=[b_lb], writes=[b_lb])
            op("dve", lambda e: e.tensor_scalar(oml_b[:], lb_b[:], -1.0, 1.0, ALU.mult, ALU.add), reads=[b_lb], writes=[b_lb])
            w_in_r = w_in.rearrange("(k p) c -> p k c", p=128)
            oti = 0
            ctx_groups = {2, 3, 4, 5, 6, 7, 12, 13, 14, 15, 16}
            tiles = [(0, 256, True)] + [(256 + 512 * i, 512, False) for i in range(4)]
            wti = 0
            for (g0, ntok, isctx) in tiles:
                v = 1 if isctx else 0
                nsub = ntok // 128
                for sub in range(nsub):
                    xb = sub % 2
                    fw.dma(xt[xb][:], xc[g0 + sub * 128:g0 + (sub + 1) * 128, :], writes=[b_xt[xb]])
                    op("act", lambda e: e.activation(junk[:], xt[xb][:], AF.Square, accum_out=ss[:, 0:1]),
                       reads=[b_xt[xb]], writes=[b_junk, b_ss])
                    op("act", lambda e: e.activation(ss[:, 1:2], ss[:, 0:1], AF.Sqrt, bias=EPS6, scale=1.0 / D),
                       reads=[b_ss, b_cst], writes=[b_ss])
                    op("dve", lambda e: e.reciprocal(ss[:, 1:2], ss[:, 1:2]), reads=[b_ss], writes=[b_ss])
                    op("act", lambda e: e.activation(junk[:], xt[xb][:], AF.Identity, scale=ss[:, 1:2], bias=ZERO),
                       reads=[b_xt[xb], b_ss, b_cst], writes=[b_junk])
                    for q in range(4):
                        for i in range(4):
                            j = q * 4 + i
                            op("pe", lambda e: e.transpose(tps[q][:, i * 128:(i + 1) * 128], junk[:, j * 128:(j + 1) * 128], ident),
                               reads=[b_junk, b_cm], writes=[b_tps[q]], inc=(i == 3))
                        for i in range(4):
                            j = q * 4 + i
                            en = "dve" if (i % 2 == 0) else "act"
                            if en == "dve":
                                op("dve", lambda e: e.tensor_scalar(hT[:, j, sub * 128:(sub + 1) * 128], tps[q][:, i * 128:(i + 1) * 128],
                                                                    modA[:, j, v:v + 1], modB[:, j, v:v + 1], ALU.mult, ALU.add),
                                   reads=[b_tps[q], b_mod], writes=[b_hT[sub]])
                            else:
                                op("act", lambda e: e.activation(hT[:, j, sub * 128:(sub + 1) * 128], tps[q][:, i * 128:(i + 1) * 128],
                                                                 AF.Identity, scale=modA[:, j, v:v + 1], bias=modB[:, j, v:v + 1]),
                                   reads=[b_tps[q], b_mod], writes=[b_hT[sub]])
                for g in range(27):
                    if isctx and g not in ctx_groups:
                        continue
                    c0 = g * 512
                    ncol = min(512, NCOLS - c0)
                    wb = wti % 2
                    wti += 1
                    fw.dma3(wt[wb][:, :, :ncol], w_in_r[:, :, c0:c0 + ncol], 16, writes=[b_wt[wb]])
                    if c0 < 5120:
                        typ = ["silu", "id", "fg0", "fg1", "silu"][c0 // 1024]
                        for sub in range(nsub):
                            a = sub % 4
                            for k in range(16):
                                op("pe", lambda e: e.matmul(acc[a][:, :ncol], hT[:, k, sub * 128:(sub + 1) * 128], wt[wb][:, k, :ncol],
                                                            start=(k == 0), stop=(k == 15)),
                                   reads=[b_hT[sub], b_wt[wb]], writes=[b_acc[a]], inc=(k == 15))
                            o_ = oti % NOT
                            oti += 1
                            if typ == "silu":
                                op("act", lambda e: e.activation(ot[o_][:], acc[a][:], AF.Silu), reads=[b_acc[a]], writes=[b_ot[o_]])
                            elif typ == "id":
                                op("dve", lambda e: e.tensor_copy(ot[o_][:], acc[a][:]), reads=[b_acc[a]], writes=[b_ot[o_]])
                            else:
                                dd = int(typ[2])
                                cc0 = c0 - (2048 + dd * 1024)
                                op("act", lambda e: e.activation(ot[o_][:], acc[a][:], AF.Sigmoid), reads=[b_acc[a]], writes=[b_ot[o_]])
                                op("dve", lambda e: e.tensor_tensor(ot[o_][:], ot[o_][:], oml_b[:, dd, cc0:cc0 + 512], ALU.mult),
                                   reads=[b_ot[o_], b_lb], writes=[b_ot[o_]])
                                op("dve", lambda e: e.tensor_tensor(ot[o_][:], ot[o_][:], lb_b[:, dd, cc0:cc0 + 512], ALU.add),
                                   reads=[b_ot[o_], b_lb], writes=[b_ot[o_]])
                            fw.dma(HGtok[g0 + sub * 128:g0 + (sub + 1) * 128, c0:c0 + 512], ot[o_][:],
                                   reads=[b_ot[o_]], writes=[sbuf_of("HGtok")], q=SQ)
                    else:
                        for m in range(ncol // 128):
                            mc = c0 // 128 + m
                            if isctx and not (48 <= mc <= 65):
                                continue
                            a = m % 4
                            for k in range(16):
                                op("pe", lambda e: e.matmul(acc[a][:, :ntok], wt[wb][:, k, m * 128:(m + 1) * 128], hT[:, k, :ntok],
                                                            start=(k == 0), stop=(k == 15)),
                                   reads=b_hT[:nsub] + [b_wt[wb]], writes=[b_acc[a]], inc=(k == 15))
                            o_ = oti % NOT
                            oti += 1
                            if mc <= 65:
                                op("dve", lambda e: e.tensor_copy(ot[o_][:, :ntok], acc[a][:, :ntok]), reads=[b_acc[a]], writes=[b_ot[o_]])
                            else:
                                fn = AF.Silu if mc <= 73 else AF.Sigmoid
                                op("act", lambda e: e.activation(ot[o_][:, :ntok], acc[a][:, :ntok], fn), reads=[b_acc[a]], writes=[b_ot[o_]])
                            fw.dma(RWfm[(mc - 40) * 128:(mc - 39) * 128, g0:g0 + ntok], ot[o_][:, :ntok],
                                   reads=[b_ot[o_]], writes=[sbuf_of("RWfm")], q=SQ)
        fw.barrier()

        for ph in _phase(2):
            NB = 2
            fgt = [sbt(ph, "fgt%d" % i, [CH, HGW]) for i in range(NB)]
            vtt = [sbt(ph, "vtt%d" % i, [CH, HGW]) for i in range(NB)]
            qtt = [sbt(ph, "qtt%d" % i, [CH, HGW]) for i in range(NB)]
            ztt = [sbt(ph, "ztt%d" % i, [CH, HGW]) for i in range(NB)]
            oft = [sbt(ph, "oft%d" % i, [CH, HGW]) for i in range(NB)]
            b_ld = [[Buf() for _ in range(5)] for _ in range(NB)]
            gt = sbt(ph, "gt", [CH, HGW]); b_gt = Buf()
            Et = sbt(ph, "Et", [CH, HGW]); Ei = sbt(ph, "Ei", [CH, HGW]); b_E = Buf(); b_Ei = Buf()
            kt_ = sbt(ph, "kt", [128, HGW]); qq_ = sbt(ph, "qq", [128, HGW]); b_kt = Buf(); b_qq = Buf()
            kt = kt_[0:CH, :]; qq = qq_[0:CH, :]
            ob = sbt(ph, "ob", [CH, HGW]); sq_ = sbt(ph, "sq", [128, HGW]); b_ob = Buf(); b_sq = Buf()
            sq = sq_[0:CH, :]
            ms = sbt(ph, "ms", [CH, 8]); b_ms = Buf()
            eb = sbt(ph, "eb", [128, 8]); b_eb = Buf()
            hng = sbt(ph, "hng", [CH, HGW]); b_hng = Buf()
            qkT = [sbt(ph, "qkT%d" % i, [128, 2, CH]) for i in range(2)]; b_qkT = [Buf(), Buf()]
            at = [sbt(ph, "at%d" % i, [CH, CH]) for i in range(2)]; b_at = [Buf(), Buf()]
            S = [sbt(ph, "S%d" % i, [128, 8, 128]) for i in range(2)]; b_S = [Buf(), Buf()]
            stmp = sbt(ph, "stmp", [128, 8, 128]); b_stmp = Buf()
            yT = sbt(ph, "yTs", [128, 8, 512]); b_yT = Buf()
            bc_ps = [pst(ph, "bc_ps%d" % i, [128, 512]) for i in range(2)]; b_bc = [Buf(), Buf()]
            o_ps = [pst(ph, "o_ps%d" % i, [128, 512]) for i in range(2)]; b_o = [Buf(), Buf()]
            dS_ps = [pst(ph, "dS_ps%d" % i, [128, 512]) for i in range(2)]; b_dS = [Buf(), Buf()]
            mz = pst(ph, "mz", [128, 512]); b_tp = [Buf()] * 2; b_aps = [Buf()] * 2; b_ebp = b_aps[0]
            yT_ps = pst(ph, "yT_ps", [128, 512]); b_yTp = Buf()
            fw.dma(hng[:], hgng_b[0:CH, :], writes=[b_hng])
            op("dve", lambda e: e.memset(kt_[:], 0.0), writes=[b_kt])
            op("dve", lambda e: e.memset(qq_[:], 0.0), writes=[b_qq])
            op("dve", lambda e: e.memset(sq_[:], 0.0), writes=[b_sq])
            ci = 0
            for d in range(min(2, KD)):
                tri = cm[0:CH, C_TF if d == 0 else C_TB, 0:CH]
                lastc = CH - 1 if d == 0 else 0
                onehot = cm[0:CH, C_ID, lastc:lastc + 1]
                order = list(range(NT // CH)) if d == 0 else list(range(CTX // CH - 1, -1, -1)) + list(range(NT // CH - 1, CTX // CH - 1, -1))
                cur = 0
                op("dve", lambda e: e.memset(S[0][:], 0.0), writes=[b_S[0]])
                for c in order[:KCH]:
                    isctx = c < CTX // CH
                    t0 = c * CH
                    lb_ = ci % NB
                    ci += 1
                    fgs, vs, qs, zs, ofs = fgt[lb_], vtt[lb_], qtt[lb_], ztt[lb_], oft[lb_]
                    bl = b_ld[lb_]
                    fw.dma(fgs[:], HGtok[t0:t0 + CH, 2048 + d * 1024:3072 + d * 1024], reads=[sbuf_of("HGtok")], writes=[bl[0]])
                    fw.dma(vs[:], HGtok[t0:t0 + CH, 1024:2048], reads=[sbuf_of("HGtok")], writes=[bl[1]])
                    if not isctx:
                        fw.dma(qs[:], HGtok[t0:t0 + CH, 0:1024], reads=[sbuf_of("HGtok")], writes=[bl[2]])
                        if d == 1:
                            fw.dma(zs[:], HGtok[t0:t0 + CH, 4096:5120], reads=[sbuf_of("HGtok")], writes=[bl[3]])
                            fw.dma(ofs[:], OFs[t0 - CTX:t0 - CTX + CH, :], reads=[sbuf_of("OFs")], writes=[bl[4]])
                    op("act", lambda e: e.activation(gt[:], fgs[:], AF.Ln), reads=[bl[0]], writes=[b_gt])
                    for n in range(2):
                        op("pe", lambda e: e.matmul(bc_ps[n][0:CH, :], tri, gt[:, n * 512:(n + 1) * 512], start=True, stop=True),
                           reads=[b_gt, b_cm], writes=[b_bc[n]])
                    for n in range(2):
                        sl = slice(n * 512, (n + 1) * 512)
                        op("act", lambda e: e.activation(Ei[:, sl], bc_ps[n][0:CH, :], AF.Exp, scale=-1.0), reads=[b_bc[n]], writes=[b_Ei])
                        op("act", lambda e: e.activation(Et[:, sl], bc_ps[n][0:CH, :], AF.Exp), reads=[b_bc[n]], writes=[b_E])
                    op("dve", lambda e: e.tensor_scalar(kt[:], fgs[:], -1.0, 1.0, ALU.mult, ALU.add), reads=[bl[0]], writes=[b_kt])
                    op("dve", lambda e: e.tensor_tensor(kt[:], kt[:], Ei[:], ALU.mult), reads=[b_kt, b_Ei], writes=[b_kt])
                    if not isctx:
                        op("dve", lambda e: e.tensor_tensor(qq[:], qs[:], Et[:], ALU.mult), reads=[bl[2], b_E], writes=[b_qq])
                    for h in range(8):
                        op("pe", lambda e: e.matmul(yT_ps[:, 400 + h:401 + h], Et[:, h * 128:(h + 1) * 128], onehot, start=True, stop=True),
                           reads=[b_E, b_cm], writes=[b_ebp], inc=(h == 7))
                    op("dve", lambda e: e.tensor_copy(eb[:], yT_ps[:, 400:408]), reads=[b_ebp], writes=[b_eb])
                    nxt = 1 - cur
                    for h in range(8):
                        hs = slice(h * 128, (h + 1) * 128)
                        pp = h % 2
                        if not isctx:
                            tpv = mz[:, pp * 256:(pp + 1) * 256]
                        if (not isctx) and not (KX & 1):
                            op("pe", lambda e: e.transpose(tpv[:, 0:128], qq_[:, hs], ident), reads=[b_qq, b_cm], writes=[b_tp[pp]], inc=False)
                            op("pe", lambda e: e.transpose(tpv[:, 128:256], kt_[:, hs], ident), reads=[b_kt, b_cm], writes=[b_tp[pp]])
                            op("dve", lambda e: e.tensor_copy(qkT[pp][:], tpv.rearrange("p (a b) -> p a b", b=128)[:, :, 0:CH]),
                               reads=[b_tp[pp]], writes=[b_qkT[pp]])
                            apv = yT_ps[0:CH, 256 + pp * 64:256 + pp * 64 + CH]
                        if (not isctx) and not (KX & 2):
                            op("pe", lambda e: e.matmul(apv, qkT[pp][:, 1, :], qkT[pp][:, 0, :], start=True, stop=True),
                               reads=[b_qkT[pp]], writes=[b_aps[pp]])
                            op("dve", lambda e: e.tensor_tensor(at[pp][:], apv, tri, ALU.mult), reads=[b_aps[pp], b_cm], writes=[b_at[pp]])
                            opv = o_ps[h // 4][0:CH, (h % 4) * 128:(h % 4 + 1) * 128]
                        if (not isctx) and not (KX & 4):
                            op("pe", lambda e: e.matmul(opv, at[pp][:], vs[:, hs], start=True, stop=False),
                               reads=[b_at[pp], bl[1]], writes=[b_o[h // 4]], inc=False)
                            op("pe", lambda e: e.matmul(opv, qkT[pp][:, 0, :], S[cur][:, h, :], start=False, stop=True),
                               reads=[b_qkT[pp], b_S[cur]], writes=[b_o[h // 4]])
                        dsv = dS_ps[h // 4][:, (h % 4) * 128:(h % 4 + 1) * 128]
                        op("pe", lambda e: e.matmul(dsv, kt[:, hs], vs[:, hs], start=True, stop=True),
                           reads=[b_kt, bl[1]], writes=[b_dS[h // 4]])
                    for n in range(2):
                        op("dve", lambda e: e.tensor_tensor(stmp[:, n * 4:(n + 1) * 4, :], dS_ps[n][:].rearrange("p (h v) -> p h v", v=128),
                                                            S[cur][:, n * 4:(n + 1) * 4, :], ALU.add),
                           reads=[b_dS[n], b_S[cur]], writes=[b_stmp])
                    op("dve", lambda e: e.tensor_tensor(S[nxt][:], stmp[:], eb[:].unsqueeze(2).to_broadcast([128, 8, 128]), ALU.mult),
                       reads=[b_stmp, b_eb], writes=[b_S[nxt]])
                    cur = nxt
                    if isctx or (KX & 8):
                        continue
                    xt0 = t0 - 256
                    if d == 0:
                        for n in range(2):
                            op("dve", lambda e: e.tensor_copy(ob[:, n * 512:(n + 1) * 512], o_ps[n][0:CH, :]),
                               reads=[b_o[n]], writes=[b_ob])
                        fw.dma(OFs[xt0:xt0 + CH, :], ob[:], reads=[b_ob], writes=[sbuf_of("OFs")], q=SQ)
                    else:
                        for n in range(2):
                            op("dve", lambda e: e.tensor_tensor(ob[:, n * 512:(n + 1) * 512], o_ps[n][0:CH, :], ofs[:, n * 512:(n + 1) * 512], ALU.add),
                               reads=[b_o[n], bl[4]], writes=[b_ob])
                        op("dve", lambda e: e.tensor_tensor(sq[:], ob[:], ob[:], ALU.mult), reads=[b_ob], writes=[b_sq])
                        op("dve", lambda e: e.tensor_reduce(ms[:], sq[:].rearrange("p (h v) -> p h v", v=128), AX.X, ALU.add), reads=[b_sq], writes=[b_ms])
                        op("act", lambda e: e.activation(ms[:], ms[:], AF.Sqrt, bias=cst[0:CH, 0:1], scale=1.0 / 128), reads=[b_ms, b_cst], writes=[b_ms])
                        op("dve", lambda e: e.reciprocal(ms[:], ms[:]), reads=[b_ms], writes=[b_ms])
                        op("dve", lambda e: e.tensor_tensor(sq[:].rearrange("p (h v) -> p h v", v=128), ob[:].rearrange("p (h v) -> p h v", v=128),
                                                            ms[:].unsqueeze(2).to_broadcast([CH, 8, 128]), ALU.mult),
                           reads=[b_ob, b_ms], writes=[b_sq])
                        op("dve", lambda e: e.tensor_tensor(sq[:], sq[:], hng[:], ALU.mult), reads=[b_sq, b_hng], writes=[b_sq])
                        op("dve", lambda e: e.tensor_tensor(sq[:], sq[:], zs[:], ALU.mult), reads=[b_sq, bl[3]], writes=[b_sq])
                        for h in range(8):
                            op("pe", lambda e: e.transpose(bc_ps[h // 4][:, (h % 4) * 128:(h % 4 + 1) * 128], sq_[:, h * 128:(h + 1) * 128], ident),
                               reads=[b_sq, b_cm], writes=[b_bc[h // 4]], inc=(h % 4 == 3))
                        for n in range(2):
                            yo_ = xt0 % 512
                            op("dve", lambda e: e.tensor_copy(yT[:, n * 4:(n + 1) * 4, yo_:yo_ + CH], bc_ps[n][:].rearrange("p (h t) -> p h t", t=128)[:, :, 0:CH]),
                               reads=[b_bc[n]], writes=[b_yT])
                        if xt0 % 512 == 0:
                            fw.dma3(YH.rearrange("(h p) t -> p h t", p=128)[:, :, xt0:xt0 + 512], yT[:], 8, reads=[b_yT], writes=[sbuf_of("YH")], q=SQ)
        fw.barrier()

        for ph in _phase(3):
            W = 640
            mu = sbt(ph, "mu", [128, 26, 4]); omu = sbt(ph, "omu", [128, 26]); b_mu = Buf()
            w0t = sbt(ph, "w0t", [128, 8, 2]); a0t = sbt(ph, "a0t", [128, 8, 2]); kvt = sbt(ph, "kvt", [128, 8, 3]); omka = sbt(ph, "omka", [128, 8])
            w2t = sbt(ph, "w2t", [128, RWW]); a2t = sbt(ph, "a2t", [128, RWW]); b_par = Buf()
            raw = [sbt(ph, "raw%d" % i, [128, W]) for i in range(3)]; b_raw = [Buf() for _ in range(3)]
            rawl = [sbt(ph, "rawl%d" % i, [128, W]) for i in range(2)]; b_rawl = [Buf(), Buf()]
            sh = [sbt(ph, "sh%d" % i, [128, 512]) for i in range(3)]; b_sh = [Buf() for _ in range(3)]
            tw = sbt(ph, "tw", [128, 512]); als = sbt(ph, "als", [128, 512]); b_tw = Buf(); b_als = Buf()
            kkr = sbt(ph, "kkr", [128, 512]); kk = sbt(ph, "kk", [128, 512]); t1 = sbt(ph, "t1", [128, 512]); b_kkr = Buf(); b_kk = Buf(); b_t1 = Buf()
            lgw = sbt(ph, "lgw", [128, 512]); av = sbt(ph, "av", [128, 512]); b_lgw = Buf(); b_av = Buf()
            kd = [sbt(ph, "kd%d" % i, [128, 512]) for i in range(2)]; b_kd = [Buf(), Buf()]
            bv = sbt(ph, "bv", [128, 512]); b_bv = Buf()
            lt = sbt(ph, "lt", [128, 128]); b_lt = Buf()
            Ecw = sbt(ph, "Ecw", [128, 512]); Einv = sbt(ph, "Einv", [128, 512]); Eex = sbt(ph, "Eex", [128, 512]); b_Ecw = Buf(); b_Einv = Buf(); b_Eex = Buf()
            res = [sbt(ph, "res%d" % i, [128, 512]) for i in range(4)]; b_res = [Buf() for _ in range(4)]
            tm = [[sbt(ph, "tm%d_%d" % (a_, s_), [128, RWW]) for s_ in range(4)] for a_ in range(5)]; b_tm = [[Buf() for _ in range(4)] for _ in range(5)]
            p_a = pst(ph, "p_a", [128, 512]); p_b = pst(ph, "p_b", [128, 512]); p_cw = pst(ph, "p_cw", [128, 512]); p_tq = [pst(ph, "p_t%d" % i, [128, 512]) for i in range(4)]
            p_s = pst(ph, "p_s", [128, 512])
            b_pa = Buf(); b_pb = Buf(); b_pcw = Buf(); b_pt = [Buf() for _ in range(4)]; b_ps = Buf()
            for (tl, src) in [(mu, mu_fm), (w0t, w0_fm), (a0t, a0_fm), (kvt, kvec_fm), (w2t, w2_d), (a2t, a2_d)]:
                fw.dma(tl[:], src, writes=[b_par if tl is not mu else b_mu])
            op("dve", lambda e: e.tensor_reduce(omu[:], mu[:], AX.X, ALU.add), reads=[b_mu], writes=[b_mu])
            op("dve", lambda e: e.tensor_scalar(omu[:], omu[:], -1.0, 1.0, ALU.mult, ALU.add), reads=[b_mu], writes=[b_mu])
            op("dve", lambda e: e.tensor_scalar(omka[:], kvt[:, :, 1], -1.0, 1.0, ALU.mult, ALU.add), reads=[b_par], writes=[b_par])
            omuc = sbt(ph, "omuc", [128, 26])
            op("dve", lambda e: e.tensor_tensor(omuc[:], mu[:, :, 0], mu[:, :, 1], ALU.add), reads=[b_mu], writes=[b_mu])
            op("dve", lambda e: e.tensor_scalar(omuc[:], omuc[:], -1.0, 1.0, ALU.mult, ALU.add), reads=[b_mu], writes=[b_mu])

            def shift(dst, b_dst, rawt, b_rawt, chunk, g0, ntok, isctx, eng="dve"):
                rows = RWfm[chunk * 128:(chunk + 1) * 128, :]
                if isctx:
                    fw.dma(rawt[:, 0:256], rows[:, 0:256], reads=[sbuf_of("RWfm")], writes=[b_rawt])
                    P = rawt[:, 0:256]
                    op(eng, lambda e: e.tensor_scalar(dst[:, 0:256], P, omuc[:, chunk:chunk + 1], None, ALU.mult), reads=[b_rawt, b_mu], writes=[b_dst])
                    op(eng, lambda e: e.scalar_tensor_tensor(dst[:, 1:256], P[:, 0:255], mu[:, chunk, 0:1], dst[:, 1:256], ALU.mult, ALU.add),
                       reads=[b_rawt, b_mu, b_dst], writes=[b_dst])
                    op(eng, lambda e: e.scalar_tensor_tensor(dst[:, 0:255], P[:, 1:256], mu[:, chunk, 1:2], dst[:, 0:255], ALU.mult, ALU.add),
                       reads=[b_rawt, b_mu, b_dst], writes=[b_dst])
                    return
                lo = g0 - 64
                hi = g0 + ntok + 64
                first = (g0 == CTX)
                last = (g0 + ntok == NT)
                if first:
                    op(eng, lambda e: e.memset(rawt[:, 0:64], 0.0), writes=[b_rawt])
                if last:
                    op(eng, lambda e: e.memset(rawt[:, W - 64:W], 0.0), writes=[b_rawt])
                a_ = 64 if first else 0
                b_ = W - 64 if last else W
                fw.dma(rawt[:, a_:b_], rows[:, lo + a_:lo + b_], reads=[sbuf_of("RWfm")], writes=[b_rawt])
                P3 = rawt[:].rearrange("p (r c) -> p r c", c=64)
                Pc = P3[:, 1:9, :]
                d3 = dst[:].rearrange("p (r c) -> p r c", c=64)
                op(eng, lambda e: e.tensor_scalar(d3, Pc, omu[:, chunk:chunk + 1], None, ALU.mult), reads=[b_rawt, b_mu], writes=[b_dst])
                op(eng, lambda e: e.scalar_tensor_tensor(d3[:, :, 1:], Pc[:, :, :-1], mu[:, chunk, 0:1], d3[:, :, 1:], ALU.mult, ALU.add),
                   reads=[b_rawt, b_mu, b_dst], writes=[b_dst])
                op(eng, lambda e: e.scalar_tensor_tensor(d3[:, :, :-1], Pc[:, :, 1:], mu[:, chunk, 1:2], d3[:, :, :-1], ALU.mult, ALU.add),
                   reads=[b_rawt, b_mu, b_dst], writes=[b_dst])
                op(eng, lambda e: e.scalar_tensor_tensor(d3, P3[:, 0:8, :], mu[:, chunk, 2:3], d3, ALU.mult, ALU.add),
                   reads=[b_rawt, b_mu, b_dst], writes=[b_dst])
                op(eng, lambda e: e.scalar_tensor_tensor(d3, P3[:, 2:10, :], mu[:, chunk, 3:4], d3, ALU.mult, ALU.add),
                   reads=[b_rawt, b_mu, b_dst], writes=[b_dst])

            tiles = [(0, 256, True)] + [(256 + 512 * i, 512, False) for i in range(4)]
            ri = 0
            for (g0, ntok, isctx) in tiles[KT0:KT]:
                nsub = ntok // 128
                blk0 = g0 // 128
                N_ = slice(0, ntok)
                shift(tw, b_tw, rawl[0], b_rawl[0], 24, g0, ntok, isctx)
                if not (KX & 32):
                    op("act", lambda e: e.activation(tw[:, N_], tw[:, N_], AF.Tanh), reads=[b_tw], writes=[b_tw])
                shift(als, b_als, rawl[1], b_rawl[1], 25, g0, ntok, isctx)
                for j in range(KJ):
                    if not isctx:
                        shift(sh[0], b_sh[0], raw[0], b_raw[0], j, g0, ntok, isctx)
                    shift(sh[1], b_sh[1], raw[1], b_raw[1], 8 + j, g0, ntok, isctx)
                    shift(sh[2], b_sh[2], raw[2], b_raw[2], 16 + j, g0, ntok, isctx)
                    rs, ks, vs = sh[0], sh[1], sh[2]
                    op("dve", lambda e: e.tensor_scalar(kkr[:, N_], ks[:, N_], kvt[:, j, 0:1], None, ALU.mult), reads=[b_sh[1], b_par], writes=[b_kkr])
                    op("dve", lambda e: e.tensor_tensor(t1[:, N_], kkr[:, N_], kkr[:, N_], ALU.mult), reads=[b_kkr], writes=[b_t1])
                    op("pe", lambda e: e.matmul(p_a[:, N_], cm[:, C_BONES, :], t1[:, N_], start=True, stop=True), reads=[b_cm, b_t1], writes=[b_pa])
                    op("act", lambda e: e.activation(t1[:, N_], p_a[:, N_], AF.Sqrt, bias=EPS12, scale=1.0), reads=[b_pa, b_cst], writes=[b_t1])
                    op("dve", lambda e: e.reciprocal(t1[:, N_], t1[:, N_]), reads=[b_t1], writes=[b_t1])
                    op("dve", lambda e: e.tensor_tensor(kk[:, N_], kkr[:, N_], t1[:, N_], ALU.mult), reads=[b_kkr, b_t1], writes=[b_kk])
                    for sub in range(nsub):
                        q_ = sub % 4
                        op("pe", lambda e: e.transpose(p_tq[q_][:, 0:128], vs[:, sub * 128:(sub + 1) * 128], ident),
                           reads=[b_sh[2], b_cm], writes=[b_pt[q_]])
                        op("dve", lambda e: e.tensor_copy(tm[0][sub][:, j * 128:(j + 1) * 128], p_tq[q_][:, 0:128]), reads=[b_pt[q_]], writes=[b_tm[0][sub]])
                    for d in range(2):
                        ds = slice(d * 64, (d + 1) * 64)
                        js = slice(j * 128, (j + 1) * 128)
                        op("pe", lambda e: e.matmul(p_a[:, N_], w2t[ds, js], tw[ds, N_], start=True, stop=True), reads=[b_par, b_tw], writes=[b_pa])
                        op("act", lambda e: e.activation(lgw[:, N_], p_a[:, N_], AF.Sigmoid, bias=w0t[:, j, d:d + 1], scale=1.0), reads=[b_pa, b_par], writes=[b_lgw])
                        op("dve", lambda e: e.tensor_scalar(lgw[:, N_], lgw[:, N_], -0.6065306597126334, None, ALU.mult), reads=[b_lgw], writes=[b_lgw])
                        op("pe", lambda e: e.matmul(p_b[:, N_], a2t[ds, js], als[ds, N_], start=True, stop=True), reads=[b_par, b_als], writes=[b_pb])
                        op("act", lambda e: e.activation(av[:, N_], p_b[:, N_], AF.Sigmoid, bias=a0t[:, j, d:d + 1], scale=1.0), reads=[b_pb, b_par], writes=[b_av])
                        op("dve", lambda e: e.tensor_scalar(t1[:, N_], av[:, N_], kvt[:, j, 1:2], omka[:, j:j + 1], ALU.mult, ALU.add), reads=[b_av, b_par], writes=[b_t1])
                        op("dve", lambda e: e.tensor_tensor(kd[d][:, N_], t1[:, N_], ks[:, N_], ALU.mult), reads=[b_t1, b_sh[1]], writes=[b_kd[d]])
                        op("dve", lambda e: e.tensor_tensor(bv[:, N_], kk[:, N_], av[:, N_], ALU.mult), reads=[b_kk, b_av], writes=[b_bv])
                        tri = cm[:, C_TF if d == 0 else C_TB, :]
                        for sub in range(nsub):
                            ss_ = slice(sub * 128, (sub + 1) * 128)
                            q_ = sub % 4
                            op("pe", lambda e: e.transpose(p_tq[q_][:, 0:128], lgw[:, ss_], ident), reads=[b_lgw, b_cm], writes=[b_pt[q_]])
                            op("dve", lambda e: e.tensor_copy(lt[:], p_tq[q_][:, 0:128]), reads=[b_pt[q_]], writes=[b_lt])
                            op("pe", lambda e: e.matmul(p_cw[:, ss_], lt[:], tri, start=True, stop=True), reads=[b_lt, b_cm], writes=[b_pcw])
                        op("act", lambda e: e.activation(Ecw[:, N_], p_cw[:, N_], AF.Exp), reads=[b_pcw], writes=[b_Ecw])
                        op("act", lambda e: e.activation(Einv[:, N_], p_cw[:, N_], AF.Exp, scale=-1.0), reads=[b_pcw], writes=[b_Einv])
                        op("dve", lambda e: e.tensor_tensor(t1[:, N_], p_cw[:, N_], lgw[:, N_], ALU.subtract), reads=[b_pcw, b_lgw], writes=[b_t1])
                        op("act", lambda e: e.activation(Eex[:, N_], t1[:, N_], AF.Exp), reads=[b_t1], writes=[b_Eex])
                        lastc = 127 if d == 0 else 0
                        for sub in range(nsub):
                            op("dve", lambda e: e.tensor_copy(wc[d][:, j, blk0 + sub:blk0 + sub + 1], Ecw[:, sub * 128 + lastc:sub * 128 + lastc + 1]),
                               reads=[b_Ecw], writes=[b_wc[d]])
                        prods = [(kk, b_kk, Eex, b_Eex, AL[d], "AL%d" % d), (bv, b_bv, Einv, b_Einv, BE[d], "BE%d" % d),
                                 (kd[d], b_kd[d], Einv, b_Einv, KA[d], "KA%d" % d)]
                        if not isctx:
                            prods.append((rs, b_sh[0], Ecw, b_Ecw, RH[d], "RH%d" % d))
                        for pi_, (x_, bx_, y_, by_, dst, nm) in enumerate(prods):
                            op("dve", lambda e: e.tensor_tensor(res[pi_][:, N_], x_[:, N_], y_[:, N_], ALU.mult), reads=[bx_, by_], writes=[b_res[pi_]])
                            if not (KX & 64):
                                fw.dma(dst[js, g0:g0 + ntok], res[pi_][:, N_], reads=[b_res[pi_]], writes=[sbuf_of(nm)], q=SQ)
                            if pi_ in (1, 2):
                                dstT = BEt[d] if pi_ == 1 else KAt[d]
                                nmT = ("BEt%d" if pi_ == 1 else "KAt%d") % d
                                for sub in range(nsub):
                                    q_ = sub % 4
                                    srcT = kk if (KX & 128) else res[pi_]
                                    op("pe", lambda e: e.transpose(p_tq[q_][:, 0:128], srcT[:, sub * 128:(sub + 1) * 128], ident),
                                       reads=[b_res[pi_], b_cm], writes=[b_pt[q_]])
                                    ta = 1 + 2 * d + (pi_ - 1)
                                    op("dve", lambda e: e.tensor_copy(tm[ta][sub][:, js], p_tq[q_][:, 0:128]), reads=[b_pt[q_]], writes=[b_tm[ta][sub]])
                    if not isctx:
                        op("dve", lambda e: e.tensor_tensor(t1[:], kd[0][:], kd[1][:], ALU.add), reads=[b_kd[0], b_kd[1]], writes=[b_t1])
                        op("dve", lambda e: e.scalar_tensor_tensor(t1[:], t1[:], kvt[:, j, 2:3], rs[:], ALU.mult, ALU.mult), reads=[b_t1, b_par, b_sh[0]], writes=[b_t1])
                        op("pe", lambda e: e.matmul(p_s[:], cm[:, C_BONES, :], t1[:], start=True, stop=True), reads=[b_cm, b_t1], writes=[b_ps])
                        op("dve", lambda e: e.tensor_tensor(res[3][:], p_s[:], vs[:], ALU.mult), reads=[b_ps, b_sh[2]], writes=[b_res[3]])
                        fw.dma(BON[j * 128:(j + 1) * 128, g0 - CTX:g0 - CTX + 512], res[3][:], reads=[b_res[3]], writes=[sbuf_of("BON")], q=SQ)
                for sub in range(nsub if not (KX & 16) else 0):
                    rows = slice(g0 + sub * 128, g0 + (sub + 1) * 128)
                    for ta, (dstT, nmT) in enumerate([(Vt, "Vt"), (BEt[0], "BEt0"), (KAt[0], "KAt0"), (BEt[1], "BEt1"), (KAt[1], "KAt1")]):
                        fw.dma(dstT[rows, :], tm[ta][sub][:], reads=[b_tm[ta][sub]], writes=[sbuf_of(nmT)], q=SQ)
        fw.barrier()

        for ph in _phase(4):
            J = 8
            aT = [sbt(ph, "aT%d" % j, [128, 128]) for j in range(J)]
            bT = [sbt(ph, "bT%d" % j, [128, 128]) for j in range(J)]
            kT = [sbt(ph, "kT%d" % j, [128, 128]) for j in range(J)]
            rT = [sbt(ph, "rT%d" % j, [128, 128]) for j in range(J)]
            btkA = sbt(ph, "btkA", [128, RWW]); ktkA = sbt(ph, "ktkA", [128, RWW]); vtkA = sbt(ph, "vtkA", [128, RWW])
            btk = [btkA[:, j * 128:(j + 1) * 128] for j in range(J)]
            ktk = [ktkA[:, j * 128:(j + 1) * 128] for j in range(J)]
            vtk = [vtkA[:, j * 128:(j + 1) * 128] for j in range(J)]
            b_tokA = [Buf(), Buf(), Buf()]
            b_in = [[Buf() for _ in range(7)] for _ in range(J)]
            Pm = [[sbt(ph, "Pm%d_%d" % (j, i), [128, 2, 128]) for i in range(2)] for j in range(J)]
            PTm = [[sbt(ph, "PTm%d_%d" % (j, i), [128, 2, 128]) for i in range(2)] for j in range(J)]
            TTm = [[sbt(ph, "TTm%d_%d" % (j, i), [128, 2, 128]) for i in range(2)] for j in range(J)]
            b_P = [[Buf(), Buf()] for _ in range(J)]; b_PT = [[Buf(), Buf()] for _ in range(J)]; b_TT = [[Buf(), Buf()] for _ in range(J)]
            AkT = [sbt(ph, "AkT%d" % j, [128, 2, 128]) for j in range(J)]
            BbT = [sbt(ph, "BbT%d" % j, [128, 2, 128]) for j in range(J)]
            BkT = [sbt(ph, "BkT%d" % j, [128, 2, 128]) for j in range(J)]
            b_AkT = [Buf() for _ in range(J)]; b_BbT = [Buf() for _ in range(J)]; b_BkT = [Buf() for _ in range(J)]
            St = [[sbt(ph, "St%d_%d" % (j, i), [128, 64]) for i in range(2)] for j in range(J)]
            b_St = [[Buf(), Buf()] for _ in range(J)]
            Rn = [sbt(ph, "Rn%d" % i, [128, 2, 64]) for i in range(2)]; b_Rn = [Buf(), Buf()]
            Us = [sbt(ph, "Us%d" % i, [128, 2, 64]) for i in range(2)]; b_Us = [Buf(), Buf()]
            stt_ = [sbt(ph, "stt%d" % i, [128, 64]) for i in range(2)]; b_stt = [Buf(), Buf()]
            Ys = [sbt(ph, "Ys%d" % i, [128, 2, 64]) for i in range(2)]; b_Ys = [Buf(), Buf()]
            Yf = [sbt(ph, "Yf%d" % i, [128, 2, 64]) for i in range(2)]; b_Yf = [Buf(), Buf()]
            cen = [sbt(ph, "cen%d" % i, [128, 2, 64]) for i in range(2)]; b_cen = [Buf(), Buf()]
            gsq = [sbt(ph, "gsq%d" % i, [128, 2, 64]) for i in range(2)]; b_gsq = [Buf(), Buf()]
            gst = [sbt(ph, "gst%d" % i, [128, 4]) for i in range(2)]; b_gst = [Buf(), Buf()]
            bon = [sbt(ph, "bon%d" % i, [128, 128]) for i in range(2)]; zr = [sbt(ph, "zr%d" % i, [128, 128]) for i in range(2)]
            b_bon = [Buf(), Buf()]; b_zr = [Buf(), Buf()]
            yo = [sbt(ph, "yo%d" % i, [128, 128]) for i in range(2)]; b_yo = [Buf(), Buf()]
            gng = sbt(ph, "gng", [128, RWW]); gnb = sbt(ph, "gnb", [128, RWW]); b_gn = Buf()
            fw.dma(gng[:], gng_b, writes=[b_gn]); fw.dma(gnb[:], gnb_b, writes=[b_gn])
            G = [pst(ph, "G%d" % i, [128, 1024]) for i in range(4)]
            b_G = [[Buf(), Buf()] for _ in range(4)]

            def gv(g, q):
                return G[g][:].rearrange("p (h q s) -> p h q s", h=2, q=4)[:, :, q, :]

            def gs(g, q):
                return G[g][:].rearrange("p (h c) -> p h c", h=2)[:, :, q * 64:(q + 1) * 64]

            KB = int(os.environ.get("KB", "99"))
            for d in range(min(2, KD)):
                m_lo = cm[:, C_GT if d == 0 else C_LT, :]
                m_up = cm[:, C_LT if d == 0 else C_GT, :]
                m_upi = cm[:, C_TF if d == 0 else C_TB, :]
                bc3 = lambda m: m.unsqueeze(1).to_broadcast([128, 2, 128])
                order = list(range(18)) if d == 0 else [1, 0] + list(range(17, 1, -1))
                cur = 0
                for j in range(J):
                    op("dve", lambda e: e.memset(St[j][0][:], 0.0), writes=[b_St[j][0]])
                for blk in order[:KB]:
                    isctx = blk < 2
                    g0 = blk * 128
                    x0 = g0 - CTX
                    tsl = slice(g0, g0 + 128)
                    fw.dma(btkA[:], BEt[d][tsl, :], reads=[sbuf_of("BEt%d" % d)], writes=[b_tokA[0]])
                    fw.dma(ktkA[:], KAt[d][tsl, :], reads=[sbuf_of("KAt%d" % d)], writes=[b_tokA[1]])
                    fw.dma(vtkA[:], Vt[tsl, :], reads=[sbuf_of("Vt")], writes=[b_tokA[2]])
                    for j in range(J):
                        js = slice(j * 128, (j + 1) * 128)
                        bi = b_in[j]
                        fw.dma(aT[j][:], AL[d][js, tsl], reads=[sbuf_of("AL%d" % d)], writes=[bi[0]])
                        fw.dma(bT[j][:], BE[d][js, tsl], reads=[sbuf_of("BE%d" % d)], writes=[bi[1]])
                        fw.dma(kT[j][:], KA[d][js, tsl], reads=[sbuf_of("KA%d" % d)], writes=[bi[2]])
                        if not isctx:
                            fw.dma(rT[j][:], RH[d][js, tsl], reads=[sbuf_of("RH%d" % d)], writes=[bi[3]])
                        bi[4], bi[5], bi[6] = b_tokA
                    for j in range(J):
                        bi = b_in[j]
                        for h in range(2):
                            hs = slice(h * 64, (h + 1) * 64)
                            op("pe", lambda e: e.matmul(gv(0, 0)[:, h, :], aT[j][hs, :], bT[j][hs, :], start=True, stop=True),
                               reads=[bi[0], bi[1]], writes=[b_G[0][h]])
                            op("pe", lambda e: e.matmul(gv(0, 1)[:, h, :], bT[j][hs, :], aT[j][hs, :], start=True, stop=True),
                               reads=[bi[0], bi[1]], writes=[b_G[0][h]])
                            op("pe", lambda e: e.matmul(gv(0, 2)[:, h, :], kT[j][hs, :], aT[j][hs, :], start=True, stop=True),
                               reads=[bi[0], bi[2]], writes=[b_G[0][h]])
                            if not isctx:
                                op("pe", lambda e: e.matmul(gv(0, 3)[:, h, :], bT[j][hs, :], rT[j][hs, :], start=True, stop=True),
                                   reads=[bi[1], bi[3]], writes=[b_G[0][h]])
                                op("pe", lambda e: e.matmul(gv(1, 0)[:, h, :], kT[j][hs, :], rT[j][hs, :], start=True, stop=True),
                                   reads=[bi[2], bi[3]], writes=[b_G[1][h]])
                        op("dve", lambda e: e.scalar_tensor_tensor(Pm[j][0][:], gv(0, 0), -1.0, bc3(m_lo), ALU.mult, ALU.mult),
                           reads=b_G[0] + [b_cm], writes=[b_P[j][0]])
                        op("dve", lambda e: e.scalar_tensor_tensor(PTm[j][0][:], gv(0, 1), -1.0, bc3(m_up), ALU.mult, ALU.mult),
                           reads=b_G[0] + [b_cm], writes=[b_PT[j][0]])
                        op("dve", lambda e: e.tensor_tensor(TTm[j][0][:], PTm[j][0][:], bc3(ident), ALU.add),
                           reads=[b_PT[j][0], b_cm], writes=[b_TT[j][0]])
                        op("dve", lambda e: e.tensor_tensor(AkT[j][:], gv(0, 2), bc3(m_up), ALU.mult),
                           reads=b_G[0] + [b_cm], writes=[b_AkT[j]])
                        if not isctx:
                            op("dve", lambda e: e.tensor_tensor(BbT[j][:], gv(0, 3), bc3(m_upi), ALU.mult),
                               reads=b_G[0] + [b_cm], writes=[b_BbT[j]])
                            op("dve", lambda e: e.tensor_tensor(BkT[j][:], gv(1, 0), bc3(m_upi), ALU.mult),
                               reads=b_G[1] + [b_cm], writes=[b_BkT[j]])
                    for i in range(1, 7):
                        a_, n_ = (i - 1) % 2, i % 2
                        for j in range(J):
                            for h in range(2):
                                op("pe", lambda e: e.matmul(gv(1, 1)[:, h, :], PTm[j][a_][:, h, :], Pm[j][a_][:, h, :], start=True, stop=True),
                                   reads=[b_P[j][a_], b_PT[j][a_]], writes=[b_G[1][h]])
                                if i < 6:
                                    op("pe", lambda e: e.matmul(gv(1, 2)[:, h, :], Pm[j][a_][:, h, :], PTm[j][a_][:, h, :], start=True, stop=True),
                                       reads=[b_P[j][a_], b_PT[j][a_]], writes=[b_G[1][h]])
                            op("dve", lambda e: e.tensor_copy(Pm[j][n_][:], gv(1, 1)), reads=b_G[1], writes=[b_P[j][n_]])
                            if i < 6:
                                op("dve", lambda e: e.tensor_copy(PTm[j][n_][:], gv(1, 2)), reads=b_G[1], writes=[b_PT[j][n_]])
                            for h in range(2):
                                op("pe", lambda e: e.matmul(gv(1, 3)[:, h, :], Pm[j][n_][:, h, :], TTm[j][a_][:, h, :], start=True, stop=True),
                                   reads=[b_P[j][n_], b_TT[j][a_]], writes=[b_G[1][h]])
                            op("dve", lambda e: e.tensor_tensor(TTm[j][n_][:], gv(1, 3), TTm[j][a_][:], ALU.add),
                               reads=b_G[1] + [b_TT[j][a_]], writes=[b_TT[j][n_]])
                    TTf = 0
                    nxt = 1 - cur
                    for j in range(J):
                        pr = j % 2
                        bi = b_in[j]
                        js = slice(j * 128, (j + 1) * 128)
                        Rv, Uv, Yv = gs(2, 0), gs(2, 1), gs(2, 2)
                        for h in range(2):
                            hs = slice(h * 64, (h + 1) * 64)
                            op("pe", lambda e: e.matmul(Rv[:, h, :], aT[j][hs, :], St[j][cur][hs, :], start=True, stop=False),
                               reads=[bi[0], b_St[j][cur]], writes=[b_G[2][h]])
                            op("pe", lambda e: e.matmul(Rv[:, h, :], AkT[j][:, h, :], vtk[j][:, hs], start=False, stop=True),
                               reads=[b_AkT[j], bi[6]], writes=[b_G[2][h]])
                        op("dve", lambda e: e.tensor_scalar(Rn[pr][:], Rv, -1.0, None, ALU.mult), reads=b_G[2], writes=[b_Rn[pr]])
                        for h in range(2):
                            op("pe", lambda e: e.matmul(Uv[:, h, :], TTm[j][TTf][:, h, :], Rn[pr][:, h, :], start=True, stop=True),
                               reads=[b_TT[j][TTf], b_Rn[pr]], writes=[b_G[2][h]])
                        op("dve", lambda e: e.tensor_copy(Us[pr][:], Uv), reads=b_G[2], writes=[b_Us[pr]])
                        SSv = G[3][:, 0:128]
                        op("pe", lambda e: e.matmul(SSv, btk[j], Us[pr][:].rearrange("p h v -> p (h v)"), start=True, stop=False),
                           reads=[bi[4], b_Us[pr]], writes=[b_G[3][0]])
                        op("pe", lambda e: e.matmul(SSv, ktk[j], vtk[j], start=False, stop=True),
                           reads=[bi[5], bi[6]], writes=[b_G[3][0]])
                        for h in range(2):
                            hs = slice(h * 64, (h + 1) * 64)
                            op("dve", lambda e: e.tensor_tensor(stt_[pr][hs, :], SSv[hs, h * 64:(h + 1) * 64], St[j][cur][hs, :], ALU.add),
                               reads=[b_G[3][0], b_St[j][cur]], writes=[b_stt[pr]])
                        op("dve", lambda e: e.tensor_scalar(St[j][nxt][:], stt_[pr][:], wc[d][:, j, blk:blk + 1], None, ALU.mult),
                           reads=[b_stt[pr], b_wc[d]], writes=[b_St[j][nxt]])
                        if isctx:
                            continue
                        for h in range(2):
                            hs = slice(h * 64, (h + 1) * 64)
                            op("pe", lambda e: e.matmul(Yv[:, h, :], rT[j][hs, :], St[j][cur][hs, :], start=True, stop=False),
                               reads=[bi[3], b_St[j][cur]], writes=[b_G[2][h]])
                            op("pe", lambda e: e.matmul(Yv[:, h, :], BbT[j][:, h, :], Us[pr][:, h, :], start=False, stop=False),
                               reads=[b_BbT[j], b_Us[pr]], writes=[b_G[2][h]])
                            op("pe", lambda e: e.matmul(Yv[:, h, :], BkT[j][:, h, :], vtk[j][:, hs], start=False, stop=True),
                               reads=[b_BkT[j], bi[6]], writes=[b_G[2][h]])
                        if d == 0:
                            op("dve", lambda e: e.tensor_copy(Ys[pr][:], Yv), reads=b_G[2], writes=[b_Ys[pr]])
                            fw.dma(YF[x0:x0 + 128, js], Ys[pr][:].rearrange("p h v -> p (h v)"), reads=[b_Ys[pr]], writes=[sbuf_of("YF")], q=SQ)
                        else:
                            fw.dma(Yf[pr][:].rearrange("p h v -> p (h v)"), YF[x0:x0 + 128, js], reads=[sbuf_of("YF")], writes=[b_Yf[pr]])
                            fw.dma(bon[pr][:], BON[js, x0:x0 + 128], reads=[sbuf_of("BON")], writes=[b_bon[pr]])
                            fw.dma(zr[pr][:], RWfm[(26 + j) * 128:(27 + j) * 128, g0:g0 + 128], reads=[sbuf_of("RWfm")], writes=[b_zr[pr]])
                            op("dve", lambda e: e.tensor_tensor(Ys[pr][:], Yv, Yf[pr][:], ALU.add), reads=b_G[2] + [b_Yf[pr]], writes=[b_Ys[pr]])
                            g_ = gst[pr]
                            op("dve", lambda e: e.tensor_reduce(g_[:, 0:2], Ys[pr][:], AX.X, ALU.add), reads=[b_Ys[pr]], writes=[b_gst[pr]])
                            op("dve", lambda e: e.tensor_scalar(g_[:, 0:2], g_[:, 0:2], -1.0 / 64, None, ALU.mult), reads=[b_gst[pr]], writes=[b_gst[pr]])
                            op("dve", lambda e: e.tensor_tensor(cen[pr][:], Ys[pr][:], g_[:, 0:2].unsqueeze(2).to_broadcast([128, 2, 64]), ALU.add),
                               reads=[b_Ys[pr], b_gst[pr]], writes=[b_cen[pr]])
                            op("dve", lambda e: e.tensor_tensor(gsq[pr][:], cen[pr][:], cen[pr][:], ALU.mult), reads=[b_cen[pr]], writes=[b_gsq[pr]])
                            op("dve", lambda e: e.tensor_reduce(g_[:, 2:4], gsq[pr][:], AX.X, ALU.add), reads=[b_gsq[pr]], writes=[b_gst[pr]])
                            op("act", lambda e: e.activation(g_[:, 2:4], g_[:, 2:4], AF.Sqrt, bias=EPSGN, scale=1.0 / 64), reads=[b_gst[pr], b_cst], writes=[b_gst[pr]])
                            op("dve", lambda e: e.reciprocal(g_[:, 2:4], g_[:, 2:4]), reads=[b_gst[pr]], writes=[b_gst[pr]])
                            op("dve", lambda e: e.tensor_tensor(cen[pr][:], cen[pr][:], g_[:, 2:4].unsqueeze(2).to_broadcast([128, 2, 64]), ALU.mult),
                               reads=[b_cen[pr], b_gst[pr]], writes=[b_cen[pr]])
                            cf = cen[pr][:].rearrange("p h v -> p (h v)")
                            op("dve", lambda e: e.tensor_tensor(cf, cf, gng[:, js], ALU.mult), reads=[b_cen[pr], b_gn], writes=[b_cen[pr]])
                            op("dve", lambda e: e.tensor_tensor(cf, cf, gnb[:, js], ALU.add), reads=[b_cen[pr], b_gn], writes=[b_cen[pr]])
                            yTv = G[3][:, 512:640]
                            op("pe", lambda e: e.transpose(yTv, cf, ident), reads=[b_cen[pr], b_cm], writes=[b_G[3][1]])
                            op("dve", lambda e: e.tensor_tensor(yo[pr][:], yTv, bon[pr][:], ALU.add), reads=[b_G[3][1], b_bon[pr]], writes=[b_yo[pr]])
                            op("dve", lambda e: e.tensor_tensor(yo[pr][:], yo[pr][:], zr[pr][:], ALU.mult), reads=[b_yo[pr], b_zr[pr]], writes=[b_yo[pr]])
                            fw.dma(YR[js, x0:x0 + 128], yo[pr][:], reads=[b_yo[pr]], writes=[sbuf_of("YR")], q=SQ)
                    cur = nxt
        fw.barrier()

        for ph in _phase(5):
            TT_ = 256
            yh = sbt(ph, "yh", [128, 8, TT_]); yr = sbt(ph, "yr", [128, 8, TT_]); b_yh = Buf(); b_yr = Buf()
            gh = [sbt(ph, "gh%d" % i, [128, 4, TT_]) for i in range(2)]; gr = [sbt(ph, "gr%d" % i, [128, 4, TT_]) for i in range(2)]
            b_gh = [Buf(), Buf()]; b_gr = [Buf(), Buf()]
            mT = sbt(ph, "mT", [128, 16, TT_]); b_mT = Buf()
            whg = [sbt(ph, "whg0", [128, 8, 512])] * 2; wrw = [sbt(ph, "wrw0", [128, 8, 512])] * 2
            b_whg = [Buf()] * 2; b_wrw = [Buf()] * 2
            wo = [sbt(ph, "wo0", [128, 16, 512])] * 2; b_wo = [Buf()] * 2
            xr = [sbt(ph, "xr%d" % i, [128, D]) for i in range(2)]; b_xr = [Buf(), Buf()]
            xn = [sbt(ph, "xn%d" % i, [128, D]) for i in range(2)]; b_xn = [Buf(), Buf()]
            junk3 = sbt(ph, "junk3", [128, D]); b_j3 = Buf()
            ss3 = sbt(ph, "ss3", [128, 2]); b_ss3 = Buf()
            fgt_ = sbt(ph, "fgt_", [128, D]); b_fg = Buf()
            tmpm = [sbt(ph, "tmpm%d" % i, [128, TT_]) for i in range(2)]; b_tmpm = [Buf(), Buf()]
            pp1 = [pst(ph, "pp1_%d" % i, [128, 512]) for i in range(2)]; pp2 = [pst(ph, "pp2_%d" % i, [128, 512]) for i in range(2)]
            b_pp1 = [Buf(), Buf()]; b_pp2 = [Buf(), Buf()]
            po = [pst(ph, "po%d" % i, [128, 512]) for i in range(4)]; b_po = [Buf() for _ in range(4)]
            fw.dma(fgt_[:], fg_b, writes=[b_fg])
            whg_r = w_hg_o.rearrange("(k p) c -> p k c", p=128)
            wrw_r = w_rw_o.rearrange("(k p) c -> p k c", p=128)
            wo_r = w_o.rearrange("(k p) c -> p k c", p=128)
            YH_r = YH.rearrange("(k p) t -> p k t", p=128)
            YR_r = YR.rearrange("(k p) t -> p k t", p=128)
            G_r = RWfm[(74 - 40) * 128:, :].rearrange("(k p) t -> p k t", p=128)
            wi = 0
            woi = 0
            xi = 0
            for tt in range(SEQ // TT_):
                x0 = tt * TT_
                g0 = x0 + CTX
                fw.dma3(yh[:], YH_r[:, :, x0:x0 + TT_], 8, reads=[sbuf_of("YH")], writes=[b_yh])
                fw.dma3(yr[:], YR_r[:, :, x0:x0 + TT_], 8, reads=[sbuf_of("YR")], writes=[b_yr])
                for mg in range(4):
                    wb = wi % 2
                    wi += 1
                    cs = slice(mg * 512, (mg + 1) * 512)
                    fw.dma3(whg[wb][:], whg_r[:, :, cs], 8, writes=[b_whg[wb]])
                    fw.dma3(wrw[wb][:], wrw_r[:, :, cs], 8, writes=[b_wrw[wb]])
                    fw.dma3(gh[wb][:], G_r[:, mg * 4:(mg + 1) * 4, g0:g0 + TT_], 4, reads=[sbuf_of("RWfm")], writes=[b_gh[wb]])
                    fw.dma3(gr[wb][:], G_r[:, 16 + mg * 4:16 + (mg + 1) * 4, g0:g0 + TT_], 4, reads=[sbuf_of("RWfm")], writes=[b_gr[wb]])
                    for mm in range(4):
                        m = mg * 4 + mm
                        a = mm % 2
                        for k in range(8):
                            op("pe", lambda e: e.matmul(pp1[a][:, :TT_], whg[wb][:, k, mm * 128:(mm + 1) * 128], yh[:, k, :], start=(k == 0), stop=(k == 7)),
                               reads=[b_whg[wb], b_yh], writes=[b_pp1[a]], inc=(k == 7))
                        for k in range(8):
                            op("pe", lambda e: e.matmul(pp2[a][:, :TT_], wrw[wb][:, k, mm * 128:(mm + 1) * 128], yr[:, k, :], start=(k == 0), stop=(k == 7)),
                               reads=[b_wrw[wb], b_yr], writes=[b_pp2[a]], inc=(k == 7))
                        op("dve", lambda e: e.tensor_tensor(tmpm[a][:], pp1[a][:, :TT_], gh[wb][:, mm, :], ALU.mult), reads=[b_pp1[a], b_gh[wb]], writes=[b_tmpm[a]])
                        op("dve", lambda e: e.tensor_tensor(mT[:, m, :], pp2[a][:, :TT_], gr[wb][:, mm, :], ALU.mult), reads=[b_pp2[a], b_gr[wb]], writes=[b_mT])
                        op("dve", lambda e: e.tensor_tensor(mT[:, m, :], mT[:, m, :], tmpm[a][:], ALU.add), reads=[b_mT, b_tmpm[a]], writes=[b_mT])
                for sub in range(TT_ // 128):
                    xb = xi % 2
                    xi += 1
                    fw.dma(xr[xb][:], xc[g0 + sub * 128:g0 + (sub + 1) * 128, :], writes=[b_xr[xb]])
                for n in range(4):
                    ob_ = woi % 2
                    woi += 1
                    fw.dma3(wo[ob_][:], wo_r[:, :, n * 512:(n + 1) * 512], 16, writes=[b_wo[ob_]])
                    for sub in range(TT_ // 128):
                        xb = (xi - (TT_ // 128) + sub) % 2
                        a = (n * 2 + sub) % 4
                        for k in range(16):
                            op("pe", lambda e: e.matmul(po[a][:], mT[:, k, sub * 128:(sub + 1) * 128], wo[ob_][:, k, :], start=(k == 0), stop=(k == 15)),
                               reads=[b_mT, b_wo[ob_]], writes=[b_po[a]], inc=(k == 15))
                        ns = slice(n * 512, (n + 1) * 512)
                        op("dve", lambda e: e.tensor_tensor(xn[xb][:, ns], po[a][:], gate_b[:, ns], ALU.mult), reads=[b_po[a], b_gate], writes=[b_xn[xb]])
                        op("dve", lambda e: e.tensor_tensor(xn[xb][:, ns], xn[xb][:, ns], xr[xb][:, ns], ALU.add), reads=[b_xn[xb], b_xr[xb]], writes=[b_xn[xb]])
                for sub in range(TT_ // 128):
                    xb = (xi - (TT_ // 128) + sub) % 2
                    op("act", lambda e: e.activation(junk3[:], xn[xb][:], AF.Square, accum_out=ss3[:, 0:1]), reads=[b_xn[xb]], writes=[b_j3, b_ss3])
                    op("act", lambda e: e.activation(ss3[:, 1:2], ss3[:, 0:1], AF.Sqrt, bias=EPS6, scale=1.0 / D), reads=[b_ss3, b_cst], writes=[b_ss3])
                    op("dve", lambda e: e.reciprocal(ss3[:, 1:2], ss3[:, 1:2]), reads=[b_ss3], writes=[b_ss3])
                    op("dve", lambda e: e.scalar_tensor_tensor(xn[xb][:], xn[xb][:], ss3[:, 1:2], fgt_[:], ALU.mult, ALU.mult),
                       reads=[b_xn[xb], b_ss3, b_fg], writes=[b_xn[xb]])
                    fw.dma(out[x0 + sub * 128:x0 + (sub + 1) * 128, :], xn[xb][:], reads=[b_xn[xb]], writes=[sbuf_of("out")], q=SQ)
        fw.barrier()
    print("bass program built: %d instructions" % fw.ninstr, flush=True)
    dbg_names = ["HGtok", "RWfm", "OFs", "YH", "AL0", "BE0", "KA0", "RH0", "AL1", "BE1", "KA1", "RH1", "BEt0", "KAt0", "Vt", "BON", "YF", "YR"]
    return nc, dbg_names


def _host_inputs(b, inp):
    f = lambda a: np.ascontiguousarray(a, dtype=np.float32)
    fm = lambda v: f(np.asarray(v).reshape(-1, 128).T)
    bc = lambda v: f(np.broadcast_to(np.asarray(v).reshape(1, -1), (128, np.asarray(v).size)))
    m = {}
    m["xc"] = f(np.concatenate([inp["ctx"][b], inp["x"][b]], axis=0))
    m["cc"] = f(np.stack([fm(inp["c"][b]), fm(inp["c_ctx"])], axis=-1))
    m["ada_w"] = f(inp["ada_w"][0].reshape(16, 128, 3 * D))
    m["ada_b_fm"] = fm(inp["ada_b"][0])
    m["ada_b_g"] = f(inp["ada_b"][0][2 * D:].reshape(1, D))
    m["norm_g_fm"] = fm(inp["norm_g"][0])
    m["w_in"] = f(inp["w_in"][0])
    m["hg_lb_b"] = f(np.broadcast_to(inp["hg_lb"][None], (128, 2, 2, HGW)))
    m["hgng_b"] = bc(inp["hg_norm_g"][0])
    mu = inp["rw_mu"][0]
    m["mu_fm"] = f(mu.reshape(4, 26, 128).transpose(2, 1, 0))
    m["w0_fm"] = f(inp["rw_w0"][0].reshape(2, 8, 128).transpose(2, 1, 0))
    m["a0_fm"] = f(inp["rw_a0"][0].reshape(2, 8, 128).transpose(2, 1, 0))
    m["w2"] = f(inp["rw_w2"][0].reshape(128, RWW))
    m["a2"] = f(inp["rw_a2"][0].reshape(128, RWW))
    kv = np.stack([inp["rw_kk"][0], inp["rw_ka"][0], inp["rw_rk"][0]], axis=0)
    m["kvec_fm"] = f(kv.reshape(3, 8, 128).transpose(2, 1, 0))
    m["gng_b"] = bc(inp["rw_gn_g"][0])
    m["gnb_b"] = bc(inp["rw_gn_b"][0])
    m["w_hg_o"] = f(inp["w_hg_out"][0])
    m["w_rw_o"] = f(inp["w_rw_out"][0])
    m["w_o"] = f(inp["w_out"][0])
    m["fg_b"] = bc(inp["final_g"])
    m["cm"] = make_cm()
    cst = np.zeros((128, 8), np.float32)
    cst[:, 0] = 1e-6
    cst[:, 1] = 1e-12
    cst[:, 2] = 64e-5
    cst[:, 3] = 0.0
    cst[:, 4] = 1.0
    m["cst"] = cst
    return m


_LAST = {}


def kernel(**inputs):
    inp = {k: np.asarray(v) for k, v in inputs.items()}
    nb = inp["x"].shape[0]
    nc, dbg = build_program()
    in_maps = [_host_inputs(b, inp) for b in range(nb)]
    res = run_bass_kernel_spmd(nc, in_maps, core_ids=list(range(nb)))
    if DEBUG:
        _LAST["res"] = res
    return np.stack([np.asarray(r["out"], dtype=np.float32) for r in res.results], axis=0)
```

```python
import os
import numpy as np
from contextlib import ExitStack
import concourse.bass as bass
import concourse.mybir as mybir
from concourse.bass_utils import run_bass_kernel_spmd

F32 = mybir.dt.float32
AF = mybir.ActivationFunctionType
ALU = mybir.AluOpType
AX = mybir.AxisListType

D = 2048
SEQ = 2048
CTX = 256
NT = SEQ + CTX
NCOLS = 13568
HGW = 1024
RWW = 1024
CH = 32
DEBUG = bool(os.environ.get("KDEBUG"))
PH = int(os.environ.get("KPHASE", "9"))
SQ = os.environ.get("KSQ", "pool")
ONLY = int(os.environ.get("KONLY", "-1"))
KLIM = int(os.environ.get("KLIM", "-1"))
KCH = int(os.environ.get("KCH", "99"))
KX = int(os.environ.get("KX", "0"))
KD = int(os.environ.get("KD", "2"))
KT = int(os.environ.get("KT", "9"))
KJ = int(os.environ.get("KJ", "8"))
KT0 = int(os.environ.get("KT0", "0"))


_FWREF = []


def _phase(n):
    if PH >= n and (ONLY < 0 or n == ONLY):
        fw = _FWREF[-1]
        fw.emitted = 0
        fw.limit = KLIM if (n == ONLY and KLIM >= 0) else None
        with ExitStack() as ph:
            yield ph
        print('phase', n, 'emitted', fw.emitted, flush=True)
        fw.limit = None


class Buf:
    __slots__ = ("name", "w", "r")

    def __init__(self, name=""):
        self.name = name
        self.w = None
        self.r = {}


class FW:
    NDMA = 14

    def __init__(self, nc, stack):
        self.nc = nc
        self.eng = {"pe": nc.tensor, "act": nc.scalar, "dve": nc.vector, "pool": nc.gpsimd, "sp": nc.sync}
        self.sem = {}
        self.cnt = {}
        for k in ["pe", "act", "dve", "pool"]:
            self.sem[k] = stack.enter_context(nc.semaphore("s_" + k))
            self.cnt[k] = 0
        for i in range(self.NDMA):
            k = "dma%d" % i
            self.sem[k] = stack.enter_context(nc.semaphore("s_" + k))
            self.cnt[k] = 0
        self.dma_i = 0
        self.waited = {e: {} for e in self.eng}
        self.ninstr = 0
        self.emitted = 0
        self.limit = None

    def _need(self, e, deps):
        for k, v in deps.items():
            if self.waited[e].get(k, 0) >= v:
                continue
            self.eng[e].wait_ge(self.sem[k], v)
            self.waited[e][k] = v

    def _collect(self, e, reads, writes):
        deps = {}

        def add(ev):
            if ev is None:
                return
            k, v = ev
            if k == e and e == "pe":
                return
            if deps.get(k, 0) < v:
                deps[k] = v
        for b in reads:
            add(b.w)
        for b in writes:
            add(b.w)
            for k, v in b.r.items():
                add((k, v))
        return deps

    def _mark(self, ev, reads, writes):
        k, v = ev
        for b in reads:
            if b.r.get(k, 0) < v:
                b.r[k] = v
        for b in writes:
            b.w = ev
            b.r = {}

    def _skip(self):
        self.emitted += 1
        return self.limit is not None and self.emitted > self.limit

    def op(self, e, fn, reads=(), writes=(), inc=True):
        if self._skip():
            return None
        deps = self._collect(e, reads, writes)
        self._need(e, deps)
        ins = fn(self.eng[e])
        self.ninstr += 1
        if inc:
            self.cnt[e] += 1
            ins.then_inc(self.sem[e], 1)
            ev = (e, self.cnt[e])
        else:
            ev = (e, self.cnt[e] + 1)
        self._mark(ev, reads, writes)
        return ins

    def dma(self, out, in_, reads=(), writes=(), q="sp", **kw):
        if self._skip():
            return
        i = self.dma_i
        self.dma_i += 1
        k = "dma%d" % (i % self.NDMA)
        deps = self._collect(q, reads, writes)
        if self.cnt[k] > 0 and deps.get(k, 0) < self.cnt[k]:
            deps[k] = self.cnt[k]
        self._need(q, deps)
        self.cnt[k] += 16
        self.eng[q].dma_start(out=out, in_=in_, **kw).then_inc(self.sem[k], 16)
        self.ninstr += 1
        self._mark((k, self.cnt[k]), reads, writes)

    def dma3(self, out, in_, n, **kw):
        for k in range(n):
            self.dma(out[:, k], in_[:, k], **kw)

    def barrier(self):
        for e in self.eng:
            deps = {k: v for k, v in self.cnt.items() if v > 0 and k != e}
            self._need(e, deps)


C_ID, C_TF, C_TB, C_BONES, C_LT, C_GT = 0, 1, 2, 3, 4, 5
NCM = 6


def make_cm():
    p = np.arange(128)[:, None]
    f = np.arange(128)[None, :]
    cm = np.zeros((128, NCM, 128), np.float32)
    cm[:, C_ID] = (p == f)
    cm[:, C_TF] = (p <= f)
    cm[:, C_TB] = (p >= f)
    cm[:, C_BONES] = ((p // 64) == (f // 64))
    cm[:, C_LT] = (p < f)
    cm[:, C_GT] = (p > f)
    return cm


def build_program():
    nc = bass.Bass("TRN2", target_bir_lowering=False)
    dt = lambda name, shape, kind="ExternalInput": nc.dram_tensor(name, shape, F32, kind=kind).ap()
    SCR = "ExternalOutput" if DEBUG else "Internal"
    xc = dt("xc", [NT, D])
    cc_d = dt("cc", [128, 16, 2])
    ada_w = dt("ada_w", [16, 128, 3 * D])
    ada_b_fm = dt("ada_b_fm", [128, 48])
    ada_b_g = dt("ada_b_g", [1, D])
    norm_g_fm = dt("norm_g_fm", [128, 16])
    w_in = dt("w_in", [D, NCOLS])
    hg_lb_b = dt("hg_lb_b", [128, 2, 2, HGW])
    hgng_b = dt("hgng_b", [128, HGW])
    mu_fm = dt("mu_fm", [128, 26, 4])
    w0_fm = dt("w0_fm", [128, 8, 2])
    a0_fm = dt("a0_fm", [128, 8, 2])
    w2_d = dt("w2", [128, RWW])
    a2_d = dt("a2", [128, RWW])
    kvec_fm = dt("kvec_fm", [128, 8, 3])
    gng_b = dt("gng_b", [128, RWW])
    gnb_b = dt("gnb_b", [128, RWW])
    w_hg_o = dt("w_hg_o", [HGW, D])
    w_rw_o = dt("w_rw_o", [RWW, D])
    w_o = dt("w_o", [D, D])
    fg_b = dt("fg_b", [128, D])
    cm_d = dt("cm", [128, NCM, 128])
    cst_d = dt("cst", [128, 8])
    out = dt("out", [SEQ, D], kind="ExternalOutput")
    HGtok = dt("HGtok", [NT, 5120], SCR)
    RWfm = dt("RWfm", [NCOLS - 5120, NT], SCR)
    OFs = dt("OFs", [SEQ, HGW], SCR)
    YH = dt("YH", [HGW, SEQ], SCR)
    AL = [dt("AL%d" % d, [RWW, NT], SCR) for d in range(2)]
    BE = [dt("BE%d" % d, [RWW, NT], SCR) for d in range(2)]
    KA = [dt("KA%d" % d, [RWW, NT], SCR) for d in range(2)]
    RH = [dt("RH%d" % d, [RWW, NT], SCR) for d in range(2)]
    BEt = [dt("BEt%d" % d, [NT, RWW], SCR) for d in range(2)]
    KAt = [dt("KAt%d" % d, [NT, RWW], SCR) for d in range(2)]
    Vt = dt("Vt", [NT, RWW], SCR)
    BON = dt("BON", [RWW, SEQ], SCR)
    YF = dt("YF", [SEQ, RWW], SCR)
    YR = dt("YR", [RWW, SEQ], SCR)
    scr_bufs = {}

    def sbuf_of(name):
        if name not in scr_bufs:
            scr_bufs[name] = Buf(name)
        return scr_bufs[name]

    with ExitStack() as st:
        fw = FW(nc, st)
        _FWREF.append(fw)
        _acct = {}

        def sbt(stack, name, shape):
            _acct[id(stack)] = _acct.get(id(stack), 0) + int(np.prod(shape[1:])) * 4
            if os.environ.get("KACCT"):
                print("sbuf", name, shape, "stack total KiB", _acct[id(stack)] / 1024.0, flush=True)
            return stack.enter_context(nc.sbuf_tensor("sb_" + name, shape, F32))
        pst = lambda stack, name, shape: stack.enter_context(nc.psum_tensor("ps_" + name, shape, F32))
        op = fw.op
        cm = sbt(st, "cm", [128, NCM, 128]); b_cm = Buf()
        cst = sbt(st, "cst", [128, 8]); b_cst = Buf()
        modA = sbt(st, "modA", [128, 16, 2]); modB = sbt(st, "modB", [128, 16, 2]); b_mod = Buf()
        gate_b = sbt(st, "gate_b", [128, D]); b_gate = Buf()
        wc = [sbt(st, "wc%d" % d, [128, 8, 18]) for d in range(2)]; b_wc = [Buf(), Buf()]
        fw.dma(cm[:], cm_d, writes=[b_cm])
        fw.dma(cst[:], cst_d, writes=[b_cst])
        ident = cm[:, C_ID, :]
        EPS6, EPS12, EPSGN, ZERO, ONE = (cst[:, i:i + 1] for i in range(5))

        for ph in _phase(0):
            adw = [sbt(ph, "adw%d" % i, [128, 3 * D]) for i in range(2)]; b_adw = [Buf(), Buf()]
            cct = sbt(ph, "cct", [128, 16, 2]); scc = sbt(ph, "scc", [128, 16, 2]); b_cc = Buf(); b_scc = Buf()
            mod = sbt(ph, "mod", [128, 48, 2]); b_modt = Buf()
            adb = sbt(ph, "adb", [128, 48]); ng = sbt(ph, "ng", [128, 16]); b_sm = Buf()
            adbg = sbt(ph, "adbg", [1, D]); grow = sbt(ph, "grow", [1, D]); b_grow = Buf()
            ps_mod = pst(ph, "ps_mod", [128, 96]); b_psm = Buf()
            ps_g = [pst(ph, "ps_g%d" % i, [128, 512]) for i in range(4)]; b_psg = [Buf() for _ in range(4)]
            fw.dma(cct[:], cc_d, writes=[b_cc])
            fw.dma(adb[:], ada_b_fm, writes=[b_sm])
            fw.dma(ng[:], norm_g_fm, writes=[b_sm])
            fw.dma(adbg[:], ada_b_g, writes=[b_sm])
            op("act", lambda e: e.activation(scc[:], cct[:], AF.Silu), reads=[b_cc], writes=[b_scc])
            op("dve", lambda e: e.memset(mod[:], 0.0), writes=[b_modt])
            for k in range(16):
                fw.dma(adw[k % 2][:], ada_w[k], writes=[b_adw[k % 2]])
                for m in range(48):
                    op("pe", lambda e: e.matmul(ps_mod[:, 2 * m:2 * m + 2], adw[k % 2][:, m * 128:(m + 1) * 128],
                                                scc[:, k, :], start=True, stop=True),
                       reads=[b_adw[k % 2], b_scc], writes=[b_psm], inc=(m == 47))
                op("dve", lambda e: e.tensor_tensor(mod[:], mod[:], ps_mod[:].rearrange("p (m v) -> p m v", v=2), ALU.add),
                   reads=[b_psm, b_modt], writes=[b_modt])
                for n in range(4):
                    op("pe", lambda e: e.matmul(ps_g[n][0:1, :], scc[:, k, 0:1], adw[k % 2][:, 2 * D + n * 512:2 * D + (n + 1) * 512],
                                                start=(k == 0), stop=(k == 15)),
                       reads=[b_adw[k % 2], b_scc], writes=[b_psg[n]])
            op("dve", lambda e: e.tensor_tensor(mod[:], mod[:], adb[:].unsqueeze(2).to_broadcast([128, 48, 2]), ALU.add),
               reads=[b_sm, b_modt], writes=[b_modt])
            op("dve", lambda e: e.tensor_scalar(modA[:], mod[:, 16:32, :], 1.0, None, ALU.add), reads=[b_modt], writes=[b_mod])
            op("dve", lambda e: e.tensor_tensor(modA[:], modA[:], ng[:].unsqueeze(2).to_broadcast([128, 16, 2]), ALU.mult),
               reads=[b_sm, b_mod], writes=[b_mod])
            op("dve", lambda e: e.tensor_copy(modB[:], mod[:, 0:16, :]), reads=[b_modt], writes=[b_mod])
            for n in range(4):
                op("dve", lambda e: e.tensor_tensor(grow[:, n * 512:(n + 1) * 512], ps_g[n][0:1, :], adbg[:, n * 512:(n + 1) * 512], ALU.add),
                   reads=[b_psg[n], b_sm], writes=[b_grow])
            for n in range(4):
                op("pe", lambda e: e.matmul(ps_g[n][:], cm[0:1, C_TF, :], grow[:, n * 512:(n + 1) * 512], start=True, stop=True),
                   reads=[b_cm, b_grow], writes=[b_psg[n]])
                op("act", lambda e: e.activation(gate_b[:, n * 512:(n + 1) * 512], ps_g[n][:], AF.Identity, scale=1.0),
                   reads=[b_psg[n]], writes=[b_gate])
        fw.barrier()

        for ph in _phase(1):
            lbt = sbt(ph, "lbt", [128, 2, 2, HGW]); lb_b = sbt(ph, "lb_b", [128, 2, HGW]); oml_b = sbt(ph, "oml_b", [128, 2, HGW])
            b_lbt = Buf(); b_lb = Buf()
            xt = [sbt(ph, "xt%d" % i, [128, D]) for i in range(2)]; b_xt = [Buf(), Buf()]
            junk = sbt(ph, "junk", [128, D]); b_junk = Buf()
            ss = sbt(ph, "ss", [128, 2]); b_ss = Buf()
            hT = sbt(ph, "hT", [128, 16, 512]); b_hT = [Buf() for _ in range(4)]
            wt = [sbt(ph, "wt%d" % i, [128, 16, 512]) for i in range(2)]; b_wt = [Buf(), Buf()]
            NOT = 4
            ot = [sbt(ph, "ot%d" % i, [128, 512]) for i in range(NOT)]; b_ot = [Buf() for _ in range(NOT)]
            tps = [pst(ph, "tps%d" % i, [128, 512]) for i in range(4)]; b_tps = [Buf() for _ in range(4)]
            acc = [pst(ph, "acc%d" % i, [128, 512]) for i in range(4)]; b_acc = [Buf() for _ in range(4)]
            fw.dma(lbt[:], hg_lb_b, writes=[b_lbt])
            op("dve", lambda e: e.tensor_tensor(lb_b[:], lbt[:, :, 0, :], lbt[:, :, 1, :], ALU.subtract), reads=[b_lbt], writes=[b_lb])
            op("act", lambda e: e.activation(lb_b[:], lb_b[:], AF.Sigmoid), reads=[b_lb], writes=[b_lb])
            op("dve", lambda e: e.tensor_scalar(oml_b[:], lb_b[:], -1.0, 1.0, ALU.mult, ALU.add), reads=[b_lb], writes=[b_lb])
            w_in_r = w_in.rearrange("(k p) c -> p k c", p=128)
            oti = 0
            ctx_groups = {2, 3, 4, 5, 6, 7, 12, 13, 14, 15, 16}
            tiles = [(0, 256, True)] + [(256 + 512 * i, 512, False) for i in range(4)]
            wti = 0
            for (g0, ntok, isctx) in tiles:
                v = 1 if isctx else 0
                nsub = ntok // 128
                for sub in range(nsub):
                    xb = sub % 2
                    fw.dma(xt[xb][:], xc[g0 + sub * 128:g0 + (sub + 1) * 128, :], writes=[b_xt[xb]])
                    op("act", lambda e: e.activation(junk[:], xt[xb][:], AF.Square, accum_out=ss[:, 0:1]),
                       reads=[b_xt[xb]], writes=[b_junk, b_ss])
                    op("act", lambda e: e.activation(ss[:, 1:2], ss[:, 0:1], AF.Sqrt, bias=EPS6, scale=1.0 / D),
                       reads=[b_ss, b_cst], writes=[b_ss])
                    op("dve", lambda e: e.reciprocal(ss[:, 1:2], ss[:, 1:2]), reads=[b_ss], writes=[b_ss])
                    op("act", lambda e: e.activation(junk[:], xt[xb][:], AF.Identity, scale=ss[:, 1:2], bias=ZERO),
                       reads=[b_xt[xb], b_ss, b_cst], writes=[b_junk])
                    for q in range(4):
                        for i in range(4):
                            j = q * 4 + i
                            op("pe", lambda e: e.transpose(tps[q][:, i * 128:(i + 1) * 128], junk[:, j * 128:(j + 1) * 128], ident),
                               reads=[b_junk, b_cm], writes=[b_tps[q]], inc=(i == 3))
                        for i in range(4):
                            j = q * 4 + i
                            en = "dve" if (i % 2 == 0) else "act"
                            if en == "dve":
                                op("dve", lambda e: e.tensor_scalar(hT[:, j, sub * 128:(sub + 1) * 128], tps[q][:, i * 128:(i + 1) * 128],
                                                                    modA[:, j, v:v + 1], modB[:, j, v:v + 1], ALU.mult, ALU.add),
                                   reads=[b_tps[q], b_mod], writes=[b_hT[sub]])
                            else:
                                op("act", lambda e: e.activation(hT[:, j, sub * 128:(sub + 1) * 128], tps[q][:, i * 128:(i + 1) * 128],
                                                                 AF.Identity, scale=modA[:, j, v:v + 1], bias=modB[:, j, v:v + 1]),
                                   reads=[b_tps[q], b_mod], writes=[b_hT[sub]])
                for g in range(27):
                    if isctx and g not in ctx_groups:
                        continue
                    c0 = g * 512
                    ncol = min(512, NCOLS - c0)
                    wb = wti % 2
                    wti += 1
                    fw.dma3(wt[wb][:, :, :ncol], w_in_r[:, :, c0:c0 + ncol], 16, writes=[b_wt[wb]])
                    if c0 < 5120:
                        typ = ["silu", "id", "fg0", "fg1", "silu"][c0 // 1024]
                        for sub in range(nsub):
                            a = sub % 4
                            for k in range(16):
                                op("pe", lambda e: e.matmul(acc[a][:, :ncol], hT[:, k, sub * 128:(sub + 1) * 128], wt[wb][:, k, :ncol],
                                                            start=(k == 0), stop=(k == 15)),
                                   reads=[b_hT[sub], b_wt[wb]], writes=[b_acc[a]], inc=(k == 15))
                            o_ = oti % NOT
                            oti += 1
                            if typ == "silu":
                                op("act", lambda e: e.activation(ot[o_][:], acc[a][:], AF.Silu), reads=[b_acc[a]], writes=[b_ot[o_]])
                            elif typ == "id":
                                op("dve", lambda e: e.tensor_copy(ot[o_][:], acc[a][:]), reads=[b_acc[a]], writes=[b_ot[o_]])
                            else:
                                dd = int(typ[2])
                                cc0 = c0 - (2048 + dd * 1024)
                                op("act", lambda e: e.activation(ot[o_][:], acc[a][:], AF.Sigmoid), reads=[b_acc[a]], writes=[b_ot[o_]])
                                op("dve", lambda e: e.tensor_tensor(ot[o_][:], ot[o_][:], oml_b[:, dd, cc0:cc0 + 512], ALU.mult),
                                   reads=[b_ot[o_], b_lb], writes=[b_ot[o_]])
                                op("dve", lambda e: e.tensor_tensor(ot[o_][:], ot[o_][:], lb_b[:, dd, cc0:cc0 + 512], ALU.add),
                                   reads=[b_ot[o_], b_lb], writes=[b_ot[o_]])
                            fw.dma(HGtok[g0 + sub * 128:g0 + (sub + 1) * 128, c0:c0 + 512], ot[o_][:],
                                   reads=[b_ot[o_]], writes=[sbuf_of("HGtok")], q=SQ)
                    else:
                        for m in range(ncol // 128):
                            mc = c0 // 128 + m
                            if isctx and not (48 <= mc <= 65):
                                continue
                            a = m % 4
                            for k in range(16):
                                op("pe", lambda e: e.matmul(acc[a][:, :ntok], wt[wb][:, k, m * 128:(m + 1) * 128], hT[:, k, :ntok],
                                                            start=(k == 0), stop=(k == 15)),
                                   reads=b_hT[:nsub] + [b_wt[wb]], writes=[b_acc[a]], inc=(k == 15))
                            o_ = oti % NOT
                            oti += 1
                            if mc <= 65:
                                op("dve", lambda e: e.tensor_copy(ot[o_][:, :ntok], acc[a][:, :ntok]), reads=[b_acc[a]], writes=[b_ot[o_]])
                            else:
                                fn = AF.Silu if mc <= 73 else AF.Sigmoid
                                op("act", lambda e: e.activation(ot[o_][:, :ntok], acc[a][:, :ntok], fn), reads=[b_acc[a]], writes=[b_ot[o_]])
                            fw.dma(RWfm[(mc - 40) * 128:(mc - 39) * 128, g0:g0 + ntok], ot[o_][:, :ntok],
                                   reads=[b_ot[o_]], writes=[sbuf_of("RWfm")], q=SQ)
        fw.barrier()

        for ph in _phase(2):
            NB = 2
            fgt = [sbt(ph, "fgt%d" % i, [CH, HGW]) for i in range(NB)]
            vtt = [sbt(ph, "vtt%d" % i, [CH, HGW]) for i in range(NB)]
            qtt = [sbt(ph, "qtt%d" % i, [CH, HGW]) for i in range(NB)]
            ztt = [sbt(ph, "ztt%d" % i, [CH, HGW]) for i in range(NB)]
            oft = [sbt(ph, "oft%d" % i, [CH, HGW]) for i in range(NB)]
            b_ld = [[Buf() for _ in range(5)] for _ in range(NB)]
            gt = sbt(ph, "gt", [CH, HGW]); b_gt = Buf()
            Et = sbt(ph, "Et", [CH, HGW]); Ei = sbt(ph, "Ei", [CH, HGW]); b_E = Buf(); b_Ei = Buf()
            kt_ = sbt(ph, "kt", [128, HGW]); qq_ = sbt(ph, "qq", [128, HGW]); b_kt = Buf(); b_qq = Buf()
            kt = kt_[0:CH, :]; qq = qq_[0:CH, :]
            ob = sbt(ph, "ob", [CH, HGW]); sq_ = sbt(ph, "sq", [128, HGW]); b_ob = Buf(); b_sq = Buf()
            sq = sq_[0:CH, :]
            ms = sbt(ph, "ms", [CH, 8]); b_ms = Buf()
            eb = sbt(ph, "eb", [128, 8]); b_eb = Buf()
            hng = sbt(ph, "hng", [CH, HGW]); b_hng = Buf()
            qkT = [sbt(ph, "qkT%d" % i, [128, 2, CH]) for i in range(2)]; b_qkT = [Buf(), Buf()]
            at = [sbt(ph, "at%d" % i, [CH, CH]) for i in range(2)]; b_at = [Buf(), Buf()]
            S = [sbt(ph, "S%d" % i, [128, 8, 128]) for i in range(2)]; b_S = [Buf(), Buf()]
            stmp = sbt(ph, "stmp", [128, 8, 128]); b_stmp = Buf()
            yT = sbt(ph, "yTs", [128, 8, 512]); b_yT = Buf()
            bc_ps = [pst(ph, "bc_ps%d" % i, [128, 512]) for i in range(2)]; b_bc = [Buf(), Buf()]
            o_ps = [pst(ph, "o_ps%d" % i, [128, 512]) for i in range(2)]; b_o = [Buf(), Buf()]
            dS_ps = [pst(ph, "dS_ps%d" % i, [128, 512]) for i in range(2)]; b_dS = [Buf(), Buf()]
            mz = pst(ph, "mz", [128, 512]); b_tp = [Buf()] * 2; b_aps = [Buf()] * 2; b_ebp = b_aps[0]
            yT_ps = pst(ph, "yT_ps", [128, 512]); b_yTp = Buf()
            fw.dma(hng[:], hgng_b[0:CH, :], writes=[b_hng])
            op("dve", lambda e: e.memset(kt_[:], 0.0), writes=[b_kt])
            op("dve", lambda e: e.memset(qq_[:], 0.0), writes=[b_qq])
            op("dve", lambda e: e.memset(sq_[:], 0.0), writes=[b_sq])
            ci = 0
            for d in range(min(2, KD)):
                tri = cm[0:CH, C_TF if d == 0 else C_TB, 0:CH]
                lastc = CH - 1 if d == 0 else 0
                onehot = cm[0:CH, C_ID, lastc:lastc + 1]
                order = list(range(NT // CH)) if d == 0 else list(range(CTX // CH - 1, -1, -1)) + list(range(NT // CH - 1, CTX // CH - 1, -1))
                cur = 0
                op("dve", lambda e: e.memset(S[0][:], 0.0), writes=[b_S[0]])
                for c in order[:KCH]:
                    isctx = c < CTX // CH
                    t0 = c * CH
                    lb_ = ci % NB
                    ci += 1
                    fgs, vs, qs, zs, ofs = fgt[lb_], vtt[lb_], qtt[lb_], ztt[lb_], oft[lb_]
                    bl = b_ld[lb_]
                    fw.dma(fgs[:], HGtok[t0:t0 + CH, 2048 + d * 1024:3072 + d * 1024], reads=[sbuf_of("HGtok")], writes=[bl[0]])
                    fw.dma(vs[:], HGtok[t0:t0 + CH, 1024:2048], reads=[sbuf_of("HGtok")], writes=[bl[1]])
                    if not isctx:
                        fw.dma(qs[:], HGtok[t0:t0 + CH, 0:1024], reads=[sbuf_of("HGtok")], writes=[bl[2]])
                        if d == 1:
                            fw.dma(zs[:], HGtok[t0:t0 + CH, 4096:5120], reads=[sbuf_of("HGtok")], writes=[bl[3]])
                            fw.dma(ofs[:], OFs[t0 - CTX:t0 - CTX + CH, :], reads=[sbuf_of("OFs")], writes=[bl[4]])
                    op("act", lambda e: e.activation(gt[:], fgs[:], AF.Ln), reads=[bl[0]], writes=[b_gt])
                    for n in range(2):
                        op("pe", lambda e: e.matmul(bc_ps[n][0:CH, :], tri, gt[:, n * 512:(n + 1) * 512], start=True, stop=True),
                           reads=[b_gt, b_cm], writes=[b_bc[n]])
                    for n in range(2):
                        sl = slice(n * 512, (n + 1) * 512)
                        op("act", lambda e: e.activation(Ei[:, sl], bc_ps[n][0:CH, :], AF.Exp, scale=-1.0), reads=[b_bc[n]], writes=[b_Ei])
                        op("act", lambda e: e.activation(Et[:, sl], bc_ps[n][0:CH, :], AF.Exp), reads=[b_bc[n]], writes=[b_E])
                    op("dve", lambda e: e.tensor_scalar(kt[:], fgs[:], -1.0, 1.0, ALU.mult, ALU.add), reads=[bl[0]], writes=[b_kt])
                    op("dve", lambda e: e.tensor_tensor(kt[:], kt[:], Ei[:], ALU.mult), reads=[b_kt, b_Ei], writes=[b_kt])
                    if not isctx:
                        op("dve", lambda e: e.tensor_tensor(qq[:], qs[:], Et[:], ALU.mult), reads=[bl[2], b_E], writes=[b_qq])
                    for h in range(8):
                        op("pe", lambda e: e.matmul(yT_ps[:, 400 + h:401 + h], Et[:, h * 128:(h + 1) * 128], onehot, start=True, stop=True),
                           reads=[b_E, b_cm], writes=[b_ebp], inc=(h == 7))
                    op("dve", lambda e: e.tensor_copy(eb[:], yT_ps[:, 400:408]), reads=[b_ebp], writes=[b_eb])
                    nxt = 1 - cur
                    for h in range(8):
                        hs = slice(h * 128, (h + 1) * 128)
                        pp = h % 2
                        if not isctx:
                            tpv = mz[:, pp * 256:(pp + 1) * 256]
                        if (not isctx) and not (KX & 1):
                            op("pe", lambda e: e.transpose(tpv[:, 0:128], qq_[:, hs], ident), reads=[b_qq, b_cm], writes=[b_tp[pp]], inc=False)
                            op("pe", lambda e: e.transpose(tpv[:, 128:256], kt_[:, hs], ident), reads=[b_kt, b_cm], writes=[b_tp[pp]])
                            op("dve", lambda e: e.tensor_copy(qkT[pp][:], tpv.rearrange("p (a b) -> p a b", b=128)[:, :, 0:CH]),
                               reads=[b_tp[pp]], writes=[b_qkT[pp]])
                            apv = yT_ps[0:CH, 256 + pp * 64:256 + pp * 64 + CH]
                        if (not isctx) and not (KX & 2):
                            op("pe", lambda e: e.matmul(apv, qkT[pp][:, 1, :], qkT[pp][:, 0, :], start=True, stop=True),
                               reads=[b_qkT[pp]], writes=[b_aps[pp]])
                            op("dve", lambda e: e.tensor_tensor(at[pp][:], apv, tri, ALU.mult), reads=[b_aps[pp], b_cm], writes=[b_at[pp]])
                            opv = o_ps[h // 4][0:CH, (h % 4) * 128:(h % 4 + 1) * 128]
                        if (not isctx) and not (KX & 4):
                            op("pe", lambda e: e.matmul(opv, at[pp][:], vs[:, hs], start=True, stop=False),
                               reads=[b_at[pp], bl[1]], writes=[b_o[h // 4]], inc=False)
                            op("pe", lambda e: e.matmul(opv, qkT[pp][:, 0, :], S[cur][:, h, :], start=False, stop=True),
                               reads=[b_qkT[pp], b_S[cur]], writes=[b_o[h // 4]])
                        dsv = dS_ps[h // 4][:, (h % 4) * 128:(h % 4 + 1) * 128]
                        op("pe", lambda e: e.matmul(dsv, kt[:, hs], vs[:, hs], start=True, stop=True),
                           reads=[b_kt, bl[1]], writes=[b_dS[h // 4]])
                    for n in range(2):
                        op("dve", lambda e: e.tensor_tensor(stmp[:, n * 4:(n + 1) * 4, :], dS_ps[n][:].rearrange("p (h v) -> p h v", v=128),
                                                            S[cur][:, n * 4:(n + 1) * 4, :], ALU.add),
                           reads=[b_dS[n], b_S[cur]], writes=[b_stmp])
                    op("dve", lambda e: e.tensor_tensor(S[nxt][:], stmp[:], eb[:].unsqueeze(2).to_broadcast([128, 8, 128]), ALU.mult),
                       reads=[b_stmp, b_eb], writes=[b_S[nxt]])
                    cur = nxt
                    if isctx or (KX & 8):
                        continue
                    xt0 = t0 - 256
                    if d == 0:
                        for n in range(2):
                            op("dve", lambda e: e.tensor_copy(ob[:, n * 512:(n + 1) * 512], o_ps[n][0:CH, :]),
                               reads=[b_o[n]], writes=[b_ob])
                        fw.dma(OFs[xt0:xt0 + CH, :], ob[:], reads=[b_ob], writes=[sbuf_of("OFs")], q=SQ)
                    else:
                        for n in range(2):
                            op("dve", lambda e: e.tensor_tensor(ob[:, n * 512:(n + 1) * 512], o_ps[n][0:CH, :], ofs[:, n * 512:(n + 1) * 512], ALU.add),
                               reads=[b_o[n], bl[4]], writes=[b_ob])
                        op("dve", lambda e: e.tensor_tensor(sq[:], ob[:], ob[:], ALU.mult), reads=[b_ob], writes=[b_sq])
                        op("dve", lambda e: e.tensor_reduce(ms[:], sq[:].rearrange("p (h v) -> p h v", v=128), AX.X, ALU.add), reads=[b_sq], writes=[b_ms])
                        op("act", lambda e: e.activation(ms[:], ms[:], AF.Sqrt, bias=cst[0:CH, 0:1], scale=1.0 / 128), reads=[b_ms, b_cst], writes=[b_ms])
                        op("dve", lambda e: e.reciprocal(ms[:], ms[:]), reads=[b_ms], writes=[b_ms])
                        op("dve", lambda e: e.tensor_tensor(sq[:].rearrange("p (h v) -> p h v", v=128), ob[:].rearrange("p (h v) -> p h v", v=128),
                                                            ms[:].unsqueeze(2).to_broadcast([CH, 8, 128]), ALU.mult),
                           reads=[b_ob, b_ms], writes=[b_sq])
                        op("dve", lambda e: e.tensor_tensor(sq[:], sq[:], hng[:], ALU.mult), reads=[b_sq, b_hng], writes=[b_sq])
                        op("dve", lambda e: e.tensor_tensor(sq[:], sq[:], zs[:], ALU.mult), reads=[b_sq, bl[3]], writes=[b_sq])
                        for h in range(8):
                            op("pe", lambda e: e.transpose(bc_ps[h // 4][:, (h % 4) * 128:(h % 4 + 1) * 128], sq_[:, h * 128:(h + 1) * 128], ident),
                               reads=[b_sq, b_cm], writes=[b_bc[h // 4]], inc=(h % 4 == 3))
                        for n in range(2):
                            yo_ = xt0 % 512
                            op("dve", lambda e: e.tensor_copy(yT[:, n * 4:(n + 1) * 4, yo_:yo_ + CH], bc_ps[n][:].rearrange("p (h t) -> p h t", t=128)[:, :, 0:CH]),
                               reads=[b_bc[n]], writes=[b_yT])
                        if xt0 % 512 == 0:
                            fw.dma3(YH.rearrange("(h p) t -> p h t", p=128)[:, :, xt0:xt0 + 512], yT[:], 8, reads=[b_yT], writes=[sbuf_of("YH")], q=SQ)
        fw.barrier()

        for ph in _phase(3):
            W = 640
            mu = sbt(ph, "mu", [128, 26, 4]); omu = sbt(ph, "omu", [128, 26]); b_mu = Buf()
            w0t = sbt(ph, "w0t", [128, 8, 2]); a0t = sbt(ph, "a0t", [128, 8, 2]); kvt = sbt(ph, "kvt", [128, 8, 3]); omka = sbt(ph, "omka", [128, 8])
            w2t = sbt(ph, "w2t", [128, RWW]); a2t = sbt(ph, "a2t", [128, RWW]); b_par = Buf()
            raw = [sbt(ph, "raw%d" % i, [128, W]) for i in range(3)]; b_raw = [Buf() for _ in range(3)]
            rawl = [sbt(ph, "rawl%d" % i, [128, W]) for i in range(2)]; b_rawl = [Buf(), Buf()]
            sh = [sbt(ph, "sh%d" % i, [128, 512]) for i in range(3)]; b_sh = [Buf() for _ in range(3)]
            tw = sbt(ph, "tw", [128, 512]); als = sbt(ph, "als", [128, 512]); b_tw = Buf(); b_als = Buf()
            kkr = sbt(ph, "kkr", [128, 512]); kk = sbt(ph, "kk", [128, 512]); t1 = sbt(ph, "t1", [128, 512]); b_kkr = Buf(); b_kk = Buf(); b_t1 = Buf()
            lgw = sbt(ph, "lgw", [128, 512]); av = sbt(ph, "av", [128, 512]); b_lgw = Buf(); b_av = Buf()
            kd = [sbt(ph, "kd%d" % i, [128, 512]) for i in range(2)]; b_kd = [Buf(), Buf()]
            bv = sbt(ph, "bv", [128, 512]); b_bv = Buf()
            lt = sbt(ph, "lt", [128, 128]); b_lt = Buf()
            Ecw = sbt(ph, "Ecw", [128, 512]); Einv = sbt(ph, "Einv", [128, 512]); Eex = sbt(ph, "Eex", [128, 512]); b_Ecw = Buf(); b_Einv = Buf(); b_Eex = Buf()
            res = [sbt(ph, "res%d" % i, [128, 512]) for i in range(4)]; b_res = [Buf() for _ in range(4)]
            tm = [[sbt(ph, "tm%d_%d" % (a_, s_), [128, RWW]) for s_ in range(4)] for a_ in range(5)]; b_tm = [[Buf() for _ in range(4)] for _ in range(5)]
            p_a = pst(ph, "p_a", [128, 512]); p_b = pst(ph, "p_b", [128, 512]); p_cw = pst(ph, "p_cw", [128, 512]); p_tq = [pst(ph, "p_t%d" % i, [128, 512]) for i in range(4)]
            p_s = pst(ph, "p_s", [128, 512])
            b_pa = Buf(); b_pb = Buf(); b_pcw = Buf(); b_pt = [Buf() for _ in range(4)]; b_ps = Buf()
            for (tl, src) in [(mu, mu_fm), (w0t, w0_fm), (a0t, a0_fm), (kvt, kvec_fm), (w2t, w2_d), (a2t, a2_d)]:
                fw.dma(tl[:], src, writes=[b_par if tl is not mu else b_mu])
            op("dve", lambda e: e.tensor_reduce(omu[:], mu[:], AX.X, ALU.add), reads=[b_mu], writes=[b_mu])
            op("dve", lambda e: e.tensor_scalar(omu[:], omu[:], -1.0, 1.0, ALU.mult, ALU.add), reads=[b_mu], writes=[b_mu])
            op("dve", lambda e: e.tensor_scalar(omka[:], kvt[:, :, 1], -1.0, 1.0, ALU.mult, ALU.add), reads=[b_par], writes=[b_par])
            omuc = sbt(ph, "omuc", [128, 26])
            op("dve", lambda e: e.tensor_tensor(omuc[:], mu[:, :, 0], mu[:, :, 1], ALU.add), reads=[b_mu], writes=[b_mu])
            op("dve", lambda e: e.tensor_scalar(omuc[:], omuc[:], -1.0, 1.0, ALU.mult, ALU.add), reads=[b_mu], writes=[b_mu])

            def shift(dst, b_dst, rawt, b_rawt, chunk, g0, ntok, isctx, eng="dve"):
                rows = RWfm[chunk * 128:(chunk + 1) * 128, :]
                if isctx:
                    fw.dma(rawt[:, 0:256], rows[:, 0:256], reads=[sbuf_of("RWfm")], writes=[b_rawt])
                    P = rawt[:, 0:256]
                    op(eng, lambda e: e.tensor_scalar(dst[:, 0:256], P, omuc[:, chunk:chunk + 1], None, ALU.mult), reads=[b_rawt, b_mu], writes=[b_dst])
                    op(eng, lambda e: e.scalar_tensor_tensor(dst[:, 1:256], P[:, 0:255], mu[:, chunk, 0:1], dst[:, 1:256], ALU.mult, ALU.add),
                       reads=[b_rawt, b_mu, b_dst], writes=[b_dst])
                    op(eng, lambda e: e.scalar_tensor_tensor(dst[:, 0:255], P[:, 1:256], mu[:, chunk, 1:2], dst[:, 0:255], ALU.mult, ALU.add),
                       reads=[b_rawt, b_mu, b_dst], writes=[b_dst])
                    return
                lo = g0 - 64
                hi = g0 + ntok + 64
                first = (g0 == CTX)
                last = (g0 + ntok == NT)
                if first:
                    op(eng, lambda e: e.memset(rawt[:, 0:64], 0.0), writes=[b_rawt])
                if last:
                    op(eng, lambda e: e.memset(rawt[:, W - 64:W], 0.0), writes=[b_rawt])
                a_ = 64 if first else 0
                b_ = W - 64 if last else W
                fw.dma(rawt[:, a_:b_], rows[:, lo + a_:lo + b_], reads=[sbuf_of("RWfm")], writes=[b_rawt])
                P3 = rawt[:].rearrange("p (r c) -> p r c", c=64)
                Pc = P3[:, 1:9, :]
                d3 = dst[:].rearrange("p (r c) -> p r c", c=64)
                op(eng, lambda e: e.tensor_scalar(d3, Pc, omu[:, chunk:chunk + 1], None, ALU.mult), reads=[b_rawt, b_mu], writes=[b_dst])
                op(eng, lambda e: e.scalar_tensor_tensor(d3[:, :, 1:], Pc[:, :, :-1], mu[:, chunk, 0:1], d3[:, :, 1:], ALU.mult, ALU.add),
                   reads=[b_rawt, b_mu, b_dst], writes=[b_dst])
                op(eng, lambda e: e.scalar_tensor_tensor(d3[:, :, :-1], Pc[:, :, 1:], mu[:, chunk, 1:2], d3[:, :, :-1], ALU.mult, ALU.add),
                   reads=[b_rawt, b_mu, b_dst], writes=[b_dst])
                op(eng, lambda e: e.scalar_tensor_tensor(d3, P3[:, 0:8, :], mu[:, chunk, 2:3], d3, ALU.mult, ALU.add),
                   reads=[b_rawt, b_mu, b_dst], writes=[b_dst])
                op(eng, lambda e: e.scalar_tensor_tensor(d3, P3[:, 2:10, :], mu[:, chunk, 3:4], d3, ALU.mult, ALU.add),
                   reads=[b_rawt, b_mu, b_dst], writes=[b_dst])

            tiles = [(0, 256, True)] + [(256 + 512 * i, 512, False) for i in range(4)]
            ri = 0
            for (g0, ntok, isctx) in tiles[KT0:KT]:
                nsub = ntok // 128
                blk0 = g0 // 128
                N_ = slice(0, ntok)
                shift(tw, b_tw, rawl[0], b_rawl[0], 24, g0, ntok, isctx)
                if not (KX & 32):
                    op("act", lambda e: e.activation(tw[:, N_], tw[:, N_], AF.Tanh), reads=[b_tw], writes=[b_tw])
                shift(als, b_als, rawl[1], b_rawl[1], 25, g0, ntok, isctx)
                for j in range(KJ):
                    if not isctx:
                        shift(sh[0], b_sh[0], raw[0], b_raw[0], j, g0, ntok, isctx)
                    shift(sh[1], b_sh[1], raw[1], b_raw[1], 8 + j, g0, ntok, isctx)
                    shift(sh[2], b_sh[2], raw[2], b_raw[2], 16 + j, g0, ntok, isctx)
                    rs, ks, vs = sh[0], sh[1], sh[2]
                    op("dve", lambda e: e.tensor_scalar(kkr[:, N_], ks[:, N_], kvt[:, j, 0:1], None, ALU.mult), reads=[b_sh[1], b_par], writes=[b_kkr])
                    op("dve", lambda e: e.tensor_tensor(t1[:, N_], kkr[:, N_], kkr[:, N_], ALU.mult), reads=[b_kkr], writes=[b_t1])
                    op("pe", lambda e: e.matmul(p_a[:, N_], cm[:, C_BONES, :], t1[:, N_], start=True, stop=True), reads=[b_cm, b_t1], writes=[b_pa])
                    op("act", lambda e: e.activation(t1[:, N_], p_a[:, N_], AF.Sqrt, bias=EPS12, scale=1.0), reads=[b_pa, b_cst], writes=[b_t1])
                    op("dve", lambda e: e.reciprocal(t1[:, N_], t1[:, N_]), reads=[b_t1], writes=[b_t1])
                    op("dve", lambda e: e.tensor_tensor(kk[:, N_], kkr[:, N_], t1[:, N_], ALU.mult), reads=[b_kkr, b_t1], writes=[b_kk])
                    for sub in range(nsub):
                        q_ = sub % 4
                        op("pe", lambda e: e.transpose(p_tq[q_][:, 0:128], vs[:, sub * 128:(sub + 1) * 128], ident),
                           reads=[b_sh[2], b_cm], writes=[b_pt[q_]])
                        op("dve", lambda e: e.tensor_copy(tm[0][sub][:, j * 128:(j + 1) * 128], p_tq[q_][:, 0:128]), reads=[b_pt[q_]], writes=[b_tm[0][sub]])
                    for d in range(2):
                        ds = slice(d * 64, (d + 1) * 64)
                        js = slice(j * 128, (j + 1) * 128)
                        op("pe", lambda e: e.matmul(p_a[:, N_], w2t[ds, js], tw[ds, N_], start=True, stop=True), reads=[b_par, b_tw], writes=[b_pa])
                        op("act", lambda e: e.activation(lgw[:, N_], p_a[:, N_], AF.Sigmoid, bias=w0t[:, j, d:d + 1], scale=1.0), reads=[b_pa, b_par], writes=[b_lgw])
                        op("dve", lambda e: e.tensor_scalar(lgw[:, N_], lgw[:, N_], -0.6065306597126334, None, ALU.mult), reads=[b_lgw], writes=[b_lgw])
                        op("pe", lambda e: e.matmul(p_b[:, N_], a2t[ds, js], als[ds, N_], start=True, stop=True), reads=[b_par, b_als], writes=[b_pb])
                        op("act", lambda e: e.activation(av[:, N_], p_b[:, N_], AF.Sigmoid, bias=a0t[:, j, d:d + 1], scale=1.0), reads=[b_pb, b_par], writes=[b_av])
                        op("dve", lambda e: e.tensor_scalar(t1[:, N_], av[:, N_], kvt[:, j, 1:2], omka[:, j:j + 1], ALU.mult, ALU.add), reads=[b_av, b_par], writes=[b_t1])
                        op("dve", lambda e: e.tensor_tensor(kd[d][:, N_], t1[:, N_], ks[:, N_], ALU.mult), reads=[b_t1, b_sh[1]], writes=[b_kd[d]])
                        op("dve", lambda e: e.tensor_tensor(bv[:, N_], kk[:, N_], av[:, N_], ALU.mult), reads=[b_kk, b_av], writes=[b_bv])
                        tri = cm[:, C_TF if d == 0 else C_TB, :]
                        for sub in range(nsub):
                            ss_ = slice(sub * 128, (sub + 1) * 128)
                            q_ = sub % 4
                            op("pe", lambda e: e.transpose(p_tq[q_][:, 0:128], lgw[:, ss_], ident), reads=[b_lgw, b_cm], writes=[b_pt[q_]])
                            op("dve", lambda e: e.tensor_copy(lt[:], p_tq[q_][:, 0:128]), reads=[b_pt[q_]], writes=[b_lt])
                            op("pe", lambda e: e.matmul(p_cw[:, ss_], lt[:], tri, start=True, stop=True), reads=[b_lt, b_cm], writes=[b_pcw])
                        op("act", lambda e: e.activation(Ecw[:, N_], p_cw[:, N_], AF.Exp), reads=[b_pcw], writes=[b_Ecw])
                        op("act", lambda e: e.activation(Einv[:, N_], p_cw[:, N_], AF.Exp, scale=-1.0), reads=[b_pcw], writes=[b_Einv])
                        op("dve", lambda e: e.tensor_tensor(t1[:, N_], p_cw[:, N_], lgw[:, N_], ALU.subtract), reads=[b_pcw, b_lgw], writes=[b_t1])
                        op("act", lambda e: e.activation(Eex[:, N_], t1[:, N_], AF.Exp), reads=[b_t1], writes=[b_Eex])
                        lastc = 127 if d == 0 else 0
                        for sub in range(nsub):
                            op("dve", lambda e: e.tensor_copy(wc[d][:, j, blk0 + sub:blk0 + sub + 1], Ecw[:, sub * 128 + lastc:sub * 128 + lastc + 1]),
                               reads=[b_Ecw], writes=[b_wc[d]])
                        prods = [(kk, b_kk, Eex, b_Eex, AL[d], "AL%d" % d), (bv, b_bv, Einv, b_Einv, BE[d], "BE%d" % d),
                                 (kd[d], b_kd[d], Einv, b_Einv, KA[d], "KA%d" % d)]
                        if not isctx:
                            prods.append((rs, b_sh[0], Ecw, b_Ecw, RH[d], "RH%d" % d))
                        for pi_, (x_, bx_, y_, by_, dst, nm) in enumerate(prods):
                            op("dve", lambda e: e.tensor_tensor(res[pi_][:, N_], x_[:, N_], y_[:, N_], ALU.mult), reads=[bx_, by_], writes=[b_res[pi_]])
                            if not (KX & 64):
                                fw.dma(dst[js, g0:g0 + ntok], res[pi_][:, N_], reads=[b_res[pi_]], writes=[sbuf_of(nm)], q=SQ)
                            if pi_ in (1, 2):
                                dstT = BEt[d] if pi_ == 1 else KAt[d]
                                nmT = ("BEt%d" if pi_ == 1 else "KAt%d") % d
                                for sub in range(nsub):
                                    q_ = sub % 4
                                    srcT = kk if (KX & 128) else res[pi_]
                                    op("pe", lambda e: e.transpose(p_tq[q_][:, 0:128], srcT[:, sub * 128:(sub + 1) * 128], ident),
                                       reads=[b_res[pi_], b_cm], writes=[b_pt[q_]])
                                    ta = 1 + 2 * d + (pi_ - 1)
                                    op("dve", lambda e: e.tensor_copy(tm[ta][sub][:, js], p_tq[q_][:, 0:128]), reads=[b_pt[q_]], writes=[b_tm[ta][sub]])
                    if not isctx:
                        op("dve", lambda e: e.tensor_tensor(t1[:], kd[0][:], kd[1][:], ALU.add), reads=[b_kd[0], b_kd[1]], writes=[b_t1])
                        op("dve", lambda e: e.scalar_tensor_tensor(t1[:], t1[:], kvt[:, j, 2:3], rs[:], ALU.mult, ALU.mult), reads=[b_t1, b_par, b_sh[0]], writes=[b_t1])
                        op("pe", lambda e: e.matmul(p_s[:], cm[:, C_BONES, :], t1[:], start=True, stop=True), reads=[b_cm, b_t1], writes=[b_ps])
                        op("dve", lambda e: e.tensor_tensor(res[3][:], p_s[:], vs[:], ALU.mult), reads=[b_ps, b_sh[2]], writes=[b_res[3]])
                        fw.dma(BON[j * 128:(j + 1) * 128, g0 - CTX:g0 - CTX + 512], res[3][:], reads=[b_res[3]], writes=[sbuf_of("BON")], q=SQ)
                for sub in range(nsub if not (KX & 16) else 0):
                    rows = slice(g0 + sub * 128, g0 + (sub + 1) * 128)
                    for ta, (dstT, nmT) in enumerate([(Vt, "Vt"), (BEt[0], "BEt0"), (KAt[0], "KAt0"), (BEt[1], "BEt1"), (KAt[1], "KAt1")]):
                        fw.dma(dstT[rows, :], tm[ta][sub][:], reads=[b_tm[ta][sub]], writes=[sbuf_of(nmT)], q=SQ)
        fw.barrier()

        for ph in _phase(4):
            J = 8
            aT = [sbt(ph, "aT%d" % j, [128, 128]) for j in range(J)]
            bT = [sbt(ph, "bT%d" % j, [128, 128]) for j in range(J)]
            kT = [sbt(ph, "kT%d" % j, [128, 128]) for j in range(J)]
            rT = [sbt(ph, "rT%d" % j, [128, 128]) for j in range(J)]
            btkA = sbt(ph, "btkA", [128, RWW]); ktkA = sbt(ph, "ktkA", [128, RWW]); vtkA = sbt(ph, "vtkA", [128, RWW])
            btk = [btkA[:, j * 128:(j + 1) * 128] for j in range(J)]
            ktk = [ktkA[:, j * 128:(j + 1) * 128] for j in range(J)]
            vtk = [vtkA[:, j * 128:(j + 1) * 128] for j in range(J)]
            b_tokA = [Buf(), Buf(), Buf()]
            b_in = [[Buf() for _ in range(7)] for _ in range(J)]
            Pm = [[sbt(ph, "Pm%d_%d" % (j, i), [128, 2, 128]) for i in range(2)] for j in range(J)]
            PTm = [[sbt(ph, "PTm%d_%d" % (j, i), [128, 2, 128]) for i in range(2)] for j in range(J)]
            TTm = [[sbt(ph, "TTm%d_%d" % (j, i), [128, 2, 128]) for i in range(2)] for j in range(J)]
            b_P = [[Buf(), Buf()] for _ in range(J)]; b_PT = [[Buf(), Buf()] for _ in range(J)]; b_TT = [[Buf(), Buf()] for _ in range(J)]
            AkT = [sbt(ph, "AkT%d" % j, [128, 2, 128]) for j in range(J)]
            BbT = [sbt(ph, "BbT%d" % j, [128, 2, 128]) for j in range(J)]
            BkT = [sbt(ph, "BkT%d" % j, [128, 2, 128]) for j in range(J)]
            b_AkT = [Buf() for _ in range(J)]; b_BbT = [Buf() for _ in range(J)]; b_BkT = [Buf() for _ in range(J)]
            St = [[sbt(ph, "St%d_%d" % (j, i), [128, 64]) for i in range(2)] for j in range(J)]
            b_St = [[Buf(), Buf()] for _ in range(J)]
            Rn = [sbt(ph, "Rn%d" % i, [128, 2, 64]) for i in range(2)]; b_Rn = [Buf(), Buf()]
            Us = [sbt(ph, "Us%d" % i, [128, 2, 64]) for i in range(2)]; b_Us = [Buf(), Buf()]
            stt_ = [sbt(ph, "stt%d" % i, [128, 64]) for i in range(2)]; b_stt = [Buf(), Buf()]
            Ys = [sbt(ph, "Ys%d" % i, [128, 2, 64]) for i in range(2)]; b_Ys = [Buf(), Buf()]
            Yf = [sbt(ph, "Yf%d" % i, [128, 2, 64]) for i in range(2)]; b_Yf = [Buf(), Buf()]
            cen = [sbt(ph, "cen%d" % i, [128, 2, 64]) for i in range(2)]; b_cen = [Buf(), Buf()]
            gsq = [sbt(ph, "gsq%d" % i, [128, 2, 64]) for i in range(2)]; b_gsq = [Buf(), Buf()]
            gst = [sbt(ph, "gst%d" % i, [128, 4]) for i in range(2)]; b_gst = [Buf(), Buf()]
            bon = [sbt(ph, "bon%d" % i, [128, 128]) for i in range(2)]; zr = [sbt(ph, "zr%d" % i, [128, 128]) for i in range(2)]
            b_bon = [Buf(), Buf()]; b_zr = [Buf(), Buf()]
            yo = [sbt(ph, "yo%d" % i, [128, 128]) for i in range(2)]; b_yo = [Buf(), Buf()]
            gng = sbt(ph, "gng", [128, RWW]); gnb = sbt(ph, "gnb", [128, RWW]); b_gn = Buf()
            fw.dma(gng[:], gng_b, writes=[b_gn]); fw.dma(gnb[:], gnb_b, writes=[b_gn])
            G = [pst(ph, "G%d" % i, [128, 1024]) for i in range(4)]
            b_G = [[Buf(), Buf()] for _ in range(4)]

            def gv(g, q):
                return G[g][:].rearrange("p (h q s) -> p h q s", h=2, q=4)[:, :, q, :]

            def gs(g, q):
                return G[g][:].rearrange("p (h c) -> p h c", h=2)[:, :, q * 64:(q + 1) * 64]

            KB = int(os.environ.get("KB", "99"))
            for d in range(min(2, KD)):
                m_lo = cm[:, C_GT if d == 0 else C_LT, :]
                m_up = cm[:, C_LT if d == 0 else C_GT, :]
                m_upi = cm[:, C_TF if d == 0 else C_TB, :]
                bc3 = lambda m: m.unsqueeze(1).to_broadcast([128, 2, 128])
                order = list(range(18)) if d == 0 else [1, 0] + list(range(17, 1, -1))
                cur = 0
                for j in range(J):
                    op("dve", lambda e: e.memset(St[j][0][:], 0.0), writes=[b_St[j][0]])
                for blk in order[:KB]:
                    isctx = blk < 2
                    g0 = blk * 128
                    x0 = g0 - CTX
                    tsl = slice(g0, g0 + 128)
                    fw.dma(btkA[:], BEt[d][tsl, :], reads=[sbuf_of("BEt%d" % d)], writes=[b_tokA[0]])
                    fw.dma(ktkA[:], KAt[d][tsl, :], reads=[sbuf_of("KAt%d" % d)], writes=[b_tokA[1]])
                    fw.dma(vtkA[:], Vt[tsl, :], reads=[sbuf_of("Vt")], writes=[b_tokA[2]])
                    for j in range(J):
                        js = slice(j * 128, (j + 1) * 128)
                        bi = b_in[j]
                        fw.dma(aT[j][:], AL[d][js, tsl], reads=[sbuf_of("AL%d" % d)], writes=[bi[0]])
                        fw.dma(bT[j][:], BE[d][js, tsl], reads=[sbuf_of("BE%d" % d)], writes=[bi[1]])
                        fw.dma(kT[j][:], KA[d][js, tsl], reads=[sbuf_of("KA%d" % d)], writes=[bi[2]])
                        if not isctx:
                            fw.dma(rT[j][:], RH[d][js, tsl], reads=[sbuf_of("RH%d" % d)], writes=[bi[3]])
                        bi[4], bi[5], bi[6] = b_tokA
                    for j in range(J):
                        bi = b_in[j]
                        for h in range(2):
                            hs = slice(h * 64, (h + 1) * 64)
                            op("pe", lambda e: e.matmul(gv(0, 0)[:, h, :], aT[j][hs, :], bT[j][hs, :], start=True, stop=True),
                               reads=[bi[0], bi[1]], writes=[b_G[0][h]])
                            op("pe", lambda e: e.matmul(gv(0, 1)[:, h, :], bT[j][hs, :], aT[j][hs, :], start=True, stop=True),
                               reads=[bi[0], bi[1]], writes=[b_G[0][h]])
                            op("pe", lambda e: e.matmul(gv(0, 2)[:, h, :], kT[j][hs, :], aT[j][hs, :], start=True, stop=True),
                               reads=[bi[0], bi[2]], writes=[b_G[0][h]])
                            if not isctx:
                                op("pe", lambda e: e.matmul(gv(0, 3)[:, h, :], bT[j][hs, :], rT[j][hs, :], start=True, stop=True),
                                   reads=[bi[1], bi[3]], writes=[b_G[0][h]])
                                op("pe", lambda e: e.matmul(gv(1, 0)[:, h, :], kT[j][hs, :], rT[j][hs, :], start=True, stop=True),
                                   reads=[bi[2], bi[3]], writes=[b_G[1][h]])
                        op("dve", lambda e: e.scalar_tensor_tensor(Pm[j][0][:], gv(0, 0), -1.0, bc3(m_lo), ALU.mult, ALU.mult),
                           reads=b_G[0] + [b_cm], writes=[b_P[j][0]])
                        op("dve", lambda e: e.scalar_tensor_tensor(PTm[j][0][:], gv(0, 1), -1.0, bc3(m_up), ALU.mult, ALU.mult),
                           reads=b_G[0] + [b_cm], writes=[b_PT[j][0]])
                        op("dve", lambda e: e.tensor_tensor(TTm[j][0][:], PTm[j][0][:], bc3(ident), ALU.add),
                           reads=[b_PT[j][0], b_cm], writes=[b_TT[j][0]])
                        op("dve", lambda e: e.tensor_tensor(AkT[j][:], gv(0, 2), bc3(m_up), ALU.mult),
                           reads=b_G[0] + [b_cm], writes=[b_AkT[j]])
                        if not isctx:
                            op("dve", lambda e: e.tensor_tensor(BbT[j][:], gv(0, 3), bc3(m_upi), ALU.mult),
                               reads=b_G[0] + [b_cm], writes=[b_BbT[j]])
                            op("dve", lambda e: e.tensor_tensor(BkT[j][:], gv(1, 0), bc3(m_upi), ALU.mult),
                               reads=b_G[1] + [b_cm], writes=[b_BkT[j]])
                    for i in range(1, 7):
                        a_, n_ = (i - 1) % 2, i % 2
                        for j in range(J):
                            for h in range(2):
                                op("pe", lambda e: e.matmul(gv(1, 1)[:, h, :], PTm[j][a_][:, h, :], Pm[j][a_][:, h, :], start=True, stop=True),
                                   reads=[b_P[j][a_], b_PT[j][a_]], writes=[b_G[1][h]])
                                if i < 6:
                                    op("pe", lambda e: e.matmul(gv(1, 2)[:, h, :], Pm[j][a_][:, h, :], PTm[j][a_][:, h, :], start=True, stop=True),
                                       reads=[b_P[j][a_], b_PT[j][a_]], writes=[b_G[1][h]])
                            op("dve", lambda e: e.tensor_copy(Pm[j][n_][:], gv(1, 1)), reads=b_G[1], writes=[b_P[j][n_]])
                            if i < 6:
                                op("dve", lambda e: e.tensor_copy(PTm[j][n_][:], gv(1, 2)), reads=b_G[1], writes=[b_PT[j][n_]])
                            for h in range(2):
                                op("pe", lambda e: e.matmul(gv(1, 3)[:, h, :], Pm[j][n_][:, h, :], TTm[j][a_][:, h, :], start=True, stop=True),
                                   reads=[b_P[j][n_], b_TT[j][a_]], writes=[b_G[1][h]])
                            op("dve", lambda e: e.tensor_tensor(TTm[j][n_][:], gv(1, 3), TTm[j][a_][:], ALU.add),
                               reads=b_G[1] + [b_TT[j][a_]], writes=[b_TT[j][n_]])
                    TTf = 0
                    nxt = 1 - cur
                    for j in range(J):
                        pr = j % 2
                        bi = b_in[j]
                        js = slice(j * 128, (j + 1) * 128)
                        Rv, Uv, Yv = gs(2, 0), gs(2, 1), gs(2, 2)
                        for h in range(2):
                            hs = slice(h * 64, (h + 1) * 64)
                            op("pe", lambda e: e.matmul(Rv[:, h, :], aT[j][hs, :], St[j][cur][hs, :], start=True, stop=False),
                               reads=[bi[0], b_St[j][cur]], writes=[b_G[2][h]])
                            op("pe", lambda e: e.matmul(Rv[:, h, :], AkT[j][:, h, :], vtk[j][:, hs], start=False, stop=True),
                               reads=[b_AkT[j], bi[6]], writes=[b_G[2][h]])
                        op("dve", lambda e: e.tensor_scalar(Rn[pr][:], Rv, -1.0, None, ALU.mult), reads=b_G[2], writes=[b_Rn[pr]])
                        for h in range(2):
                            op("pe", lambda e: e.matmul(Uv[:, h, :], TTm[j][TTf][:, h, :], Rn[pr][:, h, :], start=True, stop=True),
                               reads=[b_TT[j][TTf], b_Rn[pr]], writes=[b_G[2][h]])
                        op("dve", lambda e: e.tensor_copy(Us[pr][:], Uv), reads=b_G[2], writes=[b_Us[pr]])
                        SSv = G[3][:, 0:128]
                        op("pe", lambda e: e.matmul(SSv, btk[j], Us[pr][:].rearrange("p h v -> p (h v)"), start=True, stop=False),
                           reads=[bi[4], b_Us[pr]], writes=[b_G[3][0]])
                        op("pe", lambda e: e.matmul(SSv, ktk[j], vtk[j], start=False, stop=True),
                           reads=[bi[5], bi[6]], writes=[b_G[3][0]])
                        for h in range(2):
                            hs = slice(h * 64, (h + 1) * 64)
                            op("dve", lambda e: e.tensor_tensor(stt_[pr][hs, :], SSv[hs, h * 64:(h + 1) * 64], St[j][cur][hs, :], ALU.add),
                               reads=[b_G[3][0], b_St[j][cur]], writes=[b_stt[pr]])
                        op("dve", lambda e: e.tensor_scalar(St[j][nxt][:], stt_[pr][:], wc[d][:, j, blk:blk + 1], None, ALU.mult),
                           reads=[b_stt[pr], b_wc[d]], writes=[b_St[j][nxt]])
                        if isctx:
                            continue
                        for h in range(2):
                            hs = slice(h * 64, (h + 1) * 64)
                            op("pe", lambda e: e.matmul(Yv[:, h, :], rT[j][hs, :], St[j][cur][hs, :], start=True, stop=False),
                               reads=[bi[3], b_St[j][cur]], writes=[b_G[2][h]])
                            op("pe", lambda e: e.matmul(Yv[:, h, :], BbT[j][:, h, :], Us[pr][:, h, :], start=False, stop=False),
                               reads=[b_BbT[j], b_Us[pr]], writes=[b_G[2][h]])
                            op("pe", lambda e: e.matmul(Yv[:, h, :], BkT[j][:, h, :], vtk[j][:, hs], start=False, stop=True),
                               reads=[b_BkT[j], bi[6]], writes=[b_G[2][h]])
                        if d == 0:
                            op("dve", lambda e: e.tensor_copy(Ys[pr][:], Yv), reads=b_G[2], writes=[b_Ys[pr]])
                            fw.dma(YF[x0:x0 + 128, js], Ys[pr][:].rearrange("p h v -> p (h v)"), reads=[b_Ys[pr]], writes=[sbuf_of("YF")], q=SQ)
                        else:
                            fw.dma(Yf[pr][:].rearrange("p h v -> p (h v)"), YF[x0:x0 + 128, js], reads=[sbuf_of("YF")], writes=[b_Yf[pr]])
                            fw.dma(bon[pr][:], BON[js, x0:x0 + 128], reads=[sbuf_of("BON")], writes=[b_bon[pr]])
                            fw.dma(zr[pr][:], RWfm[(26 + j) * 128:(27 + j) * 128, g0:g0 + 128], reads=[sbuf_of("RWfm")], writes=[b_zr[pr]])
                            op("dve", lambda e: e.tensor_tensor(Ys[pr][:], Yv, Yf[pr][:], ALU.add), reads=b_G[2] + [b_Yf[pr]], writes=[b_Ys[pr]])
                            g_ = gst[pr]
                            op("dve", lambda e: e.tensor_reduce(g_[:, 0:2], Ys[pr][:], AX.X, ALU.add), reads=[b_Ys[pr]], writes=[b_gst[pr]])
                            op("dve", lambda e: e.tensor_scalar(g_[:, 0:2], g_[:, 0:2], -1.0 / 64, None, ALU.mult), reads=[b_gst[pr]], writes=[b_gst[pr]])
                            op("dve", lambda e: e.tensor_tensor(cen[pr][:], Ys[pr][:], g_[:, 0:2].unsqueeze(2).to_broadcast([128, 2, 64]), ALU.add),
                               reads=[b_Ys[pr], b_gst[pr]], writes=[b_cen[pr]])
                            op("dve", lambda e: e.tensor_tensor(gsq[pr][:], cen[pr][:], cen[pr][:], ALU.mult), reads=[b_cen[pr]], writes=[b_gsq[pr]])
                            op("dve", lambda e: e.tensor_reduce(g_[:, 2:4], gsq[pr][:], AX.X, ALU.add), reads=[b_gsq[pr]], writes=[b_gst[pr]])
                            op("act", lambda e: e.activation(g_[:, 2:4], g_[:, 2:4], AF.Sqrt, bias=EPSGN, scale=1.0 / 64), reads=[b_gst[pr], b_cst], writes=[b_gst[pr]])
                            op("dve", lambda e: e.reciprocal(g_[:, 2:4], g_[:, 2:4]), reads=[b_gst[pr]], writes=[b_gst[pr]])
                            op("dve", lambda e: e.tensor_tensor(cen[pr][:], cen[pr][:], g_[:, 2:4].unsqueeze(2).to_broadcast([128, 2, 64]), ALU.mult),
                               reads=[b_cen[pr], b_gst[pr]], writes=[b_cen[pr]])
                            cf = cen[pr][:].rearrange("p h v -> p (h v)")
                            op("dve", lambda e: e.tensor_tensor(cf, cf, gng[:, js], ALU.mult), reads=[b_cen[pr], b_gn], writes=[b_cen[pr]])
                            op("dve", lambda e: e.tensor_tensor(cf, cf, gnb[:, js], ALU.add), reads=[b_cen[pr], b_gn], writes=[b_cen[pr]])
                            yTv = G[3][:, 512:640]
                            op("pe", lambda e: e.transpose(yTv, cf, ident), reads=[b_cen[pr], b_cm], writes=[b_G[3][1]])
                            op("dve", lambda e: e.tensor_tensor(yo[pr][:], yTv, bon[pr][:], ALU.add), reads=[b_G[3][1], b_bon[pr]], writes=[b_yo[pr]])
                            op("dve", lambda e: e.tensor_tensor(yo[pr][:], yo[pr][:], zr[pr][:], ALU.mult), reads=[b_yo[pr], b_zr[pr]], writes=[b_yo[pr]])
                            fw.dma(YR[js, x0:x0 + 128], yo[pr][:], reads=[b_yo[pr]], writes=[sbuf_of("YR")], q=SQ)
                    cur = nxt
        fw.barrier()

        for ph in _phase(5):
            TT_ = 256
            yh = sbt(ph, "yh", [128, 8, TT_]); yr = sbt(ph, "yr", [128, 8, TT_]); b_yh = Buf(); b_yr = Buf()
            gh = [sbt(ph, "gh%d" % i, [128, 4, TT_]) for i in range(2)]; gr = [sbt(ph, "gr%d" % i, [128, 4, TT_]) for i in range(2)]
            b_gh = [Buf(), Buf()]; b_gr = [Buf(), Buf()]
            mT = sbt(ph, "mT", [128, 16, TT_]); b_mT = Buf()
            whg = [sbt(ph, "whg0", [128, 8, 512])] * 2; wrw = [sbt(ph, "wrw0", [128, 8, 512])] * 2
            b_whg = [Buf()] * 2; b_wrw = [Buf()] * 2
            wo = [sbt(ph, "wo0", [128, 16, 512])] * 2; b_wo = [Buf()] * 2
            xr = [sbt(ph, "xr%d" % i, [128, D]) for i in range(2)]; b_xr = [Buf(), Buf()]
            xn = [sbt(ph, "xn%d" % i, [128, D]) for i in range(2)]; b_xn = [Buf(), Buf()]
            junk3 = sbt(ph, "junk3", [128, D]); b_j3 = Buf()
            ss3 = sbt(ph, "ss3", [128, 2]); b_ss3 = Buf()
            fgt_ = sbt(ph, "fgt_", [128, D]); b_fg = Buf()
            tmpm = [sbt(ph, "tmpm%d" % i, [128, TT_]) for i in range(2)]; b_tmpm = [Buf(), Buf()]
            pp1 = [pst(ph, "pp1_%d" % i, [128, 512]) for i in range(2)]; pp2 = [pst(ph, "pp2_%d" % i, [128, 512]) for i in range(2)]
            b_pp1 = [Buf(), Buf()]; b_pp2 = [Buf(), Buf()]
            po = [pst(ph, "po%d" % i, [128, 512]) for i in range(4)]; b_po = [Buf() for _ in range(4)]
            fw.dma(fgt_[:], fg_b, writes=[b_fg])
            whg_r = w_hg_o.rearrange("(k p) c -> p k c", p=128)
            wrw_r = w_rw_o.rearrange("(k p) c -> p k c", p=128)
            wo_r = w_o.rearrange("(k p) c -> p k c", p=128)
            YH_r = YH.rearrange("(k p) t -> p k t", p=128)
            YR_r = YR.rearrange("(k p) t -> p k t", p=128)
            G_r = RWfm[(74 - 40) * 128:, :].rearrange("(k p) t -> p k t", p=128)
            wi = 0
            woi = 0
            xi = 0
            for tt in range(SEQ // TT_):
                x0 = tt * TT_
                g0 = x0 + CTX
                fw.dma3(yh[:], YH_r[:, :, x0:x0 + TT_], 8, reads=[sbuf_of("YH")], writes=[b_yh])
                fw.dma3(yr[:], YR_r[:, :, x0:x0 + TT_], 8, reads=[sbuf_of("YR")], writes=[b_yr])
                for mg in range(4):
                    wb = wi % 2
                    wi += 1
                    cs = slice(mg * 512, (mg + 1) * 512)
                    fw.dma3(whg[wb][:], whg_r[:, :, cs], 8, writes=[b_whg[wb]])
                    fw.dma3(wrw[wb][:], wrw_r[:, :, cs], 8, writes=[b_wrw[wb]])
                    fw.dma3(gh[wb][:], G_r[:, mg * 4:(mg + 1) * 4, g0:g0 + TT_], 4, reads=[sbuf_of("RWfm")], writes=[b_gh[wb]])
                    fw.dma3(gr[wb][:], G_r[:, 16 + mg * 4:16 + (mg + 1) * 4, g0:g0 + TT_], 4, reads=[sbuf_of("RWfm")], writes=[b_gr[wb]])
                    for mm in range(4):
                        m = mg * 4 + mm
                        a = mm % 2
                        for k in range(8):
                            op("pe", lambda e: e.matmul(pp1[a][:, :TT_], whg[wb][:, k, mm * 128:(mm + 1) * 128], yh[:, k, :], start=(k == 0), stop=(k == 7)),
                               reads=[b_whg[wb], b_yh], writes=[b_pp1[a]], inc=(k == 7))
                        for k in range(8):
                            op("pe", lambda e: e.matmul(pp2[a][:, :TT_], wrw[wb][:, k, mm * 128:(mm + 1) * 128], yr[:, k, :], start=(k == 0), stop=(k == 7)),
                               reads=[b_wrw[wb], b_yr], writes=[b_pp2[a]], inc=(k == 7))
                        op("dve", lambda e: e.tensor_tensor(tmpm[a][:], pp1[a][:, :TT_], gh[wb][:, mm, :], ALU.mult), reads=[b_pp1[a], b_gh[wb]], writes=[b_tmpm[a]])
                        op("dve", lambda e: e.tensor_tensor(mT[:, m, :], pp2[a][:, :TT_], gr[wb][:, mm, :], ALU.mult), reads=[b_pp2[a], b_gr[wb]], writes=[b_mT])
                        op("dve", lambda e: e.tensor_tensor(mT[:, m, :], mT[:, m, :], tmpm[a][:], ALU.add), reads=[b_mT, b_tmpm[a]], writes=[b_mT])
                for sub in range(TT_ // 128):
                    xb = xi % 2
                    xi += 1
                    fw.dma(xr[xb][:], xc[g0 + sub * 128:g0 + (sub + 1) * 128, :], writes=[b_xr[xb]])
                for n in range(4):
                    ob_ = woi % 2
                    woi += 1
                    fw.dma3(wo[ob_][:], wo_r[:, :, n * 512:(n + 1) * 512], 16, writes=[b_wo[ob_]])
                    for sub in range(TT_ // 128):
                        xb = (xi - (TT_ // 128) + sub) % 2
                        a = (n * 2 + sub) % 4
                        for k in range(16):
                            op("pe", lambda e: e.matmul(po[a][:], mT[:, k, sub * 128:(sub + 1) * 128], wo[ob_][:, k, :], start=(k == 0), stop=(k == 15)),
                               reads=[b_mT, b_wo[ob_]], writes=[b_po[a]], inc=(k == 15))
                        ns = slice(n * 512, (n + 1) * 512)
                        op("dve", lambda e: e.tensor_tensor(xn[xb][:, ns], po[a][:], gate_b[:, ns], ALU.mult), reads=[b_po[a], b_gate], writes=[b_xn[xb]])
                        op("dve", lambda e: e.tensor_tensor(xn[xb][:, ns], xn[xb][:, ns], xr[xb][:, ns], ALU.add), reads=[b_xn[xb], b_xr[xb]], writes=[b_xn[xb]])
                for sub in range(TT_ // 128):
                    xb = (xi - (TT_ // 128) + sub) % 2
                    op("act", lambda e: e.activation(junk3[:], xn[xb][:], AF.Square, accum_out=ss3[:, 0:1]), reads=[b_xn[xb]], writes=[b_j3, b_ss3])
                    op("act", lambda e: e.activation(ss3[:, 1:2], ss3[:, 0:1], AF.Sqrt, bias=EPS6, scale=1.0 / D), reads=[b_ss3, b_cst], writes=[b_ss3])
                    op("dve", lambda e: e.reciprocal(ss3[:, 1:2], ss3[:, 1:2]), reads=[b_ss3], writes=[b_ss3])
                    op("dve", lambda e: e.scalar_tensor_tensor(xn[xb][:], xn[xb][:], ss3[:, 1:2], fgt_[:], ALU.mult, ALU.mult),
                       reads=[b_xn[xb], b_ss3, b_fg], writes=[b_xn[xb]])
                    fw.dma(out[x0 + sub * 128:x0 + (sub + 1) * 128, :], xn[xb][:], reads=[b_xn[xb]], writes=[sbuf_of("out")], q=SQ)
        fw.barrier()
    print("bass program built: %d instructions" % fw.ninstr, flush=True)
    dbg_names = ["HGtok", "RWfm", "OFs", "YH", "AL0", "BE0", "KA0", "RH0", "AL1", "BE1", "KA1", "RH1", "BEt0", "KAt0", "Vt", "BON", "YF", "YR"]
    return nc, dbg_names


def _host_inputs(b, inp):
    f = lambda a: np.ascontiguousarray(a, dtype=np.float32)
    fm = lambda v: f(np.asarray(v).reshape(-1, 128).T)
    bc = lambda v: f(np.broadcast_to(np.asarray(v).reshape(1, -1), (128, np.asarray(v).size)))
    m = {}
    m["xc"] = f(np.concatenate([inp["ctx"][b], inp["x"][b]], axis=0))
    m["cc"] = f(np.stack([fm(inp["c"][b]), fm(inp["c_ctx"])], axis=-1))
    m["ada_w"] = f(inp["ada_w"][0].reshape(16, 128, 3 * D))
    m["ada_b_fm"] = fm(inp["ada_b"][0])
    m["ada_b_g"] = f(inp["ada_b"][0][2 * D:].reshape(1, D))
    m["norm_g_fm"] = fm(inp["norm_g"][0])
    m["w_in"] = f(inp["w_in"][0])
    m["hg_lb_b"] = f(np.broadcast_to(inp["hg_lb"][None], (128, 2, 2, HGW)))
    m["hgng_b"] = bc(inp["hg_norm_g"][0])
    mu = inp["rw_mu"][0]
    m["mu_fm"] = f(mu.reshape(4, 26, 128).transpose(2, 1, 0))
    m["w0_fm"] = f(inp["rw_w0"][0].reshape(2, 8, 128).transpose(2, 1, 0))
    m["a0_fm"] = f(inp["rw_a0"][0].reshape(2, 8, 128).transpose(2, 1, 0))
    m["w2"] = f(inp["rw_w2"][0].reshape(128, RWW))
    m["a2"] = f(inp["rw_a2"][0].reshape(128, RWW))
    kv = np.stack([inp["rw_kk"][0], inp["rw_ka"][0], inp["rw_rk"][0]], axis=0)
    m["kvec_fm"] = f(kv.reshape(3, 8, 128).transpose(2, 1, 0))
    m["gng_b"] = bc(inp["rw_gn_g"][0])
    m["gnb_b"] = bc(inp["rw_gn_b"][0])
    m["w_hg_o"] = f(inp["w_hg_out"][0])
    m["w_rw_o"] = f(inp["w_rw_out"][0])
    m["w_o"] = f(inp["w_out"][0])
    m["fg_b"] = bc(inp["final_g"])
    m["cm"] = make_cm()
    cst = np.zeros((128, 8), np.float32)
    cst[:, 0] = 1e-6
    cst[:, 1] = 1e-12
    cst[:, 2] = 64e-5
    cst[:, 3] = 0.0
    cst[:, 4] = 1.0
    m["cst"] = cst
    return m


_LAST = {}


def kernel(**inputs):
    inp = {k: np.asarray(v) for k, v in inputs.items()}
    nb = inp["x"].shape[0]
    nc, dbg = build_program()
    in_maps = [_host_inputs(b, inp) for b in range(nb)]
    res = run_bass_kernel_spmd(nc, in_maps, core_ids=list(range(nb)))
    if DEBUG:
        _LAST["res"] = res
    return np.stack([np.asarray(r["out"], dtype=np.float32) for r in res.results], axis=0)
```

```python
import os
import numpy as np
from contextlib import ExitStack
import concourse.bass as bass
import concourse.mybir as mybir
from concourse.bass_utils import run_bass_kernel_spmd

F32 = mybir.dt.float32
BF16 = mybir.dt.bfloat16
AF = mybir.ActivationFunctionType
ALU = mybir.AluOpType
AX = mybir.AxisListType

D = 2048
SEQ = 2048
CTX = 256
NT = SEQ + CTX
NCOLS = 13568
HGW = 1024
RWW = 1024
CH = 32
DEBUG = bool(os.environ.get("KDEBUG"))
PH = int(os.environ.get("KPHASE", "9"))
SQ = os.environ.get("KSQ", "pool")
ONLY = int(os.environ.get("KONLY", "-1"))
KLIM = int(os.environ.get("KLIM", "-1"))
CASTE = os.environ.get("KCAST", "pool")
KCH = int(os.environ.get("KCH", "99"))
KX = int(os.environ.get("KX", "0"))
KD = int(os.environ.get("KD", "2"))
KT = int(os.environ.get("KT", "9"))
KJ = int(os.environ.get("KJ", "8"))
KT0 = int(os.environ.get("KT0", "0"))


_FWREF = []


def _phase(n):
    if PH >= n and (ONLY < 0 or n == ONLY):
        fw = _FWREF[-1]
        fw.emitted = 0
        fw.limit = KLIM if (n == ONLY and KLIM >= 0) else None
        with ExitStack() as ph:
            yield ph
        print('phase', n, 'emitted', fw.emitted, flush=True)
        fw.limit = None


class Buf:
    __slots__ = ("name", "w", "r")

    def __init__(self, name=""):
        self.name = name
        self.w = None
        self.r = {}


class FW:
    NDMA = 14

    def __init__(self, nc, stack):
        self.nc = nc
        self.eng = {"pe": nc.tensor, "act": nc.scalar, "dve": nc.vector, "pool": nc.gpsimd, "sp": nc.sync}
        self.sem = {}
        self.cnt = {}
        for k in ["pe", "act", "dve", "pool"]:
            self.sem[k] = stack.enter_context(nc.semaphore("s_" + k))
            self.cnt[k] = 0
        for i in range(self.NDMA):
            k = "dma%d" % i
            self.sem[k] = stack.enter_context(nc.semaphore("s_" + k))
            self.cnt[k] = 0
        self.dma_i = 0
        self.waited = {e: {} for e in self.eng}
        self.ninstr = 0
        self.emitted = 0
        self.limit = None

    def _need(self, e, deps):
        for k, v in deps.items():
            if self.waited[e].get(k, 0) >= v:
                continue
            self.eng[e].wait_ge(self.sem[k], v)
            self.waited[e][k] = v

    def _collect(self, e, reads, writes):
        deps = {}

        def add(ev):
            if ev is None:
                return
            k, v = ev
            if k == e and e == "pe":
                return
            if deps.get(k, 0) < v:
                deps[k] = v
        for b in reads:
            add(b.w)
        for b in writes:
            add(b.w)
            for k, v in b.r.items():
                add((k, v))
        return deps

    def _mark(self, ev, reads, writes):
        k, v = ev
        for b in reads:
            if b.r.get(k, 0) < v:
                b.r[k] = v
        for b in writes:
            b.w = ev
            b.r = {}

    def _skip(self):
        self.emitted += 1
        return self.limit is not None and self.emitted > self.limit

    def op(self, e, fn, reads=(), writes=(), inc=True):
        if self._skip():
            return None
        deps = self._collect(e, reads, writes)
        self._need(e, deps)
        ins = fn(self.eng[e])
        self.ninstr += 1
        if inc:
            self.cnt[e] += 1
            ins.then_inc(self.sem[e], 1)
            ev = (e, self.cnt[e])
        else:
            ev = (e, self.cnt[e] + 1)
        self._mark(ev, reads, writes)
        return ins

    def dma(self, out, in_, reads=(), writes=(), q="sp", **kw):
        if self._skip():
            return
        i = self.dma_i
        self.dma_i += 1
        k = "dma%d" % (i % self.NDMA)
        deps = self._collect(q, reads, writes)
        if self.cnt[k] > 0 and deps.get(k, 0) < self.cnt[k]:
            deps[k] = self.cnt[k]
        self._need(q, deps)
        self.cnt[k] += 16
        self.eng[q].dma_start(out=out, in_=in_, **kw).then_inc(self.sem[k], 16)
        self.ninstr += 1
        self._mark((k, self.cnt[k]), reads, writes)

    def dma3(self, out, in_, n, **kw):
        for k in range(n):
            self.dma(out[:, k], in_[:, k], **kw)

    def barrier(self):
        for e in self.eng:
            deps = {k: v for k, v in self.cnt.items() if v > 0 and k != e}
            self._need(e, deps)


C_ID, C_TF, C_TB, C_BONES, C_LT, C_GT = 0, 1, 2, 3, 4, 5
NCM = 6


def make_cm():
    p = np.arange(128)[:, None]
    f = np.arange(128)[None, :]
    cm = np.zeros((128, NCM, 128), np.float32)
    cm[:, C_ID] = (p == f)
    cm[:, C_TF] = (p <= f)
    cm[:, C_TB] = (p >= f)
    cm[:, C_BONES] = ((p // 64) == (f // 64))
    cm[:, C_LT] = (p < f)
    cm[:, C_GT] = (p > f)
    return cm


def build_program():
    nc = bass.Bass("TRN2", target_bir_lowering=False)
    dt = lambda name, shape, kind="ExternalInput": nc.dram_tensor(name, shape, F32, kind=kind).ap()
    SCR = "ExternalOutput" if DEBUG else "Internal"
    xc = dt("xc", [NT, D])
    cc_d = dt("cc", [128, 16, 2])
    ada_w = dt("ada_w", [16, 128, 3 * D])
    ada_b_fm = dt("ada_b_fm", [128, 48])
    ada_b_g = dt("ada_b_g", [1, D])
    norm_g_fm = dt("norm_g_fm", [128, 16])
    w_in = dt("w_in", [D, NCOLS])
    hg_lb_b = dt("hg_lb_b", [128, 2, 2, HGW])
    hgng_b = dt("hgng_b", [128, HGW])
    mu_fm = dt("mu_fm", [128, 26, 4])
    w0_fm = dt("w0_fm", [128, 8, 2])
    a0_fm = dt("a0_fm", [128, 8, 2])
    w2_d = dt("w2", [128, RWW])
    a2_d = dt("a2", [128, RWW])
    kvec_fm = dt("kvec_fm", [128, 8, 3])
    gng_b = dt("gng_b", [128, RWW])
    gnb_b = dt("gnb_b", [128, RWW])
    w_hg_o = dt("w_hg_o", [HGW, D])
    w_rw_o = dt("w_rw_o", [RWW, D])
    w_o = dt("w_o", [D, D])
    fg_b = dt("fg_b", [128, D])
    cm_d = dt("cm", [128, NCM, 128])
    cst_d = dt("cst", [128, 8])
    out = dt("out", [SEQ, D], kind="ExternalOutput")
    HGtok = dt("HGtok", [NT, 5120], SCR)
    RWfm = dt("RWfm", [NCOLS - 5120, NT], SCR)
    OFs = dt("OFs", [SEQ, HGW], SCR)
    YH = dt("YH", [HGW, SEQ], SCR)
    AL = [dt("AL%d" % d, [RWW, NT], SCR) for d in range(2)]
    BE = [dt("BE%d" % d, [RWW, NT], SCR) for d in range(2)]
    KA = [dt("KA%d" % d, [RWW, NT], SCR) for d in range(2)]
    RH = [dt("RH%d" % d, [RWW, NT], SCR) for d in range(2)]
    BEt = [dt("BEt%d" % d, [NT, RWW], SCR) for d in range(2)]
    KAt = [dt("KAt%d" % d, [NT, RWW], SCR) for d in range(2)]
    Vt = dt("Vt", [NT, RWW], SCR)
    BON = dt("BON", [RWW, SEQ], SCR)
    YF = dt("YF", [SEQ, RWW], SCR)
    YR = dt("YR", [RWW, SEQ], SCR)
    scr_bufs = {}

    def sbuf_of(name):
        if name not in scr_bufs:
            scr_bufs[name] = Buf(name)
        return scr_bufs[name]

    with ExitStack() as st:
        fw = FW(nc, st)
        _FWREF.append(fw)
        _acct = {}

        def sbt(stack, name, shape):
            _acct[id(stack)] = _acct.get(id(stack), 0) + int(np.prod(shape[1:])) * 4
            if os.environ.get("KACCT"):
                print("sbuf", name, shape, "stack total KiB", _acct[id(stack)] / 1024.0, flush=True)
            return stack.enter_context(nc.sbuf_tensor("sb_" + name, shape, F32))
        pst = lambda stack, name, shape: stack.enter_context(nc.psum_tensor("ps_" + name, shape, F32))
        op = fw.op
        cm = sbt(st, "cm", [128, NCM, 128]); b_cm = Buf()
        cst = sbt(st, "cst", [128, 8]); b_cst = Buf()
        modA = sbt(st, "modA", [128, 16, 2]); modB = sbt(st, "modB", [128, 16, 2]); b_mod = Buf()
        gate_b = sbt(st, "gate_b", [128, D]); b_gate = Buf()
        wc = [sbt(st, "wc%d" % d, [128, 8, 18]) for d in range(2)]; b_wc = [Buf(), Buf()]
        fw.dma(cm[:], cm_d, writes=[b_cm])
        fw.dma(cst[:], cst_d, writes=[b_cst])
        ident = cm[:, C_ID, :]
        EPS6, EPS12, EPSGN, ZERO, ONE = (cst[:, i:i + 1] for i in range(5))

        for ph in _phase(0):
            adw = [sbt(ph, "adw%d" % i, [128, 3 * D]) for i in range(2)]; b_adw = [Buf(), Buf()]
            cct = sbt(ph, "cct", [128, 16, 2]); scc = sbt(ph, "scc", [128, 16, 2]); b_cc = Buf(); b_scc = Buf()
            mod = sbt(ph, "mod", [128, 48, 2]); b_modt = Buf()
            adb = sbt(ph, "adb", [128, 48]); ng = sbt(ph, "ng", [128, 16]); b_sm = Buf()
            adbg = sbt(ph, "adbg", [1, D]); grow = sbt(ph, "grow", [1, D]); b_grow = Buf()
            ps_mod = pst(ph, "ps_mod", [128, 96]); b_psm = Buf()
            ps_g = [pst(ph, "ps_g%d" % i, [128, 512]) for i in range(4)]; b_psg = [Buf() for _ in range(4)]
            fw.dma(cct[:], cc_d, writes=[b_cc])
            fw.dma(adb[:], ada_b_fm, writes=[b_sm])
            fw.dma(ng[:], norm_g_fm, writes=[b_sm])
            fw.dma(adbg[:], ada_b_g, writes=[b_sm])
            op("act", lambda e: e.activation(scc[:], cct[:], AF.Silu), reads=[b_cc], writes=[b_scc])
            op("dve", lambda e: e.memset(mod[:], 0.0), writes=[b_modt])
            for k in range(16):
                fw.dma(adw[k % 2][:], ada_w[k], writes=[b_adw[k % 2]])
                for m in range(48):
                    op("pe", lambda e: e.matmul(ps_mod[:, 2 * m:2 * m + 2], adw[k % 2][:, m * 128:(m + 1) * 128],
                                                scc[:, k, :], start=True, stop=True),
                       reads=[b_adw[k % 2], b_scc], writes=[b_psm], inc=(m == 47))
                op("dve", lambda e: e.tensor_tensor(mod[:], mod[:], ps_mod[:].rearrange("p (m v) -> p m v", v=2), ALU.add),
                   reads=[b_psm, b_modt], writes=[b_modt])
                for n in range(4):
                    op("pe", lambda e: e.matmul(ps_g[n][0:1, :], scc[:, k, 0:1], adw[k % 2][:, 2 * D + n * 512:2 * D + (n + 1) * 512],
                                                start=(k == 0), stop=(k == 15)),
                       reads=[b_adw[k % 2], b_scc], writes=[b_psg[n]])
            op("dve", lambda e: e.tensor_tensor(mod[:], mod[:], adb[:].unsqueeze(2).to_broadcast([128, 48, 2]), ALU.add),
               reads=[b_sm, b_modt], writes=[b_modt])
            op("dve", lambda e: e.tensor_scalar(modA[:], mod[:, 16:32, :], 1.0, None, ALU.add), reads=[b_modt], writes=[b_mod])
            op("dve", lambda e: e.tensor_tensor(modA[:], modA[:], ng[:].unsqueeze(2).to_broadcast([128, 16, 2]), ALU.mult),
               reads=[b_sm, b_mod], writes=[b_mod])
            op("dve", lambda e: e.tensor_copy(modB[:], mod[:, 0:16, :]), reads=[b_modt], writes=[b_mod])
            for n in range(4):
                op("dve", lambda e: e.tensor_tensor(grow[:, n * 512:(n + 1) * 512], ps_g[n][0:1, :], adbg[:, n * 512:(n + 1) * 512], ALU.add),
                   reads=[b_psg[n], b_sm], writes=[b_grow])
            for n in range(4):
                op("pe", lambda e: e.matmul(ps_g[n][:], cm[0:1, C_TF, :], grow[:, n * 512:(n + 1) * 512], start=True, stop=True),
                   reads=[b_cm, b_grow], writes=[b_psg[n]])
                op("act", lambda e: e.activation(gate_b[:, n * 512:(n + 1) * 512], ps_g[n][:], AF.Identity, scale=1.0),
                   reads=[b_psg[n]], writes=[b_gate])
        fw.barrier()

        for ph in _phase(1):
            lb_b = sbt(ph, "lb_b", [128, 2, HGW]); oml_b = sbt(ph, "oml_b", [128, 2, HGW])
            b_lb = Buf()
            xt = [sbt(ph, "xt%d" % i, [128, D]) for i in range(2)]; b_xt = [Buf(), Buf()]
            junk = sbt(ph, "junk", [128, D]); b_junk = Buf()
            ss = sbt(ph, "ss", [128, 2]); b_ss = Buf()
            hT = ph.enter_context(nc.sbuf_tensor("sb_hT", [128, 16, 512], BF16)); b_hT = [Buf() for _ in range(4)]
            wt = [sbt(ph, "wt%d" % i, [128, 16, 512]) for i in range(2)]; b_wt = [Buf(), Buf()]
            wb16 = [ph.enter_context(nc.sbuf_tensor("sb_wb16_%d" % i, [128, 16, 512], BF16)) for i in range(2)]; b_wb16 = [Buf(), Buf()]
            lbt = wt[1][:, 0:8, :].rearrange("p a b -> p (a b)").rearrange("p (d l c) -> p d l c", d=2, l=2)
            b_lbt = b_wt[1]
            NOT = 4
            ot = [sbt(ph, "ot%d" % i, [128, 512]) for i in range(NOT)]; b_ot = [Buf() for _ in range(NOT)]
            tps = [pst(ph, "tps%d" % i, [128, 512]) for i in range(4)]; b_tps = [Buf() for _ in range(4)]
            acc = [pst(ph, "acc%d" % i, [128, 512]) for i in range(4)]; b_acc = [Buf() for _ in range(4)]
            fw.dma(lbt, hg_lb_b, writes=[b_lbt])
            op("dve", lambda e: e.tensor_tensor(lb_b[:], lbt[:, :, 0, :], lbt[:, :, 1, :], ALU.subtract), reads=[b_lbt], writes=[b_lb])
            op("act", lambda e: e.activation(lb_b[:], lb_b[:], AF.Sigmoid), reads=[b_lb], writes=[b_lb])
            op("dve", lambda e: e.tensor_scalar(oml_b[:], lb_b[:], -1.0, 1.0, ALU.mult, ALU.add), reads=[b_lb], writes=[b_lb])
            w_in_r = w_in.rearrange("(k p) c -> p k c", p=128)
            oti = 0
            ctx_groups = {2, 3, 4, 5, 6, 7, 12, 13, 14, 15, 16}
            tiles = [(0, 256, True)] + [(256 + 512 * i, 512, False) for i in range(4)]
            wti = 0
            for (g0, ntok, isctx) in tiles:
                v = 1 if isctx else 0
                nsub = ntok // 128
                for sub in range(nsub):
                    xb = sub % 2
                    fw.dma(xt[xb][:], xc[g0 + sub * 128:g0 + (sub + 1) * 128, :], writes=[b_xt[xb]])
                    op("act", lambda e: e.activation(junk[:], xt[xb][:], AF.Square, accum_out=ss[:, 0:1]),
                       reads=[b_xt[xb]], writes=[b_junk, b_ss])
                    op("act", lambda e: e.activation(ss[:, 1:2], ss[:, 0:1], AF.Sqrt, bias=EPS6, scale=1.0 / D),
                       reads=[b_ss, b_cst], writes=[b_ss])
                    op("dve", lambda e: e.reciprocal(ss[:, 1:2], ss[:, 1:2]), reads=[b_ss], writes=[b_ss])
                    op("act", lambda e: e.activation(junk[:], xt[xb][:], AF.Identity, scale=ss[:, 1:2], bias=ZERO),
                       reads=[b_xt[xb], b_ss, b_cst], writes=[b_junk])
                    for q in range(4):
                        for i in range(4):
                            j = q * 4 + i
                            op("pe", lambda e: e.transpose(tps[q][:, i * 128:(i + 1) * 128], junk[:, j * 128:(j + 1) * 128], ident),
                               reads=[b_junk, b_cm], writes=[b_tps[q]], inc=(i == 3))
                        for i in range(4):
                            j = q * 4 + i
                            en = "dve" if (i % 2 == 0) else "act"
                            if en == "dve":
                                op("dve", lambda e: e.tensor_scalar(hT[:, j, sub * 128:(sub + 1) * 128], tps[q][:, i * 128:(i + 1) * 128],
                                                                    modA[:, j, v:v + 1], modB[:, j, v:v + 1], ALU.mult, ALU.add),
                                   reads=[b_tps[q], b_mod], writes=[b_hT[sub]])
                            else:
                                op("act", lambda e: e.activation(hT[:, j, sub * 128:(sub + 1) * 128], tps[q][:, i * 128:(i + 1) * 128],
                                                                 AF.Identity, scale=modA[:, j, v:v + 1], bias=modB[:, j, v:v + 1]),
                                   reads=[b_tps[q], b_mod], writes=[b_hT[sub]])
                for g in range(27):
                    if isctx and g not in ctx_groups:
                        continue
                    c0 = g * 512
                    ncol = min(512, NCOLS - c0)
                    wb = wti % 2
                    wti += 1
                    fw.dma3(wt[wb][:, :, :ncol], w_in_r[:, :, c0:c0 + ncol], 16, writes=[b_wt[wb]])
                    op(CASTE, lambda e: e.tensor_copy(wb16[wb][:, :, :ncol], wt[wb][:, :, :ncol]), reads=[b_wt[wb]], writes=[b_wb16[wb]])
                    if c0 < 5120:
                        typ = ["silu", "id", "fg0", "fg1", "silu"][c0 // 1024]
                        for sub in range(nsub):
                            a = sub % 4
                            for k in range(16):
                                op("pe", lambda e: e.matmul(acc[a][:, :ncol], hT[:, k, sub * 128:(sub + 1) * 128], wb16[wb][:, k, :ncol],
                                                            start=(k == 0), stop=(k == 15)),
                                   reads=[b_hT[sub], b_wb16[wb]], writes=[b_acc[a]], inc=(k == 15))
                            o_ = oti % NOT
                            oti += 1
                            if typ == "silu":
                                op("act", lambda e: e.activation(ot[o_][:], acc[a][:], AF.Silu), reads=[b_acc[a]], writes=[b_ot[o_]])
                            elif typ == "id":
                                op("dve", lambda e: e.tensor_copy(ot[o_][:], acc[a][:]), reads=[b_acc[a]], writes=[b_ot[o_]])
                            else:
                                dd = int(typ[2])
                                cc0 = c0 - (2048 + dd * 1024)
                                op("act", lambda e: e.activation(ot[o_][:], acc[a][:], AF.Sigmoid), reads=[b_acc[a]], writes=[b_ot[o_]])
                                op("dve", lambda e: e.tensor_tensor(ot[o_][:], ot[o_][:], oml_b[:, dd, cc0:cc0 + 512], ALU.mult),
                                   reads=[b_ot[o_], b_lb], writes=[b_ot[o_]])
                                op("dve", lambda e: e.tensor_tensor(ot[o_][:], ot[o_][:], lb_b[:, dd, cc0:cc0 + 512], ALU.add),
                                   reads=[b_ot[o_], b_lb], writes=[b_ot[o_]])
                            fw.dma(HGtok[g0 + sub * 128:g0 + (sub + 1) * 128, c0:c0 + 512], ot[o_][:],
                                   reads=[b_ot[o_]], writes=[sbuf_of("HGtok")], q=SQ)
                    else:
                        for m in range(ncol // 128):
                            mc = c0 // 128 + m
                            if isctx and not (48 <= mc <= 65):
                                continue
                            a = m % 4
                            for k in range(16):
                                op("pe", lambda e: e.matmul(acc[a][:, :ntok], wb16[wb][:, k, m * 128:(m + 1) * 128], hT[:, k, :ntok],
                                                            start=(k == 0), stop=(k == 15)),
                                   reads=b_hT[:nsub] + [b_wb16[wb]], writes=[b_acc[a]], inc=(k == 15))
                            o_ = oti % NOT
                            oti += 1
                            if mc <= 65:
                                op("dve", lambda e: e.tensor_copy(ot[o_][:, :ntok], acc[a][:, :ntok]), reads=[b_acc[a]], writes=[b_ot[o_]])
                            else:
                                fn = AF.Silu if mc <= 73 else AF.Sigmoid
                                op("act", lambda e: e.activation(ot[o_][:, :ntok], acc[a][:, :ntok], fn), reads=[b_acc[a]], writes=[b_ot[o_]])
                            fw.dma(RWfm[(mc - 40) * 128:(mc - 39) * 128, g0:g0 + ntok], ot[o_][:, :ntok],
                                   reads=[b_ot[o_]], writes=[sbuf_of("RWfm")], q=SQ)
        fw.barrier()

        for ph in _phase(2):
            NB = 2
            fgt = [sbt(ph, "fgt%d" % i, [CH, HGW]) for i in range(NB)]
            vtt = [sbt(ph, "vtt%d" % i, [CH, HGW]) for i in range(NB)]
            qtt = [sbt(ph, "qtt%d" % i, [CH, HGW]) for i in range(NB)]
            ztt = [sbt(ph, "ztt%d" % i, [CH, HGW]) for i in range(NB)]
            oft = [sbt(ph, "oft%d" % i, [CH, HGW]) for i in range(NB)]
            b_ld = [[Buf() for _ in range(5)] for _ in range(NB)]
            gt = sbt(ph, "gt", [CH, HGW]); b_gt = Buf()
            Et = sbt(ph, "Et", [CH, HGW]); Ei = sbt(ph, "Ei", [CH, HGW]); b_E = Buf(); b_Ei = Buf()
            kt_ = sbt(ph, "kt", [128, HGW]); qq_ = sbt(ph, "qq", [128, HGW]); b_kt = Buf(); b_qq = Buf()
            kt = kt_[0:CH, :]; qq = qq_[0:CH, :]
            ob = sbt(ph, "ob", [CH, HGW]); sq_ = sbt(ph, "sq", [128, HGW]); b_ob = Buf(); b_sq = Buf()
            sq = sq_[0:CH, :]
            ms = sbt(ph, "ms", [CH, 8]); b_ms = Buf()
            eb = sbt(ph, "eb", [128, 8]); b_eb = Buf()
            hng = sbt(ph, "hng", [CH, HGW]); b_hng = Buf()
            qkT = [sbt(ph, "qkT%d" % i, [128, 2, CH]) for i in range(2)]; b_qkT = [Buf(), Buf()]
            at = [sbt(ph, "at%d" % i, [CH, CH]) for i in range(2)]; b_at = [Buf(), Buf()]
            S = [sbt(ph, "S%d" % i, [128, 8, 128]) for i in range(2)]; b_S = [Buf(), Buf()]
            stmp = sbt(ph, "stmp", [128, 8, 128]); b_stmp = Buf()
            yT = sbt(ph, "yTs", [128, 8, 512]); b_yT = Buf()
            bc_ps = [pst(ph, "bc_ps%d" % i, [128, 512]) for i in range(2)]; b_bc = [Buf(), Buf()]
            o_ps = [pst(ph, "o_ps%d" % i, [128, 512]) for i in range(2)]; b_o = [Buf(), Buf()]
            dS_ps = [pst(ph, "dS_ps%d" % i, [128, 512]) for i in range(2)]; b_dS = [Buf(), Buf()]
            mz = pst(ph, "mz", [128, 512]); b_tp = [Buf()] * 2; b_aps = [Buf()] * 2; b_ebp = b_aps[0]
            yT_ps = pst(ph, "yT_ps", [128, 512]); b_yTp = Buf()
            fw.dma(hng[:], hgng_b[0:CH, :], writes=[b_hng])
            op("dve", lambda e: e.memset(kt_[:], 0.0), writes=[b_kt])
            op("dve", lambda e: e.memset(qq_[:], 0.0), writes=[b_qq])
            op("dve", lambda e: e.memset(sq_[:], 0.0), writes=[b_sq])
            ci = 0
            for d in range(min(2, KD)):
                tri = cm[0:CH, C_TF if d == 0 else C_TB, 0:CH]
                lastc = CH - 1 if d == 0 else 0
                onehot = cm[0:CH, C_ID, lastc:lastc + 1]
                order = list(range(NT // CH)) if d == 0 else list(range(CTX // CH - 1, -1, -1)) + list(range(NT // CH - 1, CTX // CH - 1, -1))
                cur = 0
                op("dve", lambda e: e.memset(S[0][:], 0.0), writes=[b_S[0]])
                for c in order[:KCH]:
                    isctx = c < CTX // CH
                    t0 = c * CH
                    lb_ = ci % NB
                    ci += 1
                    fgs, vs, qs, zs, ofs = fgt[lb_], vtt[lb_], qtt[lb_], ztt[lb_], oft[lb_]
                    bl = b_ld[lb_]
                    fw.dma(fgs[:], HGtok[t0:t0 + CH, 2048 + d * 1024:3072 + d * 1024], reads=[sbuf_of("HGtok")], writes=[bl[0]])
                    fw.dma(vs[:], HGtok[t0:t0 + CH, 1024:2048], reads=[sbuf_of("HGtok")], writes=[bl[1]])
                    if not isctx:
                        fw.dma(qs[:], HGtok[t0:t0 + CH, 0:1024], reads=[sbuf_of("HGtok")], writes=[bl[2]])
                        if d == 1:
                            fw.dma(zs[:], HGtok[t0:t0 + CH, 4096:5120], reads=[sbuf_of("HGtok")], writes=[bl[3]])
                            fw.dma(ofs[:], OFs[t0 - CTX:t0 - CTX + CH, :], reads=[sbuf_of("OFs")], writes=[bl[4]])
                    op("act", lambda e: e.activation(gt[:], fgs[:], AF.Ln), reads=[bl[0]], writes=[b_gt])
                    for n in range(2):
                        op("pe", lambda e: e.matmul(bc_ps[n][0:CH, :], tri, gt[:, n * 512:(n + 1) * 512], start=True, stop=True),
                           reads=[b_gt, b_cm], writes=[b_bc[n]])
                    for n in range(2):
                        sl = slice(n * 512, (n + 1) * 512)
                        op("act", lambda e: e.activation(Ei[:, sl], bc_ps[n][0:CH, :], AF.Exp, scale=-1.0), reads=[b_bc[n]], writes=[b_Ei])
                        op("act", lambda e: e.activation(Et[:, sl], bc_ps[n][0:CH, :], AF.Exp), reads=[b_bc[n]], writes=[b_E])
                    op("dve", lambda e: e.tensor_scalar(kt[:], fgs[:], -1.0, 1.0, ALU.mult, ALU.add), reads=[bl[0]], writes=[b_kt])
                    op("dve", lambda e: e.tensor_tensor(kt[:], kt[:], Ei[:], ALU.mult), reads=[b_kt, b_Ei], writes=[b_kt])
                    if not isctx:
                        op("dve", lambda e: e.tensor_tensor(qq[:], qs[:], Et[:], ALU.mult), reads=[bl[2], b_E], writes=[b_qq])
                    for h in range(8):
                        op("pe", lambda e: e.matmul(yT_ps[:, 400 + h:401 + h], Et[:, h * 128:(h + 1) * 128], onehot, start=True, stop=True),
                           reads=[b_E, b_cm], writes=[b_ebp], inc=(h == 7))
                    op("dve", lambda e: e.tensor_copy(eb[:], yT_ps[:, 400:408]), reads=[b_ebp], writes=[b_eb])
                    nxt = 1 - cur
                    for h in range(8):
                        hs = slice(h * 128, (h + 1) * 128)
                        pp = h % 2
                        if not isctx:
                            tpv = mz[:, pp * 256:(pp + 1) * 256]
                        if (not isctx) and not (KX & 1):
                            op("pe", lambda e: e.transpose(tpv[:, 0:128], qq_[:, hs], ident), reads=[b_qq, b_cm], writes=[b_tp[pp]], inc=False)
                            op("pe", lambda e: e.transpose(tpv[:, 128:256], kt_[:, hs], ident), reads=[b_kt, b_cm], writes=[b_tp[pp]])
                            op("dve", lambda e: e.tensor_copy(qkT[pp][:], tpv.rearrange("p (a b) -> p a b", b=128)[:, :, 0:CH]),
                               reads=[b_tp[pp]], writes=[b_qkT[pp]])
                            apv = yT_ps[0:CH, 256 + pp * 64:256 + pp * 64 + CH]
                        if (not isctx) and not (KX & 2):
                            op("pe", lambda e: e.matmul(apv, qkT[pp][:, 1, :], qkT[pp][:, 0, :], start=True, stop=True),
                               reads=[b_qkT[pp]], writes=[b_aps[pp]])
                            op("dve", lambda e: e.tensor_tensor(at[pp][:], apv, tri, ALU.mult), reads=[b_aps[pp], b_cm], writes=[b_at[pp]])
                            opv = o_ps[h // 4][0:CH, (h % 4) * 128:(h % 4 + 1) * 128]
                        if (not isctx) and not (KX & 4):
                            op("pe", lambda e: e.matmul(opv, at[pp][:], vs[:, hs], start=True, stop=False),
                               reads=[b_at[pp], bl[1]], writes=[b_o[h // 4]], inc=False)
                            op("pe", lambda e: e.matmul(opv, qkT[pp][:, 0, :], S[cur][:, h, :], start=False, stop=True),
                               reads=[b_qkT[pp], b_S[cur]], writes=[b_o[h // 4]])
                        dsv = dS_ps[h // 4][:, (h % 4) * 128:(h % 4 + 1) * 128]
                        op("pe", lambda e: e.matmul(dsv, kt[:, hs], vs[:, hs], start=True, stop=True),
                           reads=[b_kt, bl[1]], writes=[b_dS[h // 4]])
                    for n in range(2):
                        op("dve", lambda e: e.tensor_tensor(stmp[:, n * 4:(n + 1) * 4, :], dS_ps[n][:].rearrange("p (h v) -> p h v", v=128),
                                                            S[cur][:, n * 4:(n + 1) * 4, :], ALU.add),
                           reads=[b_dS[n], b_S[cur]], writes=[b_stmp])
                    op("dve", lambda e: e.tensor_tensor(S[nxt][:], stmp[:], eb[:].unsqueeze(2).to_broadcast([128, 8, 128]), ALU.mult),
                       reads=[b_stmp, b_eb], writes=[b_S[nxt]])
                    cur = nxt
                    if isctx or (KX & 8):
                        continue
                    xt0 = t0 - 256
                    if d == 0:
                        for n in range(2):
                            op("dve", lambda e: e.tensor_copy(ob[:, n * 512:(n + 1) * 512], o_ps[n][0:CH, :]),
                               reads=[b_o[n]], writes=[b_ob])
                        fw.dma(OFs[xt0:xt0 + CH, :], ob[:], reads=[b_ob], writes=[sbuf_of("OFs")], q=SQ)
                    else:
                        for n in range(2):
                            op("dve", lambda e: e.tensor_tensor(ob[:, n * 512:(n + 1) * 512], o_ps[n][0:CH, :], ofs[:, n * 512:(n + 1) * 512], ALU.add),
                               reads=[b_o[n], bl[4]], writes=[b_ob])
                        op("dve", lambda e: e.tensor_tensor(sq[:], ob[:], ob[:], ALU.mult), reads=[b_ob], writes=[b_sq])
                        op("dve", lambda e: e.tensor_reduce(ms[:], sq[:].rearrange("p (h v) -> p h v", v=128), AX.X, ALU.add), reads=[b_sq], writes=[b_ms])
                        op("act", lambda e: e.activation(ms[:], ms[:], AF.Sqrt, bias=cst[0:CH, 0:1], scale=1.0 / 128), reads=[b_ms, b_cst], writes=[b_ms])
                        op("dve", lambda e: e.reciprocal(ms[:], ms[:]), reads=[b_ms], writes=[b_ms])
                        op("dve", lambda e: e.tensor_tensor(sq[:].rearrange("p (h v) -> p h v", v=128), ob[:].rearrange("p (h v) -> p h v", v=128),
                                                            ms[:].unsqueeze(2).to_broadcast([CH, 8, 128]), ALU.mult),
                           reads=[b_ob, b_ms], writes=[b_sq])
                        op("dve", lambda e: e.tensor_tensor(sq[:], sq[:], hng[:], ALU.mult), reads=[b_sq, b_hng], writes=[b_sq])
                        op("dve", lambda e: e.tensor_tensor(sq[:], sq[:], zs[:], ALU.mult), reads=[b_sq, bl[3]], writes=[b_sq])
                        for h in range(8):
                            op("pe", lambda e: e.transpose(bc_ps[h // 4][:, (h % 4) * 128:(h % 4 + 1) * 128], sq_[:, h * 128:(h + 1) * 128], ident),
                               reads=[b_sq, b_cm], writes=[b_bc[h // 4]], inc=(h % 4 == 3))
                        for n in range(2):
                            yo_ = xt0 % 512
                            op("dve", lambda e: e.tensor_copy(yT[:, n * 4:(n + 1) * 4, yo_:yo_ + CH], bc_ps[n][:].rearrange("p (h t) -> p h t", t=128)[:, :, 0:CH]),
                               reads=[b_bc[n]], writes=[b_yT])
                        if xt0 % 512 == 0:
                            fw.dma3(YH.rearrange("(h p) t -> p h t", p=128)[:, :, xt0:xt0 + 512], yT[:], 8, reads=[b_yT], writes=[sbuf_of("YH")], q=SQ)
        fw.barrier()

        for ph in _phase(3):
            W = 640
            mu = sbt(ph, "mu", [128, 26, 4]); omu = sbt(ph, "omu", [128, 26]); b_mu = Buf()
            w0t = sbt(ph, "w0t", [128, 8, 2]); a0t = sbt(ph, "a0t", [128, 8, 2]); kvt = sbt(ph, "kvt", [128, 8, 3]); omka = sbt(ph, "omka", [128, 8])
            w2t = sbt(ph, "w2t", [128, RWW]); a2t = sbt(ph, "a2t", [128, RWW]); b_par = Buf()
            raw = [sbt(ph, "raw%d" % i, [128, W]) for i in range(3)]; b_raw = [Buf() for _ in range(3)]
            rawl = [sbt(ph, "rawl%d" % i, [128, W]) for i in range(2)]; b_rawl = [Buf(), Buf()]
            sh = [sbt(ph, "sh%d" % i, [128, 512]) for i in range(3)]; b_sh = [Buf() for _ in range(3)]
            tw = sbt(ph, "tw", [128, 512]); als = sbt(ph, "als", [128, 512]); b_tw = Buf(); b_als = Buf()
            kkr = sbt(ph, "kkr", [128, 512]); kk = sbt(ph, "kk", [128, 512]); t1 = sbt(ph, "t1", [128, 512]); b_kkr = Buf(); b_kk = Buf(); b_t1 = Buf()
            lgw = sbt(ph, "lgw", [128, 512]); av = sbt(ph, "av", [128, 512]); b_lgw = Buf(); b_av = Buf()
            kd = [sbt(ph, "kd%d" % i, [128, 512]) for i in range(2)]; b_kd = [Buf(), Buf()]
            bv = sbt(ph, "bv", [128, 512]); b_bv = Buf()
            lt = sbt(ph, "lt", [128, 128]); b_lt = Buf()
            Ecw = sbt(ph, "Ecw", [128, 512]); Einv = sbt(ph, "Einv", [128, 512]); Eex = sbt(ph, "Eex", [128, 512]); b_Ecw = Buf(); b_Einv = Buf(); b_Eex = Buf()
            res = [sbt(ph, "res%d" % i, [128, 512]) for i in range(4)]; b_res = [Buf() for _ in range(4)]
            tm = [[sbt(ph, "tm%d_%d" % (a_, s_), [128, RWW]) for s_ in range(4)] for a_ in range(5)]; b_tm = [[Buf() for _ in range(4)] for _ in range(5)]
            p_a = pst(ph, "p_a", [128, 512]); p_b = pst(ph, "p_b", [128, 512]); p_cw = pst(ph, "p_cw", [128, 512]); p_tq = [pst(ph, "p_t%d" % i, [128, 512]) for i in range(4)]
            p_s = pst(ph, "p_s", [128, 512])
            b_pa = Buf(); b_pb = Buf(); b_pcw = Buf(); b_pt = [Buf() for _ in range(4)]; b_ps = Buf()
            for (tl, src) in [(mu, mu_fm), (w0t, w0_fm), (a0t, a0_fm), (kvt, kvec_fm), (w2t, w2_d), (a2t, a2_d)]:
                fw.dma(tl[:], src, writes=[b_par if tl is not mu else b_mu])
            op("dve", lambda e: e.tensor_reduce(omu[:], mu[:], AX.X, ALU.add), reads=[b_mu], writes=[b_mu])
            op("dve", lambda e: e.tensor_scalar(omu[:], omu[:], -1.0, 1.0, ALU.mult, ALU.add), reads=[b_mu], writes=[b_mu])
            op("dve", lambda e: e.tensor_scalar(omka[:], kvt[:, :, 1], -1.0, 1.0, ALU.mult, ALU.add), reads=[b_par], writes=[b_par])
            omuc = sbt(ph, "omuc", [128, 26])
            op("dve", lambda e: e.tensor_tensor(omuc[:], mu[:, :, 0], mu[:, :, 1], ALU.add), reads=[b_mu], writes=[b_mu])
            op("dve", lambda e: e.tensor_scalar(omuc[:], omuc[:], -1.0, 1.0, ALU.mult, ALU.add), reads=[b_mu], writes=[b_mu])

            def shift(dst, b_dst, rawt, b_rawt, chunk, g0, ntok, isctx, eng="dve"):
                rows = RWfm[chunk * 128:(chunk + 1) * 128, :]
                if isctx:
                    fw.dma(rawt[:, 0:256], rows[:, 0:256], reads=[sbuf_of("RWfm")], writes=[b_rawt])
                    P = rawt[:, 0:256]
                    op(eng, lambda e: e.tensor_scalar(dst[:, 0:256], P, omuc[:, chunk:chunk + 1], None, ALU.mult), reads=[b_rawt, b_mu], writes=[b_dst])
                    op(eng, lambda e: e.scalar_tensor_tensor(dst[:, 1:256], P[:, 0:255], mu[:, chunk, 0:1], dst[:, 1:256], ALU.mult, ALU.add),
                       reads=[b_rawt, b_mu, b_dst], writes=[b_dst])
                    op(eng, lambda e: e.scalar_tensor_tensor(dst[:, 0:255], P[:, 1:256], mu[:, chunk, 1:2], dst[:, 0:255], ALU.mult, ALU.add),
                       reads=[b_rawt, b_mu, b_dst], writes=[b_dst])
                    return
                lo = g0 - 64
                hi = g0 + ntok + 64
                first = (g0 == CTX)
                last = (g0 + ntok == NT)
                if first:
                    op(eng, lambda e: e.memset(rawt[:, 0:64], 0.0), writes=[b_rawt])
                if last:
                    op(eng, lambda e: e.memset(rawt[:, W - 64:W], 0.0), writes=[b_rawt])
                a_ = 64 if first else 0
                b_ = W - 64 if last else W
                fw.dma(rawt[:, a_:b_], rows[:, lo + a_:lo + b_], reads=[sbuf_of("RWfm")], writes=[b_rawt])
                P3 = rawt[:].rearrange("p (r c) -> p r c", c=64)
                Pc = P3[:, 1:9, :]
                d3 = dst[:].rearrange("p (r c) -> p r c", c=64)
                op(eng, lambda e: e.tensor_scalar(d3, Pc, omu[:, chunk:chunk + 1], None, ALU.mult), reads=[b_rawt, b_mu], writes=[b_dst])
                op(eng, lambda e: e.scalar_tensor_tensor(d3[:, :, 1:], Pc[:, :, :-1], mu[:, chunk, 0:1], d3[:, :, 1:], ALU.mult, ALU.add),
                   reads=[b_rawt, b_mu, b_dst], writes=[b_dst])
                op(eng, lambda e: e.scalar_tensor_tensor(d3[:, :, :-1], Pc[:, :, 1:], mu[:, chunk, 1:2], d3[:, :, :-1], ALU.mult, ALU.add),
                   reads=[b_rawt, b_mu, b_dst], writes=[b_dst])
                op(eng, lambda e: e.scalar_tensor_tensor(d3, P3[:, 0:8, :], mu[:, chunk, 2:3], d3, ALU.mult, ALU.add),
                   reads=[b_rawt, b_mu, b_dst], writes=[b_dst])
                op(eng, lambda e: e.scalar_tensor_tensor(d3, P3[:, 2:10, :], mu[:, chunk, 3:4], d3, ALU.mult, ALU.add),
                   reads=[b_rawt, b_mu, b_dst], writes=[b_dst])

            tiles = [(0, 256, True)] + [(256 + 512 * i, 512, False) for i in range(4)]
            ri = 0
            for (g0, ntok, isctx) in tiles[KT0:KT]:
                nsub = ntok // 128
                blk0 = g0 // 128
                N_ = slice(0, ntok)
                shift(tw, b_tw, rawl[0], b_rawl[0], 24, g0, ntok, isctx)
                if not (KX & 32):
                    op("act", lambda e: e.activation(tw[:, N_], tw[:, N_], AF.Tanh), reads=[b_tw], writes=[b_tw])
                shift(als, b_als, rawl[1], b_rawl[1], 25, g0, ntok, isctx)
                for j in range(KJ):
                    if not isctx:
                        shift(sh[0], b_sh[0], raw[0], b_raw[0], j, g0, ntok, isctx)
                    shift(sh[1], b_sh[1], raw[1], b_raw[1], 8 + j, g0, ntok, isctx)
                    shift(sh[2], b_sh[2], raw[2], b_raw[2], 16 + j, g0, ntok, isctx)
                    rs, ks, vs = sh[0], sh[1], sh[2]
                    op("dve", lambda e: e.tensor_scalar(kkr[:, N_], ks[:, N_], kvt[:, j, 0:1], None, ALU.mult), reads=[b_sh[1], b_par], writes=[b_kkr])
                    op("dve", lambda e: e.tensor_tensor(t1[:, N_], kkr[:, N_], kkr[:, N_], ALU.mult), reads=[b_kkr], writes=[b_t1])
                    op("pe", lambda e: e.matmul(p_a[:, N_], cm[:, C_BONES, :], t1[:, N_], start=True, stop=True), reads=[b_cm, b_t1], writes=[b_pa])
                    op("act", lambda e: e.activation(t1[:, N_], p_a[:, N_], AF.Sqrt, bias=EPS12, scale=1.0), reads=[b_pa, b_cst], writes=[b_t1])
                    op("dve", lambda e: e.reciprocal(t1[:, N_], t1[:, N_]), reads=[b_t1], writes=[b_t1])
                    op("dve", lambda e: e.tensor_tensor(kk[:, N_], kkr[:, N_], t1[:, N_], ALU.mult), reads=[b_kkr, b_t1], writes=[b_kk])
                    for sub in range(nsub):
                        q_ = sub % 4
                        op("pe", lambda e: e.transpose(p_tq[q_][:, 0:128], vs[:, sub * 128:(sub + 1) * 128], ident),
                           reads=[b_sh[2], b_cm], writes=[b_pt[q_]])
                        op("dve", lambda e: e.tensor_copy(tm[0][sub][:, j * 128:(j + 1) * 128], p_tq[q_][:, 0:128]), reads=[b_pt[q_]], writes=[b_tm[0][sub]])
                    for d in range(2):
                        ds = slice(d * 64, (d + 1) * 64)
                        js = slice(j * 128, (j + 1) * 128)
                        op("pe", lambda e: e.matmul(p_a[:, N_], w2t[ds, js], tw[ds, N_], start=True, stop=True), reads=[b_par, b_tw], writes=[b_pa])
                        op("act", lambda e: e.activation(lgw[:, N_], p_a[:, N_], AF.Sigmoid, bias=w0t[:, j, d:d + 1], scale=1.0), reads=[b_pa, b_par], writes=[b_lgw])
                        op("dve", lambda e: e.tensor_scalar(lgw[:, N_], lgw[:, N_], -0.6065306597126334, None, ALU.mult), reads=[b_lgw], writes=[b_lgw])
                        op("pe", lambda e: e.matmul(p_b[:, N_], a2t[ds, js], als[ds, N_], start=True, stop=True), reads=[b_par, b_als], writes=[b_pb])
                        op("act", lambda e: e.activation(av[:, N_], p_b[:, N_], AF.Sigmoid, bias=a0t[:, j, d:d + 1], scale=1.0), reads=[b_pb, b_par], writes=[b_av])
                        op("dve", lambda e: e.tensor_scalar(t1[:, N_], av[:, N_], kvt[:, j, 1:2], omka[:, j:j + 1], ALU.mult, ALU.add), reads=[b_av, b_par], writes=[b_t1])
                        op("dve", lambda e: e.tensor_tensor(kd[d][:, N_], t1[:, N_], ks[:, N_], ALU.mult), reads=[b_t1, b_sh[1]], writes=[b_kd[d]])
                        op("dve", lambda e: e.tensor_tensor(bv[:, N_], kk[:, N_], av[:, N_], ALU.mult), reads=[b_kk, b_av], writes=[b_bv])
                        tri = cm[:, C_TF if d == 0 else C_TB, :]
                        for sub in range(nsub):
                            ss_ = slice(sub * 128, (sub + 1) * 128)
                            q_ = sub % 4
                            op("pe", lambda e: e.transpose(p_tq[q_][:, 0:128], lgw[:, ss_], ident), reads=[b_lgw, b_cm], writes=[b_pt[q_]])
                            op("dve", lambda e: e.tensor_copy(lt[:], p_tq[q_][:, 0:128]), reads=[b_pt[q_]], writes=[b_lt])
                            op("pe", lambda e: e.matmul(p_cw[:, ss_], lt[:], tri, start=True, stop=True), reads=[b_lt, b_cm], writes=[b_pcw])
                        op("act", lambda e: e.activation(Ecw[:, N_], p_cw[:, N_], AF.Exp), reads=[b_pcw], writes=[b_Ecw])
                        op("act", lambda e: e.activation(Einv[:, N_], p_cw[:, N_], AF.Exp, scale=-1.0), reads=[b_pcw], writes=[b_Einv])
                        op("dve", lambda e: e.tensor_tensor(t1[:, N_], p_cw[:, N_], lgw[:, N_], ALU.subtract), reads=[b_pcw, b_lgw], writes=[b_t1])
                        op("act", lambda e: e.activation(Eex[:, N_], t1[:, N_], AF.Exp), reads=[b_t1], writes=[b_Eex])
                        lastc = 127 if d == 0 else 0
                        for sub in range(nsub):
                            op("dve", lambda e: e.tensor_copy(wc[d][:, j, blk0 + sub:blk0 + sub + 1], Ecw[:, sub * 128 + lastc:sub * 128 + lastc + 1]),
                               reads=[b_Ecw], writes=[b_wc[d]])
                        prods = [(kk, b_kk, Eex, b_Eex, AL[d], "AL%d" % d), (bv, b_bv, Einv, b_Einv, BE[d], "BE%d" % d),
                                 (kd[d], b_kd[d], Einv, b_Einv, KA[d], "KA%d" % d)]
                        if not isctx:
                            prods.append((rs, b_sh[0], Ecw, b_Ecw, RH[d], "RH%d" % d))
                        for pi_, (x_, bx_, y_, by_, dst, nm) in enumerate(prods):
                            op("dve", lambda e: e.tensor_tensor(res[pi_][:, N_], x_[:, N_], y_[:, N_], ALU.mult), reads=[bx_, by_], writes=[b_res[pi_]])
                            if not (KX & 64):
                                fw.dma(dst[js, g0:g0 + ntok], res[pi_][:, N_], reads=[b_res[pi_]], writes=[sbuf_of(nm)], q=SQ)
                            if pi_ in (1, 2):
                                dstT = BEt[d] if pi_ == 1 else KAt[d]
                                nmT = ("BEt%d" if pi_ == 1 else "KAt%d") % d
                                for sub in range(nsub):
                                    q_ = sub % 4
                                    srcT = kk if (KX & 128) else res[pi_]
                                    op("pe", lambda e: e.transpose(p_tq[q_][:, 0:128], srcT[:, sub * 128:(sub + 1) * 128], ident),
                                       reads=[b_res[pi_], b_cm], writes=[b_pt[q_]])
                                    ta = 1 + 2 * d + (pi_ - 1)
                                    op("dve", lambda e: e.tensor_copy(tm[ta][sub][:, js], p_tq[q_][:, 0:128]), reads=[b_pt[q_]], writes=[b_tm[ta][sub]])
                    if not isctx:
                        op("dve", lambda e: e.tensor_tensor(t1[:], kd[0][:], kd[1][:], ALU.add), reads=[b_kd[0], b_kd[1]], writes=[b_t1])
                        op("dve", lambda e: e.scalar_tensor_tensor(t1[:], t1[:], kvt[:, j, 2:3], rs[:], ALU.mult, ALU.mult), reads=[b_t1, b_par, b_sh[0]], writes=[b_t1])
                        op("pe", lambda e: e.matmul(p_s[:], cm[:, C_BONES, :], t1[:], start=True, stop=True), reads=[b_cm, b_t1], writes=[b_ps])
                        op("dve", lambda e: e.tensor_tensor(res[3][:], p_s[:], vs[:], ALU.mult), reads=[b_ps, b_sh[2]], writes=[b_res[3]])
                        fw.dma(BON[j * 128:(j + 1) * 128, g0 - CTX:g0 - CTX + 512], res[3][:], reads=[b_res[3]], writes=[sbuf_of("BON")], q=SQ)
                for sub in range(nsub if not (KX & 16) else 0):
                    rows = slice(g0 + sub * 128, g0 + (sub + 1) * 128)
                    for ta, (dstT, nmT) in enumerate([(Vt, "Vt"), (BEt[0], "BEt0"), (KAt[0], "KAt0"), (BEt[1], "BEt1"), (KAt[1], "KAt1")]):
                        fw.dma(dstT[rows, :], tm[ta][sub][:], reads=[b_tm[ta][sub]], writes=[sbuf_of(nmT)], q=SQ)
        fw.barrier()

        for ph in _phase(4):
            J = 8
            aT = [sbt(ph, "aT%d" % j, [128, 128]) for j in range(J)]
            bT = [sbt(ph, "bT%d" % j, [128, 128]) for j in range(J)]
            kT = [sbt(ph, "kT%d" % j, [128, 128]) for j in range(J)]
            rT = [sbt(ph, "rT%d" % j, [128, 128]) for j in range(J)]
            btkA = sbt(ph, "btkA", [128, RWW]); ktkA = sbt(ph, "ktkA", [128, RWW]); vtkA = sbt(ph, "vtkA", [128, RWW])
            btk = [btkA[:, j * 128:(j + 1) * 128] for j in range(J)]
            ktk = [ktkA[:, j * 128:(j + 1) * 128] for j in range(J)]
            vtk = [vtkA[:, j * 128:(j + 1) * 128] for j in range(J)]
            b_tokA = [Buf(), Buf(), Buf()]
            b_in = [[Buf() for _ in range(7)] for _ in range(J)]
            Pm = [[sbt(ph, "Pm%d_%d" % (j, i), [128, 2, 128]) for i in range(2)] for j in range(J)]
            PTm = [[sbt(ph, "PTm%d_%d" % (j, i), [128, 2, 128]) for i in range(2)] for j in range(J)]
            TTm = [[sbt(ph, "TTm%d_%d" % (j, i), [128, 2, 128]) for i in range(2)] for j in range(J)]
            b_P = [[Buf(), Buf()] for _ in range(J)]; b_PT = [[Buf(), Buf()] for _ in range(J)]; b_TT = [[Buf(), Buf()] for _ in range(J)]
            AkT = [sbt(ph, "AkT%d" % j, [128, 2, 128]) for j in range(J)]
            BbT = [sbt(ph, "BbT%d" % j, [128, 2, 128]) for j in range(J)]
            BkT = [sbt(ph, "BkT%d" % j, [128, 2, 128]) for j in range(J)]
            b_AkT = [Buf() for _ in range(J)]; b_BbT = [Buf() for _ in range(J)]; b_BkT = [Buf() for _ in range(J)]
            St = [[sbt(ph, "St%d_%d" % (j, i), [128, 64]) for i in range(2)] for j in range(J)]
            b_St = [[Buf(), Buf()] for _ in range(J)]
            Rn = [sbt(ph, "Rn%d" % i, [128, 2, 64]) for i in range(2)]; b_Rn = [Buf(), Buf()]
            Us = [sbt(ph, "Us%d" % i, [128, 2, 64]) for i in range(2)]; b_Us = [Buf(), Buf()]
            stt_ = [sbt(ph, "stt%d" % i, [128, 64]) for i in range(2)]; b_stt = [Buf(), Buf()]
            Ys = [sbt(ph, "Ys%d" % i, [128, 2, 64]) for i in range(2)]; b_Ys = [Buf(), Buf()]
            Yf = [sbt(ph, "Yf%d" % i, [128, 2, 64]) for i in range(2)]; b_Yf = [Buf(), Buf()]
            cen = [sbt(ph, "cen%d" % i, [128, 2, 64]) for i in range(2)]; b_cen = [Buf(), Buf()]
            gsq = [sbt(ph, "gsq%d" % i, [128, 2, 64]) for i in range(2)]; b_gsq = [Buf(), Buf()]
            gst = [sbt(ph, "gst%d" % i, [128, 4]) for i in range(2)]; b_gst = [Buf(), Buf()]
            bon = [sbt(ph, "bon%d" % i, [128, 128]) for i in range(2)]; zr = [sbt(ph, "zr%d" % i, [128, 128]) for i in range(2)]
            b_bon = [Buf(), Buf()]; b_zr = [Buf(), Buf()]
            yo = [sbt(ph, "yo%d" % i, [128, 128]) for i in range(2)]; b_yo = [Buf(), Buf()]
            gng = sbt(ph, "gng", [128, RWW]); gnb = sbt(ph, "gnb", [128, RWW]); b_gn = Buf()
            fw.dma(gng[:], gng_b, writes=[b_gn]); fw.dma(gnb[:], gnb_b, writes=[b_gn])
            G = [pst(ph, "G%d" % i, [128, 1024]) for i in range(4)]
            b_G = [[Buf(), Buf()] for _ in range(4)]

            def gv(g, q):
                return G[g][:].rearrange("p (h q s) -> p h q s", h=2, q=4)[:, :, q, :]

            def gs(g, q):
                return G[g][:].rearrange("p (h c) -> p h c", h=2)[:, :, q * 64:(q + 1) * 64]

            KB = int(os.environ.get("KB", "99"))
            for d in range(min(2, KD)):
                m_lo = cm[:, C_GT if d == 0 else C_LT, :]
                m_up = cm[:, C_LT if d == 0 else C_GT, :]
                m_upi = cm[:, C_TF if d == 0 else C_TB, :]
                bc3 = lambda m: m.unsqueeze(1).to_broadcast([128, 2, 128])
                order = list(range(18)) if d == 0 else [1, 0] + list(range(17, 1, -1))
                cur = 0
                for j in range(J):
                    op("dve", lambda e: e.memset(St[j][0][:], 0.0), writes=[b_St[j][0]])
                for blk in order[:KB]:
                    isctx = blk < 2
                    g0 = blk * 128
                    x0 = g0 - CTX
                    tsl = slice(g0, g0 + 128)
                    fw.dma(btkA[:], BEt[d][tsl, :], reads=[sbuf_of("BEt%d" % d)], writes=[b_tokA[0]])
                    fw.dma(ktkA[:], KAt[d][tsl, :], reads=[sbuf_of("KAt%d" % d)], writes=[b_tokA[1]])
                    fw.dma(vtkA[:], Vt[tsl, :], reads=[sbuf_of("Vt")], writes=[b_tokA[2]])
                    for j in range(J):
                        js = slice(j * 128, (j + 1) * 128)
                        bi = b_in[j]
                        fw.dma(aT[j][:], AL[d][js, tsl], reads=[sbuf_of("AL%d" % d)], writes=[bi[0]])
                        fw.dma(bT[j][:], BE[d][js, tsl], reads=[sbuf_of("BE%d" % d)], writes=[bi[1]])
                        fw.dma(kT[j][:], KA[d][js, tsl], reads=[sbuf_of("KA%d" % d)], writes=[bi[2]])
                        if not isctx:
                            fw.dma(rT[j][:], RH[d][js, tsl], reads=[sbuf_of("RH%d" % d)], writes=[bi[3]])
                        bi[4], bi[5], bi[6] = b_tokA
                    for j in range(J):
                        bi = b_in[j]
                        for h in range(2):
                            hs = slice(h * 64, (h + 1) * 64)
                            op("pe", lambda e: e.matmul(gv(0, 0)[:, h, :], aT[j][hs, :], bT[j][hs, :], start=True, stop=True),
                               reads=[bi[0], bi[1]], writes=[b_G[0][h]])
                            op("pe", lambda e: e.matmul(gv(0, 1)[:, h, :], bT[j][hs, :], aT[j][hs, :], start=True, stop=True),
                               reads=[bi[0], bi[1]], writes=[b_G[0][h]])
                            op("pe", lambda e: e.matmul(gv(0, 2)[:, h, :], kT[j][hs, :], aT[j][hs, :], start=True, stop=True),
                               reads=[bi[0], bi[2]], writes=[b_G[0][h]])
                            if not isctx:
                                op("pe", lambda e: e.matmul(gv(0, 3)[:, h, :], bT[j][hs, :], rT[j][hs, :], start=True, stop=True),
                                   reads=[bi[1], bi[3]], writes=[b_G[0][h]])
                                op("pe", lambda e: e.matmul(gv(1, 0)[:, h, :], kT[j][hs, :], rT[j][hs, :], start=True, stop=True),
                                   reads=[bi[2], bi[3]], writes=[b_G[1][h]])
                        op("dve", lambda e: e.scalar_tensor_tensor(Pm[j][0][:], gv(0, 0), -1.0, bc3(m_lo), ALU.mult, ALU.mult),
                           reads=b_G[0] + [b_cm], writes=[b_P[j][0]])
                        op("dve", lambda e: e.scalar_tensor_tensor(PTm[j][0][:], gv(0, 1), -1.0, bc3(m_up), ALU.mult, ALU.mult),
                           reads=b_G[0] + [b_cm], writes=[b_PT[j][0]])
                        op("dve", lambda e: e.tensor_tensor(TTm[j][0][:], PTm[j][0][:], bc3(ident), ALU.add),
                           reads=[b_PT[j][0], b_cm], writes=[b_TT[j][0]])
                        op("dve", lambda e: e.tensor_tensor(AkT[j][:], gv(0, 2), bc3(m_up), ALU.mult),
                           reads=b_G[0] + [b_cm], writes=[b_AkT[j]])
                        if not isctx:
                            op("dve", lambda e: e.tensor_tensor(BbT[j][:], gv(0, 3), bc3(m_upi), ALU.mult),
                               reads=b_G[0] + [b_cm], writes=[b_BbT[j]])
                            op("dve", lambda e: e.tensor_tensor(BkT[j][:], gv(1, 0), bc3(m_upi), ALU.mult),
                               reads=b_G[1] + [b_cm], writes=[b_BkT[j]])
                    for i in range(1, 7):
                        a_, n_ = (i - 1) % 2, i % 2
                        for j in range(J):
                            for h in range(2):
                                op("pe", lambda e: e.matmul(gv(1, 1)[:, h, :], PTm[j][a_][:, h, :], Pm[j][a_][:, h, :], start=True, stop=True),
                                   reads=[b_P[j][a_], b_PT[j][a_]], writes=[b_G[1][h]])
                                if i < 6:
                                    op("pe", lambda e: e.matmul(gv(1, 2)[:, h, :], Pm[j][a_][:, h, :], PTm[j][a_][:, h, :], start=True, stop=True),
                                       reads=[b_P[j][a_], b_PT[j][a_]], writes=[b_G[1][h]])
                            op("dve", lambda e: e.tensor_copy(Pm[j][n_][:], gv(1, 1)), reads=b_G[1], writes=[b_P[j][n_]])
                            if i < 6:
                                op("dve", lambda e: e.tensor_copy(PTm[j][n_][:], gv(1, 2)), reads=b_G[1], writes=[b_PT[j][n_]])
                            for h in range(2):
                                op("pe", lambda e: e.matmul(gv(1, 3)[:, h, :], Pm[j][n_][:, h, :], TTm[j][a_][:, h, :], start=True, stop=True),
                                   reads=[b_P[j][n_], b_TT[j][a_]], writes=[b_G[1][h]])
                            op("dve", lambda e: e.tensor_tensor(TTm[j][n_][:], gv(1, 3), TTm[j][a_][:], ALU.add),
                               reads=b_G[1] + [b_TT[j][a_]], writes=[b_TT[j][n_]])
                    TTf = 0
                    nxt = 1 - cur
                    for j in range(J):
                        pr = j % 2
                        bi = b_in[j]
                        js = slice(j * 128, (j + 1) * 128)
                        Rv, Uv, Yv = gs(2, 0), gs(2, 1), gs(2, 2)
                        for h in range(2):
                            hs = slice(h * 64, (h + 1) * 64)
                            op("pe", lambda e: e.matmul(Rv[:, h, :], aT[j][hs, :], St[j][cur][hs, :], start=True, stop=False),
                               reads=[bi[0], b_St[j][cur]], writes=[b_G[2][h]])
                            op("pe", lambda e: e.matmul(Rv[:, h, :], AkT[j][:, h, :], vtk[j][:, hs], start=False, stop=True),
                               reads=[b_AkT[j], bi[6]], writes=[b_G[2][h]])
                        op("dve", lambda e: e.tensor_scalar(Rn[pr][:], Rv, -1.0, None, ALU.mult), reads=b_G[2], writes=[b_Rn[pr]])
                        for h in range(2):
                            op("pe", lambda e: e.matmul(Uv[:, h, :], TTm[j][TTf][:, h, :], Rn[pr][:, h, :], start=True, stop=True),
                               reads=[b_TT[j][TTf], b_Rn[pr]], writes=[b_G[2][h]])
                        op("dve", lambda e: e.tensor_copy(Us[pr][:], Uv), reads=b_G[2], writes=[b_Us[pr]])
                        SSv = G[3][:, 0:128]
                        op("pe", lambda e: e.matmul(SSv, btk[j], Us[pr][:].rearrange("p h v -> p (h v)"), start=True, stop=False),
                           reads=[bi[4], b_Us[pr]], writes=[b_G[3][0]])
                        op("pe", lambda e: e.matmul(SSv, ktk[j], vtk[j], start=False, stop=True),
                           reads=[bi[5], bi[6]], writes=[b_G[3][0]])
                        for h in range(2):
                            hs = slice(h * 64, (h + 1) * 64)
                            op("dve", lambda e: e.tensor_tensor(stt_[pr][hs, :], SSv[hs, h * 64:(h + 1) * 64], St[j][cur][hs, :], ALU.add),
                               reads=[b_G[3][0], b_St[j][cur]], writes=[b_stt[pr]])
                        op("dve", lambda e: e.tensor_scalar(St[j][nxt][:], stt_[pr][:], wc[d][:, j, blk:blk + 1], None, ALU.mult),
                           reads=[b_stt[pr], b_wc[d]], writes=[b_St[j][nxt]])
                        if isctx:
                            continue
                        for h in range(2):
                            hs = slice(h * 64, (h + 1) * 64)
                            op("pe", lambda e: e.matmul(Yv[:, h, :], rT[j][hs, :], St[j][cur][hs, :], start=True, stop=False),
                               reads=[bi[3], b_St[j][cur]], writes=[b_G[2][h]])
                            op("pe", lambda e: e.matmul(Yv[:, h, :], BbT[j][:, h, :], Us[pr][:, h, :], start=False, stop=False),
                               reads=[b_BbT[j], b_Us[pr]], writes=[b_G[2][h]])
                            op("pe", lambda e: e.matmul(Yv[:, h, :], BkT[j][:, h, :], vtk[j][:, hs], start=False, stop=True),
                               reads=[b_BkT[j], bi[6]], writes=[b_G[2][h]])
                        if d == 0:
                            op("dve", lambda e: e.tensor_copy(Ys[pr][:], Yv), reads=b_G[2], writes=[b_Ys[pr]])
                            fw.dma(YF[x0:x0 + 128, js], Ys[pr][:].rearrange("p h v -> p (h v)"), reads=[b_Ys[pr]], writes=[sbuf_of("YF")], q=SQ)
                        else:
                            fw.dma(Yf[pr][:].rearrange("p h v -> p (h v)"), YF[x0:x0 + 128, js], reads=[sbuf_of("YF")], writes=[b_Yf[pr]])
                            fw.dma(bon[pr][:], BON[js, x0:x0 + 128], reads=[sbuf_of("BON")], writes=[b_bon[pr]])
                            fw.dma(zr[pr][:], RWfm[(26 + j) * 128:(27 + j) * 128, g0:g0 + 128], reads=[sbuf_of("RWfm")], writes=[b_zr[pr]])
                            op("dve", lambda e: e.tensor_tensor(Ys[pr][:], Yv, Yf[pr][:], ALU.add), reads=b_G[2] + [b_Yf[pr]], writes=[b_Ys[pr]])
                            g_ = gst[pr]
                            op("dve", lambda e: e.tensor_reduce(g_[:, 0:2], Ys[pr][:], AX.X, ALU.add), reads=[b_Ys[pr]], writes=[b_gst[pr]])
                            op("dve", lambda e: e.tensor_scalar(g_[:, 0:2], g_[:, 0:2], -1.0 / 64, None, ALU.mult), reads=[b_gst[pr]], writes=[b_gst[pr]])
                            op("dve", lambda e: e.tensor_tensor(cen[pr][:], Ys[pr][:], g_[:, 0:2].unsqueeze(2).to_broadcast([128, 2, 64]), ALU.add),
                               reads=[b_Ys[pr], b_gst[pr]], writes=[b_cen[pr]])
                            op("dve", lambda e: e.tensor_tensor(gsq[pr][:], cen[pr][:], cen[pr][:], ALU.mult), reads=[b_cen[pr]], writes=[b_gsq[pr]])
                            op("dve", lambda e: e.tensor_reduce(g_[:, 2:4], gsq[pr][:], AX.X, ALU.add), reads=[b_gsq[pr]], writes=[b_gst[pr]])
                            op("act", lambda e: e.activation(g_[:, 2:4], g_[:, 2:4], AF.Sqrt, bias=EPSGN, scale=1.0 / 64), reads=[b_gst[pr], b_cst], writes=[b_gst[pr]])
                            op("dve", lambda e: e.reciprocal(g_[:, 2:4], g_[:, 2:4]), reads=[b_gst[pr]], writes=[b_gst[pr]])
                            op("dve", lambda e: e.tensor_tensor(cen[pr][:], cen[pr][:], g_[:, 2:4].unsqueeze(2).to_broadcast([128, 2, 64]), ALU.mult),
                               reads=[b_cen[pr], b_gst[pr]], writes=[b_cen[pr]])
                            cf = cen[pr][:].rearrange("p h v -> p (h v)")
                            op("dve", lambda e: e.tensor_tensor(cf, cf, gng[:, js], ALU.mult), reads=[b_cen[pr], b_gn], writes=[b_cen[pr]])
                            op("dve", lambda e: e.tensor_tensor(cf, cf, gnb[:, js], ALU.add), reads=[b_cen[pr], b_gn], writes=[b_cen[pr]])
                            yTv = G[3][:, 512:640]
                            op("pe", lambda e: e.transpose(yTv, cf, ident), reads=[b_cen[pr], b_cm], writes=[b_G[3][1]])
                            op("dve", lambda e: e.tensor_tensor(yo[pr][:], yTv, bon[pr][:], ALU.add), reads=[b_G[3][1], b_bon[pr]], writes=[b_yo[pr]])
                            op("dve", lambda e: e.tensor_tensor(yo[pr][:], yo[pr][:], zr[pr][:], ALU.mult), reads=[b_yo[pr], b_zr[pr]], writes=[b_yo[pr]])
                            fw.dma(YR[js, x0:x0 + 128], yo[pr][:], reads=[b_yo[pr]], writes=[sbuf_of("YR")], q=SQ)
                    cur = nxt
        fw.barrier()

        for ph in _phase(5):
            TT_ = 256
            yh = sbt(ph, "yh", [128, 8, TT_]); yr = sbt(ph, "yr", [128, 8, TT_]); b_yh = Buf(); b_yr = Buf()
            gh = [sbt(ph, "gh%d" % i, [128, 4, TT_]) for i in range(2)]; gr = [sbt(ph, "gr%d" % i, [128, 4, TT_]) for i in range(2)]
            b_gh = [Buf(), Buf()]; b_gr = [Buf(), Buf()]
            mT = sbt(ph, "mT", [128, 16, TT_]); b_mT = Buf()
            whg = [sbt(ph, "whg0", [128, 8, 512])] * 2; wrw = [sbt(ph, "wrw0", [128, 8, 512])] * 2
            b_whg = [Buf()] * 2; b_wrw = [Buf()] * 2
            wo = [sbt(ph, "wo0", [128, 16, 512])] * 2; b_wo = [Buf()] * 2
            xr = [sbt(ph, "xr%d" % i, [128, D]) for i in range(2)]; b_xr = [Buf(), Buf()]
            xn = [sbt(ph, "xn%d" % i, [128, D]) for i in range(2)]; b_xn = [Buf(), Buf()]
            junk3 = sbt(ph, "junk3", [128, D]); b_j3 = Buf()
            ss3 = sbt(ph, "ss3", [128, 2]); b_ss3 = Buf()
            fgt_ = sbt(ph, "fgt_", [128, D]); b_fg = Buf()
            tmpm = [sbt(ph, "tmpm%d" % i, [128, TT_]) for i in range(2)]; b_tmpm = [Buf(), Buf()]
            pp1 = [pst(ph, "pp1_%d" % i, [128, 512]) for i in range(2)]; pp2 = [pst(ph, "pp2_%d" % i, [128, 512]) for i in range(2)]
            b_pp1 = [Buf(), Buf()]; b_pp2 = [Buf(), Buf()]
            po = [pst(ph, "po%d" % i, [128, 512]) for i in range(4)]; b_po = [Buf() for _ in range(4)]
            fw.dma(fgt_[:], fg_b, writes=[b_fg])
            whg_r = w_hg_o.rearrange("(k p) c -> p k c", p=128)
            wrw_r = w_rw_o.rearrange("(k p) c -> p k c", p=128)
            wo_r = w_o.rearrange("(k p) c -> p k c", p=128)
            YH_r = YH.rearrange("(k p) t -> p k t", p=128)
            YR_r = YR.rearrange("(k p) t -> p k t", p=128)
            G_r = RWfm[(74 - 40) * 128:, :].rearrange("(k p) t -> p k t", p=128)
            wi = 0
            woi = 0
            xi = 0
            for tt in range(SEQ // TT_):
                x0 = tt * TT_
                g0 = x0 + CTX
                fw.dma3(yh[:], YH_r[:, :, x0:x0 + TT_], 8, reads=[sbuf_of("YH")], writes=[b_yh])
                fw.dma3(yr[:], YR_r[:, :, x0:x0 + TT_], 8, reads=[sbuf_of("YR")], writes=[b_yr])
                for mg in range(4):
                    wb = wi % 2
                    wi += 1
                    cs = slice(mg * 512, (mg + 1) * 512)
                    fw.dma3(whg[wb][:], whg_r[:, :, cs], 8, writes=[b_whg[wb]])
                    fw.dma3(wrw[wb][:], wrw_r[:, :, cs], 8, writes=[b_wrw[wb]])
                    fw.dma3(gh[wb][:], G_r[:, mg * 4:(mg + 1) * 4, g0:g0 + TT_], 4, reads=[sbuf_of("RWfm")], writes=[b_gh[wb]])
                    fw.dma3(gr[wb][:], G_r[:, 16 + mg * 4:16 + (mg + 1) * 4, g0:g0 + TT_], 4, reads=[sbuf_of("RWfm")], writes=[b_gr[wb]])
                    for mm in range(4):
                        m = mg * 4 + mm
                        a = mm % 2
                        for k in range(8):
                            op("pe", lambda e: e.matmul(pp1[a][:, :TT_], whg[wb][:, k, mm * 128:(mm + 1) * 128], yh[:, k, :], start=(k == 0), stop=(k == 7)),
                               reads=[b_whg[wb], b_yh], writes=[b_pp1[a]], inc=(k == 7))
                        for k in range(8):
                            op("pe", lambda e: e.matmul(pp2[a][:, :TT_], wrw[wb][:, k, mm * 128:(mm + 1) * 128], yr[:, k, :], start=(k == 0), stop=(k == 7)),
                               reads=[b_wrw[wb], b_yr], writes=[b_pp2[a]], inc=(k == 7))
                        op("dve", lambda e: e.tensor_tensor(tmpm[a][:], pp1[a][:, :TT_], gh[wb][:, mm, :], ALU.mult), reads=[b_pp1[a], b_gh[wb]], writes=[b_tmpm[a]])
                        op("dve", lambda e: e.tensor_tensor(mT[:, m, :], pp2[a][:, :TT_], gr[wb][:, mm, :], ALU.mult), reads=[b_pp2[a], b_gr[wb]], writes=[b_mT])
                        op("dve", lambda e: e.tensor_tensor(mT[:, m, :], mT[:, m, :], tmpm[a][:], ALU.add), reads=[b_mT, b_tmpm[a]], writes=[b_mT])
                for sub in range(TT_ // 128):
                    xb = xi % 2
                    xi += 1
                    fw.dma(xr[xb][:], xc[g0 + sub * 128:g0 + (sub + 1) * 128, :], writes=[b_xr[xb]])
                for n in range(4):
                    ob_ = woi % 2
                    woi += 1
                    fw.dma3(wo[ob_][:], wo_r[:, :, n * 512:(n + 1) * 512], 16, writes=[b_wo[ob_]])
                    for sub in range(TT_ // 128):
                        xb = (xi - (TT_ // 128) + sub) % 2
                        a = (n * 2 + sub) % 4
                        for k in range(16):
                            op("pe", lambda e: e.matmul(po[a][:], mT[:, k, sub * 128:(sub + 1) * 128], wo[ob_][:, k, :], start=(k == 0), stop=(k == 15)),
                               reads=[b_mT, b_wo[ob_]], writes=[b_po[a]], inc=(k == 15))
                        ns = slice(n * 512, (n + 1) * 512)
                        op("dve", lambda e: e.tensor_tensor(xn[xb][:, ns], po[a][:], gate_b[:, ns], ALU.mult), reads=[b_po[a], b_gate], writes=[b_xn[xb]])
                        op("dve", lambda e: e.tensor_tensor(xn[xb][:, ns], xn[xb][:, ns], xr[xb][:, ns], ALU.add), reads=[b_xn[xb], b_xr[xb]], writes=[b_xn[xb]])
                for sub in range(TT_ // 128):
                    xb = (xi - (TT_ // 128) + sub) % 2
                    op("act", lambda e: e.activation(junk3[:], xn[xb][:], AF.Square, accum_out=ss3[:, 0:1]), reads=[b_xn[xb]], writes=[b_j3, b_ss3])
                    op("act", lambda e: e.activation(ss3[:, 1:2], ss3[:, 0:1], AF.Sqrt, bias=EPS6, scale=1.0 / D), reads=[b_ss3, b_cst], writes=[b_ss3])
                    op("dve", lambda e: e.reciprocal(ss3[:, 1:2], ss3[:, 1:2]), reads=[b_ss3], writes=[b_ss3])
                    op("dve", lambda e: e.scalar_tensor_tensor(xn[xb][:], xn[xb][:], ss3[:, 1:2], fgt_[:], ALU.mult, ALU.mult),
                       reads=[b_xn[xb], b_ss3, b_fg], writes=[b_xn[xb]])
                    fw.dma(out[x0 + sub * 128:x0 + (sub + 1) * 128, :], xn[xb][:], reads=[b_xn[xb]], writes=[sbuf_of("out")], q=SQ)
        fw.barrier()
    print("bass program built: %d instructions" % fw.ninstr, flush=True)
    dbg_names = ["HGtok", "RWfm", "OFs", "YH", "AL0", "BE0", "KA0", "RH0", "AL1", "BE1", "KA1", "RH1", "BEt0", "KAt0", "Vt", "BON", "YF", "YR"]
    return nc, dbg_names


def _host_inputs(b, inp):
    f = lambda a: np.ascontiguousarray(a, dtype=np.float32)
    fm = lambda v: f(np.asarray(v).reshape(-1, 128).T)
    bc = lambda v: f(np.broadcast_to(np.asarray(v).reshape(1, -1), (128, np.asarray(v).size)))
    m = {}
    m["xc"] = f(np.concatenate([inp["ctx"][b], inp["x"][b]], axis=0))
    m["cc"] = f(np.stack([fm(inp["c"][b]), fm(inp["c_ctx"])], axis=-1))
    m["ada_w"] = f(inp["ada_w"][0].reshape(16, 128, 3 * D))
    m["ada_b_fm"] = fm(inp["ada_b"][0])
    m["ada_b_g"] = f(inp["ada_b"][0][2 * D:].reshape(1, D))
    m["norm_g_fm"] = fm(inp["norm_g"][0])
    m["w_in"] = f(inp["w_in"][0])
    m["hg_lb_b"] = f(np.broadcast_to(inp["hg_lb"][None], (128, 2, 2, HGW)))
    m["hgng_b"] = bc(inp["hg_norm_g"][0])
    mu = inp["rw_mu"][0]
    m["mu_fm"] = f(mu.reshape(4, 26, 128).transpose(2, 1, 0))
    m["w0_fm"] = f(inp["rw_w0"][0].reshape(2, 8, 128).transpose(2, 1, 0))
    m["a0_fm"] = f(inp["rw_a0"][0].reshape(2, 8, 128).transpose(2, 1, 0))
    m["w2"] = f(inp["rw_w2"][0].reshape(128, RWW))
    m["a2"] = f(inp["rw_a2"][0].reshape(128, RWW))
    kv = np.stack([inp["rw_kk"][0], inp["rw_ka"][0], inp["rw_rk"][0]], axis=0)
    m["kvec_fm"] = f(kv.reshape(3, 8, 128).transpose(2, 1, 0))
    m["gng_b"] = bc(inp["rw_gn_g"][0])
    m["gnb_b"] = bc(inp["rw_gn_b"][0])
    m["w_hg_o"] = f(inp["w_hg_out"][0])
    m["w_rw_o"] = f(inp["w_rw_out"][0])
    m["w_o"] = f(inp["w_out"][0])
    m["fg_b"] = bc(inp["final_g"])
    m["cm"] = make_cm()
    cst = np.zeros((128, 8), np.float32)
    cst[:, 0] = 1e-6
    cst[:, 1] = 1e-12
    cst[:, 2] = 64e-5
    cst[:, 3] = 0.0
    cst[:, 4] = 1.0
    m["cst"] = cst
    return m


_LAST = {}


def kernel(**inputs):
    inp = {k: np.asarray(v) for k, v in inputs.items()}
    nb = inp["x"].shape[0]
    nc, dbg = build_program()
    in_maps = [_host_inputs(b, inp) for b in range(nb)]
    res = run_bass_kernel_spmd(nc, in_maps, core_ids=list(range(nb)))
    if DEBUG:
        _LAST["res"] = res
    return np.stack([np.asarray(r["out"], dtype=np.float32) for r in res.results], axis=0)
```

```python
import os
import numpy as np
from contextlib import ExitStack
import concourse.bass as bass
import concourse.mybir as mybir
from concourse.bass_utils import run_bass_kernel_spmd

F32 = mybir.dt.float32
BF16 = mybir.dt.bfloat16
AF = mybir.ActivationFunctionType
ALU = mybir.AluOpType
AX = mybir.AxisListType

D = 2048
SEQ = 2048
CTX = 256
NT = SEQ + CTX
NCOLS = 13568
HGW = 1024
RWW = 1024
CH = 32
DEBUG = bool(os.environ.get("KDEBUG"))
PH = int(os.environ.get("KPHASE", "9"))
SQ = os.environ.get("KSQ", "pool")
ONLY = int(os.environ.get("KONLY", "-1"))
KLIM = int(os.environ.get("KLIM", "-1"))
CASTE = os.environ.get("KCAST", "pool")
KCH = int(os.environ.get("KCH", "99"))
KX = int(os.environ.get("KX", "0"))
KD = int(os.environ.get("KD", "2"))
KT = int(os.environ.get("KT", "9"))
KJ = int(os.environ.get("KJ", "8"))
KT0 = int(os.environ.get("KT0", "0"))


_FWREF = []


def _phase(n):
    if PH >= n and (ONLY < 0 or n == ONLY):
        fw = _FWREF[-1]
        fw.emitted = 0
        fw.limit = KLIM if (n == ONLY and KLIM >= 0) else None
        with ExitStack() as ph:
            yield ph
        print('phase', n, 'emitted', fw.emitted, flush=True)
        fw.limit = None


class Buf:
    __slots__ = ("name", "w", "r")

    def __init__(self, name=""):
        self.name = name
        self.w = None
        self.r = {}


class FW:
    NDMA = 14

    def __init__(self, nc, stack):
        self.nc = nc
        self.eng = {"pe": nc.tensor, "act": nc.scalar, "dve": nc.vector, "pool": nc.gpsimd, "sp": nc.sync}
        self.sem = {}
        self.cnt = {}
        for k in ["pe", "act", "dve", "pool"]:
            self.sem[k] = stack.enter_context(nc.semaphore("s_" + k))
            self.cnt[k] = 0
        for i in range(self.NDMA):
            k = "dma%d" % i
            self.sem[k] = stack.enter_context(nc.semaphore("s_" + k))
            self.cnt[k] = 0
        self.dma_i = 0
        self.waited = {e: {} for e in self.eng}
        self.ninstr = 0
        self.emitted = 0
        self.limit = None

    def _need(self, e, deps):
        for k, v in deps.items():
            if self.waited[e].get(k, 0) >= v:
                continue
            self.eng[e].wait_ge(self.sem[k], v)
            self.waited[e][k] = v

    def _collect(self, e, reads, writes):
        deps = {}

        def add(ev):
            if ev is None:
                return
            k, v = ev
            if k == e and e == "pe":
                return
            if deps.get(k, 0) < v:
                deps[k] = v
        for b in reads:
            add(b.w)
        for b in writes:
            add(b.w)
            for k, v in b.r.items():
                add((k, v))
        return deps

    def _mark(self, ev, reads, writes):
        k, v = ev
        for b in reads:
            if b.r.get(k, 0) < v:
                b.r[k] = v
        for b in writes:
            b.w = ev
            b.r = {}

    def _skip(self):
        self.emitted += 1
        return self.limit is not None and self.emitted > self.limit

    def op(self, e, fn, reads=(), writes=(), inc=True):
        if self._skip():
            return None
        deps = self._collect(e, reads, writes)
        self._need(e, deps)
        ins = fn(self.eng[e])
        self.ninstr += 1
        if inc:
            self.cnt[e] += 1
            ins.then_inc(self.sem[e], 1)
            ev = (e, self.cnt[e])
        else:
            ev = (e, self.cnt[e] + 1)
        self._mark(ev, reads, writes)
        return ins

    def dma(self, out, in_, reads=(), writes=(), q="sp", **kw):
        if self._skip():
            return
        i = self.dma_i
        self.dma_i += 1
        k = "dma%d" % (i % self.NDMA)
        deps = self._collect(q, reads, writes)
        if self.cnt[k] > 0 and deps.get(k, 0) < self.cnt[k]:
            deps[k] = self.cnt[k]
        self._need(q, deps)
        self.cnt[k] += 16
        self.eng[q].dma_start(out=out, in_=in_, **kw).then_inc(self.sem[k], 16)
        self.ninstr += 1
        self._mark((k, self.cnt[k]), reads, writes)

    def dma3(self, out, in_, n, **kw):
        for k in range(n):
            self.dma(out[:, k], in_[:, k], **kw)

    def barrier(self):
        for e in self.eng:
            deps = {k: v for k, v in self.cnt.items() if v > 0 and k != e}
            self._need(e, deps)


C_ID, C_TF, C_TB, C_BONES, C_LT, C_GT = 0, 1, 2, 3, 4, 5
NCM = 6


def make_cm():
    p = np.arange(128)[:, None]
    f = np.arange(128)[None, :]
    cm = np.zeros((128, NCM, 128), np.float32)
    cm[:, C_ID] = (p == f)
    cm[:, C_TF] = (p <= f)
    cm[:, C_TB] = (p >= f)
    cm[:, C_BONES] = ((p // 64) == (f // 64))
    cm[:, C_LT] = (p < f)
    cm[:, C_GT] = (p > f)
    return cm


def build_program():
    nc = bass.Bass("TRN2", target_bir_lowering=False)
    dt = lambda name, shape, kind="ExternalInput": nc.dram_tensor(name, shape, F32, kind=kind).ap()
    SCR = "ExternalOutput" if DEBUG else "Internal"
    xc = dt("xc", [NT, D])
    cc_d = dt("cc", [128, 16, 2])
    ada_w = dt("ada_w", [16, 128, 3 * D])
    ada_b_fm = dt("ada_b_fm", [128, 48])
    ada_b_g = dt("ada_b_g", [1, D])
    norm_g_fm = dt("norm_g_fm", [128, 16])
    w_in = dt("w_in", [D, NCOLS])
    hg_lb_b = dt("hg_lb_b", [128, 2, 2, HGW])
    hgng_b = dt("hgng_b", [128, HGW])
    mu_fm = dt("mu_fm", [128, 26, 4])
    w0_fm = dt("w0_fm", [128, 8, 2])
    a0_fm = dt("a0_fm", [128, 8, 2])
    w2_d = dt("w2", [128, RWW])
    a2_d = dt("a2", [128, RWW])
    kvec_fm = dt("kvec_fm", [128, 8, 3])
    gng_b = dt("gng_b", [128, RWW])
    gnb_b = dt("gnb_b", [128, RWW])
    w_hg_o = dt("w_hg_o", [HGW, D])
    w_rw_o = dt("w_rw_o", [RWW, D])
    w_o = dt("w_o", [D, D])
    fg_b = dt("fg_b", [128, D])
    cm_d = dt("cm", [128, NCM, 128])
    cst_d = dt("cst", [128, 8])
    out = dt("out", [SEQ, D], kind="ExternalOutput")
    HGtok = dt("HGtok", [NT, 5120], SCR)
    RWfm = dt("RWfm", [NCOLS - 5120, NT], SCR)
    OFs = dt("OFs", [SEQ, HGW], SCR)
    YH = dt("YH", [HGW, SEQ], SCR)
    AL = [dt("AL%d" % d, [RWW, NT], SCR) for d in range(2)]
    BE = [dt("BE%d" % d, [RWW, NT], SCR) for d in range(2)]
    KA = [dt("KA%d" % d, [RWW, NT], SCR) for d in range(2)]
    RH = [dt("RH%d" % d, [RWW, NT], SCR) for d in range(2)]
    BEt = [dt("BEt%d" % d, [NT, RWW], SCR) for d in range(2)]
    KAt = [dt("KAt%d" % d, [NT, RWW], SCR) for d in range(2)]
    Vt = dt("Vt", [NT, RWW], SCR)
    BON = dt("BON", [RWW, SEQ], SCR)
    YF = dt("YF", [SEQ, RWW], SCR)
    YR = dt("YR", [RWW, SEQ], SCR)
    scr_bufs = {}

    def sbuf_of(name):
        if name not in scr_bufs:
            scr_bufs[name] = Buf(name)
        return scr_bufs[name]

    with ExitStack() as st:
        fw = FW(nc, st)
        _FWREF.append(fw)
        _acct = {}

        def sbt(stack, name, shape):
            _acct[id(stack)] = _acct.get(id(stack), 0) + int(np.prod(shape[1:])) * 4
            if os.environ.get("KACCT"):
                print("sbuf", name, shape, "stack total KiB", _acct[id(stack)] / 1024.0, flush=True)
            return stack.enter_context(nc.sbuf_tensor("sb_" + name, shape, F32))
        pst = lambda stack, name, shape: stack.enter_context(nc.psum_tensor("ps_" + name, shape, F32))
        op = fw.op
        cm = sbt(st, "cm", [128, NCM, 128]); b_cm = Buf()
        cst = sbt(st, "cst", [128, 8]); b_cst = Buf()
        modA = sbt(st, "modA", [128, 16, 2]); modB = sbt(st, "modB", [128, 16, 2]); b_mod = Buf()
        gate_b = sbt(st, "gate_b", [128, D]); b_gate = Buf()
        wc = [sbt(st, "wc%d" % d, [128, 8, 18]) for d in range(2)]; b_wc = [Buf(), Buf()]
        fw.dma(cm[:], cm_d, writes=[b_cm])
        fw.dma(cst[:], cst_d, writes=[b_cst])
        ident = cm[:, C_ID, :]
        EPS6, EPS12, EPSGN, ZERO, ONE = (cst[:, i:i + 1] for i in range(5))

        for ph in _phase(0):
            adw = [sbt(ph, "adw%d" % i, [128, 3 * D]) for i in range(2)]; b_adw = [Buf(), Buf()]
            cct = sbt(ph, "cct", [128, 16, 2]); scc = sbt(ph, "scc", [128, 16, 2]); b_cc = Buf(); b_scc = Buf()
            mod = sbt(ph, "mod", [128, 48, 2]); b_modt = Buf()
            adb = sbt(ph, "adb", [128, 48]); ng = sbt(ph, "ng", [128, 16]); b_sm = Buf()
            adbg = sbt(ph, "adbg", [1, D]); grow = sbt(ph, "grow", [1, D]); b_grow = Buf()
            ps_mod = pst(ph, "ps_mod", [128, 96]); b_psm = Buf()
            ps_g = [pst(ph, "ps_g%d" % i, [128, 512]) for i in range(4)]; b_psg = [Buf() for _ in range(4)]
            fw.dma(cct[:], cc_d, writes=[b_cc])
            fw.dma(adb[:], ada_b_fm, writes=[b_sm])
            fw.dma(ng[:], norm_g_fm, writes=[b_sm])
            fw.dma(adbg[:], ada_b_g, writes=[b_sm])
            op("act", lambda e: e.activation(scc[:], cct[:], AF.Silu), reads=[b_cc], writes=[b_scc])
            op("dve", lambda e: e.memset(mod[:], 0.0), writes=[b_modt])
            for k in range(16):
                fw.dma(adw[k % 2][:], ada_w[k], writes=[b_adw[k % 2]])
                for m in range(48):
                    op("pe", lambda e: e.matmul(ps_mod[:, 2 * m:2 * m + 2], adw[k % 2][:, m * 128:(m + 1) * 128],
                                                scc[:, k, :], start=True, stop=True),
                       reads=[b_adw[k % 2], b_scc], writes=[b_psm], inc=(m == 47))
                op("dve", lambda e: e.tensor_tensor(mod[:], mod[:], ps_mod[:].rearrange("p (m v) -> p m v", v=2), ALU.add),
                   reads=[b_psm, b_modt], writes=[b_modt])
                for n in range(4):
                    op("pe", lambda e: e.matmul(ps_g[n][0:1, :], scc[:, k, 0:1], adw[k % 2][:, 2 * D + n * 512:2 * D + (n + 1) * 512],
                                                start=(k == 0), stop=(k == 15)),
                       reads=[b_adw[k % 2], b_scc], writes=[b_psg[n]])
            op("dve", lambda e: e.tensor_tensor(mod[:], mod[:], adb[:].unsqueeze(2).to_broadcast([128, 48, 2]), ALU.add),
               reads=[b_sm, b_modt], writes=[b_modt])
            op("dve", lambda e: e.tensor_scalar(modA[:], mod[:, 16:32, :], 1.0, None, ALU.add), reads=[b_modt], writes=[b_mod])
            op("dve", lambda e: e.tensor_tensor(modA[:], modA[:], ng[:].unsqueeze(2).to_broadcast([128, 16, 2]), ALU.mult),
               reads=[b_sm, b_mod], writes=[b_mod])
            op("dve", lambda e: e.tensor_copy(modB[:], mod[:, 0:16, :]), reads=[b_modt], writes=[b_mod])
            for n in range(4):
                op("dve", lambda e: e.tensor_tensor(grow[:, n * 512:(n + 1) * 512], ps_g[n][0:1, :], adbg[:, n * 512:(n + 1) * 512], ALU.add),
                   reads=[b_psg[n], b_sm], writes=[b_grow])
            for n in range(4):
                op("pe", lambda e: e.matmul(ps_g[n][:], cm[0:1, C_TF, :], grow[:, n * 512:(n + 1) * 512], start=True, stop=True),
                   reads=[b_cm, b_grow], writes=[b_psg[n]])
                op("act", lambda e: e.activation(gate_b[:, n * 512:(n + 1) * 512], ps_g[n][:], AF.Identity, scale=1.0),
                   reads=[b_psg[n]], writes=[b_gate])
        fw.barrier()

        for ph in _phase(1):
            lb_b = sbt(ph, "lb_b", [128, 2, HGW]); oml_b = sbt(ph, "oml_b", [128, 2, HGW])
            b_lb = Buf()
            xt = [sbt(ph, "xt%d" % i, [128, D]) for i in range(2)]; b_xt = [Buf(), Buf()]
            junk = sbt(ph, "junk", [128, D]); b_junk = Buf()
            ss = sbt(ph, "ss", [128, 2]); b_ss = Buf()
            hT = ph.enter_context(nc.sbuf_tensor("sb_hT", [128, 16, 1024], BF16)); b_hT = [Buf() for _ in range(8)]
            wt = [sbt(ph, "wt0", [128, 16, 512])] * 2; b_wt = [Buf()] * 2
            wb16 = [ph.enter_context(nc.sbuf_tensor("sb_wb16_%d" % i, [128, 16, 512], BF16)) for i in range(2)]; b_wb16 = [Buf(), Buf()]
            lbt = wt[1][:, 0:8, :].rearrange("p a b -> p (a b)").rearrange("p (d l c) -> p d l c", d=2, l=2)
            b_lbt = b_wt[1]
            NOT = 4
            ot = [sbt(ph, "ot%d" % i, [128, 512]) for i in range(NOT)]; b_ot = [Buf() for _ in range(NOT)]
            tps = [pst(ph, "tps%d" % i, [128, 512]) for i in range(4)]; b_tps = [Buf() for _ in range(4)]
            acc = [pst(ph, "acc%d" % i, [128, 512]) for i in range(4)]; b_acc = [Buf() for _ in range(4)]
            fw.dma(lbt, hg_lb_b, writes=[b_lbt])
            op("dve", lambda e: e.tensor_tensor(lb_b[:], lbt[:, :, 0, :], lbt[:, :, 1, :], ALU.subtract), reads=[b_lbt], writes=[b_lb])
            op("act", lambda e: e.activation(lb_b[:], lb_b[:], AF.Sigmoid), reads=[b_lb], writes=[b_lb])
            op("dve", lambda e: e.tensor_scalar(oml_b[:], lb_b[:], -1.0, 1.0, ALU.mult, ALU.add), reads=[b_lb], writes=[b_lb])
            w_in_r = w_in.rearrange("(k p) c -> p k c", p=128)
            oti = 0
            ctx_groups = {2, 3, 4, 5, 6, 7, 12, 13, 14, 15, 16}
            tiles = [(0, 256, True)] + [(256 + 1024 * i, 1024, False) for i in range(2)]
            wti = 0
            for (g0, ntok, isctx) in tiles:
                v = 1 if isctx else 0
                nsub = ntok // 128
                for sub in range(nsub):
                    xb = sub % 2
                    fw.dma(xt[xb][:], xc[g0 + sub * 128:g0 + (sub + 1) * 128, :], writes=[b_xt[xb]])
                    op("act", lambda e: e.activation(junk[:], xt[xb][:], AF.Square, accum_out=ss[:, 0:1]),
                       reads=[b_xt[xb]], writes=[b_junk, b_ss])
                    op("act", lambda e: e.activation(ss[:, 1:2], ss[:, 0:1], AF.Sqrt, bias=EPS6, scale=1.0 / D),
                       reads=[b_ss, b_cst], writes=[b_ss])
                    op("dve", lambda e: e.reciprocal(ss[:, 1:2], ss[:, 1:2]), reads=[b_ss], writes=[b_ss])
                    op("act", lambda e: e.activation(junk[:], xt[xb][:], AF.Identity, scale=ss[:, 1:2], bias=ZERO),
                       reads=[b_xt[xb], b_ss, b_cst], writes=[b_junk])
                    for q in range(4):
                        for i in range(4):
                            j = q * 4 + i
                            op("pe", lambda e: e.transpose(tps[q][:, i * 128:(i + 1) * 128], junk[:, j * 128:(j + 1) * 128], ident),
                               reads=[b_junk, b_cm], writes=[b_tps[q]], inc=(i == 3))
                        for i in range(4):
                            j = q * 4 + i
                            en = "dve" if (i % 2 == 0) else "act"
                            if en == "dve":
                                op("dve", lambda e: e.tensor_scalar(hT[:, j, sub * 128:(sub + 1) * 128], tps[q][:, i * 128:(i + 1) * 128],
                                                                    modA[:, j, v:v + 1], modB[:, j, v:v + 1], ALU.mult, ALU.add),
                                   reads=[b_tps[q], b_mod], writes=[b_hT[sub]])
                            else:
                                op("act", lambda e: e.activation(hT[:, j, sub * 128:(sub + 1) * 128], tps[q][:, i * 128:(i + 1) * 128],
                                                                 AF.Identity, scale=modA[:, j, v:v + 1], bias=modB[:, j, v:v + 1]),
                                   reads=[b_tps[q], b_mod], writes=[b_hT[sub]])
                for g in range(27):
                    if isctx and g not in ctx_groups:
                        continue
                    c0 = g * 512
                    ncol = min(512, NCOLS - c0)
                    wb = wti % 2
                    wti += 1
                    fw.dma3(wt[wb][:, :, :ncol], w_in_r[:, :, c0:c0 + ncol], 16, writes=[b_wt[wb]])
                    op(CASTE, lambda e: e.tensor_copy(wb16[wb][:, :, :ncol], wt[wb][:, :, :ncol]), reads=[b_wt[wb]], writes=[b_wb16[wb]])
                    if c0 < 5120:
                        typ = ["silu", "id", "fg0", "fg1", "silu"][c0 // 1024]
                        for sub in range(nsub):
                            a = sub % 4
                            for k in range(16):
                                op("pe", lambda e: e.matmul(acc[a][:, :ncol], hT[:, k, sub * 128:(sub + 1) * 128], wb16[wb][:, k, :ncol],
                                                            start=(k == 0), stop=(k == 15)),
                                   reads=[b_hT[sub], b_wb16[wb]], writes=[b_acc[a]], inc=(k == 15))
                            o_ = oti % NOT
                            oti += 1
                            if typ == "silu":
                                op("act", lambda e: e.activation(ot[o_][:], acc[a][:], AF.Silu), reads=[b_acc[a]], writes=[b_ot[o_]])
                            elif typ == "id":
                                op("dve", lambda e: e.tensor_copy(ot[o_][:], acc[a][:]), reads=[b_acc[a]], writes=[b_ot[o_]])
                            else:
                                dd = int(typ[2])
                                cc0 = c0 - (2048 + dd * 1024)
                                op("act", lambda e: e.activation(ot[o_][:], acc[a][:], AF.Sigmoid), reads=[b_acc[a]], writes=[b_ot[o_]])
                                op("dve", lambda e: e.tensor_tensor(ot[o_][:], ot[o_][:], oml_b[:, dd, cc0:cc0 + 512], ALU.mult),
                                   reads=[b_ot[o_], b_lb], writes=[b_ot[o_]])
                                op("dve", lambda e: e.tensor_tensor(ot[o_][:], ot[o_][:], lb_b[:, dd, cc0:cc0 + 512], ALU.add),
                                   reads=[b_ot[o_], b_lb], writes=[b_ot[o_]])
                            fw.dma(HGtok[g0 + sub * 128:g0 + (sub + 1) * 128, c0:c0 + 512], ot[o_][:],
                                   reads=[b_ot[o_]], writes=[sbuf_of("HGtok")], q=SQ)
                    else:
                        for m in range(ncol // 128):
                            mc = c0 // 128 + m
                            if isctx and not (48 <= mc <= 65):
                                continue
                            for hf in range((ntok + 511) // 512):
                                nt_ = min(512, ntok - hf * 512)
                                tk0 = hf * 512
                                a = (m * 2 + hf) % 4
                                for k in range(16):
                                    op("pe", lambda e: e.matmul(acc[a][:, :nt_], wb16[wb][:, k, m * 128:(m + 1) * 128], hT[:, k, tk0:tk0 + nt_],
                                                                start=(k == 0), stop=(k == 15)),
                                       reads=b_hT[tk0 // 128:(tk0 + nt_) // 128] + [b_wb16[wb]], writes=[b_acc[a]], inc=(k == 15))
                                o_ = oti % NOT
                                oti += 1
                                if mc <= 65:
                                    op("dve", lambda e: e.tensor_copy(ot[o_][:, :nt_], acc[a][:, :nt_]), reads=[b_acc[a]], writes=[b_ot[o_]])
                                else:
                                    fn = AF.Silu if mc <= 73 else AF.Sigmoid
                                    op("act", lambda e: e.activation(ot[o_][:, :nt_], acc[a][:, :nt_], fn), reads=[b_acc[a]], writes=[b_ot[o_]])
                                fw.dma(RWfm[(mc - 40) * 128:(mc - 39) * 128, g0 + tk0:g0 + tk0 + nt_], ot[o_][:, :nt_],
                                       reads=[b_ot[o_]], writes=[sbuf_of("RWfm")], q=SQ)
        fw.barrier()

        for ph in _phase(2):
            NB = 2
            fgt = [sbt(ph, "fgt%d" % i, [CH, HGW]) for i in range(NB)]
            vtt = [sbt(ph, "vtt%d" % i, [CH, HGW]) for i in range(NB)]
            qtt = [sbt(ph, "qtt%d" % i, [CH, HGW]) for i in range(NB)]
            ztt = [sbt(ph, "ztt%d" % i, [CH, HGW]) for i in range(NB)]
            oft = [sbt(ph, "oft%d" % i, [CH, HGW]) for i in range(NB)]
            b_ld = [[Buf() for _ in range(5)] for _ in range(NB)]
            gt = sbt(ph, "gt", [CH, HGW]); b_gt = Buf()
            Et = sbt(ph, "Et", [CH, HGW]); Ei = sbt(ph, "Ei", [CH, HGW]); b_E = Buf(); b_Ei = Buf()
            kt_ = sbt(ph, "kt", [128, HGW]); qq_ = sbt(ph, "qq", [128, HGW]); b_kt = Buf(); b_qq = Buf()
            kt = kt_[0:CH, :]; qq = qq_[0:CH, :]
            ob = sbt(ph, "ob", [CH, HGW]); sq_ = sbt(ph, "sq", [128, HGW]); b_ob = Buf(); b_sq = Buf()
            sq = sq_[0:CH, :]
            ms = sbt(ph, "ms", [CH, 8]); b_ms = Buf()
            eb = sbt(ph, "eb", [128, 8]); b_eb = Buf()
            hng = sbt(ph, "hng", [CH, HGW]); b_hng = Buf()
            qkT = [sbt(ph, "qkT%d" % i, [128, 2, CH]) for i in range(2)]; b_qkT = [Buf(), Buf()]
            at = [sbt(ph, "at%d" % i, [CH, CH]) for i in range(2)]; b_at = [Buf(), Buf()]
            S = [sbt(ph, "S%d" % i, [128, 8, 128]) for i in range(2)]; b_S = [Buf(), Buf()]
            stmp = sbt(ph, "stmp", [128, 8, 128]); b_stmp = Buf()
            yT = sbt(ph, "yTs", [128, 8, 512]); b_yT = Buf()
            bc_ps = [pst(ph, "bc_ps%d" % i, [128, 512]) for i in range(2)]; b_bc = [Buf(), Buf()]
            o_ps = [pst(ph, "o_ps%d" % i, [128, 512]) for i in range(2)]; b_o = [Buf(), Buf()]
            dS_ps = [pst(ph, "dS_ps%d" % i, [128, 512]) for i in range(2)]; b_dS = [Buf(), Buf()]
            mz = pst(ph, "mz", [128, 512]); b_tp = [Buf()] * 2; b_aps = [Buf()] * 2; b_ebp = b_aps[0]
            yT_ps = pst(ph, "yT_ps", [128, 512]); b_yTp = Buf()
            fw.dma(hng[:], hgng_b[0:CH, :], writes=[b_hng])
            op("dve", lambda e: e.memset(kt_[:], 0.0), writes=[b_kt])
            op("dve", lambda e: e.memset(qq_[:], 0.0), writes=[b_qq])
            op("dve", lambda e: e.memset(sq_[:], 0.0), writes=[b_sq])
            ci = 0
            for d in range(min(2, KD)):
                tri = cm[0:CH, C_TF if d == 0 else C_TB, 0:CH]
                lastc = CH - 1 if d == 0 else 0
                onehot = cm[0:CH, C_ID, lastc:lastc + 1]
                order = list(range(NT // CH)) if d == 0 else list(range(CTX // CH - 1, -1, -1)) + list(range(NT // CH - 1, CTX // CH - 1, -1))
                cur = 0
                op("dve", lambda e: e.memset(S[0][:], 0.0), writes=[b_S[0]])
                for c in order[:KCH]:
                    isctx = c < CTX // CH
                    t0 = c * CH
                    lb_ = ci % NB
                    ci += 1
                    fgs, vs, qs, zs, ofs = fgt[lb_], vtt[lb_], qtt[lb_], ztt[lb_], oft[lb_]
                    bl = b_ld[lb_]
                    fw.dma(fgs[:], HGtok[t0:t0 + CH, 2048 + d * 1024:3072 + d * 1024], reads=[sbuf_of("HGtok")], writes=[bl[0]])
                    fw.dma(vs[:], HGtok[t0:t0 + CH, 1024:2048], reads=[sbuf_of("HGtok")], writes=[bl[1]])
                    if not isctx:
                        fw.dma(qs[:], HGtok[t0:t0 + CH, 0:1024], reads=[sbuf_of("HGtok")], writes=[bl[2]])
                        if d == 1:
                            fw.dma(zs[:], HGtok[t0:t0 + CH, 4096:5120], reads=[sbuf_of("HGtok")], writes=[bl[3]])
                            fw.dma(ofs[:], OFs[t0 - CTX:t0 - CTX + CH, :], reads=[sbuf_of("OFs")], writes=[bl[4]])
                    op("act", lambda e: e.activation(gt[:], fgs[:], AF.Ln), reads=[bl[0]], writes=[b_gt])
                    for n in range(2):
                        op("pe", lambda e: e.matmul(bc_ps[n][0:CH, :], tri, gt[:, n * 512:(n + 1) * 512], start=True, stop=True),
                           reads=[b_gt, b_cm], writes=[b_bc[n]])
                    for n in range(2):
                        sl = slice(n * 512, (n + 1) * 512)
                        op("act", lambda e: e.activation(Ei[:, sl], bc_ps[n][0:CH, :], AF.Exp, scale=-1.0), reads=[b_bc[n]], writes=[b_Ei])
                        op("act", lambda e: e.activation(Et[:, sl], bc_ps[n][0:CH, :], AF.Exp), reads=[b_bc[n]], writes=[b_E])
                    op("dve", lambda e: e.tensor_scalar(kt[:], fgs[:], -1.0, 1.0, ALU.mult, ALU.add), reads=[bl[0]], writes=[b_kt])
                    op("dve", lambda e: e.tensor_tensor(kt[:], kt[:], Ei[:], ALU.mult), reads=[b_kt, b_Ei], writes=[b_kt])
                    if not isctx:
                        op("dve", lambda e: e.tensor_tensor(qq[:], qs[:], Et[:], ALU.mult), reads=[bl[2], b_E], writes=[b_qq])
                    for h in range(8):
                        op("pe", lambda e: e.matmul(yT_ps[:, 400 + h:401 + h], Et[:, h * 128:(h + 1) * 128], onehot, start=True, stop=True),
                           reads=[b_E, b_cm], writes=[b_ebp], inc=(h == 7))
                    op("dve", lambda e: e.tensor_copy(eb[:], yT_ps[:, 400:408]), reads=[b_ebp], writes=[b_eb])
                    nxt = 1 - cur
                    for h in range(8):
                        hs = slice(h * 128, (h + 1) * 128)
                        pp = h % 2
                        if not isctx:
                            tpv = mz[:, pp * 256:(pp + 1) * 256]
                        if (not isctx) and not (KX & 1):
                            op("pe", lambda e: e.transpose(tpv[:, 0:128], qq_[:, hs], ident), reads=[b_qq, b_cm], writes=[b_tp[pp]], inc=False)
                            op("pe", lambda e: e.transpose(tpv[:, 128:256], kt_[:, hs], ident), reads=[b_kt, b_cm], writes=[b_tp[pp]])
                            op("dve", lambda e: e.tensor_copy(qkT[pp][:], tpv.rearrange("p (a b) -> p a b", b=128)[:, :, 0:CH]),
                               reads=[b_tp[pp]], writes=[b_qkT[pp]])
                            apv = yT_ps[0:CH, 256 + pp * 64:256 + pp * 64 + CH]
                        if (not isctx) and not (KX & 2):
                            op("pe", lambda e: e.matmul(apv, qkT[pp][:, 1, :], qkT[pp][:, 0, :], start=True, stop=True),
                               reads=[b_qkT[pp]], writes=[b_aps[pp]])
                            op("dve", lambda e: e.tensor_tensor(at[pp][:], apv, tri, ALU.mult), reads=[b_aps[pp], b_cm], writes=[b_at[pp]])
                            opv = o_ps[h // 4][0:CH, (h % 4) * 128:(h % 4 + 1) * 128]
                        if (not isctx) and not (KX & 4):
                            op("pe", lambda e: e.matmul(opv, at[pp][:], vs[:, hs], start=True, stop=False),
                               reads=[b_at[pp], bl[1]], writes=[b_o[h // 4]], inc=False)
                            op("pe", lambda e: e.matmul(opv, qkT[pp][:, 0, :], S[cur][:, h, :], start=False, stop=True),
                               reads=[b_qkT[pp], b_S[cur]], writes=[b_o[h // 4]])
                        dsv = dS_ps[h // 4][:, (h % 4) * 128:(h % 4 + 1) * 128]
                        op("pe", lambda e: e.matmul(dsv, kt[:, hs], vs[:, hs], start=True, stop=True),
                           reads=[b_kt, bl[1]], writes=[b_dS[h // 4]])
                    for n in range(2):
                        op("dve", lambda e: e.tensor_tensor(stmp[:, n * 4:(n + 1) * 4, :], dS_ps[n][:].rearrange("p (h v) -> p h v", v=128),
                                                            S[cur][:, n * 4:(n + 1) * 4, :], ALU.add),
                           reads=[b_dS[n], b_S[cur]], writes=[b_stmp])
                    op("dve", lambda e: e.tensor_tensor(S[nxt][:], stmp[:], eb[:].unsqueeze(2).to_broadcast([128, 8, 128]), ALU.mult),
                       reads=[b_stmp, b_eb], writes=[b_S[nxt]])
                    cur = nxt
                    if isctx or (KX & 8):
                        continue
                    xt0 = t0 - 256
                    if d == 0:
                        for n in range(2):
                            op("dve", lambda e: e.tensor_copy(ob[:, n * 512:(n + 1) * 512], o_ps[n][0:CH, :]),
                               reads=[b_o[n]], writes=[b_ob])
                        fw.dma(OFs[xt0:xt0 + CH, :], ob[:], reads=[b_ob], writes=[sbuf_of("OFs")], q=SQ)
                    else:
                        for n in range(2):
                            op("dve", lambda e: e.tensor_tensor(ob[:, n * 512:(n + 1) * 512], o_ps[n][0:CH, :], ofs[:, n * 512:(n + 1) * 512], ALU.add),
                               reads=[b_o[n], bl[4]], writes=[b_ob])
                        op("dve", lambda e: e.tensor_tensor(sq[:], ob[:], ob[:], ALU.mult), reads=[b_ob], writes=[b_sq])
                        op("dve", lambda e: e.tensor_reduce(ms[:], sq[:].rearrange("p (h v) -> p h v", v=128), AX.X, ALU.add), reads=[b_sq], writes=[b_ms])
                        op("act", lambda e: e.activation(ms[:], ms[:], AF.Sqrt, bias=cst[0:CH, 0:1], scale=1.0 / 128), reads=[b_ms, b_cst], writes=[b_ms])
                        op("dve", lambda e: e.reciprocal(ms[:], ms[:]), reads=[b_ms], writes=[b_ms])
                        op("dve", lambda e: e.tensor_tensor(sq[:].rearrange("p (h v) -> p h v", v=128), ob[:].rearrange("p (h v) -> p h v", v=128),
                                                            ms[:].unsqueeze(2).to_broadcast([CH, 8, 128]), ALU.mult),
                           reads=[b_ob, b_ms], writes=[b_sq])
                        op("dve", lambda e: e.tensor_tensor(sq[:], sq[:], hng[:], ALU.mult), reads=[b_sq, b_hng], writes=[b_sq])
                        op("dve", lambda e: e.tensor_tensor(sq[:], sq[:], zs[:], ALU.mult), reads=[b_sq, bl[3]], writes=[b_sq])
                        for h in range(8):
                            op("pe", lambda e: e.transpose(bc_ps[h // 4][:, (h % 4) * 128:(h % 4 + 1) * 128], sq_[:, h * 128:(h + 1) * 128], ident),
                               reads=[b_sq, b_cm], writes=[b_bc[h // 4]], inc=(h % 4 == 3))
                        for n in range(2):
                            yo_ = xt0 % 512
                            op("dve", lambda e: e.tensor_copy(yT[:, n * 4:(n + 1) * 4, yo_:yo_ + CH], bc_ps[n][:].rearrange("p (h t) -> p h t", t=128)[:, :, 0:CH]),
                               reads=[b_bc[n]], writes=[b_yT])
                        if xt0 % 512 == 0:
                            fw.dma3(YH.rearrange("(h p) t -> p h t", p=128)[:, :, xt0:xt0 + 512], yT[:], 8, reads=[b_yT], writes=[sbuf_of("YH")], q=SQ)
        fw.barrier()

        for ph in _phase(3):
            W = 640
            mu = sbt(ph, "mu", [128, 26, 4]); omu = sbt(ph, "omu", [128, 26]); b_mu = Buf()
            w0t = sbt(ph, "w0t", [128, 8, 2]); a0t = sbt(ph, "a0t", [128, 8, 2]); kvt = sbt(ph, "kvt", [128, 8, 3]); omka = sbt(ph, "omka", [128, 8])
            w2t = sbt(ph, "w2t", [128, RWW]); a2t = sbt(ph, "a2t", [128, RWW]); b_par = Buf()
            raw = [sbt(ph, "raw%d" % i, [128, W]) for i in range(3)]; b_raw = [Buf() for _ in range(3)]
            rawl = [sbt(ph, "rawl%d" % i, [128, W]) for i in range(2)]; b_rawl = [Buf(), Buf()]
            sh = [sbt(ph, "sh%d" % i, [128, 512]) for i in range(3)]; b_sh = [Buf() for _ in range(3)]
            tw = sbt(ph, "tw", [128, 512]); als = sbt(ph, "als", [128, 512]); b_tw = Buf(); b_als = Buf()
            kkr = sbt(ph, "kkr", [128, 512]); kk = sbt(ph, "kk", [128, 512]); t1 = sbt(ph, "t1", [128, 512]); b_kkr = Buf(); b_kk = Buf(); b_t1 = Buf()
            lgw = sbt(ph, "lgw", [128, 512]); av = sbt(ph, "av", [128, 512]); b_lgw = Buf(); b_av = Buf()
            kd = [sbt(ph, "kd%d" % i, [128, 512]) for i in range(2)]; b_kd = [Buf(), Buf()]
            bv = sbt(ph, "bv", [128, 512]); b_bv = Buf()
            lt = sbt(ph, "lt", [128, 128]); b_lt = Buf()
            Ecw = sbt(ph, "Ecw", [128, 512]); Einv = sbt(ph, "Einv", [128, 512]); Eex = sbt(ph, "Eex", [128, 512]); b_Ecw = Buf(); b_Einv = Buf(); b_Eex = Buf()
            res = [sbt(ph, "res%d" % i, [128, 512]) for i in range(4)]; b_res = [Buf() for _ in range(4)]
            tm = [[sbt(ph, "tm%d_%d" % (a_, s_), [128, RWW]) for s_ in range(4)] for a_ in range(5)]; b_tm = [[Buf() for _ in range(4)] for _ in range(5)]
            p_a = pst(ph, "p_a", [128, 512]); p_b = pst(ph, "p_b", [128, 512]); p_cw = pst(ph, "p_cw", [128, 512]); p_tq = [pst(ph, "p_t%d" % i, [128, 512]) for i in range(4)]
            p_s = pst(ph, "p_s", [128, 512])
            b_pa = Buf(); b_pb = Buf(); b_pcw = Buf(); b_pt = [Buf() for _ in range(4)]; b_ps = Buf()
            for (tl, src) in [(mu, mu_fm), (w0t, w0_fm), (a0t, a0_fm), (kvt, kvec_fm), (w2t, w2_d), (a2t, a2_d)]:
                fw.dma(tl[:], src, writes=[b_par if tl is not mu else b_mu])
            op("dve", lambda e: e.tensor_reduce(omu[:], mu[:], AX.X, ALU.add), reads=[b_mu], writes=[b_mu])
            op("dve", lambda e: e.tensor_scalar(omu[:], omu[:], -1.0, 1.0, ALU.mult, ALU.add), reads=[b_mu], writes=[b_mu])
            op("dve", lambda e: e.tensor_scalar(omka[:], kvt[:, :, 1], -1.0, 1.0, ALU.mult, ALU.add), reads=[b_par], writes=[b_par])
            omuc = sbt(ph, "omuc", [128, 26])
            op("dve", lambda e: e.tensor_tensor(omuc[:], mu[:, :, 0], mu[:, :, 1], ALU.add), reads=[b_mu], writes=[b_mu])
            op("dve", lambda e: e.tensor_scalar(omuc[:], omuc[:], -1.0, 1.0, ALU.mult, ALU.add), reads=[b_mu], writes=[b_mu])

            def shift(dst, b_dst, rawt, b_rawt, chunk, g0, ntok, isctx, eng="dve"):
                rows = RWfm[chunk * 128:(chunk + 1) * 128, :]
                if isctx:
                    fw.dma(rawt[:, 0:256], rows[:, 0:256], reads=[sbuf_of("RWfm")], writes=[b_rawt])
                    P = rawt[:, 0:256]
                    op(eng, lambda e: e.tensor_scalar(dst[:, 0:256], P, omuc[:, chunk:chunk + 1], None, ALU.mult), reads=[b_rawt, b_mu], writes=[b_dst])
                    op(eng, lambda e: e.scalar_tensor_tensor(dst[:, 1:256], P[:, 0:255], mu[:, chunk, 0:1], dst[:, 1:256], ALU.mult, ALU.add),
                       reads=[b_rawt, b_mu, b_dst], writes=[b_dst])
                    op(eng, lambda e: e.scalar_tensor_tensor(dst[:, 0:255], P[:, 1:256], mu[:, chunk, 1:2], dst[:, 0:255], ALU.mult, ALU.add),
                       reads=[b_rawt, b_mu, b_dst], writes=[b_dst])
                    return
                lo = g0 - 64
                hi = g0 + ntok + 64
                first = (g0 == CTX)
                last = (g0 + ntok == NT)
                if first:
                    op(eng, lambda e: e.memset(rawt[:, 0:64], 0.0), writes=[b_rawt])
                if last:
                    op(eng, lambda e: e.memset(rawt[:, W - 64:W], 0.0), writes=[b_rawt])
                a_ = 64 if first else 0
                b_ = W - 64 if last else W
                fw.dma(rawt[:, a_:b_], rows[:, lo + a_:lo + b_], reads=[sbuf_of("RWfm")], writes=[b_rawt])
                P3 = rawt[:].rearrange("p (r c) -> p r c", c=64)
                Pc = P3[:, 1:9, :]
                d3 = dst[:].rearrange("p (r c) -> p r c", c=64)
                op(eng, lambda e: e.tensor_scalar(d3, Pc, omu[:, chunk:chunk + 1], None, ALU.mult), reads=[b_rawt, b_mu], writes=[b_dst])
                op(eng, lambda e: e.scalar_tensor_tensor(d3[:, :, 1:], Pc[:, :, :-1], mu[:, chunk, 0:1], d3[:, :, 1:], ALU.mult, ALU.add),
                   reads=[b_rawt, b_mu, b_dst], writes=[b_dst])
                op(eng, lambda e: e.scalar_tensor_tensor(d3[:, :, :-1], Pc[:, :, 1:], mu[:, chunk, 1:2], d3[:, :, :-1], ALU.mult, ALU.add),
                   reads=[b_rawt, b_mu, b_dst], writes=[b_dst])
                op(eng, lambda e: e.scalar_tensor_tensor(d3, P3[:, 0:8, :], mu[:, chunk, 2:3], d3, ALU.mult, ALU.add),
                   reads=[b_rawt, b_mu, b_dst], writes=[b_dst])
                op(eng, lambda e: e.scalar_tensor_tensor(d3, P3[:, 2:10, :], mu[:, chunk, 3:4], d3, ALU.mult, ALU.add),
                   reads=[b_rawt, b_mu, b_dst], writes=[b_dst])

            tiles = [(0, 256, True)] + [(256 + 512 * i, 512, False) for i in range(4)]
            ri = 0
            for (g0, ntok, isctx) in tiles[KT0:KT]:
                nsub = ntok // 128
                blk0 = g0 // 128
                N_ = slice(0, ntok)
                shift(tw, b_tw, rawl[0], b_rawl[0], 24, g0, ntok, isctx)
                if not (KX & 32):
                    op("act", lambda e: e.activation(tw[:, N_], tw[:, N_], AF.Tanh), reads=[b_tw], writes=[b_tw])
                shift(als, b_als, rawl[1], b_rawl[1], 25, g0, ntok, isctx)
                for j in range(KJ):
                    if not isctx:
                        shift(sh[0], b_sh[0], raw[0], b_raw[0], j, g0, ntok, isctx)
                    shift(sh[1], b_sh[1], raw[1], b_raw[1], 8 + j, g0, ntok, isctx)
                    shift(sh[2], b_sh[2], raw[2], b_raw[2], 16 + j, g0, ntok, isctx)
                    rs, ks, vs = sh[0], sh[1], sh[2]
                    op("dve", lambda e: e.tensor_scalar(kkr[:, N_], ks[:, N_], kvt[:, j, 0:1], None, ALU.mult), reads=[b_sh[1], b_par], writes=[b_kkr])
                    op("dve", lambda e: e.tensor_tensor(t1[:, N_], kkr[:, N_], kkr[:, N_], ALU.mult), reads=[b_kkr], writes=[b_t1])
                    op("pe", lambda e: e.matmul(p_a[:, N_], cm[:, C_BONES, :], t1[:, N_], start=True, stop=True), reads=[b_cm, b_t1], writes=[b_pa])
                    op("act", lambda e: e.activation(t1[:, N_], p_a[:, N_], AF.Sqrt, bias=EPS12, scale=1.0), reads=[b_pa, b_cst], writes=[b_t1])
                    op("dve", lambda e: e.reciprocal(t1[:, N_], t1[:, N_]), reads=[b_t1], writes=[b_t1])
                    op("dve", lambda e: e.tensor_tensor(kk[:, N_], kkr[:, N_], t1[:, N_], ALU.mult), reads=[b_kkr, b_t1], writes=[b_kk])
                    for sub in range(nsub):
                        q_ = sub % 4
                        op("pe", lambda e: e.transpose(p_tq[q_][:, 0:128], vs[:, sub * 128:(sub + 1) * 128], ident),
                           reads=[b_sh[2], b_cm], writes=[b_pt[q_]])
                        op("dve", lambda e: e.tensor_copy(tm[0][sub][:, j * 128:(j + 1) * 128], p_tq[q_][:, 0:128]), reads=[b_pt[q_]], writes=[b_tm[0][sub]])
                    for d in range(2):
                        ds = slice(d * 64, (d + 1) * 64)
                        js = slice(j * 128, (j + 1) * 128)
                        op("pe", lambda e: e.matmul(p_a[:, N_], w2t[ds, js], tw[ds, N_], start=True, stop=True), reads=[b_par, b_tw], writes=[b_pa])
                        op("act", lambda e: e.activation(lgw[:, N_], p_a[:, N_], AF.Sigmoid, bias=w0t[:, j, d:d + 1], scale=1.0), reads=[b_pa, b_par], writes=[b_lgw])
                        op("dve", lambda e: e.tensor_scalar(lgw[:, N_], lgw[:, N_], -0.6065306597126334, None, ALU.mult), reads=[b_lgw], writes=[b_lgw])
                        op("pe", lambda e: e.matmul(p_b[:, N_], a2t[ds, js], als[ds, N_], start=True, stop=True), reads=[b_par, b_als], writes=[b_pb])
                        op("act", lambda e: e.activation(av[:, N_], p_b[:, N_], AF.Sigmoid, bias=a0t[:, j, d:d + 1], scale=1.0), reads=[b_pb, b_par], writes=[b_av])
                        op("dve", lambda e: e.tensor_scalar(t1[:, N_], av[:, N_], kvt[:, j, 1:2], omka[:, j:j + 1], ALU.mult, ALU.add), reads=[b_av, b_par], writes=[b_t1])
                        op("dve", lambda e: e.tensor_tensor(kd[d][:, N_], t1[:, N_], ks[:, N_], ALU.mult), reads=[b_t1, b_sh[1]], writes=[b_kd[d]])
                        op("dve", lambda e: e.tensor_tensor(bv[:, N_], kk[:, N_], av[:, N_], ALU.mult), reads=[b_kk, b_av], writes=[b_bv])
                        tri = cm[:, C_TF if d == 0 else C_TB, :]
                        for sub in range(nsub):
                            ss_ = slice(sub * 128, (sub + 1) * 128)
                            q_ = sub % 4
                            op("pe", lambda e: e.transpose(p_tq[q_][:, 0:128], lgw[:, ss_], ident), reads=[b_lgw, b_cm], writes=[b_pt[q_]])
                            op("dve", lambda e: e.tensor_copy(lt[:], p_tq[q_][:, 0:128]), reads=[b_pt[q_]], writes=[b_lt])
                            op("pe", lambda e: e.matmul(p_cw[:, ss_], lt[:], tri, start=True, stop=True), reads=[b_lt, b_cm], writes=[b_pcw])
                        op("act", lambda e: e.activation(Ecw[:, N_], p_cw[:, N_], AF.Exp), reads=[b_pcw], writes=[b_Ecw])
                        op("act", lambda e: e.activation(Einv[:, N_], p_cw[:, N_], AF.Exp, scale=-1.0), reads=[b_pcw], writes=[b_Einv])
                        op("dve", lambda e: e.tensor_tensor(t1[:, N_], p_cw[:, N_], lgw[:, N_], ALU.subtract), reads=[b_pcw, b_lgw], writes=[b_t1])
                        op("act", lambda e: e.activation(Eex[:, N_], t1[:, N_], AF.Exp), reads=[b_t1], writes=[b_Eex])
                        lastc = 127 if d == 0 else 0
                        for sub in range(nsub):
                            op("dve", lambda e: e.tensor_copy(wc[d][:, j, blk0 + sub:blk0 + sub + 1], Ecw[:, sub * 128 + lastc:sub * 128 + lastc + 1]),
                               reads=[b_Ecw], writes=[b_wc[d]])
                        prods = [(kk, b_kk, Eex, b_Eex, AL[d], "AL%d" % d), (bv, b_bv, Einv, b_Einv, BE[d], "BE%d" % d),
                                 (kd[d], b_kd[d], Einv, b_Einv, KA[d], "KA%d" % d)]
                        if not isctx:
                            prods.append((rs, b_sh[0], Ecw, b_Ecw, RH[d], "RH%d" % d))
                        for pi_, (x_, bx_, y_, by_, dst, nm) in enumerate(prods):
                            op("dve", lambda e: e.tensor_tensor(res[pi_][:, N_], x_[:, N_], y_[:, N_], ALU.mult), reads=[bx_, by_], writes=[b_res[pi_]])
                            if not (KX & 64):
                                fw.dma(dst[js, g0:g0 + ntok], res[pi_][:, N_], reads=[b_res[pi_]], writes=[sbuf_of(nm)], q=SQ)
                            if pi_ in (1, 2):
                                dstT = BEt[d] if pi_ == 1 else KAt[d]
                                nmT = ("BEt%d" if pi_ == 1 else "KAt%d") % d
                                for sub in range(nsub):
                                    q_ = sub % 4
                                    srcT = kk if (KX & 128) else res[pi_]
                                    op("pe", lambda e: e.transpose(p_tq[q_][:, 0:128], srcT[:, sub * 128:(sub + 1) * 128], ident),
                                       reads=[b_res[pi_], b_cm], writes=[b_pt[q_]])
                                    ta = 1 + 2 * d + (pi_ - 1)
                                    op("dve", lambda e: e.tensor_copy(tm[ta][sub][:, js], p_tq[q_][:, 0:128]), reads=[b_pt[q_]], writes=[b_tm[ta][sub]])
                    if not isctx:
                        op("dve", lambda e: e.tensor_tensor(t1[:], kd[0][:], kd[1][:], ALU.add), reads=[b_kd[0], b_kd[1]], writes=[b_t1])
                        op("dve", lambda e: e.scalar_tensor_tensor(t1[:], t1[:], kvt[:, j, 2:3], rs[:], ALU.mult, ALU.mult), reads=[b_t1, b_par, b_sh[0]], writes=[b_t1])
                        op("pe", lambda e: e.matmul(p_s[:], cm[:, C_BONES, :], t1[:], start=True, stop=True), reads=[b_cm, b_t1], writes=[b_ps])
                        op("dve", lambda e: e.tensor_tensor(res[3][:], p_s[:], vs[:], ALU.mult), reads=[b_ps, b_sh[2]], writes=[b_res[3]])
                        fw.dma(BON[j * 128:(j + 1) * 128, g0 - CTX:g0 - CTX + 512], res[3][:], reads=[b_res[3]], writes=[sbuf_of("BON")], q=SQ)
                for sub in range(nsub if not (KX & 16) else 0):
                    rows = slice(g0 + sub * 128, g0 + (sub + 1) * 128)
                    for ta, (dstT, nmT) in enumerate([(Vt, "Vt"), (BEt[0], "BEt0"), (KAt[0], "KAt0"), (BEt[1], "BEt1"), (KAt[1], "KAt1")]):
                        fw.dma(dstT[rows, :], tm[ta][sub][:], reads=[b_tm[ta][sub]], writes=[sbuf_of(nmT)], q=SQ)
        fw.barrier()

        for ph in _phase(4):
            J = 8
            aT = [sbt(ph, "aT%d" % j, [128, 128]) for j in range(J)]
            bT = [sbt(ph, "bT%d" % j, [128, 128]) for j in range(J)]
            kT = [sbt(ph, "kT%d" % j, [128, 128]) for j in range(J)]
            rT = [sbt(ph, "rT%d" % j, [128, 128]) for j in range(J)]
            btkA = sbt(ph, "btkA", [128, RWW]); ktkA = sbt(ph, "ktkA", [128, RWW]); vtkA = sbt(ph, "vtkA", [128, RWW])
            btk = [btkA[:, j * 128:(j + 1) * 128] for j in range(J)]
            ktk = [ktkA[:, j * 128:(j + 1) * 128] for j in range(J)]
            vtk = [vtkA[:, j * 128:(j + 1) * 128] for j in range(J)]
            b_tokA = [Buf(), Buf(), Buf()]
            b_in = [[Buf() for _ in range(7)] for _ in range(J)]
            Pm = [[sbt(ph, "Pm%d_%d" % (j, i), [128, 2, 128]) for i in range(2)] for j in range(J)]
            PTm = [[sbt(ph, "PTm%d_%d" % (j, i), [128, 2, 128]) for i in range(2)] for j in range(J)]
            TTm = [[sbt(ph, "TTm%d_%d" % (j, i), [128, 2, 128]) for i in range(2)] for j in range(J)]
            b_P = [[Buf(), Buf()] for _ in range(J)]; b_PT = [[Buf(), Buf()] for _ in range(J)]; b_TT = [[Buf(), Buf()] for _ in range(J)]
            AkT = [sbt(ph, "AkT%d" % j, [128, 2, 128]) for j in range(J)]
            BbT = [sbt(ph, "BbT%d" % j, [128, 2, 128]) for j in range(J)]
            BkT = [sbt(ph, "BkT%d" % j, [128, 2, 128]) for j in range(J)]
            b_AkT = [Buf() for _ in range(J)]; b_BbT = [Buf() for _ in range(J)]; b_BkT = [Buf() for _ in range(J)]
            St = [[sbt(ph, "St%d_%d" % (j, i), [128, 64]) for i in range(2)] for j in range(J)]
            b_St = [[Buf(), Buf()] for _ in range(J)]
            Rn = [sbt(ph, "Rn%d" % i, [128, 2, 64]) for i in range(2)]; b_Rn = [Buf(), Buf()]
            Us = [sbt(ph, "Us%d" % i, [128, 2, 64]) for i in range(2)]; b_Us = [Buf(), Buf()]
            stt_ = [sbt(ph, "stt%d" % i, [128, 64]) for i in range(2)]; b_stt = [Buf(), Buf()]
            Ys = [sbt(ph, "Ys%d" % i, [128, 2, 64]) for i in range(2)]; b_Ys = [Buf(), Buf()]
            Yf = [sbt(ph, "Yf%d" % i, [128, 2, 64]) for i in range(2)]; b_Yf = [Buf(), Buf()]
            cen = [sbt(ph, "cen%d" % i, [128, 2, 64]) for i in range(2)]; b_cen = [Buf(), Buf()]
            gsq = [sbt(ph, "gsq%d" % i, [128, 2, 64]) for i in range(2)]; b_gsq = [Buf(), Buf()]
            gst = [sbt(ph, "gst%d" % i, [128, 4]) for i in range(2)]; b_gst = [Buf(), Buf()]
            bon = [sbt(ph, "bon%d" % i, [128, 128]) for i in range(2)]; zr = [sbt(ph, "zr%d" % i, [128, 128]) for i in range(2)]
            b_bon = [Buf(), Buf()]; b_zr = [Buf(), Buf()]
            yo = [sbt(ph, "yo%d" % i, [128, 128]) for i in range(2)]; b_yo = [Buf(), Buf()]
            gng = sbt(ph, "gng", [128, RWW]); gnb = sbt(ph, "gnb", [128, RWW]); b_gn = Buf()
            fw.dma(gng[:], gng_b, writes=[b_gn]); fw.dma(gnb[:], gnb_b, writes=[b_gn])
            G = [pst(ph, "G%d" % i, [128, 1024]) for i in range(4)]
            b_G = [[Buf(), Buf()] for _ in range(4)]

            def gv(g, q):
                return G[g][:].rearrange("p (h q s) -> p h q s", h=2, q=4)[:, :, q, :]

            def gs(g, q):
                return G[g][:].rearrange("p (h c) -> p h c", h=2)[:, :, q * 64:(q + 1) * 64]

            KB = int(os.environ.get("KB", "99"))
            for d in range(min(2, KD)):
                m_lo = cm[:, C_GT if d == 0 else C_LT, :]
                m_up = cm[:, C_LT if d == 0 else C_GT, :]
                m_upi = cm[:, C_TF if d == 0 else C_TB, :]
                bc3 = lambda m: m.unsqueeze(1).to_broadcast([128, 2, 128])
                order = list(range(18)) if d == 0 else [1, 0] + list(range(17, 1, -1))
                cur = 0
                for j in range(J):
                    op("dve", lambda e: e.memset(St[j][0][:], 0.0), writes=[b_St[j][0]])
                for blk in order[:KB]:
                    isctx = blk < 2
                    g0 = blk * 128
                    x0 = g0 - CTX
                    tsl = slice(g0, g0 + 128)
                    fw.dma(btkA[:], BEt[d][tsl, :], reads=[sbuf_of("BEt%d" % d)], writes=[b_tokA[0]])
                    fw.dma(ktkA[:], KAt[d][tsl, :], reads=[sbuf_of("KAt%d" % d)], writes=[b_tokA[1]])
                    fw.dma(vtkA[:], Vt[tsl, :], reads=[sbuf_of("Vt")], writes=[b_tokA[2]])
                    for j in range(J):
                        js = slice(j * 128, (j + 1) * 128)
                        bi = b_in[j]
                        fw.dma(aT[j][:], AL[d][js, tsl], reads=[sbuf_of("AL%d" % d)], writes=[bi[0]])
                        fw.dma(bT[j][:], BE[d][js, tsl], reads=[sbuf_of("BE%d" % d)], writes=[bi[1]])
                        fw.dma(kT[j][:], KA[d][js, tsl], reads=[sbuf_of("KA%d" % d)], writes=[bi[2]])
                        if not isctx:
                            fw.dma(rT[j][:], RH[d][js, tsl], reads=[sbuf_of("RH%d" % d)], writes=[bi[3]])
                        bi[4], bi[5], bi[6] = b_tokA
                    for j in range(J):
                        bi = b_in[j]
                        for h in range(2):
                            hs = slice(h * 64, (h + 1) * 64)
                            op("pe", lambda e: e.matmul(gv(0, 0)[:, h, :], aT[j][hs, :], bT[j][hs, :], start=True, stop=True),
                               reads=[bi[0], bi[1]], writes=[b_G[0][h]])
                            op("pe", lambda e: e.matmul(gv(0, 1)[:, h, :], bT[j][hs, :], aT[j][hs, :], start=True, stop=True),
                               reads=[bi[0], bi[1]], writes=[b_G[0][h]])
                            op("pe", lambda e: e.matmul(gv(0, 2)[:, h, :], kT[j][hs, :], aT[j][hs, :], start=True, stop=True),
                               reads=[bi[0], bi[2]], writes=[b_G[0][h]])
                            if not isctx:
                                op("pe", lambda e: e.matmul(gv(0, 3)[:, h, :], bT[j][hs, :], rT[j][hs, :], start=True, stop=True),
                                   reads=[bi[1], bi[3]], writes=[b_G[0][h]])
                                op("pe", lambda e: e.matmul(gv(1, 0)[:, h, :], kT[j][hs, :], rT[j][hs, :], start=True, stop=True),
                                   reads=[bi[2], bi[3]], writes=[b_G[1][h]])
                        op("dve", lambda e: e.scalar_tensor_tensor(Pm[j][0][:], gv(0, 0), -1.0, bc3(m_lo), ALU.mult, ALU.mult),
                           reads=b_G[0] + [b_cm], writes=[b_P[j][0]])
                        op("dve", lambda e: e.scalar_tensor_tensor(PTm[j][0][:], gv(0, 1), -1.0, bc3(m_up), ALU.mult, ALU.mult),
                           reads=b_G[0] + [b_cm], writes=[b_PT[j][0]])
                        op("dve", lambda e: e.tensor_tensor(TTm[j][0][:], PTm[j][0][:], bc3(ident), ALU.add),
                           reads=[b_PT[j][0], b_cm], writes=[b_TT[j][0]])
                        op("dve", lambda e: e.tensor_tensor(AkT[j][:], gv(0, 2), bc3(m_up), ALU.mult),
                           reads=b_G[0] + [b_cm], writes=[b_AkT[j]])
                        if not isctx:
                            op("dve", lambda e: e.tensor_tensor(BbT[j][:], gv(0, 3), bc3(m_upi), ALU.mult),
                               reads=b_G[0] + [b_cm], writes=[b_BbT[j]])
                            op("dve", lambda e: e.tensor_tensor(BkT[j][:], gv(1, 0), bc3(m_upi), ALU.mult),
                               reads=b_G[1] + [b_cm], writes=[b_BkT[j]])
                    for i in range(1, 7):
                        a_, n_ = (i - 1) % 2, i % 2
                        for j in range(J):
                            for h in range(2):
                                op("pe", lambda e: e.matmul(gv(1, 1)[:, h, :], PTm[j][a_][:, h, :], Pm[j][a_][:, h, :], start=True, stop=True),
                                   reads=[b_P[j][a_], b_PT[j][a_]], writes=[b_G[1][h]])
                                if i < 6:
                                    op("pe", lambda e: e.matmul(gv(1, 2)[:, h, :], Pm[j][a_][:, h, :], PTm[j][a_][:, h, :], start=True, stop=True),
                                       reads=[b_P[j][a_], b_PT[j][a_]], writes=[b_G[1][h]])
                            op("dve", lambda e: e.tensor_copy(Pm[j][n_][:], gv(1, 1)), reads=b_G[1], writes=[b_P[j][n_]])
                            if i < 6:
                                op("dve", lambda e: e.tensor_copy(PTm[j][n_][:], gv(1, 2)), reads=b_G[1], writes=[b_PT[j][n_]])
                            for h in range(2):
                                op("pe", lambda e: e.matmul(gv(1, 3)[:, h, :], Pm[j][n_][:, h, :], TTm[j][a_][:, h, :], start=True, stop=True),
                                   reads=[b_P[j][n_], b_TT[j][a_]], writes=[b_G[1][h]])
                            op("dve", lambda e: e.tensor_tensor(TTm[j][n_][:], gv(1, 3), TTm[j][a_][:], ALU.add),
                               reads=b_G[1] + [b_TT[j][a_]], writes=[b_TT[j][n_]])
                    TTf = 0
                    nxt = 1 - cur
                    for j in range(J):
                        pr = j % 2
                        bi = b_in[j]
                        js = slice(j * 128, (j + 1) * 128)
                        Rv, Uv, Yv = gs(2, 0), gs(2, 1), gs(2, 2)
                        for h in range(2):
                            hs = slice(h * 64, (h + 1) * 64)
                            op("pe", lambda e: e.matmul(Rv[:, h, :], aT[j][hs, :], St[j][cur][hs, :], start=True, stop=False),
                               reads=[bi[0], b_St[j][cur]], writes=[b_G[2][h]])
                            op("pe", lambda e: e.matmul(Rv[:, h, :], AkT[j][:, h, :], vtk[j][:, hs], start=False, stop=True),
                               reads=[b_AkT[j], bi[6]], writes=[b_G[2][h]])
                        op("dve", lambda e: e.tensor_scalar(Rn[pr][:], Rv, -1.0, None, ALU.mult), reads=b_G[2], writes=[b_Rn[pr]])
                        for h in range(2):
                            op("pe", lambda e: e.matmul(Uv[:, h, :], TTm[j][TTf][:, h, :], Rn[pr][:, h, :], start=True, stop=True),
                               reads=[b_TT[j][TTf], b_Rn[pr]], writes=[b_G[2][h]])
                        op("dve", lambda e: e.tensor_copy(Us[pr][:], Uv), reads=b_G[2], writes=[b_Us[pr]])
                        SSv = G[3][:, 0:128]
                        op("pe", lambda e: e.matmul(SSv, btk[j], Us[pr][:].rearrange("p h v -> p (h v)"), start=True, stop=False),
                           reads=[bi[4], b_Us[pr]], writes=[b_G[3][0]])
                        op("pe", lambda e: e.matmul(SSv, ktk[j], vtk[j], start=False, stop=True),
                           reads=[bi[5], bi[6]], writes=[b_G[3][0]])
                        for h in range(2):
                            hs = slice(h * 64, (h + 1) * 64)
                            op("dve", lambda e: e.tensor_tensor(stt_[pr][hs, :], SSv[hs, h * 64:(h + 1) * 64], St[j][cur][hs, :], ALU.add),
                               reads=[b_G[3][0], b_St[j][cur]], writes=[b_stt[pr]])
                        op("dve", lambda e: e.tensor_scalar(St[j][nxt][:], stt_[pr][:], wc[d][:, j, blk:blk + 1], None, ALU.mult),
                           reads=[b_stt[pr], b_wc[d]], writes=[b_St[j][nxt]])
                        if isctx:
                            continue
                        for h in range(2):
                            hs = slice(h * 64, (h + 1) * 64)
                            op("pe", lambda e: e.matmul(Yv[:, h, :], rT[j][hs, :], St[j][cur][hs, :], start=True, stop=False),
                               reads=[bi[3], b_St[j][cur]], writes=[b_G[2][h]])
                            op("pe", lambda e: e.matmul(Yv[:, h, :], BbT[j][:, h, :], Us[pr][:, h, :], start=False, stop=False),
                               reads=[b_BbT[j], b_Us[pr]], writes=[b_G[2][h]])
                            op("pe", lambda e: e.matmul(Yv[:, h, :], BkT[j][:, h, :], vtk[j][:, hs], start=False, stop=True),
                               reads=[b_BkT[j], bi[6]], writes=[b_G[2][h]])
                        if d == 0:
                            op("dve", lambda e: e.tensor_copy(Ys[pr][:], Yv), reads=b_G[2], writes=[b_Ys[pr]])
                            fw.dma(YF[x0:x0 + 128, js], Ys[pr][:].rearrange("p h v -> p (h v)"), reads=[b_Ys[pr]], writes=[sbuf_of("YF")], q=SQ)
                        else:
                            fw.dma(Yf[pr][:].rearrange("p h v -> p (h v)"), YF[x0:x0 + 128, js], reads=[sbuf_of("YF")], writes=[b_Yf[pr]])
                            fw.dma(bon[pr][:], BON[js, x0:x0 + 128], reads=[sbuf_of("BON")], writes=[b_bon[pr]])
                            fw.dma(zr[pr][:], RWfm[(26 + j) * 128:(27 + j) * 128, g0:g0 + 128], reads=[sbuf_of("RWfm")], writes=[b_zr[pr]])
                            op("dve", lambda e: e.tensor_tensor(Ys[pr][:], Yv, Yf[pr][:], ALU.add), reads=b_G[2] + [b_Yf[pr]], writes=[b_Ys[pr]])
                            g_ = gst[pr]
                            op("dve", lambda e: e.tensor_reduce(g_[:, 0:2], Ys[pr][:], AX.X, ALU.add), reads=[b_Ys[pr]], writes=[b_gst[pr]])
                            op("dve", lambda e: e.tensor_scalar(g_[:, 0:2], g_[:, 0:2], -1.0 / 64, None, ALU.mult), reads=[b_gst[pr]], writes=[b_gst[pr]])
                            op("dve", lambda e: e.tensor_tensor(cen[pr][:], Ys[pr][:], g_[:, 0:2].unsqueeze(2).to_broadcast([128, 2, 64]), ALU.add),
                               reads=[b_Ys[pr], b_gst[pr]], writes=[b_cen[pr]])
                            op("dve", lambda e: e.tensor_tensor(gsq[pr][:], cen[pr][:], cen[pr][:], ALU.mult), reads=[b_cen[pr]], writes=[b_gsq[pr]])
                            op("dve", lambda e: e.tensor_reduce(g_[:, 2:4], gsq[pr][:], AX.X, ALU.add), reads=[b_gsq[pr]], writes=[b_gst[pr]])
                            op("act", lambda e: e.activation(g_[:, 2:4], g_[:, 2:4], AF.Sqrt, bias=EPSGN, scale=1.0 / 64), reads=[b_gst[pr], b_cst], writes=[b_gst[pr]])
                            op("dve", lambda e: e.reciprocal(g_[:, 2:4], g_[:, 2:4]), reads=[b_gst[pr]], writes=[b_gst[pr]])
                            op("dve", lambda e: e.tensor_tensor(cen[pr][:], cen[pr][:], g_[:, 2:4].unsqueeze(2).to_broadcast([128, 2, 64]), ALU.mult),
                               reads=[b_cen[pr], b_gst[pr]], writes=[b_cen[pr]])
                            cf = cen[pr][:].rearrange("p h v -> p (h v)")
                            op("dve", lambda e: e.tensor_tensor(cf, cf, gng[:, js], ALU.mult), reads=[b_cen[pr], b_gn], writes=[b_cen[pr]])
                            op("dve", lambda e: e.tensor_tensor(cf, cf, gnb[:, js], ALU.add), reads=[b_cen[pr], b_gn], writes=[b_cen[pr]])
                            yTv = G[3][:, 512:640]
                            op("pe", lambda e: e.transpose(yTv, cf, ident), reads=[b_cen[pr], b_cm], writes=[b_G[3][1]])
                            op("dve", lambda e: e.tensor_tensor(yo[pr][:], yTv, bon[pr][:], ALU.add), reads=[b_G[3][1], b_bon[pr]], writes=[b_yo[pr]])
                            op("dve", lambda e: e.tensor_tensor(yo[pr][:], yo[pr][:], zr[pr][:], ALU.mult), reads=[b_yo[pr], b_zr[pr]], writes=[b_yo[pr]])
                            fw.dma(YR[js, x0:x0 + 128], yo[pr][:], reads=[b_yo[pr]], writes=[sbuf_of("YR")], q=SQ)
                    cur = nxt
        fw.barrier()

        for ph in _phase(5):
            TT_ = 256
            yh = sbt(ph, "yh", [128, 8, TT_]); yr = sbt(ph, "yr", [128, 8, TT_]); b_yh = Buf(); b_yr = Buf()
            gh = [sbt(ph, "gh%d" % i, [128, 4, TT_]) for i in range(2)]; gr = [sbt(ph, "gr%d" % i, [128, 4, TT_]) for i in range(2)]
            b_gh = [Buf(), Buf()]; b_gr = [Buf(), Buf()]
            mT = sbt(ph, "mT", [128, 16, TT_]); b_mT = Buf()
            whg = [sbt(ph, "whg0", [128, 8, 512])] * 2; wrw = [sbt(ph, "wrw0", [128, 8, 512])] * 2
            b_whg = [Buf()] * 2; b_wrw = [Buf()] * 2
            wo = [sbt(ph, "wo0", [128, 16, 512])] * 2; b_wo = [Buf()] * 2
            xr = [sbt(ph, "xr%d" % i, [128, D]) for i in range(2)]; b_xr = [Buf(), Buf()]
            xn = [sbt(ph, "xn%d" % i, [128, D]) for i in range(2)]; b_xn = [Buf(), Buf()]
            junk3 = sbt(ph, "junk3", [128, D]); b_j3 = Buf()
            ss3 = sbt(ph, "ss3", [128, 2]); b_ss3 = Buf()
            fgt_ = sbt(ph, "fgt_", [128, D]); b_fg = Buf()
            tmpm = [sbt(ph, "tmpm%d" % i, [128, TT_]) for i in range(2)]; b_tmpm = [Buf(), Buf()]
            pp1 = [pst(ph, "pp1_%d" % i, [128, 512]) for i in range(2)]; pp2 = [pst(ph, "pp2_%d" % i, [128, 512]) for i in range(2)]
            b_pp1 = [Buf(), Buf()]; b_pp2 = [Buf(), Buf()]
            po = [pst(ph, "po%d" % i, [128, 512]) for i in range(4)]; b_po = [Buf() for _ in range(4)]
            fw.dma(fgt_[:], fg_b, writes=[b_fg])
            whg_r = w_hg_o.rearrange("(k p) c -> p k c", p=128)
            wrw_r = w_rw_o.rearrange("(k p) c -> p k c", p=128)
            wo_r = w_o.rearrange("(k p) c -> p k c", p=128)
            YH_r = YH.rearrange("(k p) t -> p k t", p=128)
            YR_r = YR.rearrange("(k p) t -> p k t", p=128)
            G_r = RWfm[(74 - 40) * 128:, :].rearrange("(k p) t -> p k t", p=128)
            wi = 0
            woi = 0
            xi = 0
            for tt in range(SEQ // TT_):
                x0 = tt * TT_
                g0 = x0 + CTX
                fw.dma3(yh[:], YH_r[:, :, x0:x0 + TT_], 8, reads=[sbuf_of("YH")], writes=[b_yh])
                fw.dma3(yr[:], YR_r[:, :, x0:x0 + TT_], 8, reads=[sbuf_of("YR")], writes=[b_yr])
                for mg in range(4):
                    wb = wi % 2
                    wi += 1
                    cs = slice(mg * 512, (mg + 1) * 512)
                    fw.dma3(whg[wb][:], whg_r[:, :, cs], 8, writes=[b_whg[wb]])
                    fw.dma3(wrw[wb][:], wrw_r[:, :, cs], 8, writes=[b_wrw[wb]])
                    fw.dma3(gh[wb][:], G_r[:, mg * 4:(mg + 1) * 4, g0:g0 + TT_], 4, reads=[sbuf_of("RWfm")], writes=[b_gh[wb]])
                    fw.dma3(gr[wb][:], G_r[:, 16 + mg * 4:16 + (mg + 1) * 4, g0:g0 + TT_], 4, reads=[sbuf_of("RWfm")], writes=[b_gr[wb]])
                    for mm in range(4):
                        m = mg * 4 + mm
                        a = mm % 2
                        for k in range(8):
                            op("pe", lambda e: e.matmul(pp1[a][:, :TT_], whg[wb][:, k, mm * 128:(mm + 1) * 128], yh[:, k, :], start=(k == 0), stop=(k == 7)),
                               reads=[b_whg[wb], b_yh], writes=[b_pp1[a]], inc=(k == 7))
                        for k in range(8):
                            op("pe", lambda e: e.matmul(pp2[a][:, :TT_], wrw[wb][:, k, mm * 128:(mm + 1) * 128], yr[:, k, :], start=(k == 0), stop=(k == 7)),
                               reads=[b_wrw[wb], b_yr], writes=[b_pp2[a]], inc=(k == 7))
                        op("dve", lambda e: e.tensor_tensor(tmpm[a][:], pp1[a][:, :TT_], gh[wb][:, mm, :], ALU.mult), reads=[b_pp1[a], b_gh[wb]], writes=[b_tmpm[a]])
                        op("dve", lambda e: e.tensor_tensor(mT[:, m, :], pp2[a][:, :TT_], gr[wb][:, mm, :], ALU.mult), reads=[b_pp2[a], b_gr[wb]], writes=[b_mT])
                        op("dve", lambda e: e.tensor_tensor(mT[:, m, :], mT[:, m, :], tmpm[a][:], ALU.add), reads=[b_mT, b_tmpm[a]], writes=[b_mT])
                for sub in range(TT_ // 128):
                    xb = xi % 2
                    xi += 1
                    fw.dma(xr[xb][:], xc[g0 + sub * 128:g0 + (sub + 1) * 128, :], writes=[b_xr[xb]])
                for n in range(4):
                    ob_ = woi % 2
                    woi += 1
                    fw.dma3(wo[ob_][:], wo_r[:, :, n * 512:(n + 1) * 512], 16, writes=[b_wo[ob_]])
                    for sub in range(TT_ // 128):
                        xb = (xi - (TT_ // 128) + sub) % 2
                        a = (n * 2 + sub) % 4
                        for k in range(16):
                            op("pe", lambda e: e.matmul(po[a][:], mT[:, k, sub * 128:(sub + 1) * 128], wo[ob_][:, k, :], start=(k == 0), stop=(k == 15)),
                               reads=[b_mT, b_wo[ob_]], writes=[b_po[a]], inc=(k == 15))
                        ns = slice(n * 512, (n + 1) * 512)
                        op("dve", lambda e: e.tensor_tensor(xn[xb][:, ns], po[a][:], gate_b[:, ns], ALU.mult), reads=[b_po[a], b_gate], writes=[b_xn[xb]])
                        op("dve", lambda e: e.tensor_tensor(xn[xb][:, ns], xn[xb][:, ns], xr[xb][:, ns], ALU.add), reads=[b_xn[xb], b_xr[xb]], writes=[b_xn[xb]])
                for sub in range(TT_ // 128):
                    xb = (xi - (TT_ // 128) + sub) % 2
                    op("act", lambda e: e.activation(junk3[:], xn[xb][:], AF.Square, accum_out=ss3[:, 0:1]), reads=[b_xn[xb]], writes=[b_j3, b_ss3])
                    op("act", lambda e: e.activation(ss3[:, 1:2], ss3[:, 0:1], AF.Sqrt, bias=EPS6, scale=1.0 / D), reads=[b_ss3, b_cst], writes=[b_ss3])
                    op("dve", lambda e: e.reciprocal(ss3[:, 1:2], ss3[:, 1:2]), reads=[b_ss3], writes=[b_ss3])
                    op("dve", lambda e: e.scalar_tensor_tensor(xn[xb][:], xn[xb][:], ss3[:, 1:2], fgt_[:], ALU.mult, ALU.mult),
                       reads=[b_xn[xb], b_ss3, b_fg], writes=[b_xn[xb]])
                    fw.dma(out[x0 + sub * 128:x0 + (sub + 1) * 128, :], xn[xb][:], reads=[b_xn[xb]], writes=[sbuf_of("out")], q=SQ)
        fw.barrier()
    print("bass program built: %d instructions" % fw.ninstr, flush=True)
    dbg_names = ["HGtok", "RWfm", "OFs", "YH", "AL0", "BE0", "KA0", "RH0", "AL1", "BE1", "KA1", "RH1", "BEt0", "KAt0", "Vt", "BON", "YF", "YR"]
    return nc, dbg_names


def _host_inputs(b, inp):
    f = lambda a: np.ascontiguousarray(a, dtype=np.float32)
    fm = lambda v: f(np.asarray(v).reshape(-1, 128).T)
    bc = lambda v: f(np.broadcast_to(np.asarray(v).reshape(1, -1), (128, np.asarray(v).size)))
    m = {}
    m["xc"] = f(np.concatenate([inp["ctx"][b], inp["x"][b]], axis=0))
    m["cc"] = f(np.stack([fm(inp["c"][b]), fm(inp["c_ctx"])], axis=-1))
    m["ada_w"] = f(inp["ada_w"][0].reshape(16, 128, 3 * D))
    m["ada_b_fm"] = fm(inp["ada_b"][0])
    m["ada_b_g"] = f(inp["ada_b"][0][2 * D:].reshape(1, D))
    m["norm_g_fm"] = fm(inp["norm_g"][0])
    m["w_in"] = f(inp["w_in"][0])
    m["hg_lb_b"] = f(np.broadcast_to(inp["hg_lb"][None], (128, 2, 2, HGW)))
    m["hgng_b"] = bc(inp["hg_norm_g"][0])
    mu = inp["rw_mu"][0]
    m["mu_fm"] = f(mu.reshape(4, 26, 128).transpose(2, 1, 0))
    m["w0_fm"] = f(inp["rw_w0"][0].reshape(2, 8, 128).transpose(2, 1, 0))
    m["a0_fm"] = f(inp["rw_a0"][0].reshape(2, 8, 128).transpose(2, 1, 0))
    m["w2"] = f(inp["rw_w2"][0].reshape(128, RWW))
    m["a2"] = f(inp["rw_a2"][0].reshape(128, RWW))
    kv = np.stack([inp["rw_kk"][0], inp["rw_ka"][0], inp["rw_rk"][0]], axis=0)
    m["kvec_fm"] = f(kv.reshape(3, 8, 128).transpose(2, 1, 0))
    m["gng_b"] = bc(inp["rw_gn_g"][0])
    m["gnb_b"] = bc(inp["rw_gn_b"][0])
    m["w_hg_o"] = f(inp["w_hg_out"][0])
    m["w_rw_o"] = f(inp["w_rw_out"][0])
    m["w_o"] = f(inp["w_out"][0])
    m["fg_b"] = bc(inp["final_g"])
    m["cm"] = make_cm()
    cst = np.zeros((128, 8), np.float32)
    cst[:, 0] = 1e-6
    cst[:, 1] = 1e-12
    cst[:, 2] = 64e-5
    cst[:, 3] = 0.0
    cst[:, 4] = 1.0
    m["cst"] = cst
    return m


_LAST = {}


def kernel(**inputs):
    inp = {k: np.asarray(v) for k, v in inputs.items()}
    nb = inp["x"].shape[0]
    nc, dbg = build_program()
    in_maps = [_host_inputs(b, inp) for b in range(nb)]
    res = run_bass_kernel_spmd(nc, in_maps, core_ids=list(range(nb)))
    if DEBUG:
        _LAST["res"] = res
    return np.stack([np.asarray(r["out"], dtype=np.float32) for r in res.results], axis=0)
```

```python
import os
import numpy as np
from contextlib import ExitStack
import concourse.bass as bass
import concourse.mybir as mybir
from concourse.bass_utils import run_bass_kernel_spmd

F32 = mybir.dt.float32
BF16 = mybir.dt.bfloat16
AF = mybir.ActivationFunctionType
ALU = mybir.AluOpType
AX = mybir.AxisListType

D = 2048
SEQ = 2048
CTX = 256
NT = SEQ + CTX
NCOLS = 13568
HGW = 1024
RWW = 1024
CH = 32
DEBUG = bool(os.environ.get("KDEBUG"))
PH = int(os.environ.get("KPHASE", "9"))
SQ = os.environ.get("KSQ", "pool")
ONLY = int(os.environ.get("KONLY", "-1"))
KLIM = int(os.environ.get("KLIM", "-1"))
CASTE = os.environ.get("KCAST", "pool")
KCH = int(os.environ.get("KCH", "99"))
KX = int(os.environ.get("KX", "0"))
KD = int(os.environ.get("KD", "2"))
KT = int(os.environ.get("KT", "9"))
KJ = int(os.environ.get("KJ", "8"))
KT0 = int(os.environ.get("KT0", "0"))


_FWREF = []


def _phase(n):
    if PH >= n and (ONLY < 0 or n == ONLY):
        fw = _FWREF[-1]
        fw.emitted = 0
        fw.limit = KLIM if (n == ONLY and KLIM >= 0) else None
        with ExitStack() as ph:
            yield ph
        print('phase', n, 'emitted', fw.emitted, flush=True)
        fw.limit = None


class Buf:
    __slots__ = ("name", "w", "r")

    def __init__(self, name=""):
        self.name = name
        self.w = None
        self.r = {}


class FW:
    NDMA = 14

    def __init__(self, nc, stack):
        self.nc = nc
        self.eng = {"pe": nc.tensor, "act": nc.scalar, "dve": nc.vector, "pool": nc.gpsimd, "sp": nc.sync}
        self.sem = {}
        self.cnt = {}
        for k in ["pe", "act", "dve", "pool"]:
            self.sem[k] = stack.enter_context(nc.semaphore("s_" + k))
            self.cnt[k] = 0
        for i in range(self.NDMA):
            k = "dma%d" % i
            self.sem[k] = stack.enter_context(nc.semaphore("s_" + k))
            self.cnt[k] = 0
        self.dma_i = 0
        self.waited = {e: {} for e in self.eng}
        self.ninstr = 0
        self.emitted = 0
        self.limit = None

    def _need(self, e, deps):
        for k, v in deps.items():
            if self.waited[e].get(k, 0) >= v:
                continue
            self.eng[e].wait_ge(self.sem[k], v)
            self.waited[e][k] = v

    def _collect(self, e, reads, writes):
        deps = {}

        def add(ev):
            if ev is None:
                return
            k, v = ev
            if k == e and e == "pe":
                return
            if deps.get(k, 0) < v:
                deps[k] = v
        for b in reads:
            add(b.w)
        for b in writes:
            add(b.w)
            for k, v in b.r.items():
                add((k, v))
        return deps

    def _mark(self, ev, reads, writes):
        k, v = ev
        for b in reads:
            if b.r.get(k, 0) < v:
                b.r[k] = v
        for b in writes:
            b.w = ev
            b.r = {}

    def _skip(self):
        self.emitted += 1
        return self.limit is not None and self.emitted > self.limit

    def op(self, e, fn, reads=(), writes=(), inc=True):
        if self._skip():
            return None
        deps = self._collect(e, reads, writes)
        self._need(e, deps)
        ins = fn(self.eng[e])
        self.ninstr += 1
        if inc:
            self.cnt[e] += 1
            ins.then_inc(self.sem[e], 1)
            ev = (e, self.cnt[e])
        else:
            ev = (e, self.cnt[e] + 1)
        self._mark(ev, reads, writes)
        return ins

    def dma(self, out, in_, reads=(), writes=(), q="sp", **kw):
        if self._skip():
            return
        i = self.dma_i
        self.dma_i += 1
        k = "dma%d" % (i % self.NDMA)
        deps = self._collect(q, reads, writes)
        if self.cnt[k] > 0 and deps.get(k, 0) < self.cnt[k]:
            deps[k] = self.cnt[k]
        self._need(q, deps)
        self.cnt[k] += 16
        self.eng[q].dma_start(out=out, in_=in_, **kw).then_inc(self.sem[k], 16)
        self.ninstr += 1
        self._mark((k, self.cnt[k]), reads, writes)

    def dma3(self, out, in_, n, **kw):
        for k in range(n):
            self.dma(out[:, k], in_[:, k], **kw)

    def barrier(self):
        for e in self.eng:
            deps = {k: v for k, v in self.cnt.items() if v > 0 and k != e}
            self._need(e, deps)


C_ID, C_TF, C_TB, C_BONES, C_LT, C_GT = 0, 1, 2, 3, 4, 5
NCM = 6


def make_cm():
    p = np.arange(128)[:, None]
    f = np.arange(128)[None, :]
    cm = np.zeros((128, NCM, 128), np.float32)
    cm[:, C_ID] = (p == f)
    cm[:, C_TF] = (p <= f)
    cm[:, C_TB] = (p >= f)
    cm[:, C_BONES] = ((p // 64) == (f // 64))
    cm[:, C_LT] = (p < f)
    cm[:, C_GT] = (p > f)
    return cm


def build_program():
    nc = bass.Bass("TRN2", target_bir_lowering=False)
    dt = lambda name, shape, kind="ExternalInput": nc.dram_tensor(name, shape, F32, kind=kind).ap()
    SCR = "ExternalOutput" if DEBUG else "Internal"
    xc = dt("xc", [NT, D])
    cc_d = dt("cc", [128, 16, 2])
    ada_w = dt("ada_w", [16, 128, 3 * D])
    ada_b_fm = dt("ada_b_fm", [128, 48])
    ada_b_g = dt("ada_b_g", [1, D])
    norm_g_fm = dt("norm_g_fm", [128, 16])
    w_in = dt("w_in", [D, NCOLS])
    hg_lb_b = dt("hg_lb_b", [128, 2, 2, HGW])
    hgng_b = dt("hgng_b", [128, HGW])
    mu_fm = dt("mu_fm", [128, 26, 4])
    w0_fm = dt("w0_fm", [128, 8, 2])
    a0_fm = dt("a0_fm", [128, 8, 2])
    w2_d = dt("w2", [128, RWW])
    a2_d = dt("a2", [128, RWW])
    kvec_fm = dt("kvec_fm", [128, 8, 3])
    gng_b = dt("gng_b", [128, RWW])
    gnb_b = dt("gnb_b", [128, RWW])
    w_hg_o = dt("w_hg_o", [HGW, D])
    w_rw_o = dt("w_rw_o", [RWW, D])
    w_o = dt("w_o", [D, D])
    fg_b = dt("fg_b", [128, D])
    cm_d = dt("cm", [128, NCM, 128])
    cst_d = dt("cst", [128, 8])
    out = dt("out", [SEQ, D], kind="ExternalOutput")
    HGtok = dt("HGtok", [NT, 5120], SCR)
    RWfm = dt("RWfm", [NCOLS - 5120, NT], SCR)
    OFs = dt("OFs", [SEQ, HGW], SCR)
    YH = dt("YH", [HGW, SEQ], SCR)
    AL = [dt("AL%d" % d, [RWW, NT], SCR) for d in range(2)]
    BE = [dt("BE%d" % d, [RWW, NT], SCR) for d in range(2)]
    KA = [dt("KA%d" % d, [RWW, NT], SCR) for d in range(2)]
    RH = [dt("RH%d" % d, [RWW, NT], SCR) for d in range(2)]
    BEt = [dt("BEt%d" % d, [NT, RWW], SCR) for d in range(2)]
    KAt = [dt("KAt%d" % d, [NT, RWW], SCR) for d in range(2)]
    Vt = dt("Vt", [NT, RWW], SCR)
    BON = dt("BON", [RWW, SEQ], SCR)
    YF = dt("YF", [SEQ, RWW], SCR)
    YR = dt("YR", [RWW, SEQ], SCR)
    scr_bufs = {}

    def sbuf_of(name):
        if name not in scr_bufs:
            scr_bufs[name] = Buf(name)
        return scr_bufs[name]

    with ExitStack() as st:
        fw = FW(nc, st)
        _FWREF.append(fw)
        _acct = {}

        def sbt(stack, name, shape):
            _acct[id(stack)] = _acct.get(id(stack), 0) + int(np.prod(shape[1:])) * 4
            if os.environ.get("KACCT"):
                print("sbuf", name, shape, "stack total KiB", _acct[id(stack)] / 1024.0, flush=True)
            return stack.enter_context(nc.sbuf_tensor("sb_" + name, shape, F32))
        pst = lambda stack, name, shape: stack.enter_context(nc.psum_tensor("ps_" + name, shape, F32))
        op = fw.op
        cm = sbt(st, "cm", [128, NCM, 128]); b_cm = Buf()
        cst = sbt(st, "cst", [128, 8]); b_cst = Buf()
        modA = sbt(st, "modA", [128, 16, 2]); modB = sbt(st, "modB", [128, 16, 2]); b_mod = Buf()
        gate_b = sbt(st, "gate_b", [128, D]); b_gate = Buf()
        wc = [sbt(st, "wc%d" % d, [128, 8, 18]) for d in range(2)]; b_wc = [Buf(), Buf()]
        fw.dma(cm[:], cm_d, writes=[b_cm])
        fw.dma(cst[:], cst_d, writes=[b_cst])
        ident = cm[:, C_ID, :]
        EPS6, EPS12, EPSGN, ZERO, ONE = (cst[:, i:i + 1] for i in range(5))

        for ph in _phase(0):
            adw = [sbt(ph, "adw%d" % i, [128, 3 * D]) for i in range(2)]; b_adw = [Buf(), Buf()]
            cct = sbt(ph, "cct", [128, 16, 2]); scc = sbt(ph, "scc", [128, 16, 2]); b_cc = Buf(); b_scc = Buf()
            mod = sbt(ph, "mod", [128, 48, 2]); b_modt = Buf()
            adb = sbt(ph, "adb", [128, 48]); ng = sbt(ph, "ng", [128, 16]); b_sm = Buf()
            adbg = sbt(ph, "adbg", [1, D]); grow = sbt(ph, "grow", [1, D]); b_grow = Buf()
            ps_mod = pst(ph, "ps_mod", [128, 96]); b_psm = Buf()
            ps_g = [pst(ph, "ps_g%d" % i, [128, 512]) for i in range(4)]; b_psg = [Buf() for _ in range(4)]
            fw.dma(cct[:], cc_d, writes=[b_cc])
            fw.dma(adb[:], ada_b_fm, writes=[b_sm])
            fw.dma(ng[:], norm_g_fm, writes=[b_sm])
            fw.dma(adbg[:], ada_b_g, writes=[b_sm])
            op("act", lambda e: e.activation(scc[:], cct[:], AF.Silu), reads=[b_cc], writes=[b_scc])
            op("dve", lambda e: e.memset(mod[:], 0.0), writes=[b_modt])
            for k in range(16):
                fw.dma(adw[k % 2][:], ada_w[k], writes=[b_adw[k % 2]])
                for m in range(48):
                    op("pe", lambda e: e.matmul(ps_mod[:, 2 * m:2 * m + 2], adw[k % 2][:, m * 128:(m + 1) * 128],
                                                scc[:, k, :], start=True, stop=True),
                       reads=[b_adw[k % 2], b_scc], writes=[b_psm], inc=(m == 47))
                op("dve", lambda e: e.tensor_tensor(mod[:], mod[:], ps_mod[:].rearrange("p (m v) -> p m v", v=2), ALU.add),
                   reads=[b_psm, b_modt], writes=[b_modt])
                for n in range(4):
                    op("pe", lambda e: e.matmul(ps_g[n][0:1, :], scc[:, k, 0:1], adw[k % 2][:, 2 * D + n * 512:2 * D + (n + 1) * 512],
                                                start=(k == 0), stop=(k == 15)),
                       reads=[b_adw[k % 2], b_scc], writes=[b_psg[n]])
            op("dve", lambda e: e.tensor_tensor(mod[:], mod[:], adb[:].unsqueeze(2).to_broadcast([128, 48, 2]), ALU.add),
               reads=[b_sm, b_modt], writes=[b_modt])
            op("dve", lambda e: e.tensor_scalar(modA[:], mod[:, 16:32, :], 1.0, None, ALU.add), reads=[b_modt], writes=[b_mod])
            op("dve", lambda e: e.tensor_tensor(modA[:], modA[:], ng[:].unsqueeze(2).to_broadcast([128, 16, 2]), ALU.mult),
               reads=[b_sm, b_mod], writes=[b_mod])
            op("dve", lambda e: e.tensor_copy(modB[:], mod[:, 0:16, :]), reads=[b_modt], writes=[b_mod])
            for n in range(4):
                op("dve", lambda e: e.tensor_tensor(grow[:, n * 512:(n + 1) * 512], ps_g[n][0:1, :], adbg[:, n * 512:(n + 1) * 512], ALU.add),
                   reads=[b_psg[n], b_sm], writes=[b_grow])
            for n in range(4):
                op("pe", lambda e: e.matmul(ps_g[n][:], cm[0:1, C_TF, :], grow[:, n * 512:(n + 1) * 512], start=True, stop=True),
                   reads=[b_cm, b_grow], writes=[b_psg[n]])
                op("act", lambda e: e.activation(gate_b[:, n * 512:(n + 1) * 512], ps_g[n][:], AF.Identity, scale=1.0),
                   reads=[b_psg[n]], writes=[b_gate])
        fw.barrier()

        for ph in _phase(1):
            lb_b = sbt(ph, "lb_b", [128, 2, HGW]); oml_b = sbt(ph, "oml_b", [128, 2, HGW])
            b_lb = Buf()
            xt = [sbt(ph, "xt%d" % i, [128, D]) for i in range(2)]; b_xt = [Buf(), Buf()]
            junk = sbt(ph, "junk", [128, D]); b_junk = Buf()
            ss = sbt(ph, "ss", [128, 2]); b_ss = Buf()
            hT = ph.enter_context(nc.sbuf_tensor("sb_hT", [128, 16, 1024], BF16)); b_hT = [Buf() for _ in range(8)]
            wt = [sbt(ph, "wt0", [128, 16, 512])] * 2; b_wt = [Buf()] * 2
            wb16 = [ph.enter_context(nc.sbuf_tensor("sb_wb16_%d" % i, [128, 16, 512], BF16)) for i in range(2)]; b_wb16 = [Buf(), Buf()]
            lbt = wt[1][:, 0:8, :].rearrange("p a b -> p (a b)").rearrange("p (d l c) -> p d l c", d=2, l=2)
            b_lbt = b_wt[1]
            NOT = 4
            ot = [sbt(ph, "ot%d" % i, [128, 512]) for i in range(NOT)]; b_ot = [Buf() for _ in range(NOT)]
            tps = [pst(ph, "tps%d" % i, [128, 512]) for i in range(4)]; b_tps = [Buf() for _ in range(4)]
            acc = [pst(ph, "acc%d" % i, [128, 512]) for i in range(4)]; b_acc = [Buf() for _ in range(4)]
            fw.dma(lbt, hg_lb_b, writes=[b_lbt])
            op("dve", lambda e: e.tensor_tensor(lb_b[:], lbt[:, :, 0, :], lbt[:, :, 1, :], ALU.subtract), reads=[b_lbt], writes=[b_lb])
            op("act", lambda e: e.activation(lb_b[:], lb_b[:], AF.Sigmoid), reads=[b_lb], writes=[b_lb])
            op("dve", lambda e: e.tensor_scalar(oml_b[:], lb_b[:], -1.0, 1.0, ALU.mult, ALU.add), reads=[b_lb], writes=[b_lb])
            w_in_r = w_in.rearrange("(k p) c -> p k c", p=128)
            oti = 0
            ctx_groups = {2, 3, 4, 5, 6, 7, 12, 13, 14, 15, 16}
            tiles = [(0, 256, True)] + [(256 + 1024 * i, 1024, False) for i in range(2)]
            wti = 0
            for (g0, ntok, isctx) in tiles:
                v = 1 if isctx else 0
                nsub = ntok // 128
                for sub in range(nsub):
                    xb = sub % 2
                    fw.dma(xt[xb][:], xc[g0 + sub * 128:g0 + (sub + 1) * 128, :], writes=[b_xt[xb]])
                    op("act", lambda e: e.activation(junk[:], xt[xb][:], AF.Square, accum_out=ss[:, 0:1]),
                       reads=[b_xt[xb]], writes=[b_junk, b_ss])
                    op("act", lambda e: e.activation(ss[:, 1:2], ss[:, 0:1], AF.Sqrt, bias=EPS6, scale=1.0 / D),
                       reads=[b_ss, b_cst], writes=[b_ss])
                    op("dve", lambda e: e.reciprocal(ss[:, 1:2], ss[:, 1:2]), reads=[b_ss], writes=[b_ss])
                    op("act", lambda e: e.activation(junk[:], xt[xb][:], AF.Identity, scale=ss[:, 1:2], bias=ZERO),
                       reads=[b_xt[xb], b_ss, b_cst], writes=[b_junk])
                    for q in range(4):
                        for i in range(4):
                            j = q * 4 + i
                            op("pe", lambda e: e.transpose(tps[q][:, i * 128:(i + 1) * 128], junk[:, j * 128:(j + 1) * 128], ident),
                               reads=[b_junk, b_cm], writes=[b_tps[q]], inc=(i == 3))
                        for i in range(4):
                            j = q * 4 + i
                            en = "dve" if (i % 2 == 0) else "act"
                            if en == "dve":
                                op("dve", lambda e: e.tensor_scalar(hT[:, j, sub * 128:(sub + 1) * 128], tps[q][:, i * 128:(i + 1) * 128],
                                                                    modA[:, j, v:v + 1], modB[:, j, v:v + 1], ALU.mult, ALU.add),
                                   reads=[b_tps[q], b_mod], writes=[b_hT[sub]])
                            else:
                                op("act", lambda e: e.activation(hT[:, j, sub * 128:(sub + 1) * 128], tps[q][:, i * 128:(i + 1) * 128],
                                                                 AF.Identity, scale=modA[:, j, v:v + 1], bias=modB[:, j, v:v + 1]),
                                   reads=[b_tps[q], b_mod], writes=[b_hT[sub]])
                for g in range(27):
                    if isctx and g not in ctx_groups:
                        continue
                    c0 = g * 512
                    ncol = min(512, NCOLS - c0)
                    wb = wti % 2
                    wti += 1
                    fw.dma3(wt[wb][:, :, :ncol], w_in_r[:, :, c0:c0 + ncol], 16, writes=[b_wt[wb]])
                    op(CASTE, lambda e: e.tensor_copy(wb16[wb][:, :, :ncol], wt[wb][:, :, :ncol]), reads=[b_wt[wb]], writes=[b_wb16[wb]])
                    if c0 < 5120:
                        typ = ["silu", "id", "fg0", "fg1", "silu"][c0 // 1024]
                        for sub in range(nsub):
                            a = sub % 4
                            for k in range(16):
                                op("pe", lambda e: e.matmul(acc[a][:, :ncol], hT[:, k, sub * 128:(sub + 1) * 128], wb16[wb][:, k, :ncol],
                                                            start=(k == 0), stop=(k == 15)),
                                   reads=[b_hT[sub], b_wb16[wb]], writes=[b_acc[a]], inc=(k == 15))
                            o_ = oti % NOT
                            oti += 1
                            if typ == "silu":
                                op("act", lambda e: e.activation(ot[o_][:], acc[a][:], AF.Silu), reads=[b_acc[a]], writes=[b_ot[o_]])
                            elif typ == "id":
                                op("dve", lambda e: e.tensor_copy(ot[o_][:], acc[a][:]), reads=[b_acc[a]], writes=[b_ot[o_]])
                            else:
                                dd = int(typ[2])
                                cc0 = c0 - (2048 + dd * 1024)
                                op("act", lambda e: e.activation(ot[o_][:], acc[a][:], AF.Sigmoid), reads=[b_acc[a]], writes=[b_ot[o_]])
                                op("dve", lambda e: e.tensor_tensor(ot[o_][:], ot[o_][:], oml_b[:, dd, cc0:cc0 + 512], ALU.mult),
                                   reads=[b_ot[o_], b_lb], writes=[b_ot[o_]])
                                op("dve", lambda e: e.tensor_tensor(ot[o_][:], ot[o_][:], lb_b[:, dd, cc0:cc0 + 512], ALU.add),
                                   reads=[b_ot[o_], b_lb], writes=[b_ot[o_]])
                            fw.dma(HGtok[g0 + sub * 128:g0 + (sub + 1) * 128, c0:c0 + 512], ot[o_][:],
                                   reads=[b_ot[o_]], writes=[sbuf_of("HGtok")], q=SQ)
                    else:
                        for m in range(ncol // 128):
                            mc = c0 // 128 + m
                            if isctx and not (48 <= mc <= 65):
                                continue
                            for hf in range((ntok + 511) // 512):
                                nt_ = min(512, ntok - hf * 512)
                                tk0 = hf * 512
                                a = (m * 2 + hf) % 4
                                for k in range(16):
                                    op("pe", lambda e: e.matmul(acc[a][:, :nt_], wb16[wb][:, k, m * 128:(m + 1) * 128], hT[:, k, tk0:tk0 + nt_],
                                                                start=(k == 0), stop=(k == 15)),
                                       reads=b_hT[tk0 // 128:(tk0 + nt_) // 128] + [b_wb16[wb]], writes=[b_acc[a]], inc=(k == 15))
                                o_ = oti % NOT
                                oti += 1
                                if mc <= 65:
                                    op("dve", lambda e: e.tensor_copy(ot[o_][:, :nt_], acc[a][:, :nt_]), reads=[b_acc[a]], writes=[b_ot[o_]])
                                else:
                                    fn = AF.Silu if mc <= 73 else AF.Sigmoid
                                    op("act", lambda e: e.activation(ot[o_][:, :nt_], acc[a][:, :nt_], fn), reads=[b_acc[a]], writes=[b_ot[o_]])
                                fw.dma(RWfm[(mc - 40) * 128:(mc - 39) * 128, g0 + tk0:g0 + tk0 + nt_], ot[o_][:, :nt_],
                                       reads=[b_ot[o_]], writes=[sbuf_of("RWfm")], q=SQ)
        fw.barrier()

        for ph in _phase(2):
            NB = 2
            fgt = [sbt(ph, "fgt%d" % i, [CH, HGW]) for i in range(NB)]
            vtt = [sbt(ph, "vtt%d" % i, [CH, HGW]) for i in range(NB)]
            qtt = [sbt(ph, "qtt%d" % i, [CH, HGW]) for i in range(NB)]
            ztt = [sbt(ph, "ztt%d" % i, [CH, HGW]) for i in range(NB)]
            oft = [sbt(ph, "oft%d" % i, [CH, HGW]) for i in range(NB)]
            b_ld = [[Buf() for _ in range(5)] for _ in range(NB)]
            gt = sbt(ph, "gt", [CH, HGW]); b_gt = Buf()
            Et = sbt(ph, "Et", [CH, HGW]); Ei = sbt(ph, "Ei", [CH, HGW]); b_E = Buf(); b_Ei = Buf()
            kt_ = sbt(ph, "kt", [128, HGW]); qq_ = sbt(ph, "qq", [128, HGW]); b_kt = Buf(); b_qq = Buf()
            kt = kt_[0:CH, :]; qq = qq_[0:CH, :]
            ob = sbt(ph, "ob", [CH, HGW]); sq_ = sbt(ph, "sq", [128, HGW]); b_ob = Buf(); b_sq = Buf()
            sq = sq_[0:CH, :]
            ms = sbt(ph, "ms", [CH, 8]); b_ms = Buf()
            eb = sbt(ph, "eb", [128, 8]); b_eb = Buf()
            hng = sbt(ph, "hng", [CH, HGW]); b_hng = Buf()
            qkT = [sbt(ph, "qkT%d" % i, [128, 2, CH]) for i in range(2)]; b_qkT = [Buf(), Buf()]
            at = [sbt(ph, "at%d" % i, [CH, CH]) for i in range(2)]; b_at = [Buf(), Buf()]
            S = [sbt(ph, "S%d" % i, [128, 8, 128]) for i in range(2)]; b_S = [Buf(), Buf()]
            stmp = sbt(ph, "stmp", [128, 8, 128]); b_stmp = Buf()
            yT = sbt(ph, "yTs", [128, 8, 512]); b_yT = Buf()
            bc_ps = [pst(ph, "bc_ps%d" % i, [128, 512]) for i in range(2)]; b_bc = [Buf(), Buf()]
            o_ps = [pst(ph, "o_ps%d" % i, [128, 512]) for i in range(2)]; b_o = [Buf(), Buf()]
            dS_ps = [pst(ph, "dS_ps%d" % i, [128, 512]) for i in range(2)]; b_dS = [Buf(), Buf()]
            mz = pst(ph, "mz", [128, 512]); b_tp = [Buf()] * 2; b_aps = [Buf()] * 2; b_ebp = b_aps[0]
            yT_ps = pst(ph, "yT_ps", [128, 512]); b_yTp = Buf()
            fw.dma(hng[:], hgng_b[0:CH, :], writes=[b_hng])
            op("dve", lambda e: e.memset(kt_[:], 0.0), writes=[b_kt])
            op("dve", lambda e: e.memset(qq_[:], 0.0), writes=[b_qq])
            op("dve", lambda e: e.memset(sq_[:], 0.0), writes=[b_sq])
            ci = 0
            for d in range(min(2, KD)):
                tri = cm[0:CH, C_TF if d == 0 else C_TB, 0:CH]
                lastc = CH - 1 if d == 0 else 0
                onehot = cm[0:CH, C_ID, lastc:lastc + 1]
                order = list(range(NT // CH)) if d == 0 else list(range(CTX // CH - 1, -1, -1)) + list(range(NT // CH - 1, CTX // CH - 1, -1))
                cur = 0
                op("dve", lambda e: e.memset(S[0][:], 0.0), writes=[b_S[0]])
                for c in order[:KCH]:
                    isctx = c < CTX // CH
                    t0 = c * CH
                    lb_ = ci % NB
                    ci += 1
                    fgs, vs, qs, zs, ofs = fgt[lb_], vtt[lb_], qtt[lb_], ztt[lb_], oft[lb_]
                    bl = b_ld[lb_]
                    fw.dma(fgs[:], HGtok[t0:t0 + CH, 2048 + d * 1024:3072 + d * 1024], reads=[sbuf_of("HGtok")], writes=[bl[0]])
                    fw.dma(vs[:], HGtok[t0:t0 + CH, 1024:2048], reads=[sbuf_of("HGtok")], writes=[bl[1]])
                    if not isctx:
                        fw.dma(qs[:], HGtok[t0:t0 + CH, 0:1024], reads=[sbuf_of("HGtok")], writes=[bl[2]])
                        if d == 1:
                            fw.dma(zs[:], HGtok[t0:t0 + CH, 4096:5120], reads=[sbuf_of("HGtok")], writes=[bl[3]])
                            fw.dma(ofs[:], OFs[t0 - CTX:t0 - CTX + CH, :], reads=[sbuf_of("OFs")], writes=[bl[4]])
                    op("act", lambda e: e.activation(gt[:], fgs[:], AF.Ln), reads=[bl[0]], writes=[b_gt])
                    for n in range(2):
                        op("pe", lambda e: e.matmul(bc_ps[n][0:CH, :], tri, gt[:, n * 512:(n + 1) * 512], start=True, stop=True),
                           reads=[b_gt, b_cm], writes=[b_bc[n]])
                    for n in range(2):
                        sl = slice(n * 512, (n + 1) * 512)
                        op("act", lambda e: e.activation(Ei[:, sl], bc_ps[n][0:CH, :], AF.Exp, scale=-1.0), reads=[b_bc[n]], writes=[b_Ei])
                        op("act", lambda e: e.activation(Et[:, sl], bc_ps[n][0:CH, :], AF.Exp), reads=[b_bc[n]], writes=[b_E])
                    op("dve", lambda e: e.tensor_scalar(kt[:], fgs[:], -1.0, 1.0, ALU.mult, ALU.add), reads=[bl[0]], writes=[b_kt])
                    op("dve", lambda e: e.tensor_tensor(kt[:], kt[:], Ei[:], ALU.mult), reads=[b_kt, b_Ei], writes=[b_kt])
                    if not isctx:
                        op("dve", lambda e: e.tensor_tensor(qq[:], qs[:], Et[:], ALU.mult), reads=[bl[2], b_E], writes=[b_qq])
                    for h in range(8):
                        op("pe", lambda e: e.matmul(yT_ps[:, 400 + h:401 + h], Et[:, h * 128:(h + 1) * 128], onehot, start=True, stop=True),
                           reads=[b_E, b_cm], writes=[b_ebp], inc=(h == 7))
                    op("dve", lambda e: e.tensor_copy(eb[:], yT_ps[:, 400:408]), reads=[b_ebp], writes=[b_eb])
                    nxt = 1 - cur
                    for h in range(8):
                        hs = slice(h * 128, (h + 1) * 128)
                        pp = h % 2
                        if not isctx:
                            tpv = mz[:, pp * 256:(pp + 1) * 256]
                        if (not isctx) and not (KX & 1):
                            op("pe", lambda e: e.transpose(tpv[:, 0:128], qq_[:, hs], ident), reads=[b_qq, b_cm], writes=[b_tp[pp]], inc=False)
                            op("pe", lambda e: e.transpose(tpv[:, 128:256], kt_[:, hs], ident), reads=[b_kt, b_cm], writes=[b_tp[pp]])
                            op("dve", lambda e: e.tensor_copy(qkT[pp][:], tpv.rearrange("p (a b) -> p a b", b=128)[:, :, 0:CH]),
                               reads=[b_tp[pp]], writes=[b_qkT[pp]])
                            apv = yT_ps[0:CH, 256 + pp * 64:256 + pp * 64 + CH]
                        if (not isctx) and not (KX & 2):
                            op("pe", lambda e: e.matmul(apv, qkT[pp][:, 1, :], qkT[pp][:, 0, :], start=True, stop=True),
                               reads=[b_qkT[pp]], writes=[b_aps[pp]])
                            op("dve", lambda e: e.tensor_tensor(at[pp][:], apv, tri, ALU.mult), reads=[b_aps[pp], b_cm], writes=[b_at[pp]])
                            opv = o_ps[h // 4][0:CH, (h % 4) * 128:(h % 4 + 1) * 128]
                        if (not isctx) and not (KX & 4):
                            op("pe", lambda e: e.matmul(opv, at[pp][:], vs[:, hs], start=True, stop=False),
                               reads=[b_at[pp], bl[1]], writes=[b_o[h // 4]], inc=False)
                            op("pe", lambda e: e.matmul(opv, qkT[pp][:, 0, :], S[cur][:, h, :], start=False, stop=True),
                               reads=[b_qkT[pp], b_S[cur]], writes=[b_o[h // 4]])
                        dsv = dS_ps[h // 4][:, (h % 4) * 128:(h % 4 + 1) * 128]
                        op("pe", lambda e: e.matmul(dsv, kt[:, hs], vs[:, hs], start=True, stop=True),
                           reads=[b_kt, bl[1]], writes=[b_dS[h // 4]])
                    for n in range(2):
                        op("dve", lambda e: e.tensor_tensor(stmp[:, n * 4:(n + 1) * 4, :], dS_ps[n][:].rearrange("p (h v) -> p h v", v=128),
                                                            S[cur][:, n * 4:(n + 1) * 4, :], ALU.add),
                           reads=[b_dS[n], b_S[cur]], writes=[b_stmp])
                    op("dve", lambda e: e.tensor_tensor(S[nxt][:], stmp[:], eb[:].unsqueeze(2).to_broadcast([128, 8, 128]), ALU.mult),
                       reads=[b_stmp, b_eb], writes=[b_S[nxt]])
                    cur = nxt
                    if isctx or (KX & 8):
                        continue
                    xt0 = t0 - 256
                    if d == 0:
                        for n in range(2):
                            op("dve", lambda e: e.tensor_copy(ob[:, n * 512:(n + 1) * 512], o_ps[n][0:CH, :]),
                               reads=[b_o[n]], writes=[b_ob])
                        fw.dma(OFs[xt0:xt0 + CH, :], ob[:], reads=[b_ob], writes=[sbuf_of("OFs")], q=SQ)
                    else:
                        for n in range(2):
                            op("dve", lambda e: e.tensor_tensor(ob[:, n * 512:(n + 1) * 512], o_ps[n][0:CH, :], ofs[:, n * 512:(n + 1) * 512], ALU.add),
                               reads=[b_o[n], bl[4]], writes=[b_ob])
                        op("dve", lambda e: e.tensor_tensor(sq[:], ob[:], ob[:], ALU.mult), reads=[b_ob], writes=[b_sq])
                        op("dve", lambda e: e.tensor_reduce(ms[:], sq[:].rearrange("p (h v) -> p h v", v=128), AX.X, ALU.add), reads=[b_sq], writes=[b_ms])
                        op("act", lambda e: e.activation(ms[:], ms[:], AF.Sqrt, bias=cst[0:CH, 0:1], scale=1.0 / 128), reads=[b_ms, b_cst], writes=[b_ms])
                        op("dve", lambda e: e.reciprocal(ms[:], ms[:]), reads=[b_ms], writes=[b_ms])
                        op("dve", lambda e: e.tensor_tensor(sq[:].rearrange("p (h v) -> p h v", v=128), ob[:].rearrange("p (h v) -> p h v", v=128),
                                                            ms[:].unsqueeze(2).to_broadcast([CH, 8, 128]), ALU.mult),
                           reads=[b_ob, b_ms], writes=[b_sq])
                        op("dve", lambda e: e.tensor_tensor(sq[:], sq[:], hng[:], ALU.mult), reads=[b_sq, b_hng], writes=[b_sq])
                        op("dve", lambda e: e.tensor_tensor(sq[:], sq[:], zs[:], ALU.mult), reads=[b_sq, bl[3]], writes=[b_sq])
                        for h in range(8):
                            op("pe", lambda e: e.transpose(bc_ps[h // 4][:, (h % 4) * 128:(h % 4 + 1) * 128], sq_[:, h * 128:(h + 1) * 128], ident),
                               reads=[b_sq, b_cm], writes=[b_bc[h // 4]], inc=(h % 4 == 3))
                        for n in range(2):
                            yo_ = xt0 % 512
                            op("dve", lambda e: e.tensor_copy(yT[:, n * 4:(n + 1) * 4, yo_:yo_ + CH], bc_ps[n][:].rearrange("p (h t) -> p h t", t=128)[:, :, 0:CH]),
                               reads=[b_bc[n]], writes=[b_yT])
                        if xt0 % 512 == 0:
                            fw.dma3(YH.rearrange("(h p) t -> p h t", p=128)[:, :, xt0:xt0 + 512], yT[:], 8, reads=[b_yT], writes=[sbuf_of("YH")], q=SQ)
        fw.barrier()

        for ph in _phase(3):
            W = 640
            mu = sbt(ph, "mu", [128, 26, 4]); omu = sbt(ph, "omu", [128, 26]); b_mu = Buf()
            w0t = sbt(ph, "w0t", [128, 8, 2]); a0t = sbt(ph, "a0t", [128, 8, 2]); kvt = sbt(ph, "kvt", [128, 8, 3]); omka = sbt(ph, "omka", [128, 8])
            w2t = sbt(ph, "w2t", [128, RWW]); a2t = sbt(ph, "a2t", [128, RWW]); b_par = Buf()
            raw = [sbt(ph, "raw%d" % i, [128, W]) for i in range(3)]; b_raw = [Buf() for _ in range(3)]
            rawl = [sbt(ph, "rawl%d" % i, [128, W]) for i in range(2)]; b_rawl = [Buf(), Buf()]
            sh = [sbt(ph, "sh%d" % i, [128, 512]) for i in range(3)]; b_sh = [Buf() for _ in range(3)]
            tw = sbt(ph, "tw", [128, 512]); als = sbt(ph, "als", [128, 512]); b_tw = Buf(); b_als = Buf()
            kkr = sbt(ph, "kkr", [128, 512]); kk = sbt(ph, "kk", [128, 512]); t1 = sbt(ph, "t1", [128, 512]); b_kkr = Buf(); b_kk = Buf(); b_t1 = Buf()
            lgw = sbt(ph, "lgw", [128, 512]); av = sbt(ph, "av", [128, 512]); b_lgw = Buf(); b_av = Buf()
            kd = [sbt(ph, "kd%d" % i, [128, 512]) for i in range(2)]; b_kd = [Buf(), Buf()]
            bv = sbt(ph, "bv", [128, 512]); b_bv = Buf()
            lt = sbt(ph, "lt", [128, 128]); b_lt = Buf()
            Ecw = sbt(ph, "Ecw", [128, 512]); Einv = sbt(ph, "Einv", [128, 512]); Eex = sbt(ph, "Eex", [128, 512]); b_Ecw = Buf(); b_Einv = Buf(); b_Eex = Buf()
            res = [sbt(ph, "res%d" % i, [128, 512]) for i in range(4)]; b_res = [Buf() for _ in range(4)]
            tm = [[sbt(ph, "tm%d_%d" % (a_, s_), [128, RWW]) for s_ in range(4)] for a_ in range(5)]; b_tm = [[Buf() for _ in range(4)] for _ in range(5)]
            p_a = pst(ph, "p_a", [128, 512]); p_b = pst(ph, "p_b", [128, 512]); p_cw = pst(ph, "p_cw", [128, 512]); p_tq = [pst(ph, "p_t%d" % i, [128, 512]) for i in range(4)]
            p_s = pst(ph, "p_s", [128, 512])
            b_pa = Buf(); b_pb = Buf(); b_pcw = Buf(); b_pt = [Buf() for _ in range(4)]; b_ps = Buf()
            for (tl, src) in [(mu, mu_fm), (w0t, w0_fm), (a0t, a0_fm), (kvt, kvec_fm), (w2t, w2_d), (a2t, a2_d)]:
                fw.dma(tl[:], src, writes=[b_par if tl is not mu else b_mu])
            op("dve", lambda e: e.tensor_reduce(omu[:], mu[:], AX.X, ALU.add), reads=[b_mu], writes=[b_mu])
            op("dve", lambda e: e.tensor_scalar(omu[:], omu[:], -1.0, 1.0, ALU.mult, ALU.add), reads=[b_mu], writes=[b_mu])
            op("dve", lambda e: e.tensor_scalar(omka[:], kvt[:, :, 1], -1.0, 1.0, ALU.mult, ALU.add), reads=[b_par], writes=[b_par])
            omuc = sbt(ph, "omuc", [128, 26])
            op("dve", lambda e: e.tensor_tensor(omuc[:], mu[:, :, 0], mu[:, :, 1], ALU.add), reads=[b_mu], writes=[b_mu])
            op("dve", lambda e: e.tensor_scalar(omuc[:], omuc[:], -1.0, 1.0, ALU.mult, ALU.add), reads=[b_mu], writes=[b_mu])

            def shift(dst, b_dst, rawt, b_rawt, chunk, g0, ntok, isctx, eng="dve"):
                rows = RWfm[chunk * 128:(chunk + 1) * 128, :]
                if isctx:
                    fw.dma(rawt[:, 0:256], rows[:, 0:256], reads=[sbuf_of("RWfm")], writes=[b_rawt])
                    P = rawt[:, 0:256]
                    op(eng, lambda e: e.tensor_scalar(dst[:, 0:256], P, omuc[:, chunk:chunk + 1], None, ALU.mult), reads=[b_rawt, b_mu], writes=[b_dst])
                    op(eng, lambda e: e.scalar_tensor_tensor(dst[:, 1:256], P[:, 0:255], mu[:, chunk, 0:1], dst[:, 1:256], ALU.mult, ALU.add),
                       reads=[b_rawt, b_mu, b_dst], writes=[b_dst])
                    op(eng, lambda e: e.scalar_tensor_tensor(dst[:, 0:255], P[:, 1:256], mu[:, chunk, 1:2], dst[:, 0:255], ALU.mult, ALU.add),
                       reads=[b_rawt, b_mu, b_dst], writes=[b_dst])
                    return
                lo = g0 - 64
                hi = g0 + ntok + 64
                first = (g0 == CTX)
                last = (g0 + ntok == NT)
                if first:
                    op(eng, lambda e: e.memset(rawt[:, 0:64], 0.0), writes=[b_rawt])
                if last:
                    op(eng, lambda e: e.memset(rawt[:, W - 64:W], 0.0), writes=[b_rawt])
                a_ = 64 if first else 0
                b_ = W - 64 if last else W
                fw.dma(rawt[:, a_:b_], rows[:, lo + a_:lo + b_], reads=[sbuf_of("RWfm")], writes=[b_rawt])
                P3 = rawt[:].rearrange("p (r c) -> p r c", c=64)
                Pc = P3[:, 1:9, :]
                d3 = dst[:].rearrange("p (r c) -> p r c", c=64)
                op(eng, lambda e: e.tensor_scalar(d3, Pc, omu[:, chunk:chunk + 1], None, ALU.mult), reads=[b_rawt, b_mu], writes=[b_dst])
                op(eng, lambda e: e.scalar_tensor_tensor(d3[:, :, 1:], Pc[:, :, :-1], mu[:, chunk, 0:1], d3[:, :, 1:], ALU.mult, ALU.add),
                   reads=[b_rawt, b_mu, b_dst], writes=[b_dst])
                op(eng, lambda e: e.scalar_tensor_tensor(d3[:, :, :-1], Pc[:, :, 1:], mu[:, chunk, 1:2], d3[:, :, :-1], ALU.mult, ALU.add),
                   reads=[b_rawt, b_mu, b_dst], writes=[b_dst])
                op(eng, lambda e: e.scalar_tensor_tensor(d3, P3[:, 0:8, :], mu[:, chunk, 2:3], d3, ALU.mult, ALU.add),
                   reads=[b_rawt, b_mu, b_dst], writes=[b_dst])
                op(eng, lambda e: e.scalar_tensor_tensor(d3, P3[:, 2:10, :], mu[:, chunk, 3:4], d3, ALU.mult, ALU.add),
                   reads=[b_rawt, b_mu, b_dst], writes=[b_dst])

            tiles = [(0, 256, True)] + [(256 + 512 * i, 512, False) for i in range(4)]
            ri = 0
            for (g0, ntok, isctx) in tiles[KT0:KT]:
                nsub = ntok // 128
                blk0 = g0 // 128
                N_ = slice(0, ntok)
                shift(tw, b_tw, rawl[0], b_rawl[0], 24, g0, ntok, isctx)
                if not (KX & 32):
                    op("act", lambda e: e.activation(tw[:, N_], tw[:, N_], AF.Tanh), reads=[b_tw], writes=[b_tw])
                shift(als, b_als, rawl[1], b_rawl[1], 25, g0, ntok, isctx)
                for j in range(KJ):
                    if not isctx:
                        shift(sh[0], b_sh[0], raw[0], b_raw[0], j, g0, ntok, isctx)
                    shift(sh[1], b_sh[1], raw[1], b_raw[1], 8 + j, g0, ntok, isctx)
                    shift(sh[2], b_sh[2], raw[2], b_raw[2], 16 + j, g0, ntok, isctx)
                    rs, ks, vs = sh[0], sh[1], sh[2]
                    op("dve", lambda e: e.tensor_scalar(kkr[:, N_], ks[:, N_], kvt[:, j, 0:1], None, ALU.mult), reads=[b_sh[1], b_par], writes=[b_kkr])
                    op("dve", lambda e: e.tensor_tensor(t1[:, N_], kkr[:, N_], kkr[:, N_], ALU.mult), reads=[b_kkr], writes=[b_t1])
                    op("pe", lambda e: e.matmul(p_a[:, N_], cm[:, C_BONES, :], t1[:, N_], start=True, stop=True), reads=[b_cm, b_t1], writes=[b_pa])
                    op("act", lambda e: e.activation(t1[:, N_], p_a[:, N_], AF.Sqrt, bias=EPS12, scale=1.0), reads=[b_pa, b_cst], writes=[b_t1])
                    op("dve", lambda e: e.reciprocal(t1[:, N_], t1[:, N_]), reads=[b_t1], writes=[b_t1])
                    op("dve", lambda e: e.tensor_tensor(kk[:, N_], kkr[:, N_], t1[:, N_], ALU.mult), reads=[b_kkr, b_t1], writes=[b_kk])
                    for sub in range(nsub):
                        q_ = sub % 4
                        op("pe", lambda e: e.transpose(p_tq[q_][:, 0:128], vs[:, sub * 128:(sub + 1) * 128], ident),
                           reads=[b_sh[2], b_cm], writes=[b_pt[q_]])
                        op("dve", lambda e: e.tensor_copy(tm[0][sub][:, j * 128:(j + 1) * 128], p_tq[q_][:, 0:128]), reads=[b_pt[q_]], writes=[b_tm[0][sub]])
                    for d in range(2):
                        ds = slice(d * 64, (d + 1) * 64)
                        js = slice(j * 128, (j + 1) * 128)
                        op("pe", lambda e: e.matmul(p_a[:, N_], w2t[ds, js], tw[ds, N_], start=True, stop=True), reads=[b_par, b_tw], writes=[b_pa])
                        op("act", lambda e: e.activation(lgw[:, N_], p_a[:, N_], AF.Sigmoid, bias=w0t[:, j, d:d + 1], scale=1.0), reads=[b_pa, b_par], writes=[b_lgw])
                        op("dve", lambda e: e.tensor_scalar(lgw[:, N_], lgw[:, N_], -0.6065306597126334, None, ALU.mult), reads=[b_lgw], writes=[b_lgw])
                        op("pe", lambda e: e.matmul(p_b[:, N_], a2t[ds, js], als[ds, N_], start=True, stop=True), reads=[b_par, b_als], writes=[b_pb])
                        op("act", lambda e: e.activation(av[:, N_], p_b[:, N_], AF.Sigmoid, bias=a0t[:, j, d:d + 1], scale=1.0), reads=[b_pb, b_par], writes=[b_av])
                        op("dve", lambda e: e.tensor_scalar(t1[:, N_], av[:, N_], kvt[:, j, 1:2], omka[:, j:j + 1], ALU.mult, ALU.add), reads=[b_av, b_par], writes=[b_t1])
                        op("dve", lambda e: e.tensor_tensor(kd[d][:, N_], t1[:, N_], ks[:, N_], ALU.mult), reads=[b_t1, b_sh[1]], writes=[b_kd[d]])
                        op("dve", lambda e: e.tensor_tensor(bv[:, N_], kk[:, N_], av[:, N_], ALU.mult), reads=[b_kk, b_av], writes=[b_bv])
                        tri = cm[:, C_TF if d == 0 else C_TB, :]
                        for sub in range(nsub):
                            ss_ = slice(sub * 128, (sub + 1) * 128)
                            q_ = sub % 4
                            op("pe", lambda e: e.transpose(p_tq[q_][:, 0:128], lgw[:, ss_], ident), reads=[b_lgw, b_cm], writes=[b_pt[q_]])
                            op("dve", lambda e: e.tensor_copy(lt[:], p_tq[q_][:, 0:128]), reads=[b_pt[q_]], writes=[b_lt])
                            op("pe", lambda e: e.matmul(p_cw[:, ss_], lt[:], tri, start=True, stop=True), reads=[b_lt, b_cm], writes=[b_pcw])
                        op("act", lambda e: e.activation(Ecw[:, N_], p_cw[:, N_], AF.Exp), reads=[b_pcw], writes=[b_Ecw])
                        op("act", lambda e: e.activation(Einv[:, N_], p_cw[:, N_], AF.Exp, scale=-1.0), reads=[b_pcw], writes=[b_Einv])
                        op("dve", lambda e: e.tensor_tensor(t1[:, N_], p_cw[:, N_], lgw[:, N_], ALU.subtract), reads=[b_pcw, b_lgw], writes=[b_t1])
                        op("act", lambda e: e.activation(Eex[:, N_], t1[:, N_], AF.Exp), reads=[b_t1], writes=[b_Eex])
                        lastc = 127 if d == 0 else 0
                        for sub in range(nsub):
                            op("dve", lambda e: e.tensor_copy(wc[d][:, j, blk0 + sub:blk0 + sub + 1], Ecw[:, sub * 128 + lastc:sub * 128 + lastc + 1]),
                               reads=[b_Ecw], writes=[b_wc[d]])
                        prods = [(kk, b_kk, Eex, b_Eex, AL[d], "AL%d" % d), (bv, b_bv, Einv, b_Einv, BE[d], "BE%d" % d),
                                 (kd[d], b_kd[d], Einv, b_Einv, KA[d], "KA%d" % d)]
                        if not isctx:
                            prods.append((rs, b_sh[0], Ecw, b_Ecw, RH[d], "RH%d" % d))
                        for pi_, (x_, bx_, y_, by_, dst, nm) in enumerate(prods):
                            op("dve", lambda e: e.tensor_tensor(res[pi_][:, N_], x_[:, N_], y_[:, N_], ALU.mult), reads=[bx_, by_], writes=[b_res[pi_]])
                            if not (KX & 64):
                                fw.dma(dst[js, g0:g0 + ntok], res[pi_][:, N_], reads=[b_res[pi_]], writes=[sbuf_of(nm)], q=SQ)
                            if pi_ in (1, 2):
                                dstT = BEt[d] if pi_ == 1 else KAt[d]
                                nmT = ("BEt%d" if pi_ == 1 else "KAt%d") % d
                                for sub in range(nsub):
                                    q_ = sub % 4
                                    srcT = kk if (KX & 128) else res[pi_]
                                    op("pe", lambda e: e.transpose(p_tq[q_][:, 0:128], srcT[:, sub * 128:(sub + 1) * 128], ident),
                                       reads=[b_res[pi_], b_cm], writes=[b_pt[q_]])
                                    ta = 1 + 2 * d + (pi_ - 1)
                                    op("dve", lambda e: e.tensor_copy(tm[ta][sub][:, js], p_tq[q_][:, 0:128]), reads=[b_pt[q_]], writes=[b_tm[ta][sub]])
                    if not isctx:
                        op("dve", lambda e: e.tensor_tensor(t1[:], kd[0][:], kd[1][:], ALU.add), reads=[b_kd[0], b_kd[1]], writes=[b_t1])
                        op("dve", lambda e: e.scalar_tensor_tensor(t1[:], t1[:], kvt[:, j, 2:3], rs[:], ALU.mult, ALU.mult), reads=[b_t1, b_par, b_sh[0]], writes=[b_t1])
                        op("pe", lambda e: e.matmul(p_s[:], cm[:, C_BONES, :], t1[:], start=True, stop=True), reads=[b_cm, b_t1], writes=[b_ps])
                        op("dve", lambda e: e.tensor_tensor(res[3][:], p_s[:], vs[:], ALU.mult), reads=[b_ps, b_sh[2]], writes=[b_res[3]])
                        fw.dma(BON[j * 128:(j + 1) * 128, g0 - CTX:g0 - CTX + 512], res[3][:], reads=[b_res[3]], writes=[sbuf_of("BON")], q=SQ)
                for sub in range(nsub if not (KX & 16) else 0):
                    rows = slice(g0 + sub * 128, g0 + (sub + 1) * 128)
                    for ta, (dstT, nmT) in enumerate([(Vt, "Vt"), (BEt[0], "BEt0"), (KAt[0], "KAt0"), (BEt[1], "BEt1"), (KAt[1], "KAt1")]):
                        fw.dma(dstT[rows, :], tm[ta][sub][:], reads=[b_tm[ta][sub]], writes=[sbuf_of(nmT)], q=SQ)
        fw.barrier()

        for ph in _phase(4):
            J = 8
            aT = [sbt(ph, "aT%d" % j, [128, 128]) for j in range(J)]
            bT = [sbt(ph, "bT%d" % j, [128, 128]) for j in range(J)]
            kT = [sbt(ph, "kT%d" % j, [128, 128]) for j in range(J)]
            rT = [sbt(ph, "rT%d" % j, [128, 128]) for j in range(J)]
            btkA = sbt(ph, "btkA", [128, RWW]); ktkA = sbt(ph, "ktkA", [128, RWW]); vtkA = sbt(ph, "vtkA", [128, RWW])
            btk = [btkA[:, j * 128:(j + 1) * 128] for j in range(J)]
            ktk = [ktkA[:, j * 128:(j + 1) * 128] for j in range(J)]
            vtk = [vtkA[:, j * 128:(j + 1) * 128] for j in range(J)]
            b_tokA = [Buf(), Buf(), Buf()]
            b_in = [[Buf() for _ in range(7)] for _ in range(J)]
            Pm = [[sbt(ph, "Pm%d_%d" % (j, i), [128, 2, 128]) for i in range(2)] for j in range(J)]
            PTm = [[sbt(ph, "PTm%d_%d" % (j, i), [128, 2, 128]) for i in range(2)] for j in range(J)]
            TTm = [[sbt(ph, "TTm%d_%d" % (j, i), [128, 2, 128]) for i in range(2)] for j in range(J)]
            b_P = [[Buf(), Buf()] for _ in range(J)]; b_PT = [[Buf(), Buf()] for _ in range(J)]; b_TT = [[Buf(), Buf()] for _ in range(J)]
            AkT = [sbt(ph, "AkT%d" % j, [128, 2, 128]) for j in range(J)]
            BbT = [sbt(ph, "BbT%d" % j, [128, 2, 128]) for j in range(J)]
            BkT = [sbt(ph, "BkT%d" % j, [128, 2, 128]) for j in range(J)]
            b_AkT = [Buf() for _ in range(J)]; b_BbT = [Buf() for _ in range(J)]; b_BkT = [Buf() for _ in range(J)]
            St = [[sbt(ph, "St%d_%d" % (j, i), [128, 64]) for i in range(2)] for j in range(J)]
            b_St = [[Buf(), Buf()] for _ in range(J)]
            Rn = [sbt(ph, "Rn%d" % i, [128, 2, 64]) for i in range(2)]; b_Rn = [Buf(), Buf()]
            Us = [sbt(ph, "Us%d" % i, [128, 2, 64]) for i in range(2)]; b_Us = [Buf(), Buf()]
            stt_ = [sbt(ph, "stt%d" % i, [128, 64]) for i in range(2)]; b_stt = [Buf(), Buf()]
            Ys = [sbt(ph, "Ys%d" % i, [128, 2, 64]) for i in range(2)]; b_Ys = [Buf(), Buf()]
            Yf = [sbt(ph, "Yf%d" % i, [128, 2, 64]) for i in range(2)]; b_Yf = [Buf(), Buf()]
            cen = [sbt(ph, "cen%d" % i, [128, 2, 64]) for i in range(2)]; b_cen = [Buf(), Buf()]
            gsq = [sbt(ph, "gsq%d" % i, [128, 2, 64]) for i in range(2)]; b_gsq = [Buf(), Buf()]
            gst = [sbt(ph, "gst%d" % i, [128, 4]) for i in range(2)]; b_gst = [Buf(), Buf()]
            bon = [sbt(ph, "bon%d" % i, [128, 128]) for i in range(2)]; zr = [sbt(ph, "zr%d" % i, [128, 128]) for i in range(2)]
            b_bon = [Buf(), Buf()]; b_zr = [Buf(), Buf()]
            yo = [sbt(ph, "yo%d" % i, [128, 128]) for i in range(2)]; b_yo = [Buf(), Buf()]
            gng = sbt(ph, "gng", [128, RWW]); gnb = sbt(ph, "gnb", [128, RWW]); b_gn = Buf()
            fw.dma(gng[:], gng_b, writes=[b_gn]); fw.dma(gnb[:], gnb_b, writes=[b_gn])
            G = [pst(ph, "G%d" % i, [128, 1024]) for i in range(4)]
            b_G = [[Buf(), Buf()] for _ in range(4)]

            def gv(g, q):
                return G[g][:].rearrange("p (h q s) -> p h q s", h=2, q=4)[:, :, q, :]

            def gs(g, q):
                return G[g][:].rearrange("p (h c) -> p h c", h=2)[:, :, q * 64:(q + 1) * 64]

            KB = int(os.environ.get("KB", "99"))
            for d in range(min(2, KD)):
                m_lo = cm[:, C_GT if d == 0 else C_LT, :]
                m_up = cm[:, C_LT if d == 0 else C_GT, :]
                m_upi = cm[:, C_TF if d == 0 else C_TB, :]
                bc3 = lambda m: m.unsqueeze(1).to_broadcast([128, 2, 128])
                order = list(range(18)) if d == 0 else [1, 0] + list(range(17, 1, -1))
                cur = 0
                for j in range(J):
                    op("dve", lambda e: e.memset(St[j][0][:], 0.0), writes=[b_St[j][0]])
                for blk in order[:KB]:
                    isctx = blk < 2
                    g0 = blk * 128
                    x0 = g0 - CTX
                    tsl = slice(g0, g0 + 128)
                    fw.dma(btkA[:], BEt[d][tsl, :], reads=[sbuf_of("BEt%d" % d)], writes=[b_tokA[0]])
                    fw.dma(ktkA[:], KAt[d][tsl, :], reads=[sbuf_of("KAt%d" % d)], writes=[b_tokA[1]])
                    fw.dma(vtkA[:], Vt[tsl, :], reads=[sbuf_of("Vt")], writes=[b_tokA[2]])
                    for j in range(J):
                        js = slice(j * 128, (j + 1) * 128)
                        bi = b_in[j]
                        fw.dma(aT[j][:], AL[d][js, tsl], reads=[sbuf_of("AL%d" % d)], writes=[bi[0]])
                        fw.dma(bT[j][:], BE[d][js, tsl], reads=[sbuf_of("BE%d" % d)], writes=[bi[1]])
                        fw.dma(kT[j][:], KA[d][js, tsl], reads=[sbuf_of("KA%d" % d)], writes=[bi[2]])
                        if not isctx:
                            fw.dma(rT[j][:], RH[d][js, tsl], reads=[sbuf_of("RH%d" % d)], writes=[bi[3]])
                        bi[4], bi[5], bi[6] = b_tokA
                    for j in range(J):
                        bi = b_in[j]
                        for h in range(2):
                            hs = slice(h * 64, (h + 1) * 64)
                            op("pe", lambda e: e.matmul(gv(0, 0)[:, h, :], aT[j][hs, :], bT[j][hs, :], start=True, stop=True),
                               reads=[bi[0], bi[1]], writes=[b_G[0][h]])
                            op("pe", lambda e: e.matmul(gv(0, 1)[:, h, :], bT[j][hs, :], aT[j][hs, :], start=True, stop=True),
                               reads=[bi[0], bi[1]], writes=[b_G[0][h]])
                            op("pe", lambda e: e.matmul(gv(0, 2)[:, h, :], kT[j][hs, :], aT[j][hs, :], start=True, stop=True),
                               reads=[bi[0], bi[2]], writes=[b_G[0][h]])
                            if not isctx:
                                op("pe", lambda e: e.matmul(gv(0, 3)[:, h, :], bT[j][hs, :], rT[j][hs, :], start=True, stop=True),
                                   reads=[bi[1], bi[3]], writes=[b_G[0][h]])
                                op("pe", lambda e: e.matmul(gv(1, 0)[:, h, :], kT[j][hs, :], rT[j][hs, :], start=True, stop=True),
                                   reads=[bi[2], bi[3]], writes=[b_G[1][h]])
                        op("dve", lambda e: e.scalar_tensor_tensor(Pm[j][0][:], gv(0, 0), -1.0, bc3(m_lo), ALU.mult, ALU.mult),
                           reads=b_G[0] + [b_cm], writes=[b_P[j][0]])
                        op("dve", lambda e: e.scalar_tensor_tensor(PTm[j][0][:], gv(0, 1), -1.0, bc3(m_up), ALU.mult, ALU.mult),
                           reads=b_G[0] + [b_cm], writes=[b_PT[j][0]])
                        op("dve", lambda e: e.tensor_tensor(TTm[j][0][:], PTm[j][0][:], bc3(ident), ALU.add),
                           reads=[b_PT[j][0], b_cm], writes=[b_TT[j][0]])
                        op("dve", lambda e: e.tensor_tensor(AkT[j][:], gv(0, 2), bc3(m_up), ALU.mult),
                           reads=b_G[0] + [b_cm], writes=[b_AkT[j]])
                        if not isctx:
                            op("dve", lambda e: e.tensor_tensor(BbT[j][:], gv(0, 3), bc3(m_upi), ALU.mult),
                               reads=b_G[0] + [b_cm], writes=[b_BbT[j]])
                            op("dve", lambda e: e.tensor_tensor(BkT[j][:], gv(1, 0), bc3(m_upi), ALU.mult),
                               reads=b_G[1] + [b_cm], writes=[b_BkT[j]])
                    for i in range(1, 7):
                        a_, n_ = (i - 1) % 2, i % 2
                        for j in range(J):
                            cg = j % 2
                            for h in range(2):
                                op("pe", lambda e: e.matmul(gv(cg, 1)[:, h, :], PTm[j][a_][:, h, :], Pm[j][a_][:, h, :], start=True, stop=True),
                                   reads=[b_P[j][a_], b_PT[j][a_]], writes=[b_G[cg][h]])
                                if i < 6:
                                    op("pe", lambda e: e.matmul(gv(cg, 2)[:, h, :], Pm[j][a_][:, h, :], PTm[j][a_][:, h, :], start=True, stop=True),
                                       reads=[b_P[j][a_], b_PT[j][a_]], writes=[b_G[cg][h]])
                            op("dve", lambda e: e.tensor_copy(Pm[j][n_][:], gv(cg, 1)), reads=b_G[cg], writes=[b_P[j][n_]])
                            if i < 6:
                                op("dve", lambda e: e.tensor_copy(PTm[j][n_][:], gv(cg, 2)), reads=b_G[cg], writes=[b_PT[j][n_]])
                            for h in range(2):
                                op("pe", lambda e: e.matmul(gv(cg, 3)[:, h, :], Pm[j][n_][:, h, :], TTm[j][a_][:, h, :], start=True, stop=True),
                                   reads=[b_P[j][n_], b_TT[j][a_]], writes=[b_G[cg][h]])
                            op("dve", lambda e: e.tensor_tensor(TTm[j][n_][:], gv(cg, 3), TTm[j][a_][:], ALU.add),
                               reads=b_G[cg] + [b_TT[j][a_]], writes=[b_TT[j][n_]])
                    TTf = 0
                    nxt = 1 - cur
                    for j in range(J):
                        pr = j % 2
                        bi = b_in[j]
                        js = slice(j * 128, (j + 1) * 128)
                        sg, tg = (2, 3) if j % 2 == 0 else (0, 1)
                        Rv, Uv, Yv = gs(sg, 0), gs(sg, 1), gs(sg, 2)
                        for h in range(2):
                            hs = slice(h * 64, (h + 1) * 64)
                            op("pe", lambda e: e.matmul(Rv[:, h, :], aT[j][hs, :], St[j][cur][hs, :], start=True, stop=False),
                               reads=[bi[0], b_St[j][cur]], writes=[b_G[sg][h]])
                            op("pe", lambda e: e.matmul(Rv[:, h, :], AkT[j][:, h, :], vtk[j][:, hs], start=False, stop=True),
                               reads=[b_AkT[j], bi[6]], writes=[b_G[sg][h]])
                        op("dve", lambda e: e.tensor_scalar(Rn[pr][:], Rv, -1.0, None, ALU.mult), reads=b_G[sg], writes=[b_Rn[pr]])
                        for h in range(2):
                            op("pe", lambda e: e.matmul(Uv[:, h, :], TTm[j][TTf][:, h, :], Rn[pr][:, h, :], start=True, stop=True),
                               reads=[b_TT[j][TTf], b_Rn[pr]], writes=[b_G[sg][h]])
                        op("dve", lambda e: e.tensor_copy(Us[pr][:], Uv), reads=b_G[sg], writes=[b_Us[pr]])
                        SSv = G[tg][:, 0:128]
                        op("pe", lambda e: e.matmul(SSv, btk[j], Us[pr][:].rearrange("p h v -> p (h v)"), start=True, stop=False),
                           reads=[bi[4], b_Us[pr]], writes=[b_G[tg][0]])
                        op("pe", lambda e: e.matmul(SSv, ktk[j], vtk[j], start=False, stop=True),
                           reads=[bi[5], bi[6]], writes=[b_G[tg][0]])
                        for h in range(2):
                            hs = slice(h * 64, (h + 1) * 64)
                            op("dve", lambda e: e.tensor_tensor(stt_[pr][hs, :], SSv[hs, h * 64:(h + 1) * 64], St[j][cur][hs, :], ALU.add),
                               reads=[b_G[tg][0], b_St[j][cur]], writes=[b_stt[pr]])
                        op("dve", lambda e: e.tensor_scalar(St[j][nxt][:], stt_[pr][:], wc[d][:, j, blk:blk + 1], None, ALU.mult),
                           reads=[b_stt[pr], b_wc[d]], writes=[b_St[j][nxt]])
                        if isctx:
                            continue
                        for h in range(2):
                            hs = slice(h * 64, (h + 1) * 64)
                            op("pe", lambda e: e.matmul(Yv[:, h, :], rT[j][hs, :], St[j][cur][hs, :], start=True, stop=False),
                               reads=[bi[3], b_St[j][cur]], writes=[b_G[sg][h]])
                            op("pe", lambda e: e.matmul(Yv[:, h, :], BbT[j][:, h, :], Us[pr][:, h, :], start=False, stop=False),
                               reads=[b_BbT[j], b_Us[pr]], writes=[b_G[sg][h]])
                            op("pe", lambda e: e.matmul(Yv[:, h, :], BkT[j][:, h, :], vtk[j][:, hs], start=False, stop=True),
                               reads=[b_BkT[j], bi[6]], writes=[b_G[sg][h]])
                        if d == 0:
                            op("dve", lambda e: e.tensor_copy(Ys[pr][:], Yv), reads=b_G[sg], writes=[b_Ys[pr]])
                            fw.dma(YF[x0:x0 + 128, js], Ys[pr][:].rearrange("p h v -> p (h v)"), reads=[b_Ys[pr]], writes=[sbuf_of("YF")], q=SQ)
                        else:
                            fw.dma(Yf[pr][:].rearrange("p h v -> p (h v)"), YF[x0:x0 + 128, js], reads=[sbuf_of("YF")], writes=[b_Yf[pr]])
                            fw.dma(bon[pr][:], BON[js, x0:x0 + 128], reads=[sbuf_of("BON")], writes=[b_bon[pr]])
                            fw.dma(zr[pr][:], RWfm[(26 + j) * 128:(27 + j) * 128, g0:g0 + 128], reads=[sbuf_of("RWfm")], writes=[b_zr[pr]])
                            op("dve", lambda e: e.tensor_tensor(Ys[pr][:], Yv, Yf[pr][:], ALU.add), reads=b_G[sg] + [b_Yf[pr]], writes=[b_Ys[pr]])
                            g_ = gst[pr]
                            op("dve", lambda e: e.tensor_reduce(g_[:, 0:2], Ys[pr][:], AX.X, ALU.add), reads=[b_Ys[pr]], writes=[b_gst[pr]])
                            op("dve", lambda e: e.tensor_scalar(g_[:, 0:2], g_[:, 0:2], -1.0 / 64, None, ALU.mult), reads=[b_gst[pr]], writes=[b_gst[pr]])
                            op("dve", lambda e: e.tensor_tensor(cen[pr][:], Ys[pr][:], g_[:, 0:2].unsqueeze(2).to_broadcast([128, 2, 64]), ALU.add),
                               reads=[b_Ys[pr], b_gst[pr]], writes=[b_cen[pr]])
                            op("dve", lambda e: e.tensor_tensor(gsq[pr][:], cen[pr][:], cen[pr][:], ALU.mult), reads=[b_cen[pr]], writes=[b_gsq[pr]])
                            op("dve", lambda e: e.tensor_reduce(g_[:, 2:4], gsq[pr][:], AX.X, ALU.add), reads=[b_gsq[pr]], writes=[b_gst[pr]])
                            op("act", lambda e: e.activation(g_[:, 2:4], g_[:, 2:4], AF.Sqrt, bias=EPSGN, scale=1.0 / 64), reads=[b_gst[pr], b_cst], writes=[b_gst[pr]])
                            op("dve", lambda e: e.reciprocal(g_[:, 2:4], g_[:, 2:4]), reads=[b_gst[pr]], writes=[b_gst[pr]])
                            op("dve", lambda e: e.tensor_tensor(cen[pr][:], cen[pr][:], g_[:, 2:4].unsqueeze(2).to_broadcast([128, 2, 64]), ALU.mult),
                               reads=[b_cen[pr], b_gst[pr]], writes=[b_cen[pr]])
                            cf = cen[pr][:].rearrange("p h v -> p (h v)")
                            op("dve", lambda e: e.tensor_tensor(cf, cf, gng[:, js], ALU.mult), reads=[b_cen[pr], b_gn], writes=[b_cen[pr]])
                            op("dve", lambda e: e.tensor_tensor(cf, cf, gnb[:, js], ALU.add), reads=[b_cen[pr], b_gn], writes=[b_cen[pr]])
                            yTv = G[tg][:, 512:640]
                            op("pe", lambda e: e.transpose(yTv, cf, ident), reads=[b_cen[pr], b_cm], writes=[b_G[tg][1]])
                            op("dve", lambda e: e.tensor_tensor(yo[pr][:], yTv, bon[pr][:], ALU.add), reads=[b_G[tg][1], b_bon[pr]], writes=[b_yo[pr]])
                            op("dve", lambda e: e.tensor_tensor(yo[pr][:], yo[pr][:], zr[pr][:], ALU.mult), reads=[b_yo[pr], b_zr[pr]], writes=[b_yo[pr]])
                            fw.dma(YR[js, x0:x0 + 128], yo[pr][:], reads=[b_yo[pr]], writes=[sbuf_of("YR")], q=SQ)
                    cur = nxt
        fw.barrier()

        for ph in _phase(5):
            TT_ = 256
            yh = sbt(ph, "yh", [128, 8, TT_]); yr = sbt(ph, "yr", [128, 8, TT_]); b_yh = Buf(); b_yr = Buf()
            gh = [sbt(ph, "gh%d" % i, [128, 4, TT_]) for i in range(2)]; gr = [sbt(ph, "gr%d" % i, [128, 4, TT_]) for i in range(2)]
            b_gh = [Buf(), Buf()]; b_gr = [Buf(), Buf()]
            mT = sbt(ph, "mT", [128, 16, TT_]); b_mT = Buf()
            whg = [sbt(ph, "whg0", [128, 8, 512])] * 2; wrw = [sbt(ph, "wrw0", [128, 8, 512])] * 2
            b_whg = [Buf()] * 2; b_wrw = [Buf()] * 2
            wo = [sbt(ph, "wo0", [128, 16, 512])] * 2; b_wo = [Buf()] * 2
            xr = [sbt(ph, "xr%d" % i, [128, D]) for i in range(2)]; b_xr = [Buf(), Buf()]
            xn = [sbt(ph, "xn%d" % i, [128, D]) for i in range(2)]; b_xn = [Buf(), Buf()]
            junk3 = sbt(ph, "junk3", [128, D]); b_j3 = Buf()
            ss3 = sbt(ph, "ss3", [128, 2]); b_ss3 = Buf()
            fgt_ = sbt(ph, "fgt_", [128, D]); b_fg = Buf()
            tmpm = [sbt(ph, "tmpm%d" % i, [128, TT_]) for i in range(2)]; b_tmpm = [Buf(), Buf()]
            pp1 = [pst(ph, "pp1_%d" % i, [128, 512]) for i in range(2)]; pp2 = [pst(ph, "pp2_%d" % i, [128, 512]) for i in range(2)]
            b_pp1 = [Buf(), Buf()]; b_pp2 = [Buf(), Buf()]
            po = [pst(ph, "po%d" % i, [128, 512]) for i in range(4)]; b_po = [Buf() for _ in range(4)]
            fw.dma(fgt_[:], fg_b, writes=[b_fg])
            whg_r = w_hg_o.rearrange("(k p) c -> p k c", p=128)
            wrw_r = w_rw_o.rearrange("(k p) c -> p k c", p=128)
            wo_r = w_o.rearrange("(k p) c -> p k c", p=128)
            YH_r = YH.rearrange("(k p) t -> p k t", p=128)
            YR_r = YR.rearrange("(k p) t -> p k t", p=128)
            G_r = RWfm[(74 - 40) * 128:, :].rearrange("(k p) t -> p k t", p=128)
            wi = 0
            woi = 0
            xi = 0
            for tt in range(SEQ // TT_):
                x0 = tt * TT_
                g0 = x0 + CTX
                fw.dma3(yh[:], YH_r[:, :, x0:x0 + TT_], 8, reads=[sbuf_of("YH")], writes=[b_yh])
                fw.dma3(yr[:], YR_r[:, :, x0:x0 + TT_], 8, reads=[sbuf_of("YR")], writes=[b_yr])
                for mg in range(4):
                    wb = wi % 2
                    wi += 1
                    cs = slice(mg * 512, (mg + 1) * 512)
                    fw.dma3(whg[wb][:], whg_r[:, :, cs], 8, writes=[b_whg[wb]])
                    fw.dma3(wrw[wb][:], wrw_r[:, :, cs], 8, writes=[b_wrw[wb]])
                    fw.dma3(gh[wb][:], G_r[:, mg * 4:(mg + 1) * 4, g0:g0 + TT_], 4, reads=[sbuf_of("RWfm")], writes=[b_gh[wb]])
                    fw.dma3(gr[wb][:], G_r[:, 16 + mg * 4:16 + (mg + 1) * 4, g0:g0 + TT_], 4, reads=[sbuf_of("RWfm")], writes=[b_gr[wb]])
                    for mm in range(4):
                        m = mg * 4 + mm
                        a = mm % 2
                        for k in range(8):
                            op("pe", lambda e: e.matmul(pp1[a][:, :TT_], whg[wb][:, k, mm * 128:(mm + 1) * 128], yh[:, k, :], start=(k == 0), stop=(k == 7)),
                               reads=[b_whg[wb], b_yh], writes=[b_pp1[a]], inc=(k == 7))
                        for k in range(8):
                            op("pe", lambda e: e.matmul(pp2[a][:, :TT_], wrw[wb][:, k, mm * 128:(mm + 1) * 128], yr[:, k, :], start=(k == 0), stop=(k == 7)),
                               reads=[b_wrw[wb], b_yr], writes=[b_pp2[a]], inc=(k == 7))
                        op("dve", lambda e: e.tensor_tensor(tmpm[a][:], pp1[a][:, :TT_], gh[wb][:, mm, :], ALU.mult), reads=[b_pp1[a], b_gh[wb]], writes=[b_tmpm[a]])
                        op("dve", lambda e: e.tensor_tensor(mT[:, m, :], pp2[a][:, :TT_], gr[wb][:, mm, :], ALU.mult), reads=[b_pp2[a], b_gr[wb]], writes=[b_mT])
                        op("dve", lambda e: e.tensor_tensor(mT[:, m, :], mT[:, m, :], tmpm[a][:], ALU.add), reads=[b_mT, b_tmpm[a]], writes=[b_mT])
                for sub in range(TT_ // 128):
                    xb = xi % 2
                    xi += 1
                    fw.dma(xr[xb][:], xc[g0 + sub * 128:g0 + (sub + 1) * 128, :], writes=[b_xr[xb]])
                for n in range(4):
                    ob_ = woi % 2
                    woi += 1
                    fw.dma3(wo[ob_][:], wo_r[:, :, n * 512:(n + 1) * 512], 16, writes=[b_wo[ob_]])
                    for sub in range(TT_ // 128):
                        xb = (xi - (TT_ // 128) + sub) % 2
                        a = (n * 2 + sub) % 4
                        for k in range(16):
                            op("pe", lambda e: e.matmul(po[a][:], mT[:, k, sub * 128:(sub + 1) * 128], wo[ob_][:, k, :], start=(k == 0), stop=(k == 15)),
                               reads=[b_mT, b_wo[ob_]], writes=[b_po[a]], inc=(k == 15))
                        ns = slice(n * 512, (n + 1) * 512)
                        op("dve", lambda e: e.tensor_tensor(xn[xb][:, ns], po[a][:], gate_b[:, ns], ALU.mult), reads=[b_po[a], b_gate], writes=[b_xn[xb]])
                        op("dve", lambda e: e.tensor_tensor(xn[xb][:, ns], xn[xb][:, ns], xr[xb][:, ns], ALU.add), reads=[b_xn[xb], b_xr[xb]], writes=[b_xn[xb]])
                for sub in range(TT_ // 128):
                    xb = (xi - (TT_ // 128) + sub) % 2
                    op("act", lambda e: e.activation(junk3[:], xn[xb][:], AF.Square, accum_out=ss3[:, 0:1]), reads=[b_xn[xb]], writes=[b_j3, b_ss3])
                    op("act", lambda e: e.activation(ss3[:, 1:2], ss3[:, 0:1], AF.Sqrt, bias=EPS6, scale=1.0 / D), reads=[b_ss3, b_cst], writes=[b_ss3])
                    op("dve", lambda e: e.reciprocal(ss3[:, 1:2], ss3[:, 1:2]), reads=[b_ss3], writes=[b_ss3])
                    op("dve", lambda e: e.scalar_tensor_tensor(xn[xb][:], xn[xb][:], ss3[:, 1:2], fgt_[:], ALU.mult, ALU.mult),
                       reads=[b_xn[xb], b_ss3, b_fg], writes=[b_xn[xb]])
                    fw.dma(out[x0 + sub * 128:x0 + (sub + 1) * 128, :], xn[xb][:], reads=[b_xn[xb]], writes=[sbuf_of("out")], q=SQ)
        fw.barrier()
    print("bass program built: %d instructions" % fw.ninstr, flush=True)
    dbg_names = ["HGtok", "RWfm", "OFs", "YH", "AL0", "BE0", "KA0", "RH0", "AL1", "BE1", "KA1", "RH1", "BEt0", "KAt0", "Vt", "BON", "YF", "YR"]
    return nc, dbg_names


def _host_inputs(b, inp):
    f = lambda a: np.ascontiguousarray(a, dtype=np.float32)
    fm = lambda v: f(np.asarray(v).reshape(-1, 128).T)
    bc = lambda v: f(np.broadcast_to(np.asarray(v).reshape(1, -1), (128, np.asarray(v).size)))
    m = {}
    m["xc"] = f(np.concatenate([inp["ctx"][b], inp["x"][b]], axis=0))
    m["cc"] = f(np.stack([fm(inp["c"][b]), fm(inp["c_ctx"])], axis=-1))
    m["ada_w"] = f(inp["ada_w"][0].reshape(16, 128, 3 * D))
    m["ada_b_fm"] = fm(inp["ada_b"][0])
    m["ada_b_g"] = f(inp["ada_b"][0][2 * D:].reshape(1, D))
    m["norm_g_fm"] = fm(inp["norm_g"][0])
    m["w_in"] = f(inp["w_in"][0])
    m["hg_lb_b"] = f(np.broadcast_to(inp["hg_lb"][None], (128, 2, 2, HGW)))
    m["hgng_b"] = bc(inp["hg_norm_g"][0])
    mu = inp["rw_mu"][0]
    m["mu_fm"] = f(mu.reshape(4, 26, 128).transpose(2, 1, 0))
    m["w0_fm"] = f(inp["rw_w0"][0].reshape(2, 8, 128).transpose(2, 1, 0))
    m["a0_fm"] = f(inp["rw_a0"][0].reshape(2, 8, 128).transpose(2, 1, 0))
    m["w2"] = f(inp["rw_w2"][0].reshape(128, RWW))
    m["a2"] = f(inp["rw_a2"][0].reshape(128, RWW))
    kv = np.stack([inp["rw_kk"][0], inp["rw_ka"][0], inp["rw_rk"][0]], axis=0)
    m["kvec_fm"] = f(kv.reshape(3, 8, 128).transpose(2, 1, 0))
    m["gng_b"] = bc(inp["rw_gn_g"][0])
    m["gnb_b"] = bc(inp["rw_gn_b"][0])
    m["w_hg_o"] = f(inp["w_hg_out"][0])
    m["w_rw_o"] = f(inp["w_rw_out"][0])
    m["w_o"] = f(inp["w_out"][0])
    m["fg_b"] = bc(inp["final_g"])
    m["cm"] = make_cm()
    cst = np.zeros((128, 8), np.float32)
    cst[:, 0] = 1e-6
    cst[:, 1] = 1e-12
    cst[:, 2] = 64e-5
    cst[:, 3] = 0.0
    cst[:, 4] = 1.0
    m["cst"] = cst
    return m


_LAST = {}


def kernel(**inputs):
    inp = {k: np.asarray(v) for k, v in inputs.items()}
    nb = inp["x"].shape[0]
    nc, dbg = build_program()
    in_maps = [_host_inputs(b, inp) for b in range(nb)]
    res = run_bass_kernel_spmd(nc, in_maps, core_ids=list(range(nb)))
    if DEBUG:
        _LAST["res"] = res
    return np.stack([np.asarray(r["out"], dtype=np.float32) for r in res.results], axis=0)
```

```python
import os
import numpy as np
from contextlib import ExitStack
import concourse.bass as bass
import concourse.mybir as mybir
from concourse.bass_utils import run_bass_kernel_spmd

F32 = mybir.dt.float32
BF16 = mybir.dt.bfloat16
AF = mybir.ActivationFunctionType
ALU = mybir.AluOpType
AX = mybir.AxisListType

D = 2048
SEQ = 2048
CTX = 256
NT = SEQ + CTX
NCOLS = 13568
HGW = 1024
RWW = 1024
CH = 32
DEBUG = bool(os.environ.get("KDEBUG"))
PH = int(os.environ.get("KPHASE", "9"))
SQ = os.environ.get("KSQ", "pool")
ONLY = int(os.environ.get("KONLY", "-1"))
KLIM = int(os.environ.get("KLIM", "-1"))
CASTE = os.environ.get("KCAST", "pool")
KCH = int(os.environ.get("KCH", "99"))
KX = int(os.environ.get("KX", "0"))
KD = int(os.environ.get("KD", "2"))
KT = int(os.environ.get("KT", "9"))
KJ = int(os.environ.get("KJ", "8"))
KT0 = int(os.environ.get("KT0", "0"))


_FWREF = []


def _phase(n):
    if PH >= n and (ONLY < 0 or n == ONLY):
        fw = _FWREF[-1]
        fw.emitted = 0
        fw.limit = KLIM if (n == ONLY and KLIM >= 0) else None
        with ExitStack() as ph:
            yield ph
        print('phase', n, 'emitted', fw.emitted, flush=True)
        fw.limit = None


class Buf:
    __slots__ = ("name", "w", "r")

    def __init__(self, name=""):
        self.name = name
        self.w = None
        self.r = {}


class FW:
    NDMA = 14

    def __init__(self, nc, stack):
        self.nc = nc
        self.eng = {"pe": nc.tensor, "act": nc.scalar, "dve": nc.vector, "pool": nc.gpsimd, "sp": nc.sync}
        self.sem = {}
        self.cnt = {}
        for k in ["pe", "act", "dve", "pool"]:
            self.sem[k] = stack.enter_context(nc.semaphore("s_" + k))
            self.cnt[k] = 0
        for i in range(self.NDMA):
            k = "dma%d" % i
            self.sem[k] = stack.enter_context(nc.semaphore("s_" + k))
            self.cnt[k] = 0
        self.dma_i = 0
        self.waited = {e: {} for e in self.eng}
        self.ninstr = 0
        self.emitted = 0
        self.limit = None

    def _need(self, e, deps):
        for k, v in deps.items():
            if self.waited[e].get(k, 0) >= v:
                continue
            self.eng[e].wait_ge(self.sem[k], v)
            self.waited[e][k] = v

    def _collect(self, e, reads, writes):
        deps = {}

        def add(ev):
            if ev is None:
                return
            k, v = ev
            if k == e and e == "pe":
                return
            if deps.get(k, 0) < v:
                deps[k] = v
        for b in reads:
            add(b.w)
        for b in writes:
            add(b.w)
            for k, v in b.r.items():
                add((k, v))
        return deps

    def _mark(self, ev, reads, writes):
        k, v = ev
        for b in reads:
            if b.r.get(k, 0) < v:
                b.r[k] = v
        for b in writes:
            b.w = ev
            b.r = {}

    def _skip(self):
        self.emitted += 1
        return self.limit is not None and self.emitted > self.limit

    def op(self, e, fn, reads=(), writes=(), inc=True):
        if self._skip():
            return None
        deps = self._collect(e, reads, writes)
        self._need(e, deps)
        ins = fn(self.eng[e])
        self.ninstr += 1
        if inc:
            self.cnt[e] += 1
            ins.then_inc(self.sem[e], 1)
            ev = (e, self.cnt[e])
        else:
            ev = (e, self.cnt[e] + 1)
        self._mark(ev, reads, writes)
        return ins

    def dma(self, out, in_, reads=(), writes=(), q="sp", **kw):
        if self._skip():
            return
        i = self.dma_i
        self.dma_i += 1
        k = "dma%d" % (i % self.NDMA)
        deps = self._collect(q, reads, writes)
        if self.cnt[k] > 0 and deps.get(k, 0) < self.cnt[k]:
            deps[k] = self.cnt[k]
        self._need(q, deps)
        self.cnt[k] += 16
        self.eng[q].dma_start(out=out, in_=in_, **kw).then_inc(self.sem[k], 16)
        self.ninstr += 1
        self._mark((k, self.cnt[k]), reads, writes)

    def dma3(self, out, in_, n, **kw):
        for k in range(n):
            self.dma(out[:, k], in_[:, k], **kw)

    def barrier(self):
        for e in self.eng:
            deps = {k: v for k, v in self.cnt.items() if v > 0 and k != e}
            self._need(e, deps)


C_ID, C_TF, C_TB, C_BONES, C_LT, C_GT = 0, 1, 2, 3, 4, 5
NCM = 6


def make_cm():
    p = np.arange(128)[:, None]
    f = np.arange(128)[None, :]
    cm = np.zeros((128, NCM, 128), np.float32)
    cm[:, C_ID] = (p == f)
    cm[:, C_TF] = (p <= f)
    cm[:, C_TB] = (p >= f)
    cm[:, C_BONES] = ((p // 64) == (f // 64))
    cm[:, C_LT] = (p < f)
    cm[:, C_GT] = (p > f)
    return cm


def build_program():
    nc = bass.Bass("TRN2", target_bir_lowering=False)
    dt = lambda name, shape, kind="ExternalInput": nc.dram_tensor(name, shape, F32, kind=kind).ap()
    SCR = "ExternalOutput" if DEBUG else "Internal"
    xc = dt("xc", [NT, D])
    cc_d = dt("cc", [128, 16, 2])
    ada_w = dt("ada_w", [16, 128, 3 * D])
    ada_b_fm = dt("ada_b_fm", [128, 48])
    ada_b_g = dt("ada_b_g", [1, D])
    norm_g_fm = dt("norm_g_fm", [128, 16])
    w_in = dt("w_in", [D, NCOLS])
    hg_lb_b = dt("hg_lb_b", [128, 2, 2, HGW])
    hgng_b = dt("hgng_b", [128, HGW])
    mu_fm = dt("mu_fm", [128, 26, 4])
    w0_fm = dt("w0_fm", [128, 8, 2])
    a0_fm = dt("a0_fm", [128, 8, 2])
    w2_d = dt("w2", [128, RWW])
    a2_d = dt("a2", [128, RWW])
    kvec_fm = dt("kvec_fm", [128, 8, 3])
    gng_b = dt("gng_b", [128, RWW])
    gnb_b = dt("gnb_b", [128, RWW])
    w_hg_o = dt("w_hg_o", [HGW, D])
    w_rw_o = dt("w_rw_o", [RWW, D])
    w_o = dt("w_o", [D, D])
    fg_b = dt("fg_b", [128, D])
    cm_d = dt("cm", [128, NCM, 128])
    cst_d = dt("cst", [128, 8])
    out = dt("out", [SEQ, D], kind="ExternalOutput")
    HGtok = dt("HGtok", [NT, 5120], SCR)
    RWfm = dt("RWfm", [NCOLS - 5120, NT], SCR)
    OFs = dt("OFs", [SEQ, HGW], SCR)
    YH = dt("YH", [HGW, SEQ], SCR)
    AL = [dt("AL%d" % d, [RWW, NT], SCR) for d in range(2)]
    BE = [dt("BE%d" % d, [RWW, NT], SCR) for d in range(2)]
    KA = [dt("KA%d" % d, [RWW, NT], SCR) for d in range(2)]
    RH = [dt("RH%d" % d, [RWW, NT], SCR) for d in range(2)]
    BEt = [dt("BEt%d" % d, [NT, RWW], SCR) for d in range(2)]
    KAt = [dt("KAt%d" % d, [NT, RWW], SCR) for d in range(2)]
    Vt = dt("Vt", [NT, RWW], SCR)
    BON = dt("BON", [RWW, SEQ], SCR)
    YF = dt("YF", [SEQ, RWW], SCR)
    YR = dt("YR", [RWW, SEQ], SCR)
    scr_bufs = {}

    def sbuf_of(name):
        if name not in scr_bufs:
            scr_bufs[name] = Buf(name)
        return scr_bufs[name]

    with ExitStack() as st:
        fw = FW(nc, st)
        _FWREF.append(fw)
        _acct = {}

        def sbt(stack, name, shape):
            _acct[id(stack)] = _acct.get(id(stack), 0) + int(np.prod(shape[1:])) * 4
            if os.environ.get("KACCT"):
                print("sbuf", name, shape, "stack total KiB", _acct[id(stack)] / 1024.0, flush=True)
            return stack.enter_context(nc.sbuf_tensor("sb_" + name, shape, F32))
        pst = lambda stack, name, shape: stack.enter_context(nc.psum_tensor("ps_" + name, shape, F32))
        op = fw.op
        cm = sbt(st, "cm", [128, NCM, 128]); b_cm = Buf()
        cst = sbt(st, "cst", [128, 8]); b_cst = Buf()
        modA = sbt(st, "modA", [128, 16, 2]); modB = sbt(st, "modB", [128, 16, 2]); b_mod = Buf()
        gate_b = sbt(st, "gate_b", [128, D]); b_gate = Buf()
        wc = [sbt(st, "wc%d" % d, [128, 8, 18]) for d in range(2)]; b_wc = [Buf(), Buf()]
        fw.dma(cm[:], cm_d, writes=[b_cm])
        fw.dma(cst[:], cst_d, writes=[b_cst])
        ident = cm[:, C_ID, :]
        EPS6, EPS12, EPSGN, ZERO, ONE = (cst[:, i:i + 1] for i in range(5))

        for ph in _phase(0):
            adw = [sbt(ph, "adw%d" % i, [128, 3 * D]) for i in range(2)]; b_adw = [Buf(), Buf()]
            cct = sbt(ph, "cct", [128, 16, 2]); scc = sbt(ph, "scc", [128, 16, 2]); b_cc = Buf(); b_scc = Buf()
            mod = sbt(ph, "mod", [128, 48, 2]); b_modt = Buf()
            adb = sbt(ph, "adb", [128, 48]); ng = sbt(ph, "ng", [128, 16]); b_sm = Buf()
            adbg = sbt(ph, "adbg", [1, D]); grow = sbt(ph, "grow", [1, D]); b_grow = Buf()
            ps_mod = pst(ph, "ps_mod", [128, 96]); b_psm = Buf()
            ps_g = [pst(ph, "ps_g%d" % i, [128, 512]) for i in range(4)]; b_psg = [Buf() for _ in range(4)]
            fw.dma(cct[:], cc_d, writes=[b_cc])
            fw.dma(adb[:], ada_b_fm, writes=[b_sm])
            fw.dma(ng[:], norm_g_fm, writes=[b_sm])
            fw.dma(adbg[:], ada_b_g, writes=[b_sm])
            op("act", lambda e: e.activation(scc[:], cct[:], AF.Silu), reads=[b_cc], writes=[b_scc])
            op("dve", lambda e: e.memset(mod[:], 0.0), writes=[b_modt])
            for k in range(16):
                fw.dma(adw[k % 2][:], ada_w[k], writes=[b_adw[k % 2]])
                for m in range(48):
                    op("pe", lambda e: e.matmul(ps_mod[:, 2 * m:2 * m + 2], adw[k % 2][:, m * 128:(m + 1) * 128],
                                                scc[:, k, :], start=True, stop=True),
                       reads=[b_adw[k % 2], b_scc], writes=[b_psm], inc=(m == 47))
                op("dve", lambda e: e.tensor_tensor(mod[:], mod[:], ps_mod[:].rearrange("p (m v) -> p m v", v=2), ALU.add),
                   reads=[b_psm, b_modt], writes=[b_modt])
                for n in range(4):
                    op("pe", lambda e: e.matmul(ps_g[n][0:1, :], scc[:, k, 0:1], adw[k % 2][:, 2 * D + n * 512:2 * D + (n + 1) * 512],
                                                start=(k == 0), stop=(k == 15)),
                       reads=[b_adw[k % 2], b_scc], writes=[b_psg[n]])
            op("dve", lambda e: e.tensor_tensor(mod[:], mod[:], adb[:].unsqueeze(2).to_broadcast([128, 48, 2]), ALU.add),
               reads=[b_sm, b_modt], writes=[b_modt])
            op("dve", lambda e: e.tensor_scalar(modA[:], mod[:, 16:32, :], 1.0, None, ALU.add), reads=[b_modt], writes=[b_mod])
            op("dve", lambda e: e.tensor_tensor(modA[:], modA[:], ng[:].unsqueeze(2).to_broadcast([128, 16, 2]), ALU.mult),
               reads=[b_sm, b_mod], writes=[b_mod])
            op("dve", lambda e: e.tensor_copy(modB[:], mod[:, 0:16, :]), reads=[b_modt], writes=[b_mod])
            for n in range(4):
                op("dve", lambda e: e.tensor_tensor(grow[:, n * 512:(n + 1) * 512], ps_g[n][0:1, :], adbg[:, n * 512:(n + 1) * 512], ALU.add),
                   reads=[b_psg[n], b_sm], writes=[b_grow])
            for n in range(4):
                op("pe", lambda e: e.matmul(ps_g[n][:], cm[0:1, C_TF, :], grow[:, n * 512:(n + 1) * 512], start=True, stop=True),
                   reads=[b_cm, b_grow], writes=[b_psg[n]])
                op("act", lambda e: e.activation(gate_b[:, n * 512:(n + 1) * 512], ps_g[n][:], AF.Identity, scale=1.0),
                   reads=[b_psg[n]], writes=[b_gate])
        fw.barrier()

        for ph in _phase(1):
            lb_b = sbt(ph, "lb_b", [128, 2, HGW]); oml_b = sbt(ph, "oml_b", [128, 2, HGW])
            b_lb = Buf()
            xt = [sbt(ph, "xt%d" % i, [128, D]) for i in range(2)]; b_xt = [Buf(), Buf()]
            junk = sbt(ph, "junk", [128, D]); b_junk = Buf()
            ss = sbt(ph, "ss", [128, 2]); b_ss = Buf()
            hT = ph.enter_context(nc.sbuf_tensor("sb_hT", [128, 16, 1024], BF16)); b_hT = [Buf() for _ in range(8)]
            wt = [sbt(ph, "wt0", [128, 16, 512])] * 2; b_wt = [Buf()] * 2
            wb16 = [ph.enter_context(nc.sbuf_tensor("sb_wb16_%d" % i, [128, 16, 512], BF16)) for i in range(2)]; b_wb16 = [Buf(), Buf()]
            lbt = wt[1][:, 0:8, :].rearrange("p a b -> p (a b)").rearrange("p (d l c) -> p d l c", d=2, l=2)
            b_lbt = b_wt[1]
            NOT = 4
            ot = [sbt(ph, "ot%d" % i, [128, 512]) for i in range(NOT)]; b_ot = [Buf() for _ in range(NOT)]
            tps = [pst(ph, "tps%d" % i, [128, 512]) for i in range(4)]; b_tps = [Buf() for _ in range(4)]
            acc = [pst(ph, "acc%d" % i, [128, 512]) for i in range(4)]; b_acc = [Buf() for _ in range(4)]
            fw.dma(lbt, hg_lb_b, writes=[b_lbt])
            op("dve", lambda e: e.tensor_tensor(lb_b[:], lbt[:, :, 0, :], lbt[:, :, 1, :], ALU.subtract), reads=[b_lbt], writes=[b_lb])
            op("act", lambda e: e.activation(lb_b[:], lb_b[:], AF.Sigmoid), reads=[b_lb], writes=[b_lb])
            op("dve", lambda e: e.tensor_scalar(oml_b[:], lb_b[:], -1.0, 1.0, ALU.mult, ALU.add), reads=[b_lb], writes=[b_lb])
            w_in_r = w_in.rearrange("(k p) c -> p k c", p=128)
            oti = 0
            ctx_groups = {2, 3, 4, 5, 6, 7, 12, 13, 14, 15, 16}
            tiles = [(0, 256, True)] + [(256 + 1024 * i, 1024, False) for i in range(2)]
            wti = 0
            for (g0, ntok, isctx) in tiles:
                v = 1 if isctx else 0
                nsub = ntok // 128
                for sub in range(nsub):
                    xb = sub % 2
                    fw.dma(xt[xb][:], xc[g0 + sub * 128:g0 + (sub + 1) * 128, :], writes=[b_xt[xb]])
                    op("act", lambda e: e.activation(junk[:], xt[xb][:], AF.Square, accum_out=ss[:, 0:1]),
                       reads=[b_xt[xb]], writes=[b_junk, b_ss])
                    op("act", lambda e: e.activation(ss[:, 1:2], ss[:, 0:1], AF.Sqrt, bias=EPS6, scale=1.0 / D),
                       reads=[b_ss, b_cst], writes=[b_ss])
                    op("dve", lambda e: e.reciprocal(ss[:, 1:2], ss[:, 1:2]), reads=[b_ss], writes=[b_ss])
                    op("act", lambda e: e.activation(junk[:], xt[xb][:], AF.Identity, scale=ss[:, 1:2], bias=ZERO),
                       reads=[b_xt[xb], b_ss, b_cst], writes=[b_junk])
                    for q in range(4):
                        for i in range(4):
                            j = q * 4 + i
                            op("pe", lambda e: e.transpose(tps[q][:, i * 128:(i + 1) * 128], junk[:, j * 128:(j + 1) * 128], ident),
                               reads=[b_junk, b_cm], writes=[b_tps[q]], inc=(i == 3))
                        for i in range(4):
                            j = q * 4 + i
                            en = "dve" if (i % 2 == 0) else "act"
                            if en == "dve":
                                op("dve", lambda e: e.tensor_scalar(hT[:, j, sub * 128:(sub + 1) * 128], tps[q][:, i * 128:(i + 1) * 128],
                                                                    modA[:, j, v:v + 1], modB[:, j, v:v + 1], ALU.mult, ALU.add),
                                   reads=[b_tps[q], b_mod], writes=[b_hT[sub]])
                            else:
                                op("act", lambda e: e.activation(hT[:, j, sub * 128:(sub + 1) * 128], tps[q][:, i * 128:(i + 1) * 128],
                                                                 AF.Identity, scale=modA[:, j, v:v + 1], bias=modB[:, j, v:v + 1]),
                                   reads=[b_tps[q], b_mod], writes=[b_hT[sub]])
                for g in range(27):
                    if isctx and g not in ctx_groups:
                        continue
                    c0 = g * 512
                    ncol = min(512, NCOLS - c0)
                    wb = wti % 2
                    wti += 1
                    fw.dma3(wt[wb][:, :, :ncol], w_in_r[:, :, c0:c0 + ncol], 16, writes=[b_wt[wb]])
                    op(CASTE, lambda e: e.tensor_copy(wb16[wb][:, :, :ncol], wt[wb][:, :, :ncol]), reads=[b_wt[wb]], writes=[b_wb16[wb]])
                    if c0 < 5120:
                        typ = ["silu", "id", "fg0", "fg1", "silu"][c0 // 1024]
                        for sub in range(nsub):
                            a = sub % 4
                            for k in range(16):
                                op("pe", lambda e: e.matmul(acc[a][:, :ncol], hT[:, k, sub * 128:(sub + 1) * 128], wb16[wb][:, k, :ncol],
                                                            start=(k == 0), stop=(k == 15)),
                                   reads=[b_hT[sub], b_wb16[wb]], writes=[b_acc[a]], inc=(k == 15))
                            o_ = oti % NOT
                            oti += 1
                            if typ == "silu":
                                op("act", lambda e: e.activation(ot[o_][:], acc[a][:], AF.Silu), reads=[b_acc[a]], writes=[b_ot[o_]])
                            elif typ == "id":
                                op("dve", lambda e: e.tensor_copy(ot[o_][:], acc[a][:]), reads=[b_acc[a]], writes=[b_ot[o_]])
                            else:
                                dd = int(typ[2])
                                cc0 = c0 - (2048 + dd * 1024)
                                op("act", lambda e: e.activation(ot[o_][:], acc[a][:], AF.Sigmoid), reads=[b_acc[a]], writes=[b_ot[o_]])
                                op("dve", lambda e: e.tensor_tensor(ot[o_][:], ot[o_][:], oml_b[:, dd, cc0:cc0 + 512], ALU.mult),
                                   reads=[b_ot[o_], b_lb], writes=[b_ot[o_]])
                                op("dve", lambda e: e.tensor_tensor(ot[o_][:], ot[o_][:], lb_b[:, dd, cc0:cc0 + 512], ALU.add),
                                   reads=[b_ot[o_], b_lb], writes=[b_ot[o_]])
                            fw.dma(HGtok[g0 + sub * 128:g0 + (sub + 1) * 128, c0:c0 + 512], ot[o_][:],
                                   reads=[b_ot[o_]], writes=[sbuf_of("HGtok")], q=SQ)
                    else:
                        for m in range(ncol // 128):
                            mc = c0 // 128 + m
                            if isctx and not (48 <= mc <= 65):
                                continue
                            for hf in range((ntok + 511) // 512):
                                nt_ = min(512, ntok - hf * 512)
                                tk0 = hf * 512
                                a = (m * 2 + hf) % 4
                                for k in range(16):
                                    op("pe", lambda e: e.matmul(acc[a][:, :nt_], wb16[wb][:, k, m * 128:(m + 1) * 128], hT[:, k, tk0:tk0 + nt_],
                                                                start=(k == 0), stop=(k == 15)),
                                       reads=b_hT[tk0 // 128:(tk0 + nt_) // 128] + [b_wb16[wb]], writes=[b_acc[a]], inc=(k == 15))
                                o_ = oti % NOT
                                oti += 1
                                if mc <= 65:
                                    op("dve", lambda e: e.tensor_copy(ot[o_][:, :nt_], acc[a][:, :nt_]), reads=[b_acc[a]], writes=[b_ot[o_]])
                                else:
                                    fn = AF.Silu if mc <= 73 else AF.Sigmoid
                                    op("act", lambda e: e.activation(ot[o_][:, :nt_], acc[a][:, :nt_], fn), reads=[b_acc[a]], writes=[b_ot[o_]])
                                fw.dma(RWfm[(mc - 40) * 128:(mc - 39) * 128, g0 + tk0:g0 + tk0 + nt_], ot[o_][:, :nt_],
                                       reads=[b_ot[o_]], writes=[sbuf_of("RWfm")], q=SQ)
        fw.barrier()

        for ph in _phase(2):
            NB = 2
            fgt = [sbt(ph, "fgt%d" % i, [CH, HGW]) for i in range(NB)]
            vtt = [sbt(ph, "vtt%d" % i, [CH, HGW]) for i in range(NB)]
            qtt = [sbt(ph, "qtt%d" % i, [CH, HGW]) for i in range(NB)]
            ztt = [sbt(ph, "ztt%d" % i, [CH, HGW]) for i in range(NB)]
            oft = [sbt(ph, "oft%d" % i, [CH, HGW]) for i in range(NB)]
            b_ld = [[Buf() for _ in range(5)] for _ in range(NB)]
            gt = sbt(ph, "gt", [CH, HGW]); b_gt = Buf()
            Et = sbt(ph, "Et", [CH, HGW]); Ei = sbt(ph, "Ei", [CH, HGW]); b_E = Buf(); b_Ei = Buf()
            kt_ = sbt(ph, "kt", [128, HGW]); qq_ = sbt(ph, "qq", [128, HGW]); b_kt = Buf(); b_qq = Buf()
            kt = kt_[0:CH, :]; qq = qq_[0:CH, :]
            ob = sbt(ph, "ob", [CH, HGW]); sq_ = sbt(ph, "sq", [128, HGW]); b_ob = Buf(); b_sq = Buf()
            sq = sq_[0:CH, :]
            ms = sbt(ph, "ms", [CH, 8]); b_ms = Buf()
            eb = sbt(ph, "eb", [128, 8]); b_eb = Buf()
            hng = sbt(ph, "hng", [CH, HGW]); b_hng = Buf()
            qkT = [sbt(ph, "qkT%d" % i, [128, 2, CH]) for i in range(2)]; b_qkT = [Buf(), Buf()]
            at = [sbt(ph, "at%d" % i, [CH, CH]) for i in range(2)]; b_at = [Buf(), Buf()]
            S = [sbt(ph, "S%d" % i, [128, 8, 128]) for i in range(2)]; b_S = [Buf(), Buf()]
            qkA = sbt(ph, "qkA", [128, 8, 2, CH]); b_qkA = Buf()
            atA = sbt(ph, "atA", [CH, 8, CH]); b_atA = Buf()
            stmp = sbt(ph, "stmp", [128, 8, 128]); b_stmp = Buf()
            yT = sbt(ph, "yTs", [128, 8, 512]); b_yT = Buf()
            bc_ps = [pst(ph, "bc_ps%d" % i, [128, 512]) for i in range(2)]; b_bc = [Buf(), Buf()]
            o_ps = [pst(ph, "o_ps%d" % i, [128, 512]) for i in range(2)]; b_o = [Buf(), Buf()]
            dS_ps = [pst(ph, "dS_ps%d" % i, [128, 512]) for i in range(2)]; b_dS = [Buf(), Buf()]
            mz = pst(ph, "mz", [128, 512]); b_tp = [Buf()] * 2; b_aps = [Buf()] * 2; b_ebp = b_aps[0]
            yT_ps = pst(ph, "yT_ps", [128, 512]); b_yTp = Buf()
            fw.dma(hng[:], hgng_b[0:CH, :], writes=[b_hng])
            op("dve", lambda e: e.memset(kt_[:], 0.0), writes=[b_kt])
            op("dve", lambda e: e.memset(qq_[:], 0.0), writes=[b_qq])
            op("dve", lambda e: e.memset(sq_[:], 0.0), writes=[b_sq])
            ci = 0
            for d in range(min(2, KD)):
                tri = cm[0:CH, C_TF if d == 0 else C_TB, 0:CH]
                lastc = CH - 1 if d == 0 else 0
                onehot = cm[0:CH, C_ID, lastc:lastc + 1]
                order = list(range(NT // CH)) if d == 0 else list(range(CTX // CH - 1, -1, -1)) + list(range(NT // CH - 1, CTX // CH - 1, -1))
                cur = 0
                op("dve", lambda e: e.memset(S[0][:], 0.0), writes=[b_S[0]])
                for c in order[:KCH]:
                    isctx = c < CTX // CH
                    t0 = c * CH
                    lb_ = ci % NB
                    ci += 1
                    fgs, vs, qs, zs, ofs = fgt[lb_], vtt[lb_], qtt[lb_], ztt[lb_], oft[lb_]
                    bl = b_ld[lb_]
                    fw.dma(fgs[:], HGtok[t0:t0 + CH, 2048 + d * 1024:3072 + d * 1024], reads=[sbuf_of("HGtok")], writes=[bl[0]])
                    fw.dma(vs[:], HGtok[t0:t0 + CH, 1024:2048], reads=[sbuf_of("HGtok")], writes=[bl[1]])
                    if not isctx:
                        fw.dma(qs[:], HGtok[t0:t0 + CH, 0:1024], reads=[sbuf_of("HGtok")], writes=[bl[2]])
                        if d == 1:
                            fw.dma(zs[:], HGtok[t0:t0 + CH, 4096:5120], reads=[sbuf_of("HGtok")], writes=[bl[3]])
                            fw.dma(ofs[:], OFs[t0 - CTX:t0 - CTX + CH, :], reads=[sbuf_of("OFs")], writes=[bl[4]])
                    op("act", lambda e: e.activation(gt[:], fgs[:], AF.Ln), reads=[bl[0]], writes=[b_gt])
                    for n in range(2):
                        op("pe", lambda e: e.matmul(bc_ps[n][0:CH, :], tri, gt[:, n * 512:(n + 1) * 512], start=True, stop=True),
                           reads=[b_gt, b_cm], writes=[b_bc[n]])
                    for n in range(2):
                        sl = slice(n * 512, (n + 1) * 512)
                        op("act", lambda e: e.activation(Ei[:, sl], bc_ps[n][0:CH, :], AF.Exp, scale=-1.0), reads=[b_bc[n]], writes=[b_Ei])
                        op("act", lambda e: e.activation(Et[:, sl], bc_ps[n][0:CH, :], AF.Exp), reads=[b_bc[n]], writes=[b_E])
                    op("dve", lambda e: e.tensor_scalar(kt[:], fgs[:], -1.0, 1.0, ALU.mult, ALU.add), reads=[bl[0]], writes=[b_kt])
                    op("dve", lambda e: e.tensor_tensor(kt[:], kt[:], Ei[:], ALU.mult), reads=[b_kt, b_Ei], writes=[b_kt])
                    if not isctx:
                        op("dve", lambda e: e.tensor_tensor(qq[:], qs[:], Et[:], ALU.mult), reads=[bl[2], b_E], writes=[b_qq])
                    for h in range(8):
                        op("pe", lambda e: e.matmul(yT_ps[:, h:h + 1], Et[:, h * 128:(h + 1) * 128], onehot, start=True, stop=True),
                           reads=[b_E, b_cm], writes=[b_ebp], inc=(h == 7))
                    op("dve", lambda e: e.tensor_copy(eb[:], yT_ps[:, 0:8]), reads=[b_ebp], writes=[b_eb])
                    nxt = 1 - cur
                    if not isctx:
                        for h in range(8):
                            hs = slice(h * 128, (h + 1) * 128)
                            tpv = mz[:, (h % 2) * 256:(h % 2 + 1) * 256]
                            op("pe", lambda e: e.transpose(tpv[:, 0:128], qq_[:, hs], ident), reads=[b_qq, b_cm], writes=[b_tp[0]], inc=False)
                            op("pe", lambda e: e.transpose(tpv[:, 128:256], kt_[:, hs], ident), reads=[b_kt, b_cm], writes=[b_tp[0]])
                            if h % 2 == 1:
                                for hh in range(2):
                                    tq = mz[:, hh * 256:(hh + 1) * 256]
                                    op("dve", lambda e: e.tensor_copy(qkA[:, h - 1 + hh, :, :], tq.rearrange("p (a b) -> p a b", b=128)[:, :, 0:CH]),
                                       reads=[b_tp[0]], writes=[b_qkA])
                        for h in range(8):
                            op("pe", lambda e: e.matmul(yT_ps[0:CH, 256 + h * CH:256 + (h + 1) * CH], qkA[:, h, 1, :], qkA[:, h, 0, :], start=True, stop=True),
                               reads=[b_qkA], writes=[b_aps[0]], inc=(h == 7))
                        op("dve", lambda e: e.tensor_tensor(atA[:], yT_ps[0:CH, 256:256 + 8 * CH].rearrange("p (h t) -> p h t", t=CH),
                                                            tri.unsqueeze(1).to_broadcast([CH, 8, CH]), ALU.mult),
                           reads=[b_aps[0], b_cm], writes=[b_atA])
                    for h in range(8):
                        hs = slice(h * 128, (h + 1) * 128)
                        if not isctx:
                            opv = o_ps[h // 4][0:CH, (h % 4) * 128:(h % 4 + 1) * 128]
                            op("pe", lambda e: e.matmul(opv, atA[:, h, :], vs[:, hs], start=True, stop=False),
                               reads=[b_atA, bl[1]], writes=[b_o[h // 4]], inc=False)
                            op("pe", lambda e: e.matmul(opv, qkA[:, h, 0, :], S[cur][:, h, :], start=False, stop=True),
                               reads=[b_qkA, b_S[cur]], writes=[b_o[h // 4]])
                        dsv = dS_ps[h // 4][:, (h % 4) * 128:(h % 4 + 1) * 128]
                        op("pe", lambda e: e.matmul(dsv, kt[:, hs], vs[:, hs], start=True, stop=True),
                           reads=[b_kt, bl[1]], writes=[b_dS[h // 4]])
                    for n in range(2):
                        op("dve", lambda e: e.tensor_tensor(stmp[:, n * 4:(n + 1) * 4, :], dS_ps[n][:].rearrange("p (h v) -> p h v", v=128),
                                                            S[cur][:, n * 4:(n + 1) * 4, :], ALU.add),
                           reads=[b_dS[n], b_S[cur]], writes=[b_stmp])
                    op("dve", lambda e: e.tensor_tensor(S[nxt][:], stmp[:], eb[:].unsqueeze(2).to_broadcast([128, 8, 128]), ALU.mult),
                       reads=[b_stmp, b_eb], writes=[b_S[nxt]])
                    cur = nxt
                    if isctx or (KX & 8):
                        continue
                    xt0 = t0 - 256
                    if d == 0:
                        for n in range(2):
                            op("dve", lambda e: e.tensor_copy(ob[:, n * 512:(n + 1) * 512], o_ps[n][0:CH, :]),
                               reads=[b_o[n]], writes=[b_ob])
                        fw.dma(OFs[xt0:xt0 + CH, :], ob[:], reads=[b_ob], writes=[sbuf_of("OFs")], q=SQ)
                    else:
                        for n in range(2):
                            op("dve", lambda e: e.tensor_tensor(ob[:, n * 512:(n + 1) * 512], o_ps[n][0:CH, :], ofs[:, n * 512:(n + 1) * 512], ALU.add),
                               reads=[b_o[n], bl[4]], writes=[b_ob])
                        op("dve", lambda e: e.tensor_tensor(sq[:], ob[:], ob[:], ALU.mult), reads=[b_ob], writes=[b_sq])
                        op("dve", lambda e: e.tensor_reduce(ms[:], sq[:].rearrange("p (h v) -> p h v", v=128), AX.X, ALU.add), reads=[b_sq], writes=[b_ms])
                        op("act", lambda e: e.activation(ms[:], ms[:], AF.Sqrt, bias=cst[0:CH, 0:1], scale=1.0 / 128), reads=[b_ms, b_cst], writes=[b_ms])
                        op("dve", lambda e: e.reciprocal(ms[:], ms[:]), reads=[b_ms], writes=[b_ms])
                        op("dve", lambda e: e.tensor_tensor(sq[:].rearrange("p (h v) -> p h v", v=128), ob[:].rearrange("p (h v) -> p h v", v=128),
                                                            ms[:].unsqueeze(2).to_broadcast([CH, 8, 128]), ALU.mult),
                           reads=[b_ob, b_ms], writes=[b_sq])
                        op("dve", lambda e: e.tensor_tensor(sq[:], sq[:], hng[:], ALU.mult), reads=[b_sq, b_hng], writes=[b_sq])
                        op("dve", lambda e: e.tensor_tensor(sq[:], sq[:], zs[:], ALU.mult), reads=[b_sq, bl[3]], writes=[b_sq])
                        for h in range(8):
                            op("pe", lambda e: e.transpose(bc_ps[h // 4][:, (h % 4) * 128:(h % 4 + 1) * 128], sq_[:, h * 128:(h + 1) * 128], ident),
                               reads=[b_sq, b_cm], writes=[b_bc[h // 4]], inc=(h % 4 == 3))
                        for n in range(2):
                            yo_ = xt0 % 512
                            op("dve", lambda e: e.tensor_copy(yT[:, n * 4:(n + 1) * 4, yo_:yo_ + CH], bc_ps[n][:].rearrange("p (h t) -> p h t", t=128)[:, :, 0:CH]),
                               reads=[b_bc[n]], writes=[b_yT])
                        if xt0 % 512 == 0:
                            fw.dma3(YH.rearrange("(h p) t -> p h t", p=128)[:, :, xt0:xt0 + 512], yT[:], 8, reads=[b_yT], writes=[sbuf_of("YH")], q=SQ)
        fw.barrier()

        for ph in _phase(3):
            W = 640
            mu = sbt(ph, "mu", [128, 26, 4]); omu = sbt(ph, "omu", [128, 26]); b_mu = Buf()
            w0t = sbt(ph, "w0t", [128, 8, 2]); a0t = sbt(ph, "a0t", [128, 8, 2]); kvt = sbt(ph, "kvt", [128, 8, 3]); omka = sbt(ph, "omka", [128, 8])
            w2t = sbt(ph, "w2t", [128, RWW]); a2t = sbt(ph, "a2t", [128, RWW]); b_par = Buf()
            raw = [sbt(ph, "raw%d" % i, [128, W]) for i in range(3)]; b_raw = [Buf() for _ in range(3)]
            rawl = [sbt(ph, "rawl%d" % i, [128, W]) for i in range(2)]; b_rawl = [Buf(), Buf()]
            sh = [sbt(ph, "sh%d" % i, [128, 512]) for i in range(3)]; b_sh = [Buf() for _ in range(3)]
            tw = sbt(ph, "tw", [128, 512]); als = sbt(ph, "als", [128, 512]); b_tw = Buf(); b_als = Buf()
            kkr = sbt(ph, "kkr", [128, 512]); kk = sbt(ph, "kk", [128, 512]); t1 = sbt(ph, "t1", [128, 512]); b_kkr = Buf(); b_kk = Buf(); b_t1 = Buf()
            lgw = sbt(ph, "lgw", [128, 512]); av = sbt(ph, "av", [128, 512]); b_lgw = Buf(); b_av = Buf()
            kd = [sbt(ph, "kd%d" % i, [128, 512]) for i in range(2)]; b_kd = [Buf(), Buf()]
            bv = sbt(ph, "bv", [128, 512]); b_bv = Buf()
            lt = sbt(ph, "lt", [128, 128]); b_lt = Buf()
            Ecw = sbt(ph, "Ecw", [128, 512]); Einv = sbt(ph, "Einv", [128, 512]); Eex = sbt(ph, "Eex", [128, 512]); b_Ecw = Buf(); b_Einv = Buf(); b_Eex = Buf()
            res = [sbt(ph, "res%d" % i, [128, 512]) for i in range(4)]; b_res = [Buf() for _ in range(4)]
            tm = [[sbt(ph, "tm%d_%d" % (a_, s_), [128, RWW]) for s_ in range(4)] for a_ in range(5)]; b_tm = [[Buf() for _ in range(4)] for _ in range(5)]
            p_a = pst(ph, "p_a", [128, 512]); p_b = pst(ph, "p_b", [128, 512]); p_cw = pst(ph, "p_cw", [128, 512]); p_tq = [pst(ph, "p_t%d" % i, [128, 512]) for i in range(4)]
            p_s = pst(ph, "p_s", [128, 512])
            b_pa = Buf(); b_pb = Buf(); b_pcw = Buf(); b_pt = [Buf() for _ in range(4)]; b_ps = Buf()
            for (tl, src) in [(mu, mu_fm), (w0t, w0_fm), (a0t, a0_fm), (kvt, kvec_fm), (w2t, w2_d), (a2t, a2_d)]:
                fw.dma(tl[:], src, writes=[b_par if tl is not mu else b_mu])
            op("dve", lambda e: e.tensor_reduce(omu[:], mu[:], AX.X, ALU.add), reads=[b_mu], writes=[b_mu])
            op("dve", lambda e: e.tensor_scalar(omu[:], omu[:], -1.0, 1.0, ALU.mult, ALU.add), reads=[b_mu], writes=[b_mu])
            op("dve", lambda e: e.tensor_scalar(omka[:], kvt[:, :, 1], -1.0, 1.0, ALU.mult, ALU.add), reads=[b_par], writes=[b_par])
            omuc = sbt(ph, "omuc", [128, 26])
            op("dve", lambda e: e.tensor_tensor(omuc[:], mu[:, :, 0], mu[:, :, 1], ALU.add), reads=[b_mu], writes=[b_mu])
            op("dve", lambda e: e.tensor_scalar(omuc[:], omuc[:], -1.0, 1.0, ALU.mult, ALU.add), reads=[b_mu], writes=[b_mu])

            def shift(dst, b_dst, rawt, b_rawt, chunk, g0, ntok, isctx, eng="dve"):
                rows = RWfm[chunk * 128:(chunk + 1) * 128, :]
                if isctx:
                    fw.dma(rawt[:, 0:256], rows[:, 0:256], reads=[sbuf_of("RWfm")], writes=[b_rawt])
                    P = rawt[:, 0:256]
                    op(eng, lambda e: e.tensor_scalar(dst[:, 0:256], P, omuc[:, chunk:chunk + 1], None, ALU.mult), reads=[b_rawt, b_mu], writes=[b_dst])
                    op(eng, lambda e: e.scalar_tensor_tensor(dst[:, 1:256], P[:, 0:255], mu[:, chunk, 0:1], dst[:, 1:256], ALU.mult, ALU.add),
                       reads=[b_rawt, b_mu, b_dst], writes=[b_dst])
                    op(eng, lambda e: e.scalar_tensor_tensor(dst[:, 0:255], P[:, 1:256], mu[:, chunk, 1:2], dst[:, 0:255], ALU.mult, ALU.add),
                       reads=[b_rawt, b_mu, b_dst], writes=[b_dst])
                    return
                lo = g0 - 64
                hi = g0 + ntok + 64
                first = (g0 == CTX)
                last = (g0 + ntok == NT)
                if first:
                    op(eng, lambda e: e.memset(rawt[:, 0:64], 0.0), writes=[b_rawt])
                if last:
                    op(eng, lambda e: e.memset(rawt[:, W - 64:W], 0.0), writes=[b_rawt])
                a_ = 64 if first else 0
                b_ = W - 64 if last else W
                fw.dma(rawt[:, a_:b_], rows[:, lo + a_:lo + b_], reads=[sbuf_of("RWfm")], writes=[b_rawt])
                P3 = rawt[:].rearrange("p (r c) -> p r c", c=64)
                Pc = P3[:, 1:9, :]
                d3 = dst[:].rearrange("p (r c) -> p r c", c=64)
                op(eng, lambda e: e.tensor_scalar(d3, Pc, omu[:, chunk:chunk + 1], None, ALU.mult), reads=[b_rawt, b_mu], writes=[b_dst])
                op(eng, lambda e: e.scalar_tensor_tensor(d3[:, :, 1:], Pc[:, :, :-1], mu[:, chunk, 0:1], d3[:, :, 1:], ALU.mult, ALU.add),
                   reads=[b_rawt, b_mu, b_dst], writes=[b_dst])
                op(eng, lambda e: e.scalar_tensor_tensor(d3[:, :, :-1], Pc[:, :, 1:], mu[:, chunk, 1:2], d3[:, :, :-1], ALU.mult, ALU.add),
                   reads=[b_rawt, b_mu, b_dst], writes=[b_dst])
                op(eng, lambda e: e.scalar_tensor_tensor(d3, P3[:, 0:8, :], mu[:, chunk, 2:3], d3, ALU.mult, ALU.add),
                   reads=[b_rawt, b_mu, b_dst], writes=[b_dst])
                op(eng, lambda e: e.scalar_tensor_tensor(d3, P3[:, 2:10, :], mu[:, chunk, 3:4], d3, ALU.mult, ALU.add),
                   reads=[b_rawt, b_mu, b_dst], writes=[b_dst])

            tiles = [(0, 256, True)] + [(256 + 512 * i, 512, False) for i in range(4)]
            ri = 0
            for (g0, ntok, isctx) in tiles[KT0:KT]:
                nsub = ntok // 128
                blk0 = g0 // 128
                N_ = slice(0, ntok)
                shift(tw, b_tw, rawl[0], b_rawl[0], 24, g0, ntok, isctx)
                if not (KX & 32):
                    op("act", lambda e: e.activation(tw[:, N_], tw[:, N_], AF.Tanh), reads=[b_tw], writes=[b_tw])
                shift(als, b_als, rawl[1], b_rawl[1], 25, g0, ntok, isctx)
                for j in range(KJ):
                    if not isctx:
                        shift(sh[0], b_sh[0], raw[0], b_raw[0], j, g0, ntok, isctx)
                    shift(sh[1], b_sh[1], raw[1], b_raw[1], 8 + j, g0, ntok, isctx)
                    shift(sh[2], b_sh[2], raw[2], b_raw[2], 16 + j, g0, ntok, isctx)
                    rs, ks, vs = sh[0], sh[1], sh[2]
                    op("dve", lambda e: e.tensor_scalar(kkr[:, N_], ks[:, N_], kvt[:, j, 0:1], None, ALU.mult), reads=[b_sh[1], b_par], writes=[b_kkr])
                    op("dve", lambda e: e.tensor_tensor(t1[:, N_], kkr[:, N_], kkr[:, N_], ALU.mult), reads=[b_kkr], writes=[b_t1])
                    op("pe", lambda e: e.matmul(p_a[:, N_], cm[:, C_BONES, :], t1[:, N_], start=True, stop=True), reads=[b_cm, b_t1], writes=[b_pa])
                    op("act", lambda e: e.activation(t1[:, N_], p_a[:, N_], AF.Sqrt, bias=EPS12, scale=1.0), reads=[b_pa, b_cst], writes=[b_t1])
                    op("dve", lambda e: e.reciprocal(t1[:, N_], t1[:, N_]), reads=[b_t1], writes=[b_t1])
                    op("dve", lambda e: e.tensor_tensor(kk[:, N_], kkr[:, N_], t1[:, N_], ALU.mult), reads=[b_kkr, b_t1], writes=[b_kk])
                    for sub in range(nsub):
                        q_ = sub % 4
                        op("pe", lambda e: e.transpose(p_tq[q_][:, 0:128], vs[:, sub * 128:(sub + 1) * 128], ident),
                           reads=[b_sh[2], b_cm], writes=[b_pt[q_]])
                        op("dve", lambda e: e.tensor_copy(tm[0][sub][:, j * 128:(j + 1) * 128], p_tq[q_][:, 0:128]), reads=[b_pt[q_]], writes=[b_tm[0][sub]])
                    for d in range(2):
                        ds = slice(d * 64, (d + 1) * 64)
                        js = slice(j * 128, (j + 1) * 128)
                        op("pe", lambda e: e.matmul(p_a[:, N_], w2t[ds, js], tw[ds, N_], start=True, stop=True), reads=[b_par, b_tw], writes=[b_pa])
                        op("act", lambda e: e.activation(lgw[:, N_], p_a[:, N_], AF.Sigmoid, bias=w0t[:, j, d:d + 1], scale=1.0), reads=[b_pa, b_par], writes=[b_lgw])
                        op("dve", lambda e: e.tensor_scalar(lgw[:, N_], lgw[:, N_], -0.6065306597126334, None, ALU.mult), reads=[b_lgw], writes=[b_lgw])
                        op("pe", lambda e: e.matmul(p_b[:, N_], a2t[ds, js], als[ds, N_], start=True, stop=True), reads=[b_par, b_als], writes=[b_pb])
                        op("act", lambda e: e.activation(av[:, N_], p_b[:, N_], AF.Sigmoid, bias=a0t[:, j, d:d + 1], scale=1.0), reads=[b_pb, b_par], writes=[b_av])
                        op("dve", lambda e: e.tensor_scalar(t1[:, N_], av[:, N_], kvt[:, j, 1:2], omka[:, j:j + 1], ALU.mult, ALU.add), reads=[b_av, b_par], writes=[b_t1])
                        op("dve", lambda e: e.tensor_tensor(kd[d][:, N_], t1[:, N_], ks[:, N_], ALU.mult), reads=[b_t1, b_sh[1]], writes=[b_kd[d]])
                        op("dve", lambda e: e.tensor_tensor(bv[:, N_], kk[:, N_], av[:, N_], ALU.mult), reads=[b_kk, b_av], writes=[b_bv])
                        tri = cm[:, C_TF if d == 0 else C_TB, :]
                        for sub in range(nsub):
                            ss_ = slice(sub * 128, (sub + 1) * 128)
                            q_ = sub % 4
                            op("pe", lambda e: e.transpose(p_tq[q_][:, 0:128], lgw[:, ss_], ident), reads=[b_lgw, b_cm], writes=[b_pt[q_]])
                            op("dve", lambda e: e.tensor_copy(lt[:], p_tq[q_][:, 0:128]), reads=[b_pt[q_]], writes=[b_lt])
                            op("pe", lambda e: e.matmul(p_cw[:, ss_], lt[:], tri, start=True, stop=True), reads=[b_lt, b_cm], writes=[b_pcw])
                        op("act", lambda e: e.activation(Ecw[:, N_], p_cw[:, N_], AF.Exp), reads=[b_pcw], writes=[b_Ecw])
                        op("act", lambda e: e.activation(Einv[:, N_], p_cw[:, N_], AF.Exp, scale=-1.0), reads=[b_pcw], writes=[b_Einv])
                        op("dve", lambda e: e.tensor_tensor(t1[:, N_], p_cw[:, N_], lgw[:, N_], ALU.subtract), reads=[b_pcw, b_lgw], writes=[b_t1])
                        op("act", lambda e: e.activation(Eex[:, N_], t1[:, N_], AF.Exp), reads=[b_t1], writes=[b_Eex])
                        lastc = 127 if d == 0 else 0
                        for sub in range(nsub):
                            op("dve", lambda e: e.tensor_copy(wc[d][:, j, blk0 + sub:blk0 + sub + 1], Ecw[:, sub * 128 + lastc:sub * 128 + lastc + 1]),
                               reads=[b_Ecw], writes=[b_wc[d]])
                        prods = [(kk, b_kk, Eex, b_Eex, AL[d], "AL%d" % d), (bv, b_bv, Einv, b_Einv, BE[d], "BE%d" % d),
                                 (kd[d], b_kd[d], Einv, b_Einv, KA[d], "KA%d" % d)]
                        if not isctx:
                            prods.append((rs, b_sh[0], Ecw, b_Ecw, RH[d], "RH%d" % d))
                        for pi_, (x_, bx_, y_, by_, dst, nm) in enumerate(prods):
                            op("dve", lambda e: e.tensor_tensor(res[pi_][:, N_], x_[:, N_], y_[:, N_], ALU.mult), reads=[bx_, by_], writes=[b_res[pi_]])
                            if not (KX & 64):
                                fw.dma(dst[js, g0:g0 + ntok], res[pi_][:, N_], reads=[b_res[pi_]], writes=[sbuf_of(nm)], q=SQ)
                            if pi_ in (1, 2):
                                dstT = BEt[d] if pi_ == 1 else KAt[d]
                                nmT = ("BEt%d" if pi_ == 1 else "KAt%d") % d
                                for sub in range(nsub):
                                    q_ = sub % 4
                                    srcT = kk if (KX & 128) else res[pi_]
                                    op("pe", lambda e: e.transpose(p_tq[q_][:, 0:128], srcT[:, sub * 128:(sub + 1) * 128], ident),
                                       reads=[b_res[pi_], b_cm], writes=[b_pt[q_]])
                                    ta = 1 + 2 * d + (pi_ - 1)
                                    op("dve", lambda e: e.tensor_copy(tm[ta][sub][:, js], p_tq[q_][:, 0:128]), reads=[b_pt[q_]], writes=[b_tm[ta][sub]])
                    if not isctx:
                        op("dve", lambda e: e.tensor_tensor(t1[:], kd[0][:], kd[1][:], ALU.add), reads=[b_kd[0], b_kd[1]], writes=[b_t1])
                        op("dve", lambda e: e.scalar_tensor_tensor(t1[:], t1[:], kvt[:, j, 2:3], rs[:], ALU.mult, ALU.mult), reads=[b_t1, b_par, b_sh[0]], writes=[b_t1])
                        op("pe", lambda e: e.matmul(p_s[:], cm[:, C_BONES, :], t1[:], start=True, stop=True), reads=[b_cm, b_t1], writes=[b_ps])
                        op("dve", lambda e: e.tensor_tensor(res[3][:], p_s[:], vs[:], ALU.mult), reads=[b_ps, b_sh[2]], writes=[b_res[3]])
                        fw.dma(BON[j * 128:(j + 1) * 128, g0 - CTX:g0 - CTX + 512], res[3][:], reads=[b_res[3]], writes=[sbuf_of("BON")], q=SQ)
                for sub in range(nsub if not (KX & 16) else 0):
                    rows = slice(g0 + sub * 128, g0 + (sub + 1) * 128)
                    for ta, (dstT, nmT) in enumerate([(Vt, "Vt"), (BEt[0], "BEt0"), (KAt[0], "KAt0"), (BEt[1], "BEt1"), (KAt[1], "KAt1")]):
                        fw.dma(dstT[rows, :], tm[ta][sub][:], reads=[b_tm[ta][sub]], writes=[sbuf_of(nmT)], q=SQ)
        fw.barrier()

        for ph in _phase(4):
            J = 8
            aT = [sbt(ph, "aT%d" % j, [128, 128]) for j in range(J)]
            bT = [sbt(ph, "bT%d" % j, [128, 128]) for j in range(J)]
            kT = [sbt(ph, "kT%d" % j, [128, 128]) for j in range(J)]
            rT = [sbt(ph, "rT%d" % j, [128, 128]) for j in range(J)]
            btkA = sbt(ph, "btkA", [128, RWW]); ktkA = sbt(ph, "ktkA", [128, RWW]); vtkA = sbt(ph, "vtkA", [128, RWW])
            btk = [btkA[:, j * 128:(j + 1) * 128] for j in range(J)]
            ktk = [ktkA[:, j * 128:(j + 1) * 128] for j in range(J)]
            vtk = [vtkA[:, j * 128:(j + 1) * 128] for j in range(J)]
            b_tokA = [Buf(), Buf(), Buf()]
            b_in = [[Buf() for _ in range(7)] for _ in range(J)]
            Pm = [[sbt(ph, "Pm%d_%d" % (j, i), [128, 2, 128]) for i in range(2)] for j in range(J)]
            PTm = [[sbt(ph, "PTm%d_%d" % (j, i), [128, 2, 128]) for i in range(2)] for j in range(J)]
            TTm = [[sbt(ph, "TTm%d_%d" % (j, i), [128, 2, 128]) for i in range(2)] for j in range(J)]
            b_P = [[Buf(), Buf()] for _ in range(J)]; b_PT = [[Buf(), Buf()] for _ in range(J)]; b_TT = [[Buf(), Buf()] for _ in range(J)]
            AkT = [sbt(ph, "AkT%d" % j, [128, 2, 128]) for j in range(J)]
            BbT = [sbt(ph, "BbT%d" % j, [128, 2, 128]) for j in range(J)]
            BkT = [sbt(ph, "BkT%d" % j, [128, 2, 128]) for j in range(J)]
            b_AkT = [Buf() for _ in range(J)]; b_BbT = [Buf() for _ in range(J)]; b_BkT = [Buf() for _ in range(J)]
            St = [[sbt(ph, "St%d_%d" % (j, i), [128, 64]) for i in range(2)] for j in range(J)]
            b_St = [[Buf(), Buf()] for _ in range(J)]
            Rn = [sbt(ph, "Rn%d" % i, [128, 2, 64]) for i in range(2)]; b_Rn = [Buf(), Buf()]
            Us = [sbt(ph, "Us%d" % i, [128, 2, 64]) for i in range(2)]; b_Us = [Buf(), Buf()]
            stt_ = [sbt(ph, "stt%d" % i, [128, 64]) for i in range(2)]; b_stt = [Buf(), Buf()]
            Ys = [sbt(ph, "Ys%d" % i, [128, 2, 64]) for i in range(2)]; b_Ys = [Buf(), Buf()]
            Yf = [sbt(ph, "Yf%d" % i, [128, 2, 64]) for i in range(2)]; b_Yf = [Buf(), Buf()]
            cen = [sbt(ph, "cen%d" % i, [128, 2, 64]) for i in range(2)]; b_cen = [Buf(), Buf()]
            gsq = [sbt(ph, "gsq%d" % i, [128, 2, 64]) for i in range(2)]; b_gsq = [Buf(), Buf()]
            gst = [sbt(ph, "gst%d" % i, [128, 4]) for i in range(2)]; b_gst = [Buf(), Buf()]
            bon = [sbt(ph, "bon%d" % i, [128, 128]) for i in range(2)]; zr = [sbt(ph, "zr%d" % i, [128, 128]) for i in range(2)]
            b_bon = [Buf(), Buf()]; b_zr = [Buf(), Buf()]
            yo = [sbt(ph, "yo%d" % i, [128, 128]) for i in range(2)]; b_yo = [Buf(), Buf()]
            gng = sbt(ph, "gng", [128, RWW]); gnb = sbt(ph, "gnb", [128, RWW]); b_gn = Buf()
            fw.dma(gng[:], gng_b, writes=[b_gn]); fw.dma(gnb[:], gnb_b, writes=[b_gn])
            G = [pst(ph, "G%d" % i, [128, 1024]) for i in range(4)]
            b_G = [[Buf(), Buf()] for _ in range(4)]

            def gv(g, q):
                return G[g][:].rearrange("p (h q s) -> p h q s", h=2, q=4)[:, :, q, :]

            def gs(g, q):
                return G[g][:].rearrange("p (h c) -> p h c", h=2)[:, :, q * 64:(q + 1) * 64]

            KB = int(os.environ.get("KB", "99"))
            for d in range(min(2, KD)):
                m_lo = cm[:, C_GT if d == 0 else C_LT, :]
                m_up = cm[:, C_LT if d == 0 else C_GT, :]
                m_upi = cm[:, C_TF if d == 0 else C_TB, :]
                bc3 = lambda m: m.unsqueeze(1).to_broadcast([128, 2, 128])
                order = list(range(18)) if d == 0 else [1, 0] + list(range(17, 1, -1))
                cur = 0
                for j in range(J):
                    op("dve", lambda e: e.memset(St[j][0][:], 0.0), writes=[b_St[j][0]])
                for blk in order[:KB]:
                    isctx = blk < 2
                    g0 = blk * 128
                    x0 = g0 - CTX
                    tsl = slice(g0, g0 + 128)
                    fw.dma(btkA[:], BEt[d][tsl, :], reads=[sbuf_of("BEt%d" % d)], writes=[b_tokA[0]])
                    fw.dma(ktkA[:], KAt[d][tsl, :], reads=[sbuf_of("KAt%d" % d)], writes=[b_tokA[1]])
                    fw.dma(vtkA[:], Vt[tsl, :], reads=[sbuf_of("Vt")], writes=[b_tokA[2]])
                    for j in range(J):
                        js = slice(j * 128, (j + 1) * 128)
                        bi = b_in[j]
                        fw.dma(aT[j][:], AL[d][js, tsl], reads=[sbuf_of("AL%d" % d)], writes=[bi[0]])
                        fw.dma(bT[j][:], BE[d][js, tsl], reads=[sbuf_of("BE%d" % d)], writes=[bi[1]])
                        fw.dma(kT[j][:], KA[d][js, tsl], reads=[sbuf_of("KA%d" % d)], writes=[bi[2]])
                        if not isctx:
                            fw.dma(rT[j][:], RH[d][js, tsl], reads=[sbuf_of("RH%d" % d)], writes=[bi[3]])
                        bi[4], bi[5], bi[6] = b_tokA
                    for j in range(J):
                        bi = b_in[j]
                        for h in range(2):
                            hs = slice(h * 64, (h + 1) * 64)
                            op("pe", lambda e: e.matmul(gv(0, 0)[:, h, :], aT[j][hs, :], bT[j][hs, :], start=True, stop=True),
                               reads=[bi[0], bi[1]], writes=[b_G[0][h]])
                            op("pe", lambda e: e.matmul(gv(0, 1)[:, h, :], bT[j][hs, :], aT[j][hs, :], start=True, stop=True),
                               reads=[bi[0], bi[1]], writes=[b_G[0][h]])
                            op("pe", lambda e: e.matmul(gv(0, 2)[:, h, :], kT[j][hs, :], aT[j][hs, :], start=True, stop=True),
                               reads=[bi[0], bi[2]], writes=[b_G[0][h]])
                            if not isctx:
                                op("pe", lambda e: e.matmul(gv(0, 3)[:, h, :], bT[j][hs, :], rT[j][hs, :], start=True, stop=True),
                                   reads=[bi[1], bi[3]], writes=[b_G[0][h]])
                                op("pe", lambda e: e.matmul(gv(1, 0)[:, h, :], kT[j][hs, :], rT[j][hs, :], start=True, stop=True),
                                   reads=[bi[2], bi[3]], writes=[b_G[1][h]])
                        op("dve", lambda e: e.scalar_tensor_tensor(Pm[j][0][:], gv(0, 0), -1.0, bc3(m_lo), ALU.mult, ALU.mult),
                           reads=b_G[0] + [b_cm], writes=[b_P[j][0]])
                        op("dve", lambda e: e.scalar_tensor_tensor(PTm[j][0][:], gv(0, 1), -1.0, bc3(m_up), ALU.mult, ALU.mult),
                           reads=b_G[0] + [b_cm], writes=[b_PT[j][0]])
                        op("dve", lambda e: e.tensor_tensor(TTm[j][0][:], PTm[j][0][:], bc3(ident), ALU.add),
                           reads=[b_PT[j][0], b_cm], writes=[b_TT[j][0]])
                        op("dve", lambda e: e.tensor_tensor(AkT[j][:], gv(0, 2), bc3(m_up), ALU.mult),
                           reads=b_G[0] + [b_cm], writes=[b_AkT[j]])
                        if not isctx:
                            op("dve", lambda e: e.tensor_tensor(BbT[j][:], gv(0, 3), bc3(m_upi), ALU.mult),
                               reads=b_G[0] + [b_cm], writes=[b_BbT[j]])
                            op("dve", lambda e: e.tensor_tensor(BkT[j][:], gv(1, 0), bc3(m_upi), ALU.mult),
                               reads=b_G[1] + [b_cm], writes=[b_BkT[j]])
                    for i in range(1, 7):
                        a_, n_ = (i - 1) % 2, i % 2
                        for j in range(J):
                            cg = j % 2
                            for h in range(2):
                                op("pe", lambda e: e.matmul(gv(cg, 1)[:, h, :], PTm[j][a_][:, h, :], Pm[j][a_][:, h, :], start=True, stop=True),
                                   reads=[b_P[j][a_], b_PT[j][a_]], writes=[b_G[cg][h]])
                                if i < 6:
                                    op("pe", lambda e: e.matmul(gv(cg, 2)[:, h, :], Pm[j][a_][:, h, :], PTm[j][a_][:, h, :], start=True, stop=True),
                                       reads=[b_P[j][a_], b_PT[j][a_]], writes=[b_G[cg][h]])
                            op("dve", lambda e: e.tensor_copy(Pm[j][n_][:], gv(cg, 1)), reads=b_G[cg], writes=[b_P[j][n_]])
                            if i < 6:
                                op("dve", lambda e: e.tensor_copy(PTm[j][n_][:], gv(cg, 2)), reads=b_G[cg], writes=[b_PT[j][n_]])
                            for h in range(2):
                                op("pe", lambda e: e.matmul(gv(cg, 3)[:, h, :], Pm[j][n_][:, h, :], TTm[j][a_][:, h, :], start=True, stop=True),
                                   reads=[b_P[j][n_], b_TT[j][a_]], writes=[b_G[cg][h]])
                            op("dve", lambda e: e.tensor_tensor(TTm[j][n_][:], gv(cg, 3), TTm[j][a_][:], ALU.add),
                               reads=b_G[cg] + [b_TT[j][a_]], writes=[b_TT[j][n_]])
                    TTf = 0
                    nxt = 1 - cur
                    for j in range(J):
                        pr = j % 2
                        bi = b_in[j]
                        js = slice(j * 128, (j + 1) * 128)
                        sg, tg = (2, 3) if j % 2 == 0 else (0, 1)
                        Rv, Uv, Yv = gs(sg, 0), gs(sg, 1), gs(sg, 2)
                        for h in range(2):
                            hs = slice(h * 64, (h + 1) * 64)
                            op("pe", lambda e: e.matmul(Rv[:, h, :], aT[j][hs, :], St[j][cur][hs, :], start=True, stop=False),
                               reads=[bi[0], b_St[j][cur]], writes=[b_G[sg][h]])
                            op("pe", lambda e: e.matmul(Rv[:, h, :], AkT[j][:, h, :], vtk[j][:, hs], start=False, stop=True),
                               reads=[b_AkT[j], bi[6]], writes=[b_G[sg][h]])
                        op("dve", lambda e: e.tensor_scalar(Rn[pr][:], Rv, -1.0, None, ALU.mult), reads=b_G[sg], writes=[b_Rn[pr]])
                        for h in range(2):
                            op("pe", lambda e: e.matmul(Uv[:, h, :], TTm[j][TTf][:, h, :], Rn[pr][:, h, :], start=True, stop=True),
                               reads=[b_TT[j][TTf], b_Rn[pr]], writes=[b_G[sg][h]])
                        op("dve", lambda e: e.tensor_copy(Us[pr][:], Uv), reads=b_G[sg], writes=[b_Us[pr]])
                        SSv = G[tg][:, 0:128]
                        op("pe", lambda e: e.matmul(SSv, btk[j], Us[pr][:].rearrange("p h v -> p (h v)"), start=True, stop=False),
                           reads=[bi[4], b_Us[pr]], writes=[b_G[tg][0]])
                        op("pe", lambda e: e.matmul(SSv, ktk[j], vtk[j], start=False, stop=True),
                           reads=[bi[5], bi[6]], writes=[b_G[tg][0]])
                        for h in range(2):
                            hs = slice(h * 64, (h + 1) * 64)
                            op("dve", lambda e: e.tensor_tensor(stt_[pr][hs, :], SSv[hs, h * 64:(h + 1) * 64], St[j][cur][hs, :], ALU.add),
                               reads=[b_G[tg][0], b_St[j][cur]], writes=[b_stt[pr]])
                        op("dve", lambda e: e.tensor_scalar(St[j][nxt][:], stt_[pr][:], wc[d][:, j, blk:blk + 1], None, ALU.mult),
                           reads=[b_stt[pr], b_wc[d]], writes=[b_St[j][nxt]])
                        if isctx:
                            continue
                        for h in range(2):
                            hs = slice(h * 64, (h + 1) * 64)
                            op("pe", lambda e: e.matmul(Yv[:, h, :], rT[j][hs, :], St[j][cur][hs, :], start=True, stop=False),
                               reads=[bi[3], b_St[j][cur]], writes=[b_G[sg][h]])
                            op("pe", lambda e: e.matmul(Yv[:, h, :], BbT[j][:, h, :], Us[pr][:, h, :], start=False, stop=False),
                               reads=[b_BbT[j], b_Us[pr]], writes=[b_G[sg][h]])
                            op("pe", lambda e: e.matmul(Yv[:, h, :], BkT[j][:, h, :], vtk[j][:, hs], start=False, stop=True),
                               reads=[b_BkT[j], bi[6]], writes=[b_G[sg][h]])
                        if d == 0:
                            op("dve", lambda e: e.tensor_copy(Ys[pr][:], Yv), reads=b_G[sg], writes=[b_Ys[pr]])
                            fw.dma(YF[x0:x0 + 128, js], Ys[pr][:].rearrange("p h v -> p (h v)"), reads=[b_Ys[pr]], writes=[sbuf_of("YF")], q=SQ)
                        else:
                            fw.dma(Yf[pr][:].rearrange("p h v -> p (h v)"), YF[x0:x0 + 128, js], reads=[sbuf_of("YF")], writes=[b_Yf[pr]])
                            fw.dma(bon[pr][:], BON[js, x0:x0 + 128], reads=[sbuf_of("BON")], writes=[b_bon[pr]])
                            fw.dma(zr[pr][:], RWfm[(26 + j) * 128:(27 + j) * 128, g0:g0 + 128], reads=[sbuf_of("RWfm")], writes=[b_zr[pr]])
                            op("dve", lambda e: e.tensor_tensor(Ys[pr][:], Yv, Yf[pr][:], ALU.add), reads=b_G[sg] + [b_Yf[pr]], writes=[b_Ys[pr]])
                            g_ = gst[pr]
                            op("dve", lambda e: e.tensor_reduce(g_[:, 0:2], Ys[pr][:], AX.X, ALU.add), reads=[b_Ys[pr]], writes=[b_gst[pr]])
                            op("dve", lambda e: e.tensor_scalar(g_[:, 0:2], g_[:, 0:2], -1.0 / 64, None, ALU.mult), reads=[b_gst[pr]], writes=[b_gst[pr]])
                            op("dve", lambda e: e.tensor_tensor(cen[pr][:], Ys[pr][:], g_[:, 0:2].unsqueeze(2).to_broadcast([128, 2, 64]), ALU.add),
                               reads=[b_Ys[pr], b_gst[pr]], writes=[b_cen[pr]])
                            op("dve", lambda e: e.tensor_tensor(gsq[pr][:], cen[pr][:], cen[pr][:], ALU.mult), reads=[b_cen[pr]], writes=[b_gsq[pr]])
                            op("dve", lambda e: e.tensor_reduce(g_[:, 2:4], gsq[pr][:], AX.X, ALU.add), reads=[b_gsq[pr]], writes=[b_gst[pr]])
                            op("act", lambda e: e.activation(g_[:, 2:4], g_[:, 2:4], AF.Sqrt, bias=EPSGN, scale=1.0 / 64), reads=[b_gst[pr], b_cst], writes=[b_gst[pr]])
                            op("dve", lambda e: e.reciprocal(g_[:, 2:4], g_[:, 2:4]), reads=[b_gst[pr]], writes=[b_gst[pr]])
                            op("dve", lambda e: e.tensor_tensor(cen[pr][:], cen[pr][:], g_[:, 2:4].unsqueeze(2).to_broadcast([128, 2, 64]), ALU.mult),
                               reads=[b_cen[pr], b_gst[pr]], writes=[b_cen[pr]])
                            cf = cen[pr][:].rearrange("p h v -> p (h v)")
                            op("dve", lambda e: e.tensor_tensor(cf, cf, gng[:, js], ALU.mult), reads=[b_cen[pr], b_gn], writes=[b_cen[pr]])
                            op("dve", lambda e: e.tensor_tensor(cf, cf, gnb[:, js], ALU.add), reads=[b_cen[pr], b_gn], writes=[b_cen[pr]])
                            yTv = G[tg][:, 512:640]
                            op("pe", lambda e: e.transpose(yTv, cf, ident), reads=[b_cen[pr], b_cm], writes=[b_G[tg][1]])
                            op("dve", lambda e: e.tensor_tensor(yo[pr][:], yTv, bon[pr][:], ALU.add), reads=[b_G[tg][1], b_bon[pr]], writes=[b_yo[pr]])
                            op("dve", lambda e: e.tensor_tensor(yo[pr][:], yo[pr][:], zr[pr][:], ALU.mult), reads=[b_yo[pr], b_zr[pr]], writes=[b_yo[pr]])
                            fw.dma(YR[js, x0:x0 + 128], yo[pr][:], reads=[b_yo[pr]], writes=[sbuf_of("YR")], q=SQ)
                    cur = nxt
        fw.barrier()

        for ph in _phase(5):
            TT_ = 256
            yh = sbt(ph, "yh", [128, 8, TT_]); yr = sbt(ph, "yr", [128, 8, TT_]); b_yh = Buf(); b_yr = Buf()
            gh = [sbt(ph, "gh%d" % i, [128, 4, TT_]) for i in range(2)]; gr = [sbt(ph, "gr%d" % i, [128, 4, TT_]) for i in range(2)]
            b_gh = [Buf(), Buf()]; b_gr = [Buf(), Buf()]
            mT = sbt(ph, "mT", [128, 16, TT_]); b_mT = Buf()
            whg = [sbt(ph, "whg0", [128, 8, 512])] * 2; wrw = [sbt(ph, "wrw0", [128, 8, 512])] * 2
            b_whg = [Buf()] * 2; b_wrw = [Buf()] * 2
            wo = [sbt(ph, "wo0", [128, 16, 512])] * 2; b_wo = [Buf()] * 2
            xr = [sbt(ph, "xr%d" % i, [128, D]) for i in range(2)]; b_xr = [Buf(), Buf()]
            xn = [sbt(ph, "xn%d" % i, [128, D]) for i in range(2)]; b_xn = [Buf(), Buf()]
            junk3 = sbt(ph, "junk3", [128, D]); b_j3 = Buf()
            ss3 = sbt(ph, "ss3", [128, 2]); b_ss3 = Buf()
            fgt_ = sbt(ph, "fgt_", [128, D]); b_fg = Buf()
            tmpm = [sbt(ph, "tmpm%d" % i, [128, TT_]) for i in range(2)]; b_tmpm = [Buf(), Buf()]
            pp1 = [pst(ph, "pp1_%d" % i, [128, 512]) for i in range(2)]; pp2 = [pst(ph, "pp2_%d" % i, [128, 512]) for i in range(2)]
            b_pp1 = [Buf(), Buf()]; b_pp2 = [Buf(), Buf()]
            po = [pst(ph, "po%d" % i, [128, 512]) for i in range(4)]; b_po = [Buf() for _ in range(4)]
            fw.dma(fgt_[:], fg_b, writes=[b_fg])
            whg_r = w_hg_o.rearrange("(k p) c -> p k c", p=128)
            wrw_r = w_rw_o.rearrange("(k p) c -> p k c", p=128)
            wo_r = w_o.rearrange("(k p) c -> p k c", p=128)
            YH_r = YH.rearrange("(k p) t -> p k t", p=128)
            YR_r = YR.rearrange("(k p) t -> p k t", p=128)
            G_r = RWfm[(74 - 40) * 128:, :].rearrange("(k p) t -> p k t", p=128)
            wi = 0
            woi = 0
            xi = 0
            for tt in range(SEQ // TT_):
                x0 = tt * TT_
                g0 = x0 + CTX
                fw.dma3(yh[:], YH_r[:, :, x0:x0 + TT_], 8, reads=[sbuf_of("YH")], writes=[b_yh])
                fw.dma3(yr[:], YR_r[:, :, x0:x0 + TT_], 8, reads=[sbuf_of("YR")], writes=[b_yr])
                for mg in range(4):
                    wb = wi % 2
                    wi += 1
                    cs = slice(mg * 512, (mg + 1) * 512)
                    fw.dma3(whg[wb][:], whg_r[:, :, cs], 8, writes=[b_whg[wb]])
                    fw.dma3(wrw[wb][:], wrw_r[:, :, cs], 8, writes=[b_wrw[wb]])
                    fw.dma3(gh[wb][:], G_r[:, mg * 4:(mg + 1) * 4, g0:g0 + TT_], 4, reads=[sbuf_of("RWfm")], writes=[b_gh[wb]])
                    fw.dma3(gr[wb][:], G_r[:, 16 + mg * 4:16 + (mg + 1) * 4, g0:g0 + TT_], 4, reads=[sbuf_of("RWfm")], writes=[b_gr[wb]])
                    for mm in range(4):
                        m = mg * 4 + mm
                        a = mm % 2
                        for k in range(8):
                            op("pe", lambda e: e.matmul(pp1[a][:, :TT_], whg[wb][:, k, mm * 128:(mm + 1) * 128], yh[:, k, :], start=(k == 0), stop=(k == 7)),
                               reads=[b_whg[wb], b_yh], writes=[b_pp1[a]], inc=(k == 7))
                        for k in range(8):
                            op("pe", lambda e: e.matmul(pp2[a][:, :TT_], wrw[wb][:, k, mm * 128:(mm + 1) * 128], yr[:, k, :], start=(k == 0), stop=(k == 7)),
                               reads=[b_wrw[wb], b_yr], writes=[b_pp2[a]], inc=(k == 7))
                        op("dve", lambda e: e.tensor_tensor(tmpm[a][:], pp1[a][:, :TT_], gh[wb][:, mm, :], ALU.mult), reads=[b_pp1[a], b_gh[wb]], writes=[b_tmpm[a]])
                        op("dve", lambda e: e.tensor_tensor(mT[:, m, :], pp2[a][:, :TT_], gr[wb][:, mm, :], ALU.mult), reads=[b_pp2[a], b_gr[wb]], writes=[b_mT])
                        op("dve", lambda e: e.tensor_tensor(mT[:, m, :], mT[:, m, :], tmpm[a][:], ALU.add), reads=[b_mT, b_tmpm[a]], writes=[b_mT])
                for sub in range(TT_ // 128):
                    xb = xi % 2
                    xi += 1
                    fw.dma(xr[xb][:], xc[g0 + sub * 128:g0 + (sub + 1) * 128, :], writes=[b_xr[xb]])
                for n in range(4):
                    ob_ = woi % 2
                    woi += 1
                    fw.dma3(wo[ob_][:], wo_r[:, :, n * 512:(n + 1) * 512], 16, writes=[b_wo[ob_]])
                    for sub in range(TT_ // 128):
                        xb = (xi - (TT_ // 128) + sub) % 2
                        a = (n * 2 + sub) % 4
                        for k in range(16):
                            op("pe", lambda e: e.matmul(po[a][:], mT[:, k, sub * 128:(sub + 1) * 128], wo[ob_][:, k, :], start=(k == 0), stop=(k == 15)),
                               reads=[b_mT, b_wo[ob_]], writes=[b_po[a]], inc=(k == 15))
                        ns = slice(n * 512, (n + 1) * 512)
                        op("dve", lambda e: e.tensor_tensor(xn[xb][:, ns], po[a][:], gate_b[:, ns], ALU.mult), reads=[b_po[a], b_gate], writes=[b_xn[xb]])
                        op("dve", lambda e: e.tensor_tensor(xn[xb][:, ns], xn[xb][:, ns], xr[xb][:, ns], ALU.add), reads=[b_xn[xb], b_xr[xb]], writes=[b_xn[xb]])
                for sub in range(TT_ // 128):
                    xb = (xi - (TT_ // 128) + sub) % 2
                    op("act", lambda e: e.activation(junk3[:], xn[xb][:], AF.Square, accum_out=ss3[:, 0:1]), reads=[b_xn[xb]], writes=[b_j3, b_ss3])
                    op("act", lambda e: e.activation(ss3[:, 1:2], ss3[:, 0:1], AF.Sqrt, bias=EPS6, scale=1.0 / D), reads=[b_ss3, b_cst], writes=[b_ss3])
                    op("dve", lambda e: e.reciprocal(ss3[:, 1:2], ss3[:, 1:2]), reads=[b_ss3], writes=[b_ss3])
                    op("dve", lambda e: e.scalar_tensor_tensor(xn[xb][:], xn[xb][:], ss3[:, 1:2], fgt_[:], ALU.mult, ALU.mult),
                       reads=[b_xn[xb], b_ss3, b_fg], writes=[b_xn[xb]])
                    fw.dma(out[x0 + sub * 128:x0 + (sub + 1) * 128, :], xn[xb][:], reads=[b_xn[xb]], writes=[sbuf_of("out")], q=SQ)
        fw.barrier()
    print("bass program built: %d instructions" % fw.ninstr, flush=True)
    dbg_names = ["HGtok", "RWfm", "OFs", "YH", "AL0", "BE0", "KA0", "RH0", "AL1", "BE1", "KA1", "RH1", "BEt0", "KAt0", "Vt", "BON", "YF", "YR"]
    return nc, dbg_names


def _host_inputs(b, inp):
    f = lambda a: np.ascontiguousarray(a, dtype=np.float32)
    fm = lambda v: f(np.asarray(v).reshape(-1, 128).T)
    bc = lambda v: f(np.broadcast_to(np.asarray(v).reshape(1, -1), (128, np.asarray(v).size)))
    m = {}
    m["xc"] = f(np.concatenate([inp["ctx"][b], inp["x"][b]], axis=0))
    m["cc"] = f(np.stack([fm(inp["c"][b]), fm(inp["c_ctx"])], axis=-1))
    m["ada_w"] = f(inp["ada_w"][0].reshape(16, 128, 3 * D))
    m["ada_b_fm"] = fm(inp["ada_b"][0])
    m["ada_b_g"] = f(inp["ada_b"][0][2 * D:].reshape(1, D))
    m["norm_g_fm"] = fm(inp["norm_g"][0])
    m["w_in"] = f(inp["w_in"][0])
    m["hg_lb_b"] = f(np.broadcast_to(inp["hg_lb"][None], (128, 2, 2, HGW)))
    m["hgng_b"] = bc(inp["hg_norm_g"][0])
    mu = inp["rw_mu"][0]
    m["mu_fm"] = f(mu.reshape(4, 26, 128).transpose(2, 1, 0))
    m["w0_fm"] = f(inp["rw_w0"][0].reshape(2, 8, 128).transpose(2, 1, 0))
    m["a0_fm"] = f(inp["rw_a0"][0].reshape(2, 8, 128).transpose(2, 1, 0))
    m["w2"] = f(inp["rw_w2"][0].reshape(128, RWW))
    m["a2"] = f(inp["rw_a2"][0].reshape(128, RWW))
    kv = np.stack([inp["rw_kk"][0], inp["rw_ka"][0], inp["rw_rk"][0]], axis=0)
    m["kvec_fm"] = f(kv.reshape(3, 8, 128).transpose(2, 1, 0))
    m["gng_b"] = bc(inp["rw_gn_g"][0])
    m["gnb_b"] = bc(inp["rw_gn_b"][0])
    m["w_hg_o"] = f(inp["w_hg_out"][0])
    m["w_rw_o"] = f(inp["w_rw_out"][0])
    m["w_o"] = f(inp["w_out"][0])
    m["fg_b"] = bc(inp["final_g"])
    m["cm"] = make_cm()
    cst = np.zeros((128, 8), np.float32)
    cst[:, 0] = 1e-6
    cst[:, 1] = 1e-12
    cst[:, 2] = 64e-5
    cst[:, 3] = 0.0
    cst[:, 4] = 1.0
    m["cst"] = cst
    return m


_LAST = {}


def kernel(**inputs):
    inp = {k: np.asarray(v) for k, v in inputs.items()}
    nb = inp["x"].shape[0]
    nc, dbg = build_program()
    in_maps = [_host_inputs(b, inp) for b in range(nb)]
    res = run_bass_kernel_spmd(nc, in_maps, core_ids=list(range(nb)))
    if DEBUG:
        _LAST["res"] = res
    return np.stack([np.asarray(r["out"], dtype=np.float32) for r in res.results], axis=0)
```

```python
import os
import numpy as np
from contextlib import ExitStack
import concourse.bass as bass
import concourse.mybir as mybir
from concourse.bass_utils import run_bass_kernel_spmd

F32 = mybir.dt.float32
BF16 = mybir.dt.bfloat16
AF = mybir.ActivationFunctionType
ALU = mybir.AluOpType
AX = mybir.AxisListType

D = 2048
SEQ = 2048
CTX = 256
NT = SEQ + CTX
NCOLS = 13568
HGW = 1024
RWW = 1024
CH = 32
DEBUG = bool(os.environ.get("KDEBUG"))
PH = int(os.environ.get("KPHASE", "9"))
SQ = os.environ.get("KSQ", "pool")
ONLY = int(os.environ.get("KONLY", "-1"))
KLIM = int(os.environ.get("KLIM", "-1"))
CASTE = os.environ.get("KCAST", "pool")
KCH = int(os.environ.get("KCH", "99"))
KX = int(os.environ.get("KX", "0"))
KD = int(os.environ.get("KD", "2"))
KT = int(os.environ.get("KT", "9"))
KJ = int(os.environ.get("KJ", "8"))
KT0 = int(os.environ.get("KT0", "0"))


_FWREF = []


def _phase(n):
    if PH >= n and (ONLY < 0 or n == ONLY):
        fw = _FWREF[-1]
        fw.emitted = 0
        fw.limit = KLIM if (n == ONLY and KLIM >= 0) else None
        with ExitStack() as ph:
            yield ph
        print('phase', n, 'emitted', fw.emitted, flush=True)
        fw.limit = None


class Buf:
    __slots__ = ("name", "w", "r")

    def __init__(self, name=""):
        self.name = name
        self.w = None
        self.r = {}


class FW:
    NDMA = 14

    def __init__(self, nc, stack):
        self.nc = nc
        self.eng = {"pe": nc.tensor, "act": nc.scalar, "dve": nc.vector, "pool": nc.gpsimd, "sp": nc.sync}
        self.sem = {}
        self.cnt = {}
        for k in ["pe", "act", "dve", "pool"]:
            self.sem[k] = stack.enter_context(nc.semaphore("s_" + k))
            self.cnt[k] = 0
        for i in range(self.NDMA):
            k = "dma%d" % i
            self.sem[k] = stack.enter_context(nc.semaphore("s_" + k))
            self.cnt[k] = 0
        self.dma_i = 0
        self.waited = {e: {} for e in self.eng}
        self.ninstr = 0
        self.emitted = 0
        self.limit = None

    def _need(self, e, deps):
        for k, v in deps.items():
            if self.waited[e].get(k, 0) >= v:
                continue
            self.eng[e].wait_ge(self.sem[k], v)
            self.waited[e][k] = v

    def _collect(self, e, reads, writes):
        deps = {}

        def add(ev):
            if ev is None:
                return
            k, v = ev
            if k == e and e == "pe":
                return
            if deps.get(k, 0) < v:
                deps[k] = v
        for b in reads:
            add(b.w)
        for b in writes:
            add(b.w)
            for k, v in b.r.items():
                add((k, v))
        return deps

    def _mark(self, ev, reads, writes):
        k, v = ev
        for b in reads:
            if b.r.get(k, 0) < v:
                b.r[k] = v
        for b in writes:
            b.w = ev
            b.r = {}

    def _skip(self):
        self.emitted += 1
        return self.limit is not None and self.emitted > self.limit

    def op(self, e, fn, reads=(), writes=(), inc=True):
        if self._skip():
            return None
        deps = self._collect(e, reads, writes)
        self._need(e, deps)
        ins = fn(self.eng[e])
        self.ninstr += 1
        if inc:
            self.cnt[e] += 1
            ins.then_inc(self.sem[e], 1)
            ev = (e, self.cnt[e])
        else:
            ev = (e, self.cnt[e] + 1)
        self._mark(ev, reads, writes)
        return ins

    def dma(self, out, in_, reads=(), writes=(), q="sp", **kw):
        if self._skip():
            return
        i = self.dma_i
        self.dma_i += 1
        k = "dma%d" % (i % self.NDMA)
        deps = self._collect(q, reads, writes)
        if self.cnt[k] > 0 and deps.get(k, 0) < self.cnt[k]:
            deps[k] = self.cnt[k]
        self._need(q, deps)
        self.cnt[k] += 16
        self.eng[q].dma_start(out=out, in_=in_, **kw).then_inc(self.sem[k], 16)
        self.ninstr += 1
        self._mark((k, self.cnt[k]), reads, writes)

    def dma3(self, out, in_, n, **kw):
        for k in range(n):
            self.dma(out[:, k], in_[:, k], **kw)

    def barrier(self):
        for e in self.eng:
            deps = {k: v for k, v in self.cnt.items() if v > 0 and k != e}
            self._need(e, deps)


C_ID, C_TF, C_TB, C_BONES, C_LT, C_GT = 0, 1, 2, 3, 4, 5
NCM = 6


def make_cm():
    p = np.arange(128)[:, None]
    f = np.arange(128)[None, :]
    cm = np.zeros((128, NCM, 128), np.float32)
    cm[:, C_ID] = (p == f)
    cm[:, C_TF] = (p <= f)
    cm[:, C_TB] = (p >= f)
    cm[:, C_BONES] = ((p // 64) == (f // 64))
    cm[:, C_LT] = (p < f)
    cm[:, C_GT] = (p > f)
    return cm


def build_program():
    nc = bass.Bass("TRN2", target_bir_lowering=False)
    dt = lambda name, shape, kind="ExternalInput": nc.dram_tensor(name, shape, F32, kind=kind).ap()
    SCR = "ExternalOutput" if DEBUG else "Internal"
    xc = dt("xc", [NT, D])
    cc_d = dt("cc", [128, 16, 2])
    ada_w = dt("ada_w", [16, 128, 3 * D])
    ada_b_fm = dt("ada_b_fm", [128, 48])
    ada_b_g = dt("ada_b_g", [1, D])
    norm_g_fm = dt("norm_g_fm", [128, 16])
    w_in = dt("w_in", [D, NCOLS])
    hg_lb_b = dt("hg_lb_b", [128, 2, 2, HGW])
    hgng_b = dt("hgng_b", [128, HGW])
    mu_fm = dt("mu_fm", [128, 26, 4])
    w0_fm = dt("w0_fm", [128, 8, 2])
    a0_fm = dt("a0_fm", [128, 8, 2])
    w2_d = dt("w2", [128, RWW])
    a2_d = dt("a2", [128, RWW])
    kvec_fm = dt("kvec_fm", [128, 8, 3])
    gng_b = dt("gng_b", [128, RWW])
    gnb_b = dt("gnb_b", [128, RWW])
    w_hg_o = dt("w_hg_o", [HGW, D])
    w_rw_o = dt("w_rw_o", [RWW, D])
    w_o = dt("w_o", [D, D])
    fg_b = dt("fg_b", [128, D])
    cm_d = dt("cm", [128, NCM, 128])
    cst_d = dt("cst", [128, 8])
    out = dt("out", [SEQ, D], kind="ExternalOutput")
    HGtok = dt("HGtok", [NT, 5120], SCR)
    RWfm = dt("RWfm", [NCOLS - 5120, NT], SCR)
    OFs = dt("OFs", [SEQ, HGW], SCR)
    YH = dt("YH", [HGW, SEQ], SCR)
    AL = [dt("AL%d" % d, [RWW, NT], SCR) for d in range(2)]
    BE = [dt("BE%d" % d, [RWW, NT], SCR) for d in range(2)]
    KA = [dt("KA%d" % d, [RWW, NT], SCR) for d in range(2)]
    RH = [dt("RH%d" % d, [RWW, NT], SCR) for d in range(2)]
    BEt = [dt("BEt%d" % d, [NT, RWW], SCR) for d in range(2)]
    KAt = [dt("KAt%d" % d, [NT, RWW], SCR) for d in range(2)]
    Vt = dt("Vt", [NT, RWW], SCR)
    BON = dt("BON", [RWW, SEQ], SCR)
    YF = dt("YF", [SEQ, RWW], SCR)
    YR = dt("YR", [RWW, SEQ], SCR)
    scr_bufs = {}

    def sbuf_of(name):
        if name not in scr_bufs:
            scr_bufs[name] = Buf(name)
        return scr_bufs[name]

    with ExitStack() as st:
        fw = FW(nc, st)
        _FWREF.append(fw)
        _acct = {}

        def sbt(stack, name, shape):
            _acct[id(stack)] = _acct.get(id(stack), 0) + int(np.prod(shape[1:])) * 4
            if os.environ.get("KACCT"):
                print("sbuf", name, shape, "stack total KiB", _acct[id(stack)] / 1024.0, flush=True)
            return stack.enter_context(nc.sbuf_tensor("sb_" + name, shape, F32))
        pst = lambda stack, name, shape: stack.enter_context(nc.psum_tensor("ps_" + name, shape, F32))
        op = fw.op
        cm = sbt(st, "cm", [128, NCM, 128]); b_cm = Buf()
        cst = sbt(st, "cst", [128, 8]); b_cst = Buf()
        modA = sbt(st, "modA", [128, 16, 2]); modB = sbt(st, "modB", [128, 16, 2]); b_mod = Buf()
        gate_b = sbt(st, "gate_b", [128, D]); b_gate = Buf()
        wc = [sbt(st, "wc%d" % d, [128, 8, 18]) for d in range(2)]; b_wc = [Buf(), Buf()]
        fw.dma(cm[:], cm_d, writes=[b_cm])
        fw.dma(cst[:], cst_d, writes=[b_cst])
        ident = cm[:, C_ID, :]
        EPS6, EPS12, EPSGN, ZERO, ONE = (cst[:, i:i + 1] for i in range(5))

        for ph in _phase(0):
            adw = [sbt(ph, "adw%d" % i, [128, 3 * D]) for i in range(2)]; b_adw = [Buf(), Buf()]
            cct = sbt(ph, "cct", [128, 16, 2]); scc = sbt(ph, "scc", [128, 16, 2]); b_cc = Buf(); b_scc = Buf()
            mod = sbt(ph, "mod", [128, 48, 2]); b_modt = Buf()
            adb = sbt(ph, "adb", [128, 48]); ng = sbt(ph, "ng", [128, 16]); b_sm = Buf()
            adbg = sbt(ph, "adbg", [1, D]); grow = sbt(ph, "grow", [1, D]); b_grow = Buf()
            ps_mod = pst(ph, "ps_mod", [128, 96]); b_psm = Buf()
            ps_g = [pst(ph, "ps_g%d" % i, [128, 512]) for i in range(4)]; b_psg = [Buf() for _ in range(4)]
            fw.dma(cct[:], cc_d, writes=[b_cc])
            fw.dma(adb[:], ada_b_fm, writes=[b_sm])
            fw.dma(ng[:], norm_g_fm, writes=[b_sm])
            fw.dma(adbg[:], ada_b_g, writes=[b_sm])
            op("act", lambda e: e.activation(scc[:], cct[:], AF.Silu), reads=[b_cc], writes=[b_scc])
            op("dve", lambda e: e.memset(mod[:], 0.0), writes=[b_modt])
            for k in range(16):
                fw.dma(adw[k % 2][:], ada_w[k], writes=[b_adw[k % 2]])
                for m in range(48):
                    op("pe", lambda e: e.matmul(ps_mod[:, 2 * m:2 * m + 2], adw[k % 2][:, m * 128:(m + 1) * 128],
                                                scc[:, k, :], start=True, stop=True),
                       reads=[b_adw[k % 2], b_scc], writes=[b_psm], inc=(m == 47))
                op("dve", lambda e: e.tensor_tensor(mod[:], mod[:], ps_mod[:].rearrange("p (m v) -> p m v", v=2), ALU.add),
                   reads=[b_psm, b_modt], writes=[b_modt])
                for n in range(4):
                    op("pe", lambda e: e.matmul(ps_g[n][0:1, :], scc[:, k, 0:1], adw[k % 2][:, 2 * D + n * 512:2 * D + (n + 1) * 512],
                                                start=(k == 0), stop=(k == 15)),
                       reads=[b_adw[k % 2], b_scc], writes=[b_psg[n]])
            op("dve", lambda e: e.tensor_tensor(mod[:], mod[:], adb[:].unsqueeze(2).to_broadcast([128, 48, 2]), ALU.add),
               reads=[b_sm, b_modt], writes=[b_modt])
            op("dve", lambda e: e.tensor_scalar(modA[:], mod[:, 16:32, :], 1.0, None, ALU.add), reads=[b_modt], writes=[b_mod])
            op("dve", lambda e: e.tensor_tensor(modA[:], modA[:], ng[:].unsqueeze(2).to_broadcast([128, 16, 2]), ALU.mult),
               reads=[b_sm, b_mod], writes=[b_mod])
            op("dve", lambda e: e.tensor_copy(modB[:], mod[:, 0:16, :]), reads=[b_modt], writes=[b_mod])
            for n in range(4):
                op("dve", lambda e: e.tensor_tensor(grow[:, n * 512:(n + 1) * 512], ps_g[n][0:1, :], adbg[:, n * 512:(n + 1) * 512], ALU.add),
                   reads=[b_psg[n], b_sm], writes=[b_grow])
            for n in range(4):
                op("pe", lambda e: e.matmul(ps_g[n][:], cm[0:1, C_TF, :], grow[:, n * 512:(n + 1) * 512], start=True, stop=True),
                   reads=[b_cm, b_grow], writes=[b_psg[n]])
                op("act", lambda e: e.activation(gate_b[:, n * 512:(n + 1) * 512], ps_g[n][:], AF.Identity, scale=1.0),
                   reads=[b_psg[n]], writes=[b_gate])
        fw.barrier()

        for ph in _phase(1):
            lb_b = sbt(ph, "lb_b", [128, 2, HGW]); oml_b = sbt(ph, "oml_b", [128, 2, HGW])
            b_lb = Buf()
            xt = [sbt(ph, "xt%d" % i, [128, D]) for i in range(2)]; b_xt = [Buf(), Buf()]
            junk = sbt(ph, "junk", [128, D]); b_junk = Buf()
            ss = sbt(ph, "ss", [128, 2]); b_ss = Buf()
            hT = ph.enter_context(nc.sbuf_tensor("sb_hT", [128, 16, 1024], BF16)); b_hT = [Buf() for _ in range(8)]
            wt = [sbt(ph, "wt0", [128, 16, 512])] * 2; b_wt = [Buf()] * 2
            wb16 = [ph.enter_context(nc.sbuf_tensor("sb_wb16_%d" % i, [128, 16, 512], BF16)) for i in range(2)]; b_wb16 = [Buf(), Buf()]
            lbt = wt[1][:, 0:8, :].rearrange("p a b -> p (a b)").rearrange("p (d l c) -> p d l c", d=2, l=2)
            b_lbt = b_wt[1]
            NOT = 4
            ot = [sbt(ph, "ot%d" % i, [128, 512]) for i in range(NOT)]; b_ot = [Buf() for _ in range(NOT)]
            tps = [pst(ph, "tps%d" % i, [128, 512]) for i in range(4)]; b_tps = [Buf() for _ in range(4)]
            acc = [pst(ph, "acc%d" % i, [128, 512]) for i in range(4)]; b_acc = [Buf() for _ in range(4)]
            fw.dma(lbt, hg_lb_b, writes=[b_lbt])
            op("dve", lambda e: e.tensor_tensor(lb_b[:], lbt[:, :, 0, :], lbt[:, :, 1, :], ALU.subtract), reads=[b_lbt], writes=[b_lb])
            op("act", lambda e: e.activation(lb_b[:], lb_b[:], AF.Sigmoid), reads=[b_lb], writes=[b_lb])
            op("dve", lambda e: e.tensor_scalar(oml_b[:], lb_b[:], -1.0, 1.0, ALU.mult, ALU.add), reads=[b_lb], writes=[b_lb])
            w_in_r = w_in.rearrange("(k p) c -> p k c", p=128)
            oti = 0
            ctx_groups = {2, 3, 4, 5, 6, 7, 12, 13, 14, 15, 16}
            tiles = [(0, 256, True)] + [(256 + 1024 * i, 1024, False) for i in range(2)]
            wti = 0
            for (g0, ntok, isctx) in tiles:
                v = 1 if isctx else 0
                nsub = ntok // 128
                for sub in range(nsub):
                    xb = sub % 2
                    fw.dma(xt[xb][:], xc[g0 + sub * 128:g0 + (sub + 1) * 128, :], writes=[b_xt[xb]])
                    op("act", lambda e: e.activation(junk[:], xt[xb][:], AF.Square, accum_out=ss[:, 0:1]),
                       reads=[b_xt[xb]], writes=[b_junk, b_ss])
                    op("act", lambda e: e.activation(ss[:, 1:2], ss[:, 0:1], AF.Sqrt, bias=EPS6, scale=1.0 / D),
                       reads=[b_ss, b_cst], writes=[b_ss])
                    op("dve", lambda e: e.reciprocal(ss[:, 1:2], ss[:, 1:2]), reads=[b_ss], writes=[b_ss])
                    op("act", lambda e: e.activation(junk[:], xt[xb][:], AF.Identity, scale=ss[:, 1:2], bias=ZERO),
                       reads=[b_xt[xb], b_ss, b_cst], writes=[b_junk])
                    for q in range(4):
                        for i in range(4):
                            j = q * 4 + i
                            op("pe", lambda e: e.transpose(tps[q][:, i * 128:(i + 1) * 128], junk[:, j * 128:(j + 1) * 128], ident),
                               reads=[b_junk, b_cm], writes=[b_tps[q]], inc=(i == 3))
                        for i in range(4):
                            j = q * 4 + i
                            en = "dve" if (i % 2 == 0) else "act"
                            if en == "dve":
                                op("dve", lambda e: e.tensor_scalar(hT[:, j, sub * 128:(sub + 1) * 128], tps[q][:, i * 128:(i + 1) * 128],
                                                                    modA[:, j, v:v + 1], modB[:, j, v:v + 1], ALU.mult, ALU.add),
                                   reads=[b_tps[q], b_mod], writes=[b_hT[sub]])
                            else:
                                op("act", lambda e: e.activation(hT[:, j, sub * 128:(sub + 1) * 128], tps[q][:, i * 128:(i + 1) * 128],
                                                                 AF.Identity, scale=modA[:, j, v:v + 1], bias=modB[:, j, v:v + 1]),
                                   reads=[b_tps[q], b_mod], writes=[b_hT[sub]])
                for g in range(27):
                    if isctx and g not in ctx_groups:
                        continue
                    c0 = g * 512
                    ncol = min(512, NCOLS - c0)
                    wb = wti % 2
                    wti += 1
                    fw.dma3(wt[wb][:, :, :ncol], w_in_r[:, :, c0:c0 + ncol], 16, writes=[b_wt[wb]])
                    op(CASTE, lambda e: e.tensor_copy(wb16[wb][:, :, :ncol], wt[wb][:, :, :ncol]), reads=[b_wt[wb]], writes=[b_wb16[wb]])
                    if c0 < 5120:
                        typ = ["silu", "id", "fg0", "fg1", "silu"][c0 // 1024]
                        for sub in range(nsub):
                            a = sub % 4
                            for k in range(16):
                                op("pe", lambda e: e.matmul(acc[a][:, :ncol], hT[:, k, sub * 128:(sub + 1) * 128], wb16[wb][:, k, :ncol],
                                                            start=(k == 0), stop=(k == 15)),
                                   reads=[b_hT[sub], b_wb16[wb]], writes=[b_acc[a]], inc=(k == 15))
                            o_ = oti % NOT
                            oti += 1
                            if typ == "silu":
                                op("act", lambda e: e.activation(ot[o_][:], acc[a][:], AF.Silu), reads=[b_acc[a]], writes=[b_ot[o_]])
                            elif typ == "id":
                                op("dve", lambda e: e.tensor_copy(ot[o_][:], acc[a][:]), reads=[b_acc[a]], writes=[b_ot[o_]])
                            else:
                                dd = int(typ[2])
                                cc0 = c0 - (2048 + dd * 1024)
                                op("act", lambda e: e.activation(ot[o_][:], acc[a][:], AF.Sigmoid), reads=[b_acc[a]], writes=[b_ot[o_]])
                                op("dve", lambda e: e.tensor_tensor(ot[o_][:], ot[o_][:], oml_b[:, dd, cc0:cc0 + 512], ALU.mult),
                                   reads=[b_ot[o_], b_lb], writes=[b_ot[o_]])
                                op("dve", lambda e: e.tensor_tensor(ot[o_][:], ot[o_][:], lb_b[:, dd, cc0:cc0 + 512], ALU.add),
                                   reads=[b_ot[o_], b_lb], writes=[b_ot[o_]])
                            fw.dma(HGtok[g0 + sub * 128:g0 + (sub + 1) * 128, c0:c0 + 512], ot[o_][:],
                                   reads=[b_ot[o_]], writes=[sbuf_of("HGtok")], q=SQ)
                    else:
                        for m in range(ncol // 128):
                            mc = c0 // 128 + m
                            if isctx and not (48 <= mc <= 65):
                                continue
                            for hf in range((ntok + 511) // 512):
                                nt_ = min(512, ntok - hf * 512)
                                tk0 = hf * 512
                                a = (m * 2 + hf) % 4
                                for k in range(16):
                                    op("pe", lambda e: e.matmul(acc[a][:, :nt_], wb16[wb][:, k, m * 128:(m + 1) * 128], hT[:, k, tk0:tk0 + nt_],
                                                                start=(k == 0), stop=(k == 15)),
                                       reads=b_hT[tk0 // 128:(tk0 + nt_) // 128] + [b_wb16[wb]], writes=[b_acc[a]], inc=(k == 15))
                                o_ = oti % NOT
                                oti += 1
                                if mc <= 65:
                                    op("dve", lambda e: e.tensor_copy(ot[o_][:, :nt_], acc[a][:, :nt_]), reads=[b_acc[a]], writes=[b_ot[o_]])
                                else:
                                    fn = AF.Silu if mc <= 73 else AF.Sigmoid
                                    op("act", lambda e: e.activation(ot[o_][:, :nt_], acc[a][:, :nt_], fn), reads=[b_acc[a]], writes=[b_ot[o_]])
                                fw.dma(RWfm[(mc - 40) * 128:(mc - 39) * 128, g0 + tk0:g0 + tk0 + nt_], ot[o_][:, :nt_],
                                       reads=[b_ot[o_]], writes=[sbuf_of("RWfm")], q=SQ)
        fw.barrier()

        for ph in _phase(2):
            NB = 2
            fgt = [sbt(ph, "fgt%d" % i, [CH, HGW]) for i in range(NB)]
            vtt = [sbt(ph, "vtt%d" % i, [CH, HGW]) for i in range(NB)]
            qtt = [sbt(ph, "qtt%d" % i, [CH, HGW]) for i in range(NB)]
            ztt = [sbt(ph, "ztt%d" % i, [CH, HGW]) for i in range(NB)]
            oft = [sbt(ph, "oft%d" % i, [CH, HGW]) for i in range(NB)]
            b_ld = [[Buf() for _ in range(5)] for _ in range(NB)]
            gt = sbt(ph, "gt", [CH, HGW]); b_gt = Buf()
            Et = sbt(ph, "Et", [CH, HGW]); Ei = sbt(ph, "Ei", [CH, HGW]); b_E = Buf(); b_Ei = Buf()
            kt_ = sbt(ph, "kt", [128, HGW]); qq_ = sbt(ph, "qq", [128, HGW]); b_kt = Buf(); b_qq = Buf()
            kt = kt_[0:CH, :]; qq = qq_[0:CH, :]
            ob = sbt(ph, "ob", [CH, HGW]); sq_ = sbt(ph, "sq", [128, HGW]); b_ob = Buf(); b_sq = Buf()
            sq = sq_[0:CH, :]
            ms = sbt(ph, "ms", [CH, 8]); b_ms = Buf()
            eb = sbt(ph, "eb", [128, 8]); b_eb = Buf()
            hng = sbt(ph, "hng", [CH, HGW]); b_hng = Buf()
            qkT = [sbt(ph, "qkT%d" % i, [128, 2, CH]) for i in range(2)]; b_qkT = [Buf(), Buf()]
            at = [sbt(ph, "at%d" % i, [CH, CH]) for i in range(2)]; b_at = [Buf(), Buf()]
            S = [sbt(ph, "S%d" % i, [128, 8, 128]) for i in range(2)]; b_S = [Buf(), Buf()]
            qkA = sbt(ph, "qkA", [128, 8, 2, CH]); b_qkA = Buf()
            atA = sbt(ph, "atA", [CH, 8, CH]); b_atA = Buf()
            stmp = sbt(ph, "stmp", [128, 8, 128]); b_stmp = Buf()
            yT = sbt(ph, "yTs", [128, 8, 512]); b_yT = Buf()
            bc_ps = [pst(ph, "bc_ps%d" % i, [128, 512]) for i in range(2)]; b_bc = [Buf(), Buf()]
            o_ps = [pst(ph, "o_ps%d" % i, [128, 512]) for i in range(2)]; b_o = [Buf(), Buf()]
            dS_ps = [pst(ph, "dS_ps%d" % i, [128, 512]) for i in range(2)]; b_dS = [Buf(), Buf()]
            mz = pst(ph, "mz", [128, 512]); b_tp = [Buf()] * 2; b_aps = [Buf()] * 2; b_ebp = b_aps[0]
            yT_ps = pst(ph, "yT_ps", [128, 512]); b_yTp = Buf()
            fw.dma(hng[:], hgng_b[0:CH, :], writes=[b_hng])
            op("dve", lambda e: e.memset(kt_[:], 0.0), writes=[b_kt])
            op("dve", lambda e: e.memset(qq_[:], 0.0), writes=[b_qq])
            op("dve", lambda e: e.memset(sq_[:], 0.0), writes=[b_sq])
            ci = 0
            for d in range(min(2, KD)):
                tri = cm[0:CH, C_TF if d == 0 else C_TB, 0:CH]
                lastc = CH - 1 if d == 0 else 0
                onehot = cm[0:CH, C_ID, lastc:lastc + 1]
                order = list(range(NT // CH)) if d == 0 else list(range(CTX // CH - 1, -1, -1)) + list(range(NT // CH - 1, CTX // CH - 1, -1))
                cur = 0
                op("dve", lambda e: e.memset(S[0][:], 0.0), writes=[b_S[0]])
                for c in order[:KCH]:
                    isctx = c < CTX // CH
                    t0 = c * CH
                    lb_ = ci % NB
                    ci += 1
                    fgs, vs, qs, zs, ofs = fgt[lb_], vtt[lb_], qtt[lb_], ztt[lb_], oft[lb_]
                    bl = b_ld[lb_]
                    fw.dma(fgs[:], HGtok[t0:t0 + CH, 2048 + d * 1024:3072 + d * 1024], reads=[sbuf_of("HGtok")], writes=[bl[0]])
                    fw.dma(vs[:], HGtok[t0:t0 + CH, 1024:2048], reads=[sbuf_of("HGtok")], writes=[bl[1]])
                    if not isctx:
                        fw.dma(qs[:], HGtok[t0:t0 + CH, 0:1024], reads=[sbuf_of("HGtok")], writes=[bl[2]])
                        if d == 1:
                            fw.dma(zs[:], HGtok[t0:t0 + CH, 4096:5120], reads=[sbuf_of("HGtok")], writes=[bl[3]])
                            fw.dma(ofs[:], OFs[t0 - CTX:t0 - CTX + CH, :], reads=[sbuf_of("OFs")], writes=[bl[4]])
                    op("act", lambda e: e.activation(gt[:], fgs[:], AF.Ln), reads=[bl[0]], writes=[b_gt])
                    for n in range(2):
                        op("pe", lambda e: e.matmul(bc_ps[n][0:CH, :], tri, gt[:, n * 512:(n + 1) * 512], start=True, stop=True),
                           reads=[b_gt, b_cm], writes=[b_bc[n]])
                    for n in range(2):
                        sl = slice(n * 512, (n + 1) * 512)
                        op("act", lambda e: e.activation(Ei[:, sl], bc_ps[n][0:CH, :], AF.Exp, scale=-1.0), reads=[b_bc[n]], writes=[b_Ei])
                        op("act", lambda e: e.activation(Et[:, sl], bc_ps[n][0:CH, :], AF.Exp), reads=[b_bc[n]], writes=[b_E])
                    op("dve", lambda e: e.tensor_scalar(kt[:], fgs[:], -1.0, 1.0, ALU.mult, ALU.add), reads=[bl[0]], writes=[b_kt])
                    op("dve", lambda e: e.tensor_tensor(kt[:], kt[:], Ei[:], ALU.mult), reads=[b_kt, b_Ei], writes=[b_kt])
                    if not isctx:
                        op("dve", lambda e: e.tensor_tensor(qq[:], qs[:], Et[:], ALU.mult), reads=[bl[2], b_E], writes=[b_qq])
                    for h in range(8):
                        op("pe", lambda e: e.matmul(yT_ps[:, h:h + 1], Et[:, h * 128:(h + 1) * 128], onehot, start=True, stop=True),
                           reads=[b_E, b_cm], writes=[b_ebp], inc=(h == 7))
                    op("dve", lambda e: e.tensor_copy(eb[:], yT_ps[:, 0:8]), reads=[b_ebp], writes=[b_eb])
                    nxt = 1 - cur
                    if not isctx:
                        for h in range(8):
                            hs = slice(h * 128, (h + 1) * 128)
                            tpv = mz[:, (h % 2) * 256:(h % 2 + 1) * 256]
                            op("pe", lambda e: e.transpose(tpv[:, 0:128], qq_[:, hs], ident), reads=[b_qq, b_cm], writes=[b_tp[0]], inc=False)
                            op("pe", lambda e: e.transpose(tpv[:, 128:256], kt_[:, hs], ident), reads=[b_kt, b_cm], writes=[b_tp[0]])
                            if h % 2 == 1:
                                for hh in range(2):
                                    tq = mz[:, hh * 256:(hh + 1) * 256]
                                    op("dve", lambda e: e.tensor_copy(qkA[:, h - 1 + hh, :, :], tq.rearrange("p (a b) -> p a b", b=128)[:, :, 0:CH]),
                                       reads=[b_tp[0]], writes=[b_qkA])
                        for h in range(8):
                            op("pe", lambda e: e.matmul(yT_ps[0:CH, 256 + h * CH:256 + (h + 1) * CH], qkA[:, h, 1, :], qkA[:, h, 0, :], start=True, stop=True),
                               reads=[b_qkA], writes=[b_aps[0]], inc=(h == 7))
                        op("dve", lambda e: e.tensor_tensor(atA[:], yT_ps[0:CH, 256:256 + 8 * CH].rearrange("p (h t) -> p h t", t=CH),
                                                            tri.unsqueeze(1).to_broadcast([CH, 8, CH]), ALU.mult),
                           reads=[b_aps[0], b_cm], writes=[b_atA])
                    for h in range(8):
                        hs = slice(h * 128, (h + 1) * 128)
                        if not isctx:
                            opv = o_ps[h // 4][0:CH, (h % 4) * 128:(h % 4 + 1) * 128]
                            op("pe", lambda e: e.matmul(opv, atA[:, h, :], vs[:, hs], start=True, stop=False),
                               reads=[b_atA, bl[1]], writes=[b_o[h // 4]], inc=False)
                            op("pe", lambda e: e.matmul(opv, qkA[:, h, 0, :], S[cur][:, h, :], start=False, stop=True),
                               reads=[b_qkA, b_S[cur]], writes=[b_o[h // 4]])
                        dsv = dS_ps[h // 4][:, (h % 4) * 128:(h % 4 + 1) * 128]
                        op("pe", lambda e: e.matmul(dsv, kt[:, hs], vs[:, hs], start=True, stop=True),
                           reads=[b_kt, bl[1]], writes=[b_dS[h // 4]])
                    for n in range(2):
                        op("dve", lambda e: e.tensor_tensor(stmp[:, n * 4:(n + 1) * 4, :], dS_ps[n][:].rearrange("p (h v) -> p h v", v=128),
                                                            S[cur][:, n * 4:(n + 1) * 4, :], ALU.add),
                           reads=[b_dS[n], b_S[cur]], writes=[b_stmp])
                    op("dve", lambda e: e.tensor_tensor(S[nxt][:], stmp[:], eb[:].unsqueeze(2).to_broadcast([128, 8, 128]), ALU.mult),
                       reads=[b_stmp, b_eb], writes=[b_S[nxt]])
                    cur = nxt
                    if isctx or (KX & 8):
                        continue
                    xt0 = t0 - 256
                    if d == 0:
                        for n in range(2):
                            op("dve", lambda e: e.tensor_copy(ob[:, n * 512:(n + 1) * 512], o_ps[n][0:CH, :]),
                               reads=[b_o[n]], writes=[b_ob])
                        fw.dma(OFs[xt0:xt0 + CH, :], ob[:], reads=[b_ob], writes=[sbuf_of("OFs")], q=SQ)
                    else:
                        for n in range(2):
                            op("dve", lambda e: e.tensor_tensor(ob[:, n * 512:(n + 1) * 512], o_ps[n][0:CH, :], ofs[:, n * 512:(n + 1) * 512], ALU.add),
                               reads=[b_o[n], bl[4]], writes=[b_ob])
                        op("dve", lambda e: e.tensor_tensor(sq[:], ob[:], ob[:], ALU.mult), reads=[b_ob], writes=[b_sq])
                        op("dve", lambda e: e.tensor_reduce(ms[:], sq[:].rearrange("p (h v) -> p h v", v=128), AX.X, ALU.add), reads=[b_sq], writes=[b_ms])
                        op("act", lambda e: e.activation(ms[:], ms[:], AF.Sqrt, bias=cst[0:CH, 0:1], scale=1.0 / 128), reads=[b_ms, b_cst], writes=[b_ms])
                        op("dve", lambda e: e.reciprocal(ms[:], ms[:]), reads=[b_ms], writes=[b_ms])
                        op("dve", lambda e: e.tensor_tensor(sq[:].rearrange("p (h v) -> p h v", v=128), ob[:].rearrange("p (h v) -> p h v", v=128),
                                                            ms[:].unsqueeze(2).to_broadcast([CH, 8, 128]), ALU.mult),
                           reads=[b_ob, b_ms], writes=[b_sq])
                        op("dve", lambda e: e.tensor_tensor(sq[:], sq[:], hng[:], ALU.mult), reads=[b_sq, b_hng], writes=[b_sq])
                        op("dve", lambda e: e.tensor_tensor(sq[:], sq[:], zs[:], ALU.mult), reads=[b_sq, bl[3]], writes=[b_sq])
                        for h in range(8):
                            op("pe", lambda e: e.transpose(bc_ps[h // 4][:, (h % 4) * 128:(h % 4 + 1) * 128], sq_[:, h * 128:(h + 1) * 128], ident),
                               reads=[b_sq, b_cm], writes=[b_bc[h // 4]], inc=(h % 4 == 3))
                        for n in range(2):
                            yo_ = xt0 % 512
                            op("dve", lambda e: e.tensor_copy(yT[:, n * 4:(n + 1) * 4, yo_:yo_ + CH], bc_ps[n][:].rearrange("p (h t) -> p h t", t=128)[:, :, 0:CH]),
                               reads=[b_bc[n]], writes=[b_yT])
                        if xt0 % 512 == 0:
                            fw.dma3(YH.rearrange("(h p) t -> p h t", p=128)[:, :, xt0:xt0 + 512], yT[:], 8, reads=[b_yT], writes=[sbuf_of("YH")], q=SQ)
        fw.barrier()

        for ph in _phase(3):
            W = 640
            mu = sbt(ph, "mu", [128, 26, 4]); omu = sbt(ph, "omu", [128, 26]); b_mu = Buf()
            w0t = sbt(ph, "w0t", [128, 8, 2]); a0t = sbt(ph, "a0t", [128, 8, 2]); kvt = sbt(ph, "kvt", [128, 8, 3]); omka = sbt(ph, "omka", [128, 8])
            w2t = sbt(ph, "w2t", [128, RWW]); a2t = sbt(ph, "a2t", [128, RWW]); b_par = Buf()
            raw = [sbt(ph, "raw%d" % i, [128, W]) for i in range(3)]; b_raw = [Buf() for _ in range(3)]
            rawl = [sbt(ph, "rawl%d" % i, [128, W]) for i in range(2)]; b_rawl = [Buf(), Buf()]
            sh = [sbt(ph, "sh%d" % i, [128, 512]) for i in range(3)]; b_sh = [Buf() for _ in range(3)]
            tw = sbt(ph, "tw", [128, 512]); als = sbt(ph, "als", [128, 512]); b_tw = Buf(); b_als = Buf()
            kkr = sbt(ph, "kkr", [128, 512]); kk = sbt(ph, "kk", [128, 512]); t1 = sbt(ph, "t1", [128, 512]); b_kkr = Buf(); b_kk = Buf(); b_t1 = Buf()
            lgw = sbt(ph, "lgw", [128, 512]); av = sbt(ph, "av", [128, 512]); b_lgw = Buf(); b_av = Buf()
            kd = [sbt(ph, "kd%d" % i, [128, 512]) for i in range(2)]; b_kd = [Buf(), Buf()]
            bv = sbt(ph, "bv", [128, 512]); b_bv = Buf()
            lt = sbt(ph, "lt", [128, 128]); b_lt = Buf()
            Ecw = sbt(ph, "Ecw", [128, 512]); Einv = sbt(ph, "Einv", [128, 512]); Eex = sbt(ph, "Eex", [128, 512]); b_Ecw = Buf(); b_Einv = Buf(); b_Eex = Buf()
            res = [sbt(ph, "res%d" % i, [128, 512]) for i in range(4)]; b_res = [Buf() for _ in range(4)]
            tm = [[sbt(ph, "tm%d_%d" % (a_, s_), [128, RWW]) for s_ in range(4)] for a_ in range(5)]; b_tm = [[Buf() for _ in range(4)] for _ in range(5)]
            p_a = pst(ph, "p_a", [128, 512]); p_b = pst(ph, "p_b", [128, 512]); p_cw = pst(ph, "p_cw", [128, 512]); p_tq = [pst(ph, "p_t%d" % i, [128, 512]) for i in range(4)]
            p_s = pst(ph, "p_s", [128, 512])
            b_pa = Buf(); b_pb = Buf(); b_pcw = Buf(); b_pt = [Buf() for _ in range(4)]; b_ps = Buf()
            for (tl, src) in [(mu, mu_fm), (w0t, w0_fm), (a0t, a0_fm), (kvt, kvec_fm), (w2t, w2_d), (a2t, a2_d)]:
                fw.dma(tl[:], src, writes=[b_par if tl is not mu else b_mu])
            op("dve", lambda e: e.tensor_reduce(omu[:], mu[:], AX.X, ALU.add), reads=[b_mu], writes=[b_mu])
            op("dve", lambda e: e.tensor_scalar(omu[:], omu[:], -1.0, 1.0, ALU.mult, ALU.add), reads=[b_mu], writes=[b_mu])
            op("dve", lambda e: e.tensor_scalar(omka[:], kvt[:, :, 1], -1.0, 1.0, ALU.mult, ALU.add), reads=[b_par], writes=[b_par])
            omuc = sbt(ph, "omuc", [128, 26])
            op("dve", lambda e: e.tensor_tensor(omuc[:], mu[:, :, 0], mu[:, :, 1], ALU.add), reads=[b_mu], writes=[b_mu])
            op("dve", lambda e: e.tensor_scalar(omuc[:], omuc[:], -1.0, 1.0, ALU.mult, ALU.add), reads=[b_mu], writes=[b_mu])

            def shift(dst, b_dst, rawt, b_rawt, chunk, g0, ntok, isctx, eng="dve"):
                rows = RWfm[chunk * 128:(chunk + 1) * 128, :]
                if isctx:
                    fw.dma(rawt[:, 0:256], rows[:, 0:256], reads=[sbuf_of("RWfm")], writes=[b_rawt])
                    P = rawt[:, 0:256]
                    op(eng, lambda e: e.tensor_scalar(dst[:, 0:256], P, omuc[:, chunk:chunk + 1], None, ALU.mult), reads=[b_rawt, b_mu], writes=[b_dst])
                    op(eng, lambda e: e.scalar_tensor_tensor(dst[:, 1:256], P[:, 0:255], mu[:, chunk, 0:1], dst[:, 1:256], ALU.mult, ALU.add),
                       reads=[b_rawt, b_mu, b_dst], writes=[b_dst])
                    op(eng, lambda e: e.scalar_tensor_tensor(dst[:, 0:255], P[:, 1:256], mu[:, chunk, 1:2], dst[:, 0:255], ALU.mult, ALU.add),
                       reads=[b_rawt, b_mu, b_dst], writes=[b_dst])
                    return
                lo = g0 - 64
                hi = g0 + ntok + 64
                first = (g0 == CTX)
                last = (g0 + ntok == NT)
                if first:
                    op(eng, lambda e: e.memset(rawt[:, 0:64], 0.0), writes=[b_rawt])
                if last:
                    op(eng, lambda e: e.memset(rawt[:, W - 64:W], 0.0), writes=[b_rawt])
                a_ = 64 if first else 0
                b_ = W - 64 if last else W
                fw.dma(rawt[:, a_:b_], rows[:, lo + a_:lo + b_], reads=[sbuf_of("RWfm")], writes=[b_rawt])
                P3 = rawt[:].rearrange("p (r c) -> p r c", c=64)
                Pc = P3[:, 1:9, :]
                d3 = dst[:].rearrange("p (r c) -> p r c", c=64)
                op(eng, lambda e: e.tensor_scalar(d3, Pc, omu[:, chunk:chunk + 1], None, ALU.mult), reads=[b_rawt, b_mu], writes=[b_dst])
                op(eng, lambda e: e.scalar_tensor_tensor(d3[:, :, 1:], Pc[:, :, :-1], mu[:, chunk, 0:1], d3[:, :, 1:], ALU.mult, ALU.add),
                   reads=[b_rawt, b_mu, b_dst], writes=[b_dst])
                op(eng, lambda e: e.scalar_tensor_tensor(d3[:, :, :-1], Pc[:, :, 1:], mu[:, chunk, 1:2], d3[:, :, :-1], ALU.mult, ALU.add),
                   reads=[b_rawt, b_mu, b_dst], writes=[b_dst])
                op(eng, lambda e: e.scalar_tensor_tensor(d3, P3[:, 0:8, :], mu[:, chunk, 2:3], d3, ALU.mult, ALU.add),
                   reads=[b_rawt, b_mu, b_dst], writes=[b_dst])
                op(eng, lambda e: e.scalar_tensor_tensor(d3, P3[:, 2:10, :], mu[:, chunk, 3:4], d3, ALU.mult, ALU.add),
                   reads=[b_rawt, b_mu, b_dst], writes=[b_dst])

            tiles = [(0, 256, True)] + [(256 + 512 * i, 512, False) for i in range(4)]
            ri = 0
            for (g0, ntok, isctx) in tiles[KT0:KT]:
                nsub = ntok // 128
                blk0 = g0 // 128
                N_ = slice(0, ntok)
                shift(tw, b_tw, rawl[0], b_rawl[0], 24, g0, ntok, isctx)
                if not (KX & 32):
                    op("act", lambda e: e.activation(tw[:, N_], tw[:, N_], AF.Tanh), reads=[b_tw], writes=[b_tw])
                shift(als, b_als, rawl[1], b_rawl[1], 25, g0, ntok, isctx)
                for j in range(KJ):
                    if not isctx:
                        shift(sh[0], b_sh[0], raw[0], b_raw[0], j, g0, ntok, isctx)
                    shift(sh[1], b_sh[1], raw[1], b_raw[1], 8 + j, g0, ntok, isctx)
                    shift(sh[2], b_sh[2], raw[2], b_raw[2], 16 + j, g0, ntok, isctx)
                    rs, ks, vs = sh[0], sh[1], sh[2]
                    op("dve", lambda e: e.tensor_scalar(kkr[:, N_], ks[:, N_], kvt[:, j, 0:1], None, ALU.mult), reads=[b_sh[1], b_par], writes=[b_kkr])
                    op("dve", lambda e: e.tensor_tensor(t1[:, N_], kkr[:, N_], kkr[:, N_], ALU.mult), reads=[b_kkr], writes=[b_t1])
                    op("pe", lambda e: e.matmul(p_a[:, N_], cm[:, C_BONES, :], t1[:, N_], start=True, stop=True), reads=[b_cm, b_t1], writes=[b_pa])
                    op("act", lambda e: e.activation(t1[:, N_], p_a[:, N_], AF.Sqrt, bias=EPS12, scale=1.0), reads=[b_pa, b_cst], writes=[b_t1])
                    op("dve", lambda e: e.reciprocal(t1[:, N_], t1[:, N_]), reads=[b_t1], writes=[b_t1])
                    op("dve", lambda e: e.tensor_tensor(kk[:, N_], kkr[:, N_], t1[:, N_], ALU.mult), reads=[b_kkr, b_t1], writes=[b_kk])
                    for sub in range(nsub):
                        q_ = sub % 4
                        op("pe", lambda e: e.transpose(p_tq[q_][:, 0:128], vs[:, sub * 128:(sub + 1) * 128], ident),
                           reads=[b_sh[2], b_cm], writes=[b_pt[q_]])
                        op("dve", lambda e: e.tensor_copy(tm[0][sub][:, j * 128:(j + 1) * 128], p_tq[q_][:, 0:128]), reads=[b_pt[q_]], writes=[b_tm[0][sub]])
                    for d in range(2):
                        ds = slice(d * 64, (d + 1) * 64)
                        js = slice(j * 128, (j + 1) * 128)
                        op("pe", lambda e: e.matmul(p_a[:, N_], w2t[ds, js], tw[ds, N_], start=True, stop=True), reads=[b_par, b_tw], writes=[b_pa])
                        op("act", lambda e: e.activation(lgw[:, N_], p_a[:, N_], AF.Sigmoid, bias=w0t[:, j, d:d + 1], scale=1.0), reads=[b_pa, b_par], writes=[b_lgw])
                        op("dve", lambda e: e.tensor_scalar(lgw[:, N_], lgw[:, N_], -0.6065306597126334, None, ALU.mult), reads=[b_lgw], writes=[b_lgw])
                        op("pe", lambda e: e.matmul(p_b[:, N_], a2t[ds, js], als[ds, N_], start=True, stop=True), reads=[b_par, b_als], writes=[b_pb])
                        op("act", lambda e: e.activation(av[:, N_], p_b[:, N_], AF.Sigmoid, bias=a0t[:, j, d:d + 1], scale=1.0), reads=[b_pb, b_par], writes=[b_av])
                        op("dve", lambda e: e.tensor_scalar(t1[:, N_], av[:, N_], kvt[:, j, 1:2], omka[:, j:j + 1], ALU.mult, ALU.add), reads=[b_av, b_par], writes=[b_t1])
                        op("dve", lambda e: e.tensor_tensor(kd[d][:, N_], t1[:, N_], ks[:, N_], ALU.mult), reads=[b_t1, b_sh[1]], writes=[b_kd[d]])
                        op("dve", lambda e: e.tensor_tensor(bv[:, N_], kk[:, N_], av[:, N_], ALU.mult), reads=[b_kk, b_av], writes=[b_bv])
                        tri = cm[:, C_TF if d == 0 else C_TB, :]
                        for sub in range(nsub):
                            ss_ = slice(sub * 128, (sub + 1) * 128)
                            q_ = sub % 4
                            op("pe", lambda e: e.transpose(p_tq[q_][:, 0:128], lgw[:, ss_], ident), reads=[b_lgw, b_cm], writes=[b_pt[q_]])
                            op("dve", lambda e: e.tensor_copy(lt[:], p_tq[q_][:, 0:128]), reads=[b_pt[q_]], writes=[b_lt])
                            op("pe", lambda e: e.matmul(p_cw[:, ss_], lt[:], tri, start=True, stop=True), reads=[b_lt, b_cm], writes=[b_pcw])
                        op("act", lambda e: e.activation(Ecw[:, N_], p_cw[:, N_], AF.Exp), reads=[b_pcw], writes=[b_Ecw])
                        op("act", lambda e: e.activation(Einv[:, N_], p_cw[:, N_], AF.Exp, scale=-1.0), reads=[b_pcw], writes=[b_Einv])
                        op("dve", lambda e: e.tensor_tensor(t1[:, N_], p_cw[:, N_], lgw[:, N_], ALU.subtract), reads=[b_pcw, b_lgw], writes=[b_t1])
                        op("act", lambda e: e.activation(Eex[:, N_], t1[:, N_], AF.Exp), reads=[b_t1], writes=[b_Eex])
                        lastc = 127 if d == 0 else 0
                        for sub in range(nsub):
                            op("dve", lambda e: e.tensor_copy(wc[d][:, j, blk0 + sub:blk0 + sub + 1], Ecw[:, sub * 128 + lastc:sub * 128 + lastc + 1]),
                               reads=[b_Ecw], writes=[b_wc[d]])
                        prods = [(kk, b_kk, Eex, b_Eex, AL[d], "AL%d" % d), (bv, b_bv, Einv, b_Einv, BE[d], "BE%d" % d),
                                 (kd[d], b_kd[d], Einv, b_Einv, KA[d], "KA%d" % d)]
                        if not isctx:
                            prods.append((rs, b_sh[0], Ecw, b_Ecw, RH[d], "RH%d" % d))
                        for pi_, (x_, bx_, y_, by_, dst, nm) in enumerate(prods):
                            op("dve", lambda e: e.tensor_tensor(res[pi_][:, N_], x_[:, N_], y_[:, N_], ALU.mult), reads=[bx_, by_], writes=[b_res[pi_]])
                            if not (KX & 64):
                                fw.dma(dst[js, g0:g0 + ntok], res[pi_][:, N_], reads=[b_res[pi_]], writes=[sbuf_of(nm)], q=SQ)
                            if pi_ in (1, 2):
                                dstT = BEt[d] if pi_ == 1 else KAt[d]
                                nmT = ("BEt%d" if pi_ == 1 else "KAt%d") % d
                                for sub in range(nsub):
                                    q_ = sub % 4
                                    srcT = kk if (KX & 128) else res[pi_]
                                    op("pe", lambda e: e.transpose(p_tq[q_][:, 0:128], srcT[:, sub * 128:(sub + 1) * 128], ident),
                                       reads=[b_res[pi_], b_cm], writes=[b_pt[q_]])
                                    ta = 1 + 2 * d + (pi_ - 1)
                                    op("dve", lambda e: e.tensor_copy(tm[ta][sub][:, js], p_tq[q_][:, 0:128]), reads=[b_pt[q_]], writes=[b_tm[ta][sub]])
                    if not isctx:
                        op("dve", lambda e: e.tensor_tensor(t1[:], kd[0][:], kd[1][:], ALU.add), reads=[b_kd[0], b_kd[1]], writes=[b_t1])
                        op("dve", lambda e: e.scalar_tensor_tensor(t1[:], t1[:], kvt[:, j, 2:3], rs[:], ALU.mult, ALU.mult), reads=[b_t1, b_par, b_sh[0]], writes=[b_t1])
                        op("pe", lambda e: e.matmul(p_s[:], cm[:, C_BONES, :], t1[:], start=True, stop=True), reads=[b_cm, b_t1], writes=[b_ps])
                        op("dve", lambda e: e.tensor_tensor(res[3][:], p_s[:], vs[:], ALU.mult), reads=[b_ps, b_sh[2]], writes=[b_res[3]])
                        fw.dma(BON[j * 128:(j + 1) * 128, g0 - CTX:g0 - CTX + 512], res[3][:], reads=[b_res[3]], writes=[sbuf_of("BON")], q=SQ)
                for sub in range(nsub if not (KX & 16) else 0):
                    rows = slice(g0 + sub * 128, g0 + (sub + 1) * 128)
                    for ta, (dstT, nmT) in enumerate([(Vt, "Vt"), (BEt[0], "BEt0"), (KAt[0], "KAt0"), (BEt[1], "BEt1"), (KAt[1], "KAt1")]):
                        fw.dma(dstT[rows, :], tm[ta][sub][:], reads=[b_tm[ta][sub]], writes=[sbuf_of(nmT)], q=SQ)
        fw.barrier()

        for ph in _phase(4):
            J = 8
            aT = [sbt(ph, "aT%d" % j, [128, 128]) for j in range(J)]
            bT = [sbt(ph, "bT%d" % j, [128, 128]) for j in range(J)]
            kT = [sbt(ph, "kT%d" % j, [128, 128]) for j in range(J)]
            rT = [sbt(ph, "rT%d" % j, [128, 128]) for j in range(J)]
            btkA = sbt(ph, "btkA", [128, RWW]); ktkA = sbt(ph, "ktkA", [128, RWW]); vtkA = sbt(ph, "vtkA", [128, RWW])
            btk = [btkA[:, j * 128:(j + 1) * 128] for j in range(J)]
            ktk = [ktkA[:, j * 128:(j + 1) * 128] for j in range(J)]
            vtk = [vtkA[:, j * 128:(j + 1) * 128] for j in range(J)]
            b_tokA = [Buf(), Buf(), Buf()]
            b_in = [[Buf() for _ in range(7)] for _ in range(J)]
            Pm = [[sbt(ph, "Pm%d_%d" % (j, i), [128, 2, 128]) for i in range(2)] for j in range(J)]
            PTm = [[sbt(ph, "PTm%d_%d" % (j, i), [128, 2, 128]) for i in range(2)] for j in range(J)]
            TTm = [[sbt(ph, "TTm%d_%d" % (j, i), [128, 2, 128]) for i in range(2)] for j in range(J)]
            b_P = [[Buf(), Buf()] for _ in range(J)]; b_PT = [[Buf(), Buf()] for _ in range(J)]; b_TT = [[Buf(), Buf()] for _ in range(J)]
            AkT = [sbt(ph, "AkT%d" % j, [128, 2, 128]) for j in range(J)]
            BbT = [sbt(ph, "BbT%d" % j, [128, 2, 128]) for j in range(J)]
            BkT = [sbt(ph, "BkT%d" % j, [128, 2, 128]) for j in range(J)]
            b_AkT = [Buf() for _ in range(J)]; b_BbT = [Buf() for _ in range(J)]; b_BkT = [Buf() for _ in range(J)]
            St = [[sbt(ph, "St%d_%d" % (j, i), [128, 64]) for i in range(2)] for j in range(J)]
            b_St = [[Buf(), Buf()] for _ in range(J)]
            Rn = [sbt(ph, "Rn%d" % i, [128, 2, 64]) for i in range(2)]; b_Rn = [Buf(), Buf()]
            Us = [sbt(ph, "Us%d" % i, [128, 2, 64]) for i in range(2)]; b_Us = [Buf(), Buf()]
            stt_ = [sbt(ph, "stt%d" % i, [128, 64]) for i in range(2)]; b_stt = [Buf(), Buf()]
            Ys = [sbt(ph, "Ys%d" % i, [128, 2, 64]) for i in range(2)]; b_Ys = [Buf(), Buf()]
            Yf = [sbt(ph, "Yf%d" % i, [128, 2, 64]) for i in range(2)]; b_Yf = [Buf(), Buf()]
            cen = [sbt(ph, "cen%d" % i, [128, 2, 64]) for i in range(2)]; b_cen = [Buf(), Buf()]
            gsq = [sbt(ph, "gsq%d" % i, [128, 2, 64]) for i in range(2)]; b_gsq = [Buf(), Buf()]
            gst = [sbt(ph, "gst%d" % i, [128, 4]) for i in range(2)]; b_gst = [Buf(), Buf()]
            bon = [sbt(ph, "bon%d" % i, [128, 128]) for i in range(2)]; zr = [sbt(ph, "zr%d" % i, [128, 128]) for i in range(2)]
            b_bon = [Buf(), Buf()]; b_zr = [Buf(), Buf()]
            yo = [sbt(ph, "yo%d" % i, [128, 128]) for i in range(2)]; b_yo = [Buf(), Buf()]
            gng = sbt(ph, "gng", [128, RWW]); gnb = sbt(ph, "gnb", [128, RWW]); b_gn = Buf()
            fw.dma(gng[:], gng_b, writes=[b_gn]); fw.dma(gnb[:], gnb_b, writes=[b_gn])
            G = [pst(ph, "G%d" % i, [128, 1024]) for i in range(4)]
            b_G = [[Buf(), Buf()] for _ in range(4)]

            def gv(g, q):
                return G[g][:].rearrange("p (h q s) -> p h q s", h=2, q=4)[:, :, q, :]

            def gs(g, q):
                return G[g][:].rearrange("p (h c) -> p h c", h=2)[:, :, q * 64:(q + 1) * 64]

            KB = int(os.environ.get("KB", "99"))
            for d in range(min(2, KD)):
                m_lo = cm[:, C_GT if d == 0 else C_LT, :]
                m_up = cm[:, C_LT if d == 0 else C_GT, :]
                m_upi = cm[:, C_TF if d == 0 else C_TB, :]
                bc3 = lambda m: m.unsqueeze(1).to_broadcast([128, 2, 128])
                order = list(range(18)) if d == 0 else [1, 0] + list(range(17, 1, -1))
                cur = 0
                for j in range(J):
                    op("dve", lambda e: e.memset(St[j][0][:], 0.0), writes=[b_St[j][0]])
                for blk in order[:KB]:
                    isctx = blk < 2
                    g0 = blk * 128
                    x0 = g0 - CTX
                    tsl = slice(g0, g0 + 128)
                    fw.dma(btkA[:], BEt[d][tsl, :], reads=[sbuf_of("BEt%d" % d)], writes=[b_tokA[0]])
                    fw.dma(ktkA[:], KAt[d][tsl, :], reads=[sbuf_of("KAt%d" % d)], writes=[b_tokA[1]])
                    fw.dma(vtkA[:], Vt[tsl, :], reads=[sbuf_of("Vt")], writes=[b_tokA[2]])
                    for j in range(J):
                        js = slice(j * 128, (j + 1) * 128)
                        bi = b_in[j]
                        fw.dma(aT[j][:], AL[d][js, tsl], reads=[sbuf_of("AL%d" % d)], writes=[bi[0]])
                        fw.dma(bT[j][:], BE[d][js, tsl], reads=[sbuf_of("BE%d" % d)], writes=[bi[1]])
                        fw.dma(kT[j][:], KA[d][js, tsl], reads=[sbuf_of("KA%d" % d)], writes=[bi[2]])
                        if not isctx:
                            fw.dma(rT[j][:], RH[d][js, tsl], reads=[sbuf_of("RH%d" % d)], writes=[bi[3]])
                        bi[4], bi[5], bi[6] = b_tokA
                    for j in range(J):
                        bi = b_in[j]
                        s0, s1 = (0, 1) if j % 2 == 0 else (2, 3)
                        for h in range(2):
                            hs = slice(h * 64, (h + 1) * 64)
                            op("pe", lambda e: e.matmul(gv(s0, 0)[:, h, :], aT[j][hs, :], bT[j][hs, :], start=True, stop=True),
                               reads=[bi[0], bi[1]], writes=[b_G[s0][h]])
                            op("pe", lambda e: e.matmul(gv(s0, 1)[:, h, :], bT[j][hs, :], aT[j][hs, :], start=True, stop=True),
                               reads=[bi[0], bi[1]], writes=[b_G[s0][h]])
                            op("pe", lambda e: e.matmul(gv(s0, 2)[:, h, :], kT[j][hs, :], aT[j][hs, :], start=True, stop=True),
                               reads=[bi[0], bi[2]], writes=[b_G[s0][h]])
                            if not isctx:
                                op("pe", lambda e: e.matmul(gv(s0, 3)[:, h, :], bT[j][hs, :], rT[j][hs, :], start=True, stop=True),
                                   reads=[bi[1], bi[3]], writes=[b_G[s0][h]])
                                op("pe", lambda e: e.matmul(gv(s1, 0)[:, h, :], kT[j][hs, :], rT[j][hs, :], start=True, stop=True),
                                   reads=[bi[2], bi[3]], writes=[b_G[s1][h]])
                        op("dve", lambda e: e.scalar_tensor_tensor(Pm[j][0][:], gv(s0, 0), -1.0, bc3(m_lo), ALU.mult, ALU.mult),
                           reads=b_G[s0] + [b_cm], writes=[b_P[j][0]])
                        op("dve", lambda e: e.scalar_tensor_tensor(PTm[j][0][:], gv(s0, 1), -1.0, bc3(m_up), ALU.mult, ALU.mult),
                           reads=b_G[s0] + [b_cm], writes=[b_PT[j][0]])
                        op("dve", lambda e: e.tensor_tensor(TTm[j][0][:], PTm[j][0][:], bc3(ident), ALU.add),
                           reads=[b_PT[j][0], b_cm], writes=[b_TT[j][0]])
                        op("dve", lambda e: e.tensor_tensor(AkT[j][:], gv(s0, 2), bc3(m_up), ALU.mult),
                           reads=b_G[s0] + [b_cm], writes=[b_AkT[j]])
                        if not isctx:
                            op("dve", lambda e: e.tensor_tensor(BbT[j][:], gv(s0, 3), bc3(m_upi), ALU.mult),
                               reads=b_G[s0] + [b_cm], writes=[b_BbT[j]])
                            op("dve", lambda e: e.tensor_tensor(BkT[j][:], gv(s1, 0), bc3(m_upi), ALU.mult),
                               reads=b_G[s1] + [b_cm], writes=[b_BkT[j]])
                    for i in range(1, 7):
                        a_, n_ = (i - 1) % 2, i % 2
                        for j in range(J):
                            cg = j % 2
                            for h in range(2):
                                op("pe", lambda e: e.matmul(gv(cg, 1)[:, h, :], PTm[j][a_][:, h, :], Pm[j][a_][:, h, :], start=True, stop=True),
                                   reads=[b_P[j][a_], b_PT[j][a_]], writes=[b_G[cg][h]])
                                if i < 6:
                                    op("pe", lambda e: e.matmul(gv(cg, 2)[:, h, :], Pm[j][a_][:, h, :], PTm[j][a_][:, h, :], start=True, stop=True),
                                       reads=[b_P[j][a_], b_PT[j][a_]], writes=[b_G[cg][h]])
                            op("dve", lambda e: e.tensor_copy(Pm[j][n_][:], gv(cg, 1)), reads=b_G[cg], writes=[b_P[j][n_]])
                            if i < 6:
                                op("dve", lambda e: e.tensor_copy(PTm[j][n_][:], gv(cg, 2)), reads=b_G[cg], writes=[b_PT[j][n_]])
                            for h in range(2):
                                op("pe", lambda e: e.matmul(gv(cg, 3)[:, h, :], Pm[j][n_][:, h, :], TTm[j][a_][:, h, :], start=True, stop=True),
                                   reads=[b_P[j][n_], b_TT[j][a_]], writes=[b_G[cg][h]])
                            op("dve", lambda e: e.tensor_tensor(TTm[j][n_][:], gv(cg, 3), TTm[j][a_][:], ALU.add),
                               reads=b_G[cg] + [b_TT[j][a_]], writes=[b_TT[j][n_]])
                    TTf = 0
                    nxt = 1 - cur
                    for j in range(J):
                        pr = j % 2
                        bi = b_in[j]
                        js = slice(j * 128, (j + 1) * 128)
                        sg, tg = (2, 3) if j % 2 == 0 else (0, 1)
                        Rv, Uv, Yv = gs(sg, 0), gs(sg, 1), gs(sg, 2)
                        for h in range(2):
                            hs = slice(h * 64, (h + 1) * 64)
                            op("pe", lambda e: e.matmul(Rv[:, h, :], aT[j][hs, :], St[j][cur][hs, :], start=True, stop=False),
                               reads=[bi[0], b_St[j][cur]], writes=[b_G[sg][h]])
                            op("pe", lambda e: e.matmul(Rv[:, h, :], AkT[j][:, h, :], vtk[j][:, hs], start=False, stop=True),
                               reads=[b_AkT[j], bi[6]], writes=[b_G[sg][h]])
                        op("dve", lambda e: e.tensor_scalar(Rn[pr][:], Rv, -1.0, None, ALU.mult), reads=b_G[sg], writes=[b_Rn[pr]])
                        for h in range(2):
                            op("pe", lambda e: e.matmul(Uv[:, h, :], TTm[j][TTf][:, h, :], Rn[pr][:, h, :], start=True, stop=True),
                               reads=[b_TT[j][TTf], b_Rn[pr]], writes=[b_G[sg][h]])
                        op("dve", lambda e: e.tensor_copy(Us[pr][:], Uv), reads=b_G[sg], writes=[b_Us[pr]])
                        SSv = G[tg][:, 0:128]
                        op("pe", lambda e: e.matmul(SSv, btk[j], Us[pr][:].rearrange("p h v -> p (h v)"), start=True, stop=False),
                           reads=[bi[4], b_Us[pr]], writes=[b_G[tg][0]])
                        op("pe", lambda e: e.matmul(SSv, ktk[j], vtk[j], start=False, stop=True),
                           reads=[bi[5], bi[6]], writes=[b_G[tg][0]])
                        for h in range(2):
                            hs = slice(h * 64, (h + 1) * 64)
                            op("dve", lambda e: e.tensor_tensor(stt_[pr][hs, :], SSv[hs, h * 64:(h + 1) * 64], St[j][cur][hs, :], ALU.add),
                               reads=[b_G[tg][0], b_St[j][cur]], writes=[b_stt[pr]])
                        op("dve", lambda e: e.tensor_scalar(St[j][nxt][:], stt_[pr][:], wc[d][:, j, blk:blk + 1], None, ALU.mult),
                           reads=[b_stt[pr], b_wc[d]], writes=[b_St[j][nxt]])
                        if isctx:
                            continue
                        for h in range(2):
                            hs = slice(h * 64, (h + 1) * 64)
                            op("pe", lambda e: e.matmul(Yv[:, h, :], rT[j][hs, :], St[j][cur][hs, :], start=True, stop=False),
                               reads=[bi[3], b_St[j][cur]], writes=[b_G[sg][h]])
                            op("pe", lambda e: e.matmul(Yv[:, h, :], BbT[j][:, h, :], Us[pr][:, h, :], start=False, stop=False),
                               reads=[b_BbT[j], b_Us[pr]], writes=[b_G[sg][h]])
                            op("pe", lambda e: e.matmul(Yv[:, h, :], BkT[j][:, h, :], vtk[j][:, hs], start=False, stop=True),
                               reads=[b_BkT[j], bi[6]], writes=[b_G[sg][h]])
                        if d == 0:
                            op("dve", lambda e: e.tensor_copy(Ys[pr][:], Yv), reads=b_G[sg], writes=[b_Ys[pr]])
                            fw.dma(YF[x0:x0 + 128, js], Ys[pr][:].rearrange("p h v -> p (h v)"), reads=[b_Ys[pr]], writes=[sbuf_of("YF")], q=SQ)
                        else:
                            fw.dma(Yf[pr][:].rearrange("p h v -> p (h v)"), YF[x0:x0 + 128, js], reads=[sbuf_of("YF")], writes=[b_Yf[pr]])
                            fw.dma(bon[pr][:], BON[js, x0:x0 + 128], reads=[sbuf_of("BON")], writes=[b_bon[pr]])
                            fw.dma(zr[pr][:], RWfm[(26 + j) * 128:(27 + j) * 128, g0:g0 + 128], reads=[sbuf_of("RWfm")], writes=[b_zr[pr]])
                            op("dve", lambda e: e.tensor_tensor(Ys[pr][:], Yv, Yf[pr][:], ALU.add), reads=b_G[sg] + [b_Yf[pr]], writes=[b_Ys[pr]])
                            g_ = gst[pr]
                            op("dve", lambda e: e.tensor_reduce(g_[:, 0:2], Ys[pr][:], AX.X, ALU.add), reads=[b_Ys[pr]], writes=[b_gst[pr]])
                            op("dve", lambda e: e.tensor_scalar(g_[:, 0:2], g_[:, 0:2], -1.0 / 64, None, ALU.mult), reads=[b_gst[pr]], writes=[b_gst[pr]])
                            op("dve", lambda e: e.tensor_tensor(cen[pr][:], Ys[pr][:], g_[:, 0:2].unsqueeze(2).to_broadcast([128, 2, 64]), ALU.add),
                               reads=[b_Ys[pr], b_gst[pr]], writes=[b_cen[pr]])
                            op("dve", lambda e: e.tensor_tensor(gsq[pr][:], cen[pr][:], cen[pr][:], ALU.mult), reads=[b_cen[pr]], writes=[b_gsq[pr]])
                            op("dve", lambda e: e.tensor_reduce(g_[:, 2:4], gsq[pr][:], AX.X, ALU.add), reads=[b_gsq[pr]], writes=[b_gst[pr]])
                            op("act", lambda e: e.activation(g_[:, 2:4], g_[:, 2:4], AF.Sqrt, bias=EPSGN, scale=1.0 / 64), reads=[b_gst[pr], b_cst], writes=[b_gst[pr]])
                            op("dve", lambda e: e.reciprocal(g_[:, 2:4], g_[:, 2:4]), reads=[b_gst[pr]], writes=[b_gst[pr]])
                            op("dve", lambda e: e.tensor_tensor(cen[pr][:], cen[pr][:], g_[:, 2:4].unsqueeze(2).to_broadcast([128, 2, 64]), ALU.mult),
                               reads=[b_cen[pr], b_gst[pr]], writes=[b_cen[pr]])
                            cf = cen[pr][:].rearrange("p h v -> p (h v)")
                            op("dve", lambda e: e.tensor_tensor(cf, cf, gng[:, js], ALU.mult), reads=[b_cen[pr], b_gn], writes=[b_cen[pr]])
                            op("dve", lambda e: e.tensor_tensor(cf, cf, gnb[:, js], ALU.add), reads=[b_cen[pr], b_gn], writes=[b_cen[pr]])
                            yTv = G[tg][:, 512:640]
                            op("pe", lambda e: e.transpose(yTv, cf, ident), reads=[b_cen[pr], b_cm], writes=[b_G[tg][1]])
                            op("dve", lambda e: e.tensor_tensor(yo[pr][:], yTv, bon[pr][:], ALU.add), reads=[b_G[tg][1], b_bon[pr]], writes=[b_yo[pr]])
                            op("dve", lambda e: e.tensor_tensor(yo[pr][:], yo[pr][:], zr[pr][:], ALU.mult), reads=[b_yo[pr], b_zr[pr]], writes=[b_yo[pr]])
                            fw.dma(YR[js, x0:x0 + 128], yo[pr][:], reads=[b_yo[pr]], writes=[sbuf_of("YR")], q=SQ)
                    cur = nxt
        fw.barrier()

        for ph in _phase(5):
            TT_ = 256
            yh = sbt(ph, "yh", [128, 8, TT_]); yr = sbt(ph, "yr", [128, 8, TT_]); b_yh = Buf(); b_yr = Buf()
            gh = [sbt(ph, "gh%d" % i, [128, 4, TT_]) for i in range(2)]; gr = [sbt(ph, "gr%d" % i, [128, 4, TT_]) for i in range(2)]
            b_gh = [Buf(), Buf()]; b_gr = [Buf(), Buf()]
            mT = sbt(ph, "mT", [128, 16, TT_]); b_mT = Buf()
            whg = [sbt(ph, "whg0", [128, 8, 512])] * 2; wrw = [sbt(ph, "wrw0", [128, 8, 512])] * 2
            b_whg = [Buf()] * 2; b_wrw = [Buf()] * 2
            wo = [sbt(ph, "wo0", [128, 16, 512])] * 2; b_wo = [Buf()] * 2
            xr = [sbt(ph, "xr%d" % i, [128, D]) for i in range(2)]; b_xr = [Buf(), Buf()]
            xn = [sbt(ph, "xn%d" % i, [128, D]) for i in range(2)]; b_xn = [Buf(), Buf()]
            junk3 = sbt(ph, "junk3", [128, D]); b_j3 = Buf()
            ss3 = sbt(ph, "ss3", [128, 2]); b_ss3 = Buf()
            fgt_ = sbt(ph, "fgt_", [128, D]); b_fg = Buf()
            tmpm = [sbt(ph, "tmpm%d" % i, [128, TT_]) for i in range(2)]; b_tmpm = [Buf(), Buf()]
            pp1 = [pst(ph, "pp1_%d" % i, [128, 512]) for i in range(2)]; pp2 = [pst(ph, "pp2_%d" % i, [128, 512]) for i in range(2)]
            b_pp1 = [Buf(), Buf()]; b_pp2 = [Buf(), Buf()]
            po = [pst(ph, "po%d" % i, [128, 512]) for i in range(4)]; b_po = [Buf() for _ in range(4)]
            fw.dma(fgt_[:], fg_b, writes=[b_fg])
            whg_r = w_hg_o.rearrange("(k p) c -> p k c", p=128)
            wrw_r = w_rw_o.rearrange("(k p) c -> p k c", p=128)
            wo_r = w_o.rearrange("(k p) c -> p k c", p=128)
            YH_r = YH.rearrange("(k p) t -> p k t", p=128)
            YR_r = YR.rearrange("(k p) t -> p k t", p=128)
            G_r = RWfm[(74 - 40) * 128:, :].rearrange("(k p) t -> p k t", p=128)
            wi = 0
            woi = 0
            xi = 0
            for tt in range(SEQ // TT_):
                x0 = tt * TT_
                g0 = x0 + CTX
                fw.dma3(yh[:], YH_r[:, :, x0:x0 + TT_], 8, reads=[sbuf_of("YH")], writes=[b_yh])
                fw.dma3(yr[:], YR_r[:, :, x0:x0 + TT_], 8, reads=[sbuf_of("YR")], writes=[b_yr])
                for mg in range(4):
                    wb = wi % 2
                    wi += 1
                    cs = slice(mg * 512, (mg + 1) * 512)
                    fw.dma3(whg[wb][:], whg_r[:, :, cs], 8, writes=[b_whg[wb]])
                    fw.dma3(wrw[wb][:], wrw_r[:, :, cs], 8, writes=[b_wrw[wb]])
                    fw.dma3(gh[wb][:], G_r[:, mg * 4:(mg + 1) * 4, g0:g0 + TT_], 4, reads=[sbuf_of("RWfm")], writes=[b_gh[wb]])
                    fw.dma3(gr[wb][:], G_r[:, 16 + mg * 4:16 + (mg + 1) * 4, g0:g0 + TT_], 4, reads=[sbuf_of("RWfm")], writes=[b_gr[wb]])
                    for mm in range(4):
                        m = mg * 4 + mm
                        a = mm % 2
                        for k in range(8):
                            op("pe", lambda e: e.matmul(pp1[a][:, :TT_], whg[wb][:, k, mm * 128:(mm + 1) * 128], yh[:, k, :], start=(k == 0), stop=(k == 7)),
                               reads=[b_whg[wb], b_yh], writes=[b_pp1[a]], inc=(k == 7))
                        for k in range(8):
                            op("pe", lambda e: e.matmul(pp2[a][:, :TT_], wrw[wb][:, k, mm * 128:(mm + 1) * 128], yr[:, k, :], start=(k == 0), stop=(k == 7)),
                               reads=[b_wrw[wb], b_yr], writes=[b_pp2[a]], inc=(k == 7))
                        op("dve", lambda e: e.tensor_tensor(tmpm[a][:], pp1[a][:, :TT_], gh[wb][:, mm, :], ALU.mult), reads=[b_pp1[a], b_gh[wb]], writes=[b_tmpm[a]])
                        op("dve", lambda e: e.tensor_tensor(mT[:, m, :], pp2[a][:, :TT_], gr[wb][:, mm, :], ALU.mult), reads=[b_pp2[a], b_gr[wb]], writes=[b_mT])
                        op("dve", lambda e: e.tensor_tensor(mT[:, m, :], mT[:, m, :], tmpm[a][:], ALU.add), reads=[b_mT, b_tmpm[a]], writes=[b_mT])
                for sub in range(TT_ // 128):
                    xb = xi % 2
                    xi += 1
                    fw.dma(xr[xb][:], xc[g0 + sub * 128:g0 + (sub + 1) * 128, :], writes=[b_xr[xb]])
                for n in range(4):
                    ob_ = woi % 2
                    woi += 1
                    fw.dma3(wo[ob_][:], wo_r[:, :, n * 512:(n + 1) * 512], 16, writes=[b_wo[ob_]])
                    for sub in range(TT_ // 128):
                        xb = (xi - (TT_ // 128) + sub) % 2
                        a = (n * 2 + sub) % 4
                        for k in range(16):
                            op("pe", lambda e: e.matmul(po[a][:], mT[:, k, sub * 128:(sub + 1) * 128], wo[ob_][:, k, :], start=(k == 0), stop=(k == 15)),
                               reads=[b_mT, b_wo[ob_]], writes=[b_po[a]], inc=(k == 15))
                        ns = slice(n * 512, (n + 1) * 512)
                        op("dve", lambda e: e.tensor_tensor(xn[xb][:, ns], po[a][:], gate_b[:, ns], ALU.mult), reads=[b_po[a], b_gate], writes=[b_xn[xb]])
                        op("dve", lambda e: e.tensor_tensor(xn[xb][:, ns], xn[xb][:, ns], xr[xb][:, ns], ALU.add), reads=[b_xn[xb], b_xr[xb]], writes=[b_xn[xb]])
                for sub in range(TT_ // 128):
                    xb = (xi - (TT_ // 128) + sub) % 2
                    op("act", lambda e: e.activation(junk3[:], xn[xb][:], AF.Square, accum_out=ss3[:, 0:1]), reads=[b_xn[xb]], writes=[b_j3, b_ss3])
                    op("act", lambda e: e.activation(ss3[:, 1:2], ss3[:, 0:1], AF.Sqrt, bias=EPS6, scale=1.0 / D), reads=[b_ss3, b_cst], writes=[b_ss3])
                    op("dve", lambda e: e.reciprocal(ss3[:, 1:2], ss3[:, 1:2]), reads=[b_ss3], writes=[b_ss3])
                    op("dve", lambda e: e.scalar_tensor_tensor(xn[xb][:], xn[xb][:], ss3[:, 1:2], fgt_[:], ALU.mult, ALU.mult),
                       reads=[b_xn[xb], b_ss3, b_fg], writes=[b_xn[xb]])
                    fw.dma(out[x0 + sub * 128:x0 + (sub + 1) * 128, :], xn[xb][:], reads=[b_xn[xb]], writes=[sbuf_of("out")], q=SQ)
        fw.barrier()
    print("bass program built: %d instructions" % fw.ninstr, flush=True)
    dbg_names = ["HGtok", "RWfm", "OFs", "YH", "AL0", "BE0", "KA0", "RH0", "AL1", "BE1", "KA1", "RH1", "BEt0", "KAt0", "Vt", "BON", "YF", "YR"]
    return nc, dbg_names


def _host_inputs(b, inp):
    f = lambda a: np.ascontiguousarray(a, dtype=np.float32)
    fm = lambda v: f(np.asarray(v).reshape(-1, 128).T)
    bc = lambda v: f(np.broadcast_to(np.asarray(v).reshape(1, -1), (128, np.asarray(v).size)))
    m = {}
    m["xc"] = f(np.concatenate([inp["ctx"][b], inp["x"][b]], axis=0))
    m["cc"] = f(np.stack([fm(inp["c"][b]), fm(inp["c_ctx"])], axis=-1))
    m["ada_w"] = f(inp["ada_w"][0].reshape(16, 128, 3 * D))
    m["ada_b_fm"] = fm(inp["ada_b"][0])
    m["ada_b_g"] = f(inp["ada_b"][0][2 * D:].reshape(1, D))
    m["norm_g_fm"] = fm(inp["norm_g"][0])
    m["w_in"] = f(inp["w_in"][0])
    m["hg_lb_b"] = f(np.broadcast_to(inp["hg_lb"][None], (128, 2, 2, HGW)))
    m["hgng_b"] = bc(inp["hg_norm_g"][0])
    mu = inp["rw_mu"][0]
    m["mu_fm"] = f(mu.reshape(4, 26, 128).transpose(2, 1, 0))
    m["w0_fm"] = f(inp["rw_w0"][0].reshape(2, 8, 128).transpose(2, 1, 0))
    m["a0_fm"] = f(inp["rw_a0"][0].reshape(2, 8, 128).transpose(2, 1, 0))
    m["w2"] = f(inp["rw_w2"][0].reshape(128, RWW))
    m["a2"] = f(inp["rw_a2"][0].reshape(128, RWW))
    kv = np.stack([inp["rw_kk"][0], inp["rw_ka"][0], inp["rw_rk"][0]], axis=0)
    m["kvec_fm"] = f(kv.reshape(3, 8, 128).transpose(2, 1, 0))
    m["gng_b"] = bc(inp["rw_gn_g"][0])
    m["gnb_b"] = bc(inp["rw_gn_b"][0])
    m["w_hg_o"] = f(inp["w_hg_out"][0])
    m["w_rw_o"] = f(inp["w_rw_out"][0])
    m["w_o"] = f(inp["w_out"][0])
    m["fg_b"] = bc(inp["final_g"])
    m["cm"] = make_cm()
    cst = np.zeros((128, 8), np.float32)
    cst[:, 0] = 1e-6
    cst[:, 1] = 1e-12
    cst[:, 2] = 64e-5
    cst[:, 3] = 0.0
    cst[:, 4] = 1.0
    m["cst"] = cst
    return m


_LAST = {}


def kernel(**inputs):
    inp = {k: np.asarray(v) for k, v in inputs.items()}
    nb = inp["x"].shape[0]
    nc, dbg = build_program()
    in_maps = [_host_inputs(b, inp) for b in range(nb)]
    res = run_bass_kernel_spmd(nc, in_maps, core_ids=list(range(nb)))
    if DEBUG:
        _LAST["res"] = res
    return np.stack([np.asarray(r["out"], dtype=np.float32) for r in res.results], axis=0)
```
